# Optimizing a Trainium2 kernel written in Bass

```python
import math
import jax
import jax.numpy as jnp
from jax import lax
import numpy as np

D_MODEL = 1024
BATCH = 16
SEQ = 2048
DEPTH = 2

N_EVEN = (DEPTH + 1) // 2
N_ODD = DEPTH // 2
SB_HEADS = 8
SB_HEAD_DIM = 64
SB_WIDTH = SB_HEADS * SB_HEAD_DIM
QUERY_BLOCK = 128
POOL_WINDOWS = (2, 4, 8, 16)
POOL_WIDTH = D_MODEL - SB_WIDTH
POOL_GROUP = POOL_WIDTH // len(POOL_WINDOWS)
AB_IN_WIDTH = 3 * SB_WIDTH + POOL_WIDTH
SSM_WIDTH = D_MODEL
SSM_GROUP = 16
SSM_GROUPS = SSM_WIDTH // SSM_GROUP
SSM_STATE = 64
DT_MIN = 1e-3
DT_MAX = 1e-1
MEM_LEN = 256
XA_HEADS = 4
XA_HEAD_DIM = D_MODEL // XA_HEADS
D_FF = 2816
CONV_WIDTH = 3
EPS = 1e-6

kernel_name = "hybrid_stickbreak_pool_s5_block"


def rmsnorm(x, g):
    xf = x.astype(jnp.float32)
    xf = xf * lax.rsqrt(jnp.mean(xf * xf, axis=-1, keepdims=True) + EPS)
    return (xf * g.astype(jnp.float32)).astype(x.dtype)


def stick_breaking_attention(q, k, v):
    seq = q.shape[1]
    scale = q.shape[-1] ** -0.5
    outs = []
    for t0 in range(0, seq, QUERY_BLOCK):
        t1 = t0 + QUERY_BLOCK
        z = jnp.einsum('bqhd,bkhd->bhqk', q[:, t0:t1], k[:, :t1]).astype(jnp.float32) * scale
        causal = jnp.arange(t1)[None, :] < (t0 + jnp.arange(QUERY_BLOCK))[:, None]
        log_beta = jax.nn.log_sigmoid(z)
        log_keep = jnp.where(causal, log_beta - z, 0.0)
        after = lax.cumsum(log_keep, axis=3, reverse=True) - log_keep
        w = jnp.where(causal, jnp.exp(log_beta + after), 0.0)
        outs.append(jnp.einsum('bhqk,bkhd->bqhd', w.astype(v.dtype), v[:, :t1]))
    return jnp.concatenate(outs, axis=1)


def multiscale_pool(u, w_grp, scale):
    bsz, seq, _ = u.shape
    ug = u.astype(jnp.float32).reshape(bsz, seq, len(POOL_WINDOWS), POOL_GROUP)
    cs = jnp.concatenate([jnp.zeros_like(ug[:, :1]), jnp.cumsum(ug, axis=1)], axis=1)
    t = jnp.arange(seq)
    pooled = []
    for g, win in enumerate(POOL_WINDOWS):
        cs_g = cs[:, :, g]
        lo = jnp.maximum(t + 1 - win, 0)
        cnt = jnp.minimum(t + 1, win).astype(jnp.float32)[None, :, None]
        mean = (cs_g[:, 1:] - cs_g[:, lo]) / cnt
        pooled.append(mean - ug[:, :, g])
    p = jnp.stack(pooled, axis=2)
    y = jnp.einsum('bsgc,gcd->bsgd', p, w_grp.astype(jnp.float32)).reshape(bsz, seq, POOL_WIDTH)
    return (y * scale.astype(jnp.float32)).astype(u.dtype)


def _complex_linear_combine(left, right):
    a1r, a1i, b1r, b1i = left
    a2r, a2i, b2r, b2i = right
    ar = a1r * a2r - a1i * a2i
    ai = a1r * a2i + a1i * a2r
    br = a2r * b1r - a2i * b1i + b2r
    bi = a2r * b1i + a2i * b1r + b2i
    return (ar, ai, br, bi)


def s5_ssm(u, lam_re, lam_im, log_dt, b_re, b_im, c_re, c_im, d_skip):
    bsz, seq, _ = u.shape
    f32 = jnp.float32
    uf = u.astype(f32)
    ug = uf.reshape(bsz, seq, SSM_GROUPS, SSM_GROUP)
    lam_re = lam_re.astype(f32)
    lam_im = lam_im.astype(f32)
    dt = jnp.exp(log_dt.astype(f32))[:, None]
    mag = jnp.exp(lam_re * dt)
    ang = lam_im * dt
    lb_re = mag * jnp.cos(ang)
    lb_im = mag * jnp.sin(ang)
    n_re = lb_re - 1.0
    den = lam_re * lam_re + lam_im * lam_im
    coef_re = (n_re * lam_re + lb_im * lam_im) / den
    coef_im = (lb_im * lam_re - n_re * lam_im) / den
    b_re = b_re.astype(f32)
    b_im = b_im.astype(f32)
    bb_re = coef_re[..., None] * b_re - coef_im[..., None] * b_im
    bb_im = coef_re[..., None] * b_im + coef_im[..., None] * b_re
    bu_re = jnp.einsum('bsgc,gpc->bsgp', ug, bb_re)
    bu_im = jnp.einsum('bsgc,gpc->bsgp', ug, bb_im)
    a_re = jnp.broadcast_to(lb_re, (1, seq) + lb_re.shape)
    a_im = jnp.broadcast_to(lb_im, (1, seq) + lb_im.shape)
    _, _, h_re, h_im = lax.associative_scan(
        _complex_linear_combine, (a_re, a_im, bu_re, bu_im), axis=1)
    y = (jnp.einsum('bsgp,gcp->bsgc', h_re, c_re.astype(f32))
         - jnp.einsum('bsgp,gcp->bsgc', h_im, c_im.astype(f32)))
    return y.reshape(bsz, seq, SSM_WIDTH) + d_skip.astype(f32) * uf


def memory_cross_attention(h, mem_n, w_q, w_kv, w_o):
    bsz, seq, _ = h.shape
    m = mem_n.shape[1]
    q = (h @ w_q).reshape(bsz, seq, XA_HEADS, XA_HEAD_DIM)
    k, v = jnp.split(mem_n @ w_kv, 2, axis=-1)
    k = k.reshape(bsz, m, XA_HEADS, XA_HEAD_DIM)
    v = v.reshape(bsz, m, XA_HEADS, XA_HEAD_DIM)
    scores = jnp.einsum('bshd,bmhd->bhsm', q, k).astype(jnp.float32) * (XA_HEAD_DIM ** -0.5)
    p = jax.nn.softmax(scores, axis=-1).astype(v.dtype)
    o = jnp.einsum('bhsm,bmhd->bshd', p, v).reshape(bsz, seq, D_MODEL)
    return o @ w_o


def conv_gated_mlp(h, w_up, conv_w, conv_b, w_down):
    up = h @ w_up
    seq = up.shape[1]
    padded = jnp.pad(up, ((0, 0), (CONV_WIDTH - 1, 0), (0, 0)))
    conv = conv_b
    for i in range(CONV_WIDTH):
        conv = conv + conv_w[i] * padded[:, i:i + seq]
    val, gate = jnp.split(conv, 2, axis=-1)
    return (jax.nn.silu(gate) * val) @ w_down


def setup_inputs(seed: int = 0) -> dict:
    key = jax.random.key(seed)
    ks = iter(jax.random.split(key, 40))

    def nrm(shape, fan_in):
        return jax.random.normal(next(ks), shape, jnp.float32) * (fan_in ** -0.5)

    def gain(shape):
        return 1.0 + 0.02 * jax.random.normal(next(ks), shape, jnp.float32)

    n_arange = jnp.arange(SSM_STATE, dtype=jnp.float32)
    return {
        "x": jax.random.normal(next(ks), (BATCH, SEQ, D_MODEL), jnp.float32),
        "mem": jax.random.normal(next(ks), (BATCH, MEM_LEN, D_MODEL), jnp.float32),
        "norm_mix": gain((DEPTH, D_MODEL)),
        "norm_xattn": gain((DEPTH, D_MODEL)),
        "norm_ffn": gain((DEPTH, D_MODEL)),
        "norm_mem": gain((D_MODEL,)),
        "norm_final": gain((D_MODEL,)),
        "ab_w_in": nrm((N_EVEN, D_MODEL, AB_IN_WIDTH), D_MODEL),
        "pool_w": nrm((N_EVEN, len(POOL_WINDOWS), POOL_GROUP, POOL_GROUP), POOL_GROUP),
        "pool_scale": gain((N_EVEN, POOL_WIDTH)),
        "ab_w_out": nrm((N_EVEN, SB_WIDTH + POOL_WIDTH, D_MODEL), SB_WIDTH + POOL_WIDTH),
        "ssm_w_in": nrm((N_ODD, D_MODEL, SSM_WIDTH), D_MODEL),
        "ssm_lam_re": -0.5 + 0.01 * jax.random.normal(next(ks), (N_ODD, SSM_GROUPS, SSM_STATE), jnp.float32),
        "ssm_lam_im": math.pi * n_arange + 0.01 * jax.random.normal(next(ks), (N_ODD, SSM_GROUPS, SSM_STATE), jnp.float32),
        "ssm_log_dt": jax.random.uniform(next(ks), (N_ODD, SSM_GROUPS), jnp.float32,
                                         math.log(DT_MIN), math.log(DT_MAX)),
        "ssm_b_re": nrm((N_ODD, SSM_GROUPS, SSM_STATE, SSM_GROUP), 2 * SSM_GROUP),
        "ssm_b_im": nrm((N_ODD, SSM_GROUPS, SSM_STATE, SSM_GROUP), 2 * SSM_GROUP),
        "ssm_c_re": nrm((N_ODD, SSM_GROUPS, SSM_GROUP, SSM_STATE), SSM_STATE),
        "ssm_c_im": nrm((N_ODD, SSM_GROUPS, SSM_GROUP, SSM_STATE), SSM_STATE),
        "ssm_d": jax.random.normal(next(ks), (N_ODD, SSM_WIDTH), jnp.float32),
        "ssm_w_glu": nrm((N_ODD, SSM_WIDTH, 2 * D_MODEL), SSM_WIDTH),
        "xa_w_q": nrm((DEPTH, D_MODEL, D_MODEL), D_MODEL),
        "xa_w_kv": nrm((DEPTH, D_MODEL, 2 * D_MODEL), D_MODEL),
        "xa_w_o": nrm((DEPTH, D_MODEL, D_MODEL), D_MODEL),
        "ffn_w_up": nrm((DEPTH, D_MODEL, 2 * D_FF), D_MODEL),
        "ffn_conv_w": nrm((DEPTH, CONV_WIDTH, 2 * D_FF), CONV_WIDTH),
        "ffn_conv_b": 0.01 * jax.random.normal(next(ks), (DEPTH, 2 * D_FF), jnp.float32),
        "ffn_w_down": nrm((DEPTH, D_FF, D_MODEL), D_FF),
    }


def reference(x, mem, norm_mix, norm_xattn, norm_ffn, norm_mem, norm_final,
              ab_w_in, pool_w, pool_scale, ab_w_out,
              ssm_w_in, ssm_lam_re, ssm_lam_im, ssm_log_dt, ssm_b_re, ssm_b_im,
              ssm_c_re, ssm_c_im, ssm_d, ssm_w_glu,
              xa_w_q, xa_w_kv, xa_w_o,
              ffn_w_up, ffn_conv_w, ffn_conv_b, ffn_w_down):
    bsz, seq, _ = x.shape
    mem_n = rmsnorm(mem, norm_mem)
    for layer in range(DEPTH):
        h = rmsnorm(x, norm_mix[layer])
        if layer % 2 == 0:
            e = layer // 2
            proj = h @ ab_w_in[e]
            q, k, v, u = jnp.split(proj, [SB_WIDTH, 2 * SB_WIDTH, 3 * SB_WIDTH], axis=-1)
            q = q.reshape(bsz, seq, SB_HEADS, SB_HEAD_DIM)
            k = k.reshape(bsz, seq, SB_HEADS, SB_HEAD_DIM)
            v = v.reshape(bsz, seq, SB_HEADS, SB_HEAD_DIM)
            a_out = stick_breaking_attention(q, k, v).reshape(bsz, seq, SB_WIDTH)
            p_out = multiscale_pool(u, pool_w[e], pool_scale[e])
            mix = jnp.concatenate([a_out, p_out], axis=-1) @ ab_w_out[e]
        else:
            o = layer // 2
            u = h @ ssm_w_in[o]
            y = s5_ssm(u, ssm_lam_re[o], ssm_lam_im[o], ssm_log_dt[o], ssm_b_re[o],
                       ssm_b_im[o], ssm_c_re[o], ssm_c_im[o], ssm_d[o])
            glu = jax.nn.gelu(y).astype(x.dtype) @ ssm_w_glu[o]
            val, gate = jnp.split(glu, 2, axis=-1)
            mix = val * jax.nn.sigmoid(gate)
        x = x + mix
        x = x + memory_cross_attention(rmsnorm(x, norm_xattn[layer]), mem_n,
                                       xa_w_q[layer], xa_w_kv[layer], xa_w_o[layer])
        x = x + conv_gated_mlp(rmsnorm(x, norm_ffn[layer]), ffn_w_up[layer],
                               ffn_conv_w[layer], ffn_conv_b[layer], ffn_w_down[layer])
    return rmsnorm(x, norm_final)
```

```python
import math
import numpy as np
import concourse.bass as bass
from concourse.ap import AP
import concourse.mybir as mybir
from concourse.bass_utils import run_bass_kernel_spmd

F32 = mybir.dt.float32
BF16 = mybir.dt.bfloat16
I32 = mybir.dt.int32
AF = mybir.ActivationFunctionType
ALU = mybir.AluOpType

COMPUTE = ("pe", "act", "dve", "pool")
ALLENG = ("pe", "act", "dve", "pool", "sp")
NDMA_SEMS = 40

S = 2048
D = 1024
NB = 2
DFF = 2816
NF = DFF // 128
MEM = 256


class Op:
    __slots__ = ("eng", "fn", "deps", "idx", "dma", "signal", "cnt", "sem", "clock")


class Prog:
    def __init__(self, nc):
        self.nc = nc
        self.ops = []
        self.last_write = {}
        self.readers = {}
        self.dma_hist = []
        self.n_dma = 0
        self.last_on = {}
        self.dma_since = []

    def add(self, eng, fn, reads=(), writes=(), dma=False, extra_deps=()):
        op = Op()
        op.eng, op.fn, op.dma = eng, fn, dma
        op.idx = len(self.ops)
        op.signal = False
        op.cnt = None
        op.sem = None
        op.clock = None
        deps = set(extra_deps)
        for k in reads:
            w = self.last_write.get(k)
            if w is not None:
                deps.add(w)
        for k in writes:
            w = self.last_write.get(k)
            if w is not None:
                deps.add(w)
            r = self.readers.get(k)
            if r:
                deps.update(r)
        for k in reads:
            self.readers.setdefault(k, []).append(op.idx)
        for k in writes:
            self.last_write[k] = op.idx
            self.readers[k] = []
        if dma:
            j = self.n_dma
            self.n_dma += 1
            op.sem = j % NDMA_SEMS
            if j >= NDMA_SEMS:
                deps.add(self.dma_hist[j - NDMA_SEMS])
            self.dma_hist.append(op.idx)
            self.dma_since.append(op.idx)
        else:
            self.last_on[eng] = op.idx
        deps.discard(op.idx)
        if eng == "pe" and not dma:
            deps = {d_ for d_ in deps if self.ops[d_].eng != "pe" or self.ops[d_].dma}
        op.deps = deps
        self.ops.append(op)
        return op

    def pe(self, fn, reads=(), writes=()):
        return self.add("pe", fn, reads, writes)

    def act(self, fn, reads=(), writes=()):
        return self.add("act", fn, reads, writes)

    def dve(self, fn, reads=(), writes=()):
        return self.add("dve", fn, reads, writes)

    def dma(self, fn, reads=(), writes=(), q="sp"):
        return self.add(q, fn, reads, writes, dma=True)

    def barrier(self):
        deps = set(self.last_on.values()) | set(self.dma_since)
        self.dma_since = []
        for e in ALLENG:
            self.add(e, lambda eng: eng.nop(), extra_deps=deps)
        self.last_write = {}
        self.readers = {}

    def emit(self, final_ops):
        nc = self.nc
        ops = self.ops
        for op in ops:
            for d in op.deps:
                ops[d].signal = True
        for op in final_ops:
            op.signal = True
        eng_cnt = {e: 0 for e in ALLENG}
        dma_cnt = [0] * NDMA_SEMS
        for op in ops:
            if op.dma:
                dma_cnt[op.sem] += 16
                op.cnt = dma_cnt[op.sem]
            elif op.signal:
                eng_cnt[op.eng] += 1
                op.cnt = eng_cnt[op.eng]
        sems = {e: nc.alloc_semaphore("s_" + e) for e in ALLENG}
        dsems = [nc.alloc_semaphore("d_%d" % i) for i in range(NDMA_SEMS)]

        def key_of(o):
            return ("d", o.sem) if o.dma else o.eng

        know = {e: {} for e in ALLENG}
        waits = {}
        for op in ops:
            K = know[op.eng]
            wl = []
            for d in sorted(op.deps, reverse=True):
                dop = ops[d]
                k = key_of(dop)
                if K.get(k, 0) >= dop.cnt:
                    continue
                for kk, vv in dop.clock.items():
                    if K.get(kk, 0) < vv:
                        K[kk] = vv
                K[k] = max(K.get(k, 0), dop.cnt)
                wl.append((dsems[dop.sem] if dop.dma else sems[dop.eng], dop.cnt))
            waits[op.idx] = wl
            if op.signal or op.dma:
                op.clock = dict(K)
        by_eng = {e: [] for e in ALLENG}
        for op in ops:
            by_eng[op.eng].append(op)
        fin = [(dsems[o.sem] if o.dma else sems[o.eng], o.cnt) for o in final_ops]
        self.n_inst = {e: len(by_eng[e]) for e in ALLENG}

        def run(engname, e):
            for op in by_eng[engname]:
                for (s, v) in waits[op.idx]:
                    e.wait_ge(s, v)
                ins = op.fn(e)
                if op.dma:
                    ins.then_inc(dsems[op.sem], 16)
                elif op.signal:
                    ins.then_inc(sems[op.eng], 1)
            if engname == "sp":
                for (s, v) in fin:
                    e.wait_ge(s, v)

        with nc.Block() as block:
            @block.tensor
            def _(e):
                run("pe", e)

            @block.scalar
            def _(e):
                run("act", e)

            @block.vector
            def _(e):
                run("dve", e)

            @block.gpsimd
            def _(e):
                run("pool", e)

            @block.sync
            def _(e):
                run("sp", e)


INPUT_SPECS = [
    ("x", [NB, S, D]), ("mem", [NB, MEM, D]),
    ("norm_mix", [2, D]), ("norm_xattn", [2, D]), ("norm_ffn", [2, D]), ("norm_mem", [D]), ("norm_final", [D]),
    ("ab_w_in", [1, D, 2048]), ("pool_w", [1, 4, 128, 128]), ("pool_scale", [1, 512]), ("ab_w_out", [1, D, D]),
    ("ssm_w_in", [1, D, D]), ("ssm_lam_re", [1, 64, 64]), ("ssm_lam_im", [1, 64, 64]), ("ssm_log_dt", [1, 64]),
    ("ssm_b_re", [1, 64, 64, 16]), ("ssm_b_im", [1, 64, 64, 16]), ("ssm_c_re", [1, 64, 16, 64]),
    ("ssm_c_im", [1, 64, 16, 64]), ("ssm_d", [1, D]), ("ssm_w_glu", [1, D, 2 * D]),
    ("xa_w_q", [2, D, D]), ("xa_w_kv", [2, D, 2 * D]), ("xa_w_o", [2, D, D]),
    ("ffn_w_up", [2, D, 2 * DFF]), ("ffn_conv_w", [2, 3, 2 * DFF]), ("ffn_conv_b", [2, 2 * DFF]),
    ("ffn_w_down", [2, DFF, D]),
    ("consts", [128, 12, 128]),
]


def make_consts():
    c = np.zeros((128, 12, 128), np.float32)
    j = np.arange(128)
    c[:, 0, :] = np.eye(128)
    c[:, 1, :] = -(j[:, None] > j[None, :]).astype(np.float32)
    c[:, 2, :] = -1.0
    c[:, 3, :] = 1.0
    c[:, 4, :] = (j[:, None] < j[None, :]).astype(np.float32)
    c[:, 5, :] = (j[:, None] // 32 == j[None, :] // 32).astype(np.float32)
    c[:, 6, :] = np.arange(128)[None, :]
    c[:, 7, :] = 128 + np.arange(128)[None, :]
    c[:, 8, :] = 1.0 / (1.0 + np.arange(128))[None, :]
    c[:, 9, :] = -(j[:, None] >= j[None, :]).astype(np.float32)
    c[:, 10, :] = -30000.0 * (j[:, None] >= j[None, :])
    return c


class Ctx:
    pass


_uid = [0]


def SBT(nc, name, shape, dt):
    _uid[0] += 1
    return nc.sbuf_tensor("%s_%d" % (name, _uid[0]), shape, dt)


def build_program(stop=None, nb=NB):
    nc = bass.Bass("TRN2", target_bir_lowering=False)
    P = Prog(nc)
    C = Ctx()
    C.nc, C.P = nc, P
    T = {}
    for name, shape in INPUT_SPECS:
        T[name] = nc.dram_tensor(name, shape, F32, kind="ExternalInput").ap()
    out = nc.dram_tensor("out", [NB, S, D], F32, kind="ExternalOutput").ap()
    C.T = T

    def sb(name, shape, dt=F32):
        return nc.alloc_sbuf_tensor(name, shape, dt)

    xT = sb("xT", [128, 8, S])
    cf = sb("cf", [128, 10, 128])
    cb = sb("cb", [128, 6, 128], BF16)
    gains = sb("gains", [128, 8, 8])
    convp = sb("convp", [128, 2, 4, 44])
    pscale = sb("pscale", [128, 4])
    zer = sb("zer", [128, 512], BF16)
    memT = sb("memT", [128, 8, MEM], BF16)
    ps = [nc.alloc_psum_tensor("ps%d" % i, [128, 512], F32) for i in range(8)]
    ident = cf[:, 0, :]
    maskstrict = cf[:, 4, :]

    def PSK(i):
        return ("ps", i)

    P.dma(lambda e: e.dma_start(out=cf[:], in_=T["consts"][:, 0:10, :]), writes=["cf"])
    P.dma(lambda e: e.dma_start(out=cb[:, 0:4, :], in_=T["consts"][:, 0:4, :]), writes=["cb"], q="pool")
    P.dma(lambda e: e.dma_start(out=cb[:, 4:6, :], in_=T["consts"][:, 9:11, :]), writes=["cb2"], q="pool")
    gsrc = [T["norm_mix"][0], T["norm_mix"][1], T["norm_xattn"][0], T["norm_xattn"][1],
            T["norm_ffn"][0], T["norm_ffn"][1], T["norm_mem"], T["norm_final"]]
    for i, g in enumerate(gsrc):
        P.dma(lambda e, i=i, g=g: e.dma_start(out=gains[:, i, :], in_=g.rearrange("(t p) -> p t", p=128),
                                             allow_slow_non_contiguous=True), writes=["gains"], q="act")
    for l in range(2):
        for i in range(3):
            P.dma(lambda e, l=l, i=i: e.dma_start(out=convp[:, l, i, :],
                                                  in_=T["ffn_conv_w"][l, i].rearrange("(t p) -> p t", p=128),
                                                  allow_slow_non_contiguous=True), writes=["convp"], q="act")
        P.dma(lambda e, l=l: e.dma_start(out=convp[:, l, 3, :],
                                         in_=T["ffn_conv_b"][l].rearrange("(t p) -> p t", p=128),
                                         allow_slow_non_contiguous=True), writes=["convp"], q="act")
    P.dma(lambda e: e.dma_start(out=pscale[:], in_=T["pool_scale"][0].rearrange("(t p) -> p t", p=128),
                                allow_slow_non_contiguous=True), writes=["pscale"], q="act")
    P.dve(lambda e: e.memset(zer[:], 0.0), writes=["zer"])

    def load_w(dst, src2d, key, k_tiles, col0, ncols):
        v = src2d.rearrange("(k p) n -> p k n", p=128)
        for k in range(k_tiles):
            P.dma(lambda e, k=k: e.dma_start(out=dst[:, k, :], in_=v[:, k, col0:col0 + ncols]),
                  writes=[(key, k)], q="pool")

    def rmsnorm_tile(hT, hkey, gi, t0, n, sq, rstd, part="all"):
        if part in ("all", "sq"):
            for dt in range(8):
                P.act(lambda e, dt=dt: e.activation(sq[:, dt, 0:n], xT[:, dt, t0:t0 + n], AF.Square),
                      reads=[("xT", dt)], writes=[("sq", dt)])
        if part == "sq":
            return
        def mm(e):
            for dt in range(8):
                r = e.matmul(ps[7][:, 0:n], lhsT=cb[:, 3, :], rhs=sq[:, dt, 0:n], start=(dt == 0), stop=(dt == 7))
            return r
        P.pe(mm, reads=[("sq", dt) for dt in range(8)] + ["cb"], writes=[PSK(7)])
        P.dve(lambda e: e.tensor_scalar(rstd[:, 0:n], ps[7][:, 0:n], 1.0 / D, 1e-6, ALU.mult, ALU.add),
              reads=[PSK(7)], writes=["rstd"])
        P.act(lambda e: e.activation(rstd[:, 0:n], rstd[:, 0:n], AF.Ln), reads=["rstd"], writes=["rstd"])
        P.act(lambda e: e.activation(rstd[:, 0:n], rstd[:, 0:n], AF.Exp, scale=-0.5), reads=["rstd"], writes=["rstd"])
        for dt in range(8):
            P.dve(lambda e, dt=dt: e.scalar_tensor_tensor(hT[:, dt, 0:n], xT[:, dt, t0:t0 + n], gains[:, gi, dt:dt + 1],
                                                          rstd[:, 0:n], ALU.mult, ALU.mult),
                  reads=[("xT", dt), "rstd", "gains"], writes=[(hkey, dt)])

    C.ps_rr = 0

    def linear_fm(w, wkey, act, akey, k_tiles, m_tiles, n, evac, a0=0, banks=(0, 1)):
        for m in range(m_tiles):
            bi = banks[C.ps_rr % len(banks)]
            C.ps_rr += 1
            def mm(e, m=m, bi=bi):
                for k in range(k_tiles):
                    r = e.matmul(ps[bi][:, 0:n], lhsT=w[:, k, m * 128:(m + 1) * 128], rhs=act[:, k, a0:a0 + n],
                                 start=(k == 0), stop=(k == k_tiles - 1))
                return r
            P.pe(mm, reads=[(wkey, k) for k in range(k_tiles)] + [(akey, k) for k in range(k_tiles)], writes=[PSK(bi)])
            evac(m, bi, ps[bi][:, 0:n])

    def resid_add(m, bi, pap, t0, n):
        P.dve(lambda e: e.tensor_tensor(xT[:, m, t0:t0 + n], xT[:, m, t0:t0 + n], pap, ALU.add),
              reads=[PSK(bi), ("xT", m)], writes=[("xT", m)])

    final_ops = []
    for b in range(nb):
        P.barrier()
        with SBT(nc, "xin", [128, 2, D], F32) as xin, SBT(nc, "sq", [128, 8, 512], BF16) as sq, \
                SBT(nc, "rstd", [128, 512], F32) as rstd, SBT(nc, "mn", [128, 2, D], F32) as mn, \
                SBT(nc, "ssq", [128, 4], F32) as ssq:
            for tt in range(16):
                xb = tt % 2
                P.dma(lambda e, tt=tt, xb=xb, b=b: e.dma_start(out=xin[:, xb, :], in_=T["x"][b, tt * 128:(tt + 1) * 128, :]),
                      writes=[("xin", xb)])
                for half in range(2):
                    bi = (tt * 2 + half) % 2
                    def tr(e, xb=xb, half=half, bi=bi):
                        for q in range(4):
                            dt = half * 4 + q
                            r = e.transpose(ps[bi][:, q * 128:(q + 1) * 128], xin[:, xb, dt * 128:(dt + 1) * 128], ident)
                        return r
                    P.pe(tr, reads=[("xin", xb), "cf"], writes=[PSK(bi)])
                    P.act(lambda e, tt=tt, half=half, bi=bi: e.activation(
                        xT[:, half * 4:half * 4 + 4, tt * 128:(tt + 1) * 128],
                        ps[bi][:].rearrange("p (q t) -> p q t", q=4), AF.Copy),
                        reads=[PSK(bi)], writes=[("xT", half * 4 + q) for q in range(4)])
            P.dma(lambda e, b=b: e.dma_start(out=mn[:], in_=T["mem"][b].rearrange("(t p) d -> p t d", p=128)), writes=["mn"])
            for t in range(2):
                P.act(lambda e, t=t: e.activation(xin[:, t, :], mn[:, t, :], AF.Square, accum_out=ssq[:, t:t + 1]),
                      reads=["mn"], writes=[("ssq", t), ("xin", t)])
            P.dve(lambda e: e.tensor_scalar(ssq[:, 2:4], ssq[:, 0:2], 1.0 / D, 1e-6, ALU.mult, ALU.add),
                  reads=[("ssq", 0), ("ssq", 1)], writes=["ssq2"])
            P.act(lambda e: e.activation(ssq[:, 2:4], ssq[:, 2:4], AF.Sqrt), reads=["ssq2"], writes=["ssq2"])
            P.dve(lambda e: e.reciprocal(ssq[:, 2:4], ssq[:, 2:4]), reads=["ssq2"], writes=["ssq2"])
            for t in range(2):
                P.dve(lambda e, t=t: e.tensor_scalar(mn[:, t, :], mn[:, t, :], ssq[:, 2 + t:3 + t], None, ALU.mult),
                      reads=["mn", "ssq2"], writes=["mn"])
            for t in range(2):
                for half in range(2):
                    bi = (t * 2 + half) % 2
                    def tr(e, t=t, half=half, bi=bi):
                        for q in range(4):
                            dt = half * 4 + q
                            r = e.transpose(ps[bi][:, q * 128:(q + 1) * 128], mn[:, t, dt * 128:(dt + 1) * 128], ident)
                        return r
                    P.pe(tr, reads=["mn", "cf"], writes=[PSK(bi)])
                    for q in range(4):
                        dt = half * 4 + q
                        P.dve(lambda e, t=t, q=q, dt=dt, bi=bi: e.tensor_scalar(
                            memT[:, dt, t * 128:(t + 1) * 128], ps[bi][:, q * 128:(q + 1) * 128],
                            gains[:, 6, dt:dt + 1], None, ALU.mult),
                            reads=[PSK(bi), "gains"], writes=[("memT", dt)])
        if stop == "load":
            pass
        else:
            for layer in range(2):
                if layer == 0:
                    stage_mix_ab(C, b, xT, ps, cf, cb, gains, pscale, zer, rmsnorm_tile, load_w, linear_fm, resid_add)
                else:
                    stage_mix_s5(C, b, xT, ps, cf, cb, gains, rmsnorm_tile, load_w, linear_fm, resid_add)
                if stop == "mix%d" % layer:
                    break
                stage_xattn(C, b, layer, xT, ps, cb, gains, memT, rmsnorm_tile, load_w, linear_fm, resid_add)
                if stop == "xa%d" % layer:
                    break
                stage_ffn(C, b, layer, xT, ps, gains, convp, rmsnorm_tile, load_w, linear_fm, resid_add)
                if stop == "ffn%d" % layer:
                    break
        P.barrier()
        with SBT(nc, "sq", [128, 8, 512], BF16) as sq, SBT(nc, "rstd", [128, 512], F32) as rstd, \
                SBT(nc, "yT", [128, 8, 512], F32) as yT, SBT(nc, "yo", [128, 2, D], F32) as yo:
            for tq in range(4):
                t0 = tq * 512
                if stop is None:
                    rmsnorm_tile(yT, "yT", 7, t0, 512, sq, rstd)
                else:
                    for dt in range(8):
                        P.act(lambda e, dt=dt, t0=t0: e.activation(yT[:, dt, :], xT[:, dt, t0:t0 + 512], AF.Copy),
                              reads=[("xT", dt)], writes=[("yT", dt)])
                for ts in range(4):
                    ob = ts % 2
                    for half in range(2):
                        bi = (ts * 2 + half) % 2
                        def tr(e, ts=ts, half=half, bi=bi):
                            for q in range(4):
                                dt = half * 4 + q
                                r = e.transpose(ps[bi][:, q * 128:(q + 1) * 128], yT[:, dt, ts * 128:(ts + 1) * 128], ident)
                            return r
                        P.pe(tr, reads=[("yT", dt) for dt in range(8)] + ["cf"], writes=[PSK(bi)])
                        P.act(lambda e, ob=ob, half=half, bi=bi: e.activation(yo[:, ob, half * 512:(half + 1) * 512],
                                                                               ps[bi][:], AF.Copy),
                              reads=[PSK(bi)], writes=[("yo", ob, half)])
                    tok = t0 + ts * 128
                    o = P.dma(lambda e, ob=ob, tok=tok, b=b: e.dma_start(out=out[b, tok:tok + 128, :], in_=yo[:, ob, :]),
                              reads=[("yo", ob, 0), ("yo", ob, 1)], writes=[("out", b, tok)])
                    final_ops.append(o)
    P.emit(final_ops)
    C.final = final_ops
    return nc, P


def stage_xattn(C, b, layer, xT, ps, cb, gains, memT, rmsnorm_tile, load_w, linear_fm, resid_add):
    nc, P, T = C.nc, C.P, C.T
    P.barrier()

    def PSK(i):
        return ("ps", i)
    with SBT(nc, "wq", [128, 8, D], BF16) as wq, SBT(nc, "wo", [128, 8, D], BF16) as wo, \
            SBT(nc, "wkv", [128, 8, D], BF16) as wkv, \
            SBT(nc, "KT", [128, 8, MEM], BF16) as KT, SBT(nc, "V", [128, 2, D], BF16) as V, \
            SBT(nc, "sq", [128, 8, 512], BF16) as sq, SBT(nc, "rstd", [128, 512], F32) as rstd, \
            SBT(nc, "hT", [128, 2, 8, 512], BF16) as hT, SBT(nc, "qT", [128, 8, 512], BF16) as qT, \
            SBT(nc, "pT", [128, 2, 2, 512], BF16) as pT, SBT(nc, "rs", [128, 2, 512], F32) as rs, \
            SBT(nc, "oT", [128, 8, 512], BF16) as oT:
        load_w(wkv, T["xa_w_kv"][layer], "wkv", 8, 0, D)
        load_w(wq, T["xa_w_q"][layer], "wq", 8, 0, D)

        def evK(m, bi, pap):
            P.act(lambda e: e.activation(KT[:, m, :], pap, AF.Copy), reads=[PSK(bi)], writes=[("KT", m)])
        linear_fm(wkv, "wkv", memT, "memT", 8, 8, MEM, evK)
        load_w(wkv, T["xa_w_kv"][layer], "wkv", 8, D, D)
        load_w(wo, T["xa_w_o"][layer], "wo", 8, 0, D)
        for mt in range(2):
            for nh in range(2):
                bi = (mt * 2 + nh) % 2
                def mm(e, mt=mt, nh=nh, bi=bi):
                    for k in range(8):
                        r = e.matmul(ps[bi][:], lhsT=memT[:, k, mt * 128:(mt + 1) * 128], rhs=wkv[:, k, nh * 512:(nh + 1) * 512],
                                     start=(k == 0), stop=(k == 7))
                    return r
                P.pe(mm, reads=[("wkv", k) for k in range(8)] + [("memT", k) for k in range(8)], writes=[PSK(bi)])
                P.act(lambda e, mt=mt, nh=nh, bi=bi: e.activation(V[:, mt, nh * 512:(nh + 1) * 512], ps[bi][:], AF.Copy),
                      reads=[PSK(bi)], writes=[("V", mt, nh)])
        rmsnorm_tile(hT[:, 0], "hT0", 2 + layer, 0, 512, sq, rstd)
        for tq in range(4):
            t0 = tq * 512
            tb = tq % 2

            def evQ(m, bi, pap):
                P.act(lambda e: e.activation(qT[:, m, :], pap, AF.Copy, scale=1.0 / 16.0), reads=[PSK(bi)], writes=[("qT", m)])
            linear_fm(wq, "wq", hT[:, tb], "hT%d" % tb, 8, 8, 512, evQ)
            def scores(h):
                hb = h % 2
                for mt in range(2):
                    bk = 2 + 2 * hb + mt
                    def mm(e, h=h, mt=mt, bk=bk):
                        for d in range(2):
                            r = e.matmul(ps[bk][:], lhsT=KT[:, 2 * h + d, mt * 128:(mt + 1) * 128], rhs=qT[:, 2 * h + d, :],
                                         start=(d == 0), stop=(d == 1))
                        return r
                    P.pe(mm, reads=[("KT", 2 * h), ("KT", 2 * h + 1), ("qT", 2 * h), ("qT", 2 * h + 1)], writes=[PSK(bk)])
                    P.act(lambda e, mt=mt, hb=hb, bk=bk: e.activation(pT[:, hb, mt, :], ps[bk][:], AF.Exp),
                          reads=[PSK(bk)], writes=[("pT", hb, mt)])

            def rest(h):
                hb = h % 2
                def mms(e, hb=hb):
                    e.matmul(ps[6][:], lhsT=cb[:, 3, :], rhs=pT[:, hb, 0, :], start=True, stop=False)
                    return e.matmul(ps[6][:], lhsT=cb[:, 3, :], rhs=pT[:, hb, 1, :], start=False, stop=True)
                P.pe(mms, reads=[("pT", hb, 0), ("pT", hb, 1), "cb"], writes=[PSK(6)])
                P.act(lambda e, hb=hb: e.activation(rs[:, hb, :], ps[6][:], AF.Ln), reads=[PSK(6)], writes=[("rs", hb)])
                P.act(lambda e, hb=hb: e.activation(rs[:, hb, :], rs[:, hb, :], AF.Exp, scale=-1.0), reads=[("rs", hb)], writes=[("rs", hb)])
                for d in range(2):
                    bi = 7 if d == 0 else 1
                    def mmo(e, h=h, hb=hb, d=d, bi=bi):
                        for mt in range(2):
                            r = e.matmul(ps[bi][:], lhsT=V[:, mt, h * 256 + d * 128:h * 256 + (d + 1) * 128], rhs=pT[:, hb, mt, :],
                                         start=(mt == 0), stop=(mt == 1))
                        return r
                    P.pe(mmo, reads=[("V", 0, h // 2), ("V", 1, h // 2), ("pT", hb, 0), ("pT", hb, 1)], writes=[PSK(bi)])
                    P.dve(lambda e, h=h, hb=hb, d=d, bi=bi: e.tensor_tensor(oT[:, 2 * h + d, :], ps[bi][:], rs[:, hb, :], ALU.mult),
                          reads=[PSK(bi), ("rs", hb)], writes=[("oT", 2 * h + d)])

            scores(0)
            for h in range(4):
                if h < 3:
                    scores(h + 1)
                rest(h)
            if tq + 1 < 4:
                rmsnorm_tile(hT[:, 1 - tb], "hT%d" % (1 - tb), 2 + layer, t0 + 512, 512, sq, rstd, part="sq")
            linear_fm(wo, "wo", oT, "oT", 8, 8, 512, lambda m, bi, pap, t0=t0: resid_add(m, bi, pap, t0, 512))
            if tq + 1 < 4:
                rmsnorm_tile(hT[:, 1 - tb], "hT%d" % (1 - tb), 2 + layer, t0 + 512, 512, sq, rstd, part="rest")


def stage_ffn(C, b, layer, xT, ps, gains, convp, rmsnorm_tile, load_w, linear_fm, resid_add):
    nc, P, T = C.nc, C.P, C.T
    P.barrier()

    def PSK(i):
        return ("ps", i)
    with SBT(nc, "sq", [128, 8, 512], BF16) as sq, SBT(nc, "rstd", [128, 512], F32) as rstd, \
            SBT(nc, "hT", [128, 2, 8, 512], BF16) as hT, SBT(nc, "gT", [128, NF, 512], BF16) as gT, \
            SBT(nc, "wu", [128, 2, 2, 8, 512], BF16) as wu, SBT(nc, "wd", [128, 4, D], BF16) as wd, \
            SBT(nc, "ub", [128, 3, 2, 516], F32) as ub, SBT(nc, "cv", [128, 3, 2, 512], F32) as cv, \
            SBT(nc, "halo", [128, 2 * NF, 2], F32) as halo:
        wup = T["ffn_w_up"][layer].rearrange("(k p) n -> p k n", p=128)
        wdn = T["ffn_w_down"][layer]
        P.dve(lambda e: e.memset(halo[:], 0.0), writes=["halo"])
        groups = [(0, 4), (4, 4), (8, 4), (12, 4), (16, 4), (20, 2)]
        it = 0
        git = 0
        kit = 0
        rmsnorm_tile(hT[:, 0], "hT0", 4 + layer, 0, 512, sq, rstd)
        pend = []

        def tail(pb, fp):
            P.act(lambda e: e.activation(cv[:, pb, 1, :], cv[:, pb, 1, :], AF.Silu),
                  reads=[("cv", pb, 1)], writes=[("cv", pb, 1)])
            P.dve(lambda e: e.tensor_tensor(gT[:, fp, :], cv[:, pb, 0, :], cv[:, pb, 1, :], ALU.mult),
                  reads=[("cv", pb, 0), ("cv", pb, 1)], writes=[("gT", fp)])
        for tq in range(4):
            t0 = tq * 512
            hb = tq % 2
            for (f0, nf) in groups:
                wb = git % 2
                git += 1
                for vg in range(2):
                    col0 = vg * DFF + f0 * 128
                    P.dma(lambda e, wb=wb, vg=vg, col0=col0, nf=nf: e.dma_start(out=wu[:, wb, vg, :, 0:nf * 128],
                                                                              in_=wup[:, :, col0:col0 + nf * 128]),
                          writes=[("wu", wb, vg)], q="pool")
                for fl in range(nf):
                    fp = f0 + fl
                    pb = it % 3
                    it += 1
                    for vg in range(2):
                        bi = 3 * vg + pb
                        f = vg * NF + fp
                        def mm(e, wb=wb, vg=vg, bi=bi, fl=fl, hb=hb):
                            for k in range(8):
                                r = e.matmul(ps[bi][:], lhsT=wu[:, wb, vg, k, fl * 128:(fl + 1) * 128], rhs=hT[:, hb, k, :], start=(k == 0), stop=(k == 7))
                            return r
                        P.pe(mm, reads=[("wu", wb, vg)] + [("hT%d" % hb, k) for k in range(8)], writes=[PSK(bi)])
                        P.act(lambda e, pb=pb, vg=vg, bi=bi: e.activation(ub[:, pb, vg, 2:514], ps[bi][:], AF.Copy),
                              reads=[PSK(bi)], writes=[("ub", pb, vg)])
                        P.act(lambda e, pb=pb, vg=vg, f=f: e.activation(ub[:, pb, vg, 0:2], halo[:, f, :], AF.Copy),
                              reads=["halo%d" % f, "halo"], writes=[("ubh", pb, vg)])
                        P.act(lambda e, pb=pb, vg=vg, f=f, bi=bi: e.activation(cv[:, pb, vg, :], ps[bi][:], AF.Identity,
                                                                               bias=convp[:, layer, 3, f:f + 1],
                                                                               scale=convp[:, layer, 2, f:f + 1]),
                              reads=[PSK(bi), "convp"], writes=[("cv", pb, vg)])
                        P.dve(lambda e, pb=pb, vg=vg, f=f: e.scalar_tensor_tensor(cv[:, pb, vg, :], ub[:, pb, vg, 1:513],
                                                                                  convp[:, layer, 1, f:f + 1], cv[:, pb, vg, :],
                                                                                  ALU.mult, ALU.add),
                              reads=[("ub", pb, vg), ("ubh", pb, vg), ("cv", pb, vg), "convp"], writes=[("cv", pb, vg)])
                        P.dve(lambda e, pb=pb, vg=vg, f=f: e.scalar_tensor_tensor(cv[:, pb, vg, :], ub[:, pb, vg, 0:512],
                                                                                  convp[:, layer, 0, f:f + 1], cv[:, pb, vg, :],
                                                                                  ALU.mult, ALU.add),
                              reads=[("ub", pb, vg), ("ubh", pb, vg), ("cv", pb, vg), "convp"], writes=[("cv", pb, vg)])
                        P.dve(lambda e, pb=pb, vg=vg, f=f: e.tensor_copy(halo[:, f, :], ub[:, pb, vg, 512:514]),
                              reads=[("ub", pb, vg)], writes=["halo%d" % f])
                    if pend:
                        tail(*pend.pop())
                    pend.append((pb, fp))
            if pend:
                tail(*pend.pop())
            if tq + 1 < 4:
                rmsnorm_tile(hT[:, 1 - hb], "hT%d" % (1 - hb), 4 + layer, t0 + 512, 512, sq, rstd, part="sq")
            for k in range(NF):
                db = kit % 4
                kit += 1
                P.dma(lambda e, db=db, k=k: e.dma_start(out=wd[:, db, :], in_=wdn[k * 128:(k + 1) * 128, :]),
                      writes=[("wd", db)], q="pool")
                def mm(e, db=db, k=k):
                    for m in range(8):
                        r = e.matmul(ps[m][:], lhsT=wd[:, db, m * 128:(m + 1) * 128], rhs=gT[:, k, :], start=(k == 0), stop=(k == NF - 1))
                    return r
                P.pe(mm, reads=[("wd", db), ("gT", k)], writes=[PSK(m) for m in range(8)])
            resid_add(7, 7, ps[7][:], t0, 512)
            if tq + 1 < 4:
                rmsnorm_tile(hT[:, 1 - hb], "hT%d" % (1 - hb), 4 + layer, t0 + 512, 512, sq, rstd, part="rest")
            for m in range(7):
                resid_add(m, m, ps[m][:], t0, 512)


def stage_mix_ab(C, b, xT, ps, cf, cb, gains, pscale, zer, rmsnorm_tile, load_w, linear_fm, resid_add):
    nc, P, T = C.nc, C.P, C.T
    P.barrier()
    maskstrict = cf[:, 4, :]

    def PSK(i):
        return ("ps", i)
    win = T["ab_w_in"][0]
    with SBT(nc, "hT", [128, 8, S], BF16) as hT, SBT(nc, "aT", [128, 4, S], BF16) as aT, \
            SBT(nc, "pTo", [128, 4, S], BF16) as pTo:
        with SBT(nc, "sq", [128, 8, 512], BF16) as sq, SBT(nc, "rstd", [128, 512], F32) as rstd, \
                SBT(nc, "hTt", [128, 8, 512], BF16) as hTt:
            for tq in range(4):
                rmsnorm_tile(hTt, "hTt", 0, tq * 512, 512, sq, rstd)
                for dt in range(8):
                    P.act(lambda e, dt=dt, tq=tq: e.activation(hT[:, dt, tq * 512:(tq + 1) * 512], hTt[:, dt, :], AF.Copy),
                          reads=[("hTt", dt)], writes=[("hT", dt)])
        P.barrier()
        with SBT(nc, "wu4", [128, 8, 512], BF16) as wu4, SBT(nc, "wp", [128, 128], BF16) as wp, \
                SBT(nc, "uA", [128, S], F32) as uA, SBT(nc, "uB", [128, S], F32) as uB, \
                SBT(nc, "u0", [128, S], F32) as u0, SBT(nc, "pb", [128, S], BF16) as pb:
            for g in range(4):
                w_ = 2 ** (g + 1)
                if g == 0:
                    load_w(wu4, win, "wu4", 8, 1536, 512)
                P.dma(lambda e, g=g: e.dma_start(out=wp[:], in_=T["pool_w"][0, g]), writes=["wp"], q="pool")
                for tq in range(4):
                    bi = tq % 2
                    def mm(e, tq=tq, bi=bi, g=g):
                        for k in range(8):
                            r = e.matmul(ps[bi][:], lhsT=wu4[:, k, g * 128:(g + 1) * 128], rhs=hT[:, k, tq * 512:(tq + 1) * 512], start=(k == 0), stop=(k == 7))
                        return r
                    P.pe(mm, reads=[("wu4", k) for k in range(8)] + [("hT", k) for k in range(8)], writes=[PSK(bi)])
                    P.act(lambda e, tq=tq, bi=bi: e.activation(u0[:, tq * 512:(tq + 1) * 512], ps[bi][:], AF.Copy),
                          reads=[PSK(bi)], writes=["u0"])
                src, srck = u0, "u0"
                bufs = [(uA, "uA"), (uB, "uB")]
                for st in range(g + 1):
                    sh = 2 ** st
                    dst, dstk = bufs[st % 2]
                    def stp(e, src=src, dst=dst, sh=sh):
                        e.tensor_copy(dst[:, 0:sh], src[:, 0:sh])
                        return e.tensor_tensor(dst[:, sh:S], src[:, sh:S], src[:, 0:S - sh], ALU.add)
                    P.dve(stp, reads=[srck], writes=[dstk])
                    src, srck = dst, dstk
                def pl(e, src=src, w_=w_):
                    e.scalar_tensor_tensor(pb[:, w_ - 1:S], src[:, w_ - 1:S], 1.0 / w_, u0[:, w_ - 1:S], ALU.mult, ALU.subtract)
                    return e.tensor_tensor(src[:, 0:w_ - 1], src[:, 0:w_ - 1], cf[:, 8, 0:w_ - 1], ALU.mult)
                P.dve(pl, reads=[srck, "u0", "cf"], writes=["pb0", srck])
                P.dve(lambda e, src=src, w_=w_: e.tensor_tensor(pb[:, 0:w_ - 1], src[:, 0:w_ - 1], u0[:, 0:w_ - 1], ALU.subtract),
                      reads=[srck, "u0"], writes=["pb1"])
                for tq in range(4):
                    bi = tq % 2
                    P.pe(lambda e, tq=tq, bi=bi: e.matmul(ps[bi][:], lhsT=wp[:], rhs=pb[:, tq * 512:(tq + 1) * 512], start=True, stop=True),
                         reads=["wp", "pb0", "pb1"], writes=[PSK(bi)])
                    P.act(lambda e, tq=tq, bi=bi, g=g: e.activation(pTo[:, g, tq * 512:(tq + 1) * 512], ps[bi][:], AF.Identity,
                                                                    scale=pscale[:, g:g + 1]),
                          reads=[PSK(bi), "pscale"], writes=[("pTo", g)])
        P.barrier()
        NBUF = 4
        with SBT(nc, "wqkv", [128, 8, 1536], BF16) as wqkv, SBT(nc, "qh", [64, S], BF16) as qh, \
                SBT(nc, "kh", [64, S], BF16) as kh, SBT(nc, "vh", [128, 16, 128], BF16) as vh, \
                SBT(nc, "ex", [128, NBUF, 512], F32) as ex, \
                SBT(nc, "spb", [128, NBUF, 512], BF16) as spb, \
                SBT(nc, "wsb", [128, NBUF, 512], BF16) as wsb, SBT(nc, "Ls", [128, 2, 512], F32) as Ls, \
                SBT(nc, "Lsb", [128, 4, 512], BF16) as Lsb, SBT(nc, "otmp", [64, 2, 512], BF16) as otmp:
            identb = cb[:, 0, :]
            trinc = cb[:, 4, :]
            maskneg = cb[:, 5, :]
            onesneg = cb[:, 2, :]
            git = 0
            for h in range(8):
                hp = h // 2
                if h == 0:
                    load_w(wqkv, win, "wqkv", 8, 0, 1536)
                for j3, (dst, dk, scl) in enumerate([(qh, "qh", 0.125), (kh, "kh", 1.0)]):
                    for tq in range(4):
                        bi = 4 + tq % 2
                        def mm(e, h=h, j3=j3, tq=tq, bi=bi):
                            for k in range(8):
                                r = e.matmul(ps[bi][0:64, :], lhsT=wqkv[:, k, j3 * 512 + h * 64:j3 * 512 + h * 64 + 64],
                                             rhs=hT[:, k, tq * 512:(tq + 1) * 512], start=(k == 0), stop=(k == 7))
                            return r
                        P.pe(mm, reads=[("wqkv", k) for k in range(8)] + [("hT", k) for k in range(8)], writes=[PSK(bi)])
                        P.act(lambda e, dst=dst, tq=tq, bi=bi, scl=scl: e.activation(dst[:, tq * 512:(tq + 1) * 512], ps[bi][0:64, :],
                                                                                   AF.Copy, scale=scl),
                              reads=[PSK(bi)], writes=[(dk, tq)])
                if h % 2 == 0:
                    for t4 in range(4):
                        bi = 4 + t4 % 2
                        def mmv(e, h=h, t4=t4, bi=bi):
                            for tl in range(4):
                                tt = t4 * 4 + tl
                                for k in range(8):
                                    r = e.matmul(ps[bi][:, tl * 128:(tl + 1) * 128], lhsT=hT[:, k, tt * 128:(tt + 1) * 128],
                                                 rhs=wqkv[:, k, 1024 + h * 64:1024 + h * 64 + 128], start=(k == 0), stop=(k == 7))
                            return r
                        P.pe(mmv, reads=[("wqkv", k) for k in range(8)] + [("hT", k) for k in range(8)], writes=[PSK(bi)])
                        P.dve(lambda e, t4=t4, bi=bi: e.tensor_copy(vh[:, t4 * 4:(t4 + 1) * 4, :],
                                                                    ps[bi][:].rearrange("p (t c) -> p t c", t=4)),
                              reads=[PSK(bi)], writes=[("vh", t4)])
                its = []
                for j in range(4):
                    kbs = list(range(4 * j + 3, -1, -1))
                    for ii, kb in enumerate(kbs):
                        diag = kb >= 4 * j
                        qlo = 128 * (kb - 4 * j) if diag else 0
                        its.append(dict(j=j, kb=kb, first=(ii == 0), last=(kb == 0), diag=diag, qlo=qlo, g=git))
                        git += 1

                def zmm(e, dst, it_, stop_after):
                    kb, qlo, j = it_["kb"], it_["qlo"], it_["j"]
                    q0 = 512 * j + qlo
                    r = e.matmul(dst[:, qlo:512], lhsT=kh[:, kb * 128:(kb + 1) * 128], rhs=qh[:, q0:512 * (j + 1)],
                                 start=True, stop=(stop_after and not it_["diag"]))
                    if it_["diag"]:
                        r = e.matmul(dst[:, qlo:qlo + 128], lhsT=identb, rhs=maskneg, start=False, stop=stop_after)
                    return r

                def stageA(it_):
                    r_ = it_["g"] % NBUF
                    j, kb, qlo = it_["j"], it_["kb"], it_["qlo"]
                    lb = j % 2
                    if it_["first"]:
                        P.dve(lambda e, lb=lb: e.memset(Ls[:, lb, :], 0.0), writes=[("Ls", lb)])
                    P.pe(lambda e, it_=it_, r_=r_: zmm(e, ps[r_], it_, True),
                         reads=[("kh", kb // 4), ("qh", j), "cb"], writes=[PSK(r_)])
                    P.pe(lambda e: e.matmul(ps[5][:], lhsT=cb[:, 3, :], rhs=zer[:, 0:512], start=True, stop=True),
                         reads=["zer"], writes=[PSK(5)])
                    P.act(lambda e, r_=r_, qlo=qlo: e.activation(ex[:, r_, qlo:512], ps[r_][:, qlo:512], AF.Exp),
                          reads=[PSK(r_)], writes=[("ex", r_)])
                    P.act(lambda e, r_=r_, qlo=qlo: e.activation(spb[:, r_, qlo:512], ex[:, r_, qlo:512], AF.Ln, bias=1.0),
                          reads=[("ex", r_)], writes=[("spb", r_)])
                    if not it_["last"]:
                        nqlo = max(0, 128 * (kb - 1 - 4 * j))
                        nr = (it_["g"] + 1) % 4
                        P.dve(lambda e, r_=r_, qlo=qlo, lb=lb: e.tensor_tensor(Ls[:, lb, qlo:512], Ls[:, lb, qlo:512], spb[:, r_, qlo:512], ALU.add),
                              reads=[("Ls", lb), ("spb", r_)], writes=[("Ls", lb)])
                        P.dve(lambda e, nqlo=nqlo, nr=nr, lb=lb: e.tensor_copy(Lsb[:, nr, nqlo:512], Ls[:, lb, nqlo:512]),
                              reads=[("Ls", lb)], writes=[("Lsb", nr)])

                def stageB(it_):
                    r_ = it_["g"] % NBUF
                    j, kb, qlo = it_["j"], it_["kb"], it_["qlo"]
                    ob = j % 2
                    pso = ps[6 + ob]
                    pst = ps[r_]
                    if it_["first"]:
                        P.pe(lambda e, pso=pso: e.matmul(pso[:, :], lhsT=zer[0:1, 0:128], rhs=zer[0:1, 0:512], start=True, stop=False),
                             reads=["zer"], writes=[PSK(6 + ob)])
                    def mmt(e, it_=it_, r_=r_, pst=pst, qlo=qlo):
                        r = e.matmul(pst[:, qlo:512], lhsT=trinc, rhs=spb[:, r_, qlo:512], start=False, stop=it_["first"])
                        if not it_["first"]:
                            r = e.matmul(pst[:, qlo:512], lhsT=onesneg, rhs=Lsb[:, it_["g"] % 4, qlo:512], start=False, stop=True)
                        return r
                    P.pe(mmt, reads=["cb", ("spb", r_), ("Lsb", it_["g"] % 4)], writes=[PSK(r_)])
                    P.act(lambda e, r_=r_, pst=pst, qlo=qlo: e.activation(wsb[:, r_, qlo:512], pst[:, qlo:512], AF.Exp),
                          reads=[PSK(r_)], writes=[("wsb", r_)])
                    P.pe(lambda e, pso=pso, kb=kb, r_=r_, qlo=qlo, last=it_["last"]: e.matmul(
                        pso[:, qlo:512], lhsT=vh[:, kb, :], rhs=wsb[:, r_, qlo:512], start=False, stop=last),
                        reads=[("vh", kb // 4), ("wsb", r_)], writes=[PSK(6 + ob)])
                    P.pe(lambda e: e.matmul(ps[5][:], lhsT=cb[:, 3, :], rhs=zer[:, 0:512], start=True, stop=True),
                         reads=["zer"], writes=[PSK(5)])
                    if it_["last"]:
                        if h % 2 == 0:
                            P.dve(lambda e, pso=pso, j=j, hp=hp: e.tensor_copy(aT[0:64, hp, 512 * j:512 * (j + 1)], pso[0:64, :]),
                                  reads=[PSK(6 + ob)], writes=[("aT", hp, 0)])
                        else:
                            P.dve(lambda e, pso=pso, j=j, hp=hp: e.tensor_copy(aT[64:128, hp, 512 * j:512 * (j + 1)], pso[64:128, :]),
                                  reads=[PSK(6 + ob)], writes=[("aT", hp, 1)])

                n_it = len(its)
                SK = 2
                for i in range(n_it + SK):
                    if i < n_it:
                        stageA(its[i])
                    if i >= SK:
                        stageB(its[i - SK])
        P.barrier()
        with SBT(nc, "wout", [128, 8, D], BF16) as wout:
            load_w(wout, T["ab_w_out"][0], "wout", 8, 0, D)
            for tq in range(4):
                t0 = tq * 512
                for m in range(8):
                    bi = m % 2
                    def mm(e, m=m, bi=bi, t0=t0):
                        for k in range(8):
                            src = aT if k < 4 else pTo
                            r = e.matmul(ps[bi][:], lhsT=wout[:, k, m * 128:(m + 1) * 128], rhs=src[:, k % 4, t0:t0 + 512],
                                         start=(k == 0), stop=(k == 7))
                        return r
                    P.pe(mm, reads=[("wout", k) for k in range(8)] + ["aTall"], writes=[PSK(bi)])
                    resid_add(m, bi, ps[bi][:], t0, 512)


def stage_mix_s5(C, b, xT, ps, cf, cb, gains, rmsnorm_tile, load_w, linear_fm, resid_add):
    from contextlib import ExitStack
    nc, P, T = C.nc, C.P, C.T
    P.barrier()

    def PSK(i):
        return ("ps", i)
    ident = cf[:, 0, :]
    mask32 = cf[:, 5, :]
    TWO_PI = 6.283185
    INV2PI = 1.0 / (2.0 * math.pi)

    def V(fn, r, w):
        return P.dve(fn, reads=r, writes=w)

    def Aop(fn, r, w):
        return P.act(fn, reads=r, writes=w)

    with SBT(nc, "uT", [128, 8, S], BF16) as uT, SBT(nc, "dcol", [128, 8], F32) as dcol:
        with SBT(nc, "w_in", [128, 8, D], BF16) as w_in, SBT(nc, "sq", [128, 8, 512], BF16) as sq, \
                SBT(nc, "rstd", [128, 512], F32) as rstd, SBT(nc, "hT", [128, 8, 512], BF16) as hT:
            load_w(w_in, T["ssm_w_in"][0], "w_in", 8, 0, D)
            P.dma(lambda e: e.dma_start(out=dcol[:], in_=T["ssm_d"][0].rearrange("(t p) -> p t", p=128),
                                        allow_slow_non_contiguous=True), writes=["dcol"])
            for tq in range(4):
                rmsnorm_tile(hT, "hT", 1, tq * 512, 512, sq, rstd)

                def ev(m, bi, pap, tq=tq):
                    P.act(lambda e: e.activation(uT[:, m, tq * 512:(tq + 1) * 512], pap, AF.Copy),
                          reads=[PSK(bi)], writes=[("uT", m)])
                linear_fm(w_in, "w_in", hT, "hT", 8, 8, 512, ev)
        P.barrier()
        with ExitStack() as es:
            def A(name, shape, dt=F32):
                return es.enter_context(SBT(nc, name, shape, dt))
            lre = A("lre", [128, 4]); lim = A("lim", [128, 4]); ldt = A("ldt", [128, 4])
            dtt = A("dtt", [128, 4]); ar = A("ar", [128, 4]); an = A("an", [128, 4])
            arj = A("arj", [128, 9, 4]); tj = A("tj", [128, 9, 4]); tjc = A("tjc", [128, 9, 4])
            ti = A("ti", [128, 9, 4], I32); fr = A("fr", [128, 9, 4])
            mag = A("mag", [128, 9, 4]); sinj = A("sinj", [128, 9, 4]); cosj = A("cosj", [128, 9, 4])
            Lr = A("Lr", [128, 9, 4]); Li = A("Li", [128, 9, 4])
            nre = A("nre", [128, 4]); den = A("den", [128, 4]); t1 = A("t1", [128, 4]); t2 = A("t2", [128, 4])
            cr = A("cr", [128, 4]); ci = A("ci", [128, 4]); ti8 = A("ti8", [128, 4], I32); t8f = A("t8f", [128, 4])
            Fr = A("Fr", [128, 8, 4]); Fi = A("Fi", [128, 8, 4]); f1 = A("f1", [128, 8, 4]); f2 = A("f2", [128, 8, 4])
            Bst = A("Bst", [128, 2, 4, 16]); Cin = A("Cin", [64, 2, 2, 64]); Cst = A("Cst", [128, 2, 4, 16])
            l1 = A("l1", [128, 9, 4, 16]); l2 = A("l2", [128, 9, 4, 16])
            What = A("What", [128, 8, 2, 128])
            Wt = A("Wt", [128, 8, 2, 128], BF16)
            CL = A("CL", [128, 2, 9, 4, 16])
            LB = CL[:, :, 0:8]
            Qd = A("Qd", [128, 9, 2, 4, 32], BF16)
            Qf = A("Qf", [128, 2, 128])
            TtF = A("TtF", [128, 4, 128]); Tt = A("Tt", [128, 8, 128], BF16)
            cosT = A("cosT", [128, 4, 256]); sinT = A("sinT", [128, 4, 256])
            Xp = A("Xp", [128, 2, 4, 256]); xa = A("xa", [128, 4, 256]); xb = A("xb", [128, 4, 256])
            Ssc = A("Ssc", [128, 2, 4, 256]); tk = Ssc[:, 0]; tki = Ssc[:, 1].bitcast(I32); Hb = A("Hb", [128, 2, 4, 257], BF16)
            iota256 = cf[:, 6:8, :].rearrange("p a b -> p (a b)")
            What6 = What[:].rearrange("p t r (q g c) -> p t r q g c", q=4, g=2)
            Qf5 = Qf[:].rearrange("p r (q g c) -> p r q g c", q=4, g=2)
            Wv = Wt[:].rearrange("p t r n -> p (t r) n")
            V(lambda e: e.memset(What[:], 0.0), [], ["What"])
            V(lambda e: e.memset(Qd[:], 0.0), [], ["Qpad"])
            V(lambda e: e.memset(Qf[:], 0.0), [], ["Qf"])
            V(lambda e: e.memset(Hb[:], 0.0), [], ["Hb"])

            def bc(ap, shape):
                return ap.broadcast_to(shape)

            def partA(j):
                g0 = 8 * j
                P.dma(lambda e, g0=g0: e.dma_start(out=lre[:], in_=T["ssm_lam_re"][0, g0:g0 + 8, :].rearrange("(q g) p -> (g p) q", g=2),
                                                   allow_slow_non_contiguous=True), writes=["lre"])
                P.dma(lambda e, g0=g0: e.dma_start(out=lim[:], in_=T["ssm_lam_im"][0, g0:g0 + 8, :].rearrange("(q g) p -> (g p) q", g=2),
                                                   allow_slow_non_contiguous=True), writes=["lim"])
                for g2 in range(2):
                    P.dma(lambda e, g0=g0, g2=g2: e.dma_start(
                        out=ldt[64 * g2:64 * g2 + 64, :],
                        in_=T["ssm_log_dt"][0, g0:g0 + 8].rearrange("(q g) -> g q", g=2)[g2:g2 + 1, :].broadcast_to([64, 4]),
                        allow_slow_non_contiguous=True), writes=["ldt"])
                for ri, nm in enumerate(["ssm_b_re", "ssm_b_im"]):
                    P.dma(lambda e, g0=g0, ri=ri, nm=nm: e.dma_start(
                        out=Bst[:, ri, :, :], in_=T[nm][0, g0:g0 + 8].rearrange("(q g) p c -> (g p) q c", g=2)),
                        writes=["Bst"])
                for ri, nm in enumerate(["ssm_c_re", "ssm_c_im"]):
                    for q in range(4):
                        P.dma(lambda e, g0=g0, ri=ri, nm=nm, q=q: e.dma_start(
                            out=Cin[16 * q:16 * q + 16, ri, :, :],
                            in_=T[nm][0, g0 + 2 * q:g0 + 2 * q + 2].rearrange("g c p -> c g p")), writes=["Cin"])
                def trc(e):
                    for ri in range(2):
                        r = e.transpose(ps[0][:, ri * 64:(ri + 1) * 64], Cin[:, ri, :, :].rearrange("a g p -> a (g p)"), ident[0:64, 0:64])
                    return r
                P.pe(trc, reads=["Cin", "cf"], writes=[PSK(0)])
                Aop(lambda e: e.activation(Cst[:].rearrange("p r q c -> p (r q c)"), ps[0][:, 0:128], AF.Copy), [PSK(0)], ["Cst"])
                Aop(lambda e: e.activation(dtt[:], ldt[:], AF.Exp), ["ldt"], ["dtt"])
                def f_(e):
                    e.tensor_tensor(ar[:], lre[:], dtt[:], ALU.mult)
                    return e.tensor_tensor(an[:], lim[:], dtt[:], ALU.mult)
                V(f_, ["lre", "lim", "dtt"], ["ar", "an"])
                jv = bc(cf[:, 6, 0:9][:, :, None], [128, 9, 4])
                def f_(e):
                    e.tensor_tensor(arj[:], bc(ar[:, None, :], [128, 9, 4]), jv, ALU.mult)
                    return e.scalar_tensor_tensor(tj[:], bc(an[:, None, :], [128, 9, 4]), INV2PI, jv, ALU.mult, ALU.mult)
                V(f_, ["ar", "an", "cf"], ["arj", "tj"])
                Aop(lambda e: e.activation(mag[:], arj[:], AF.Exp), ["arj"], ["mag"])
                V(lambda e: e.tensor_copy(ti[:], tj[:]), ["tj"], ["ti"])
                V(lambda e: e.tensor_tensor(fr[:], tj[:], ti[:], ALU.subtract), ["tj", "ti"], ["fr"])
                Aop(lambda e: e.activation(sinj[:], fr[:], AF.Sin, scale=TWO_PI), ["fr"], ["sinj"])
                V(lambda e: e.tensor_scalar(tjc[:], tj[:], 0.25, None, ALU.add), ["tj"], ["tjc"])
                V(lambda e: e.tensor_copy(ti[:], tjc[:]), ["tjc"], ["ti"])
                V(lambda e: e.tensor_tensor(fr[:], tjc[:], ti[:], ALU.subtract), ["tjc", "ti"], ["fr"])
                Aop(lambda e: e.activation(cosj[:], fr[:], AF.Sin, scale=TWO_PI), ["fr"], ["cosj"])
                def f_(e):
                    e.tensor_tensor(Lr[:], mag[:], cosj[:], ALU.mult)
                    return e.tensor_tensor(Li[:], mag[:], sinj[:], ALU.mult)
                V(f_, ["mag", "cosj", "sinj"], ["Lr", "Li"])
                def f_(e):
                    e.tensor_scalar(nre[:], Lr[:, 1, :], -1.0, None, ALU.add)
                    e.tensor_tensor(t1[:], lre[:], lre[:], ALU.mult)
                    return e.tensor_tensor(t2[:], lim[:], lim[:], ALU.mult)
                V(f_, ["Lr", "lre", "lim"], ["nre", "t1", "t2"])
                V(lambda e: e.tensor_tensor(den[:], t1[:], t2[:], ALU.add), ["t1", "t2"], ["den"])
                V(lambda e: e.reciprocal(den[:], den[:]), ["den"], ["den"])
                def f_(e):
                    e.tensor_tensor(t1[:], nre[:], lre[:], ALU.mult)
                    return e.tensor_tensor(t2[:], Li[:, 1, :], lim[:], ALU.mult)
                V(f_, ["nre", "lre", "Li", "lim", "den"], ["t1", "t2"])
                V(lambda e: e.tensor_tensor(cr[:], t1[:], t2[:], ALU.add), ["t1", "t2"], ["cr"])
                V(lambda e: e.tensor_tensor(cr[:], cr[:], den[:], ALU.mult), ["cr", "den"], ["cr"])
                def f_(e):
                    e.tensor_tensor(t1[:], Li[:, 1, :], lre[:], ALU.mult)
                    return e.tensor_tensor(t2[:], nre[:], lim[:], ALU.mult)
                V(f_, ["nre", "lre", "Li", "lim", "cr"], ["t1", "t2"])
                V(lambda e: e.tensor_tensor(ci[:], t1[:], t2[:], ALU.subtract), ["t1", "t2"], ["ci"])
                V(lambda e: e.tensor_tensor(ci[:], ci[:], den[:], ALU.mult), ["ci", "den"], ["ci"])
                crb = bc(cr[:, None, :], [128, 8, 4]); cib = bc(ci[:, None, :], [128, 8, 4])
                def f_(e, crb=crb, cib=cib):
                    e.tensor_tensor(f1[:], Lr[:, 0:8, :], crb, ALU.mult)
                    return e.tensor_tensor(f2[:], Li[:, 0:8, :], cib, ALU.mult)
                V(f_, ["Lr", "Li", "cr", "ci"], ["f1", "f2"])
                V(lambda e: e.tensor_tensor(Fr[:], f1[:], f2[:], ALU.subtract), ["f1", "f2"], ["Fr"])
                def f_(e, crb=crb, cib=cib):
                    e.tensor_tensor(f1[:], Lr[:, 0:8, :], cib, ALU.mult)
                    return e.tensor_tensor(f2[:], Li[:, 0:8, :], crb, ALU.mult)
                V(f_, ["Lr", "Li", "cr", "ci", "Fr"], ["f1", "f2"])
                V(lambda e: e.tensor_tensor(Fi[:], f1[:], f2[:], ALU.add), ["f1", "f2"], ["Fi"])
                sh8 = [128, 8, 4, 16]
                Frb = bc(Fr[:, :, :, None], sh8); Fib = bc(Fi[:, :, :, None], sh8)
                B0 = bc(Bst[:, 0, None, :, :], sh8); B1 = bc(Bst[:, 1, None, :, :], sh8)
                def f_(e, Frb=Frb, Fib=Fib, B0=B0, B1=B1):
                    e.tensor_tensor(l1[:, 0:8], Frb, B0, ALU.mult)
                    return e.tensor_tensor(l2[:, 0:8], Fib, B1, ALU.mult)
                V(f_, ["Fr", "Fi", "Bst"], ["l1", "l2"])
                V(lambda e: e.tensor_tensor(LB[:, 0], l1[:, 0:8], l2[:, 0:8], ALU.subtract), ["l1", "l2"], ["LB0", "CL0"])
                def f_(e, Frb=Frb, Fib=Fib, B0=B0, B1=B1):
                    e.tensor_tensor(l1[:, 0:8], Frb, B1, ALU.mult)
                    return e.tensor_tensor(l2[:, 0:8], Fib, B0, ALU.mult)
                V(f_, ["Fr", "Fi", "Bst", "LB0"], ["l1", "l2"])
                V(lambda e: e.tensor_tensor(LB[:, 1], l1[:, 0:8], l2[:, 0:8], ALU.add), ["l1", "l2"], ["LB1", "CL1"])
                def f_(e):
                    for g2 in range(2):
                        for ri in range(2):
                            r = e.tensor_copy(What6[64 * g2:64 * g2 + 64, :, ri, :, g2, :], LB[64 * g2:64 * g2 + 64, ri, :, :, :])
                    return r
                V(f_, ["LB0", "LB1"], ["What"])
            def partB(j):
                g0 = 8 * j
                for c4 in range(4):
                    bi = c4 % 2
                    def trw(e, c4=c4, bi=bi):
                        for i4 in range(4):
                            c = c4 * 4 + i4
                            r = e.transpose(ps[bi][:, i4 * 128:(i4 + 1) * 128], What[:, c // 2, c % 2, :], ident)
                        return r
                    P.pe(trw, reads=["What", "cf"], writes=[PSK(bi)])
                    if c4 % 2 == 0:
                        Aop(lambda e, c4=c4, bi=bi: e.activation(Wv[:, c4 * 4:c4 * 4 + 4, :], ps[bi][:].rearrange("p (a n) -> p a n", a=4), AF.Copy),
                            [PSK(bi)], [("Wpad", c4)])
                    else:
                        V(lambda e, c4=c4, bi=bi: e.tensor_copy(Wv[:, c4 * 4:c4 * 4 + 4, :], ps[bi][:].rearrange("p (a n) -> p a n", a=4)),
                          [PSK(bi)], [("Wpad", c4)])
                sh9 = [128, 9, 4, 16]
                Lrb = bc(Lr[:, :, :, None], sh9); Lib = bc(Li[:, :, :, None], sh9)
                C0 = bc(Cst[:, 0, None, :, :], sh9); C1 = bc(Cst[:, 1, None, :, :], sh9)
                def f_(e, Lrb=Lrb, Lib=Lib, C0=C0, C1=C1):
                    e.tensor_tensor(l1[:], Lrb, C0, ALU.mult)
                    return e.tensor_tensor(l2[:], Lib, C1, ALU.mult)
                V(f_, ["Lr", "Li", "Cst", "LB1"], ["l1", "l2"])
                V(lambda e: e.tensor_tensor(CL[:, 0], l1[:], l2[:], ALU.subtract), ["l1", "l2"], ["CL0", "LB0", "LB1"])
                def f_(e, Lrb=Lrb, Lib=Lib, C0=C0, C1=C1):
                    e.tensor_tensor(l1[:], Lib, C0, ALU.mult)
                    return e.tensor_tensor(l2[:], Lrb, C1, ALU.mult)
                V(f_, ["Lr", "Li", "Cst", "CL0"], ["l1", "l2"])
                V(lambda e: e.scalar_tensor_tensor(CL[:, 1], l1[:], -1.0, l2[:], ALU.mult, ALU.subtract), ["l1", "l2"], ["CL1", "LB0", "LB1"])
                def f_(e):
                    for g2 in range(2):
                        for ri in range(2):
                            e.tensor_copy(Qd[64 * g2:64 * g2 + 64, :, ri, :, 16 * g2:16 * g2 + 16], CL[64 * g2:64 * g2 + 64, ri, :, :, :])
                            r = e.tensor_copy(Qf5[64 * g2:64 * g2 + 64, ri, :, g2, :], CL[64 * g2:64 * g2 + 64, ri, 0, :, :])
                    return r
                V(f_, ["CL0", "CL1"], ["Qpad", "Qf"])
                for half in range(2):
                    bi = half
                    def mmt(e, half=half, bi=bi):
                        for i4 in range(4):
                            tau = half * 4 + i4
                            e.matmul(ps[bi][:, i4 * 128:(i4 + 1) * 128], lhsT=What[:, tau, 0, :], rhs=Qf[:, 0, :], start=True, stop=False)
                            r = e.matmul(ps[bi][:, i4 * 128:(i4 + 1) * 128], lhsT=What[:, tau, 1, :], rhs=Qf[:, 1, :], start=False, stop=True)
                        return r
                    P.pe(mmt, reads=["What", "Qf"], writes=[PSK(bi)])
                    m4 = bc(mask32[:, None, :], [128, 4, 128])
                    if half == 0:
                        V(lambda e, bi=bi, m4=m4: e.tensor_tensor(TtF[:], ps[bi][:].rearrange("p (a n) -> p a n", a=4), m4, ALU.mult),
                          [PSK(bi), "cf"], ["TtF"])
                        V(lambda e, j=j: e.scalar_tensor_tensor(TtF[:, 0, :], ident, dcol[:, j:j + 1], TtF[:, 0, :], ALU.mult, ALU.add),
                          ["TtF", "dcol", "cf"], ["TtF"])
                        V(lambda e: e.tensor_copy(Tt[:, 0:4, :], TtF[:]), ["TtF"], [("Tt", 0)])
                    else:
                        V(lambda e, bi=bi, m4=m4: e.tensor_tensor(Tt[:, 4:8, :], ps[bi][:].rearrange("p (a n) -> p a n", a=4), m4, ALU.mult),
                          [PSK(bi), "cf"], [("Tt", 1)])
                uv = uT[:, j, :].rearrange("p (k s) -> p s k", s=8)
                def mmx(e, uv=uv):
                    for ri in range(2):
                        for tau in range(8):
                            for q in range(4):
                                r = e.matmul(ps[2 + q][:, ri * 256:(ri + 1) * 256], lhsT=Wt[32 * q:32 * q + 32, tau, ri, :],
                                             rhs=uv[32 * q:32 * q + 32, 7 - tau, :], start=(tau == 0), stop=(tau == 7),
                                             tile_position=(32 * q, 0))
                    return r
                P.pe(mmx, reads=[("Wpad", c4) for c4 in range(4)] + [("uT", j, s_) for s_ in range(8)], writes=[PSK(2 + q) for q in range(4)])
                V(lambda e: e.tensor_copy(ti8[:], tj[:, 8, :]), ["tj"], ["ti8"])
                V(lambda e: e.tensor_tensor(t8f[:], tj[:, 8, :], ti8[:], ALU.subtract), ["tj", "ti8"], ["t8f"])
                V(lambda e: e.tensor_tensor(tk, bc(t8f[:, :, None], [128, 4, 256]), bc(iota256[:, None, :], [128, 4, 256]), ALU.mult),
                  ["t8f", "cf"], ["tk"] + [("Ssc", ri_, q_) for ri_ in range(2) for q_ in range(4)])
                V(lambda e: e.tensor_copy(tki, tk), ["tk"], ["tki"])
                V(lambda e: e.tensor_tensor(xa[:], tk, tki, ALU.subtract), ["tk", "tki"], ["xa"])
                Aop(lambda e: e.activation(sinT[:], xa[:], AF.Sin, scale=TWO_PI), ["xa"], ["sinT"])
                V(lambda e: e.tensor_scalar(tk, tk, 0.25, None, ALU.add), ["tk", "tki"], ["tk"])
                V(lambda e: e.tensor_copy(tki, tk), ["tk"], ["tki"])
                V(lambda e: e.tensor_tensor(xb[:], tk, tki, ALU.subtract), ["tk", "tki"], ["xb"])
                Aop(lambda e: e.activation(cosT[:], xb[:], AF.Sin, scale=TWO_PI), ["xb"], ["cosT"])
                for q in range(4):
                    Xr = ps[2 + q][:, 0:256]; Xi = ps[2 + q][:, 256:512]
                    def f_(e, q=q, Xr=Xr, Xi=Xi):
                        e.tensor_tensor(xa[:, q, :], cosT[:, q, :], Xr, ALU.mult)
                        return e.tensor_tensor(xb[:, q, :], sinT[:, q, :], Xi, ALU.mult)
                    V(f_, [PSK(2 + q), "cosT", "sinT", "xa", "xb"], [("xa", q), ("xb", q)])
                    V(lambda e, q=q: e.tensor_tensor(Xp[:, 0, q, :], xa[:, q, :], xb[:, q, :], ALU.add), [("xa", q), ("xb", q)], [("Xp", 0, q)])
                    def f_(e, q=q, Xr=Xr, Xi=Xi):
                        e.tensor_tensor(xa[:, q, :], cosT[:, q, :], Xi, ALU.mult)
                        return e.tensor_tensor(xb[:, q, :], sinT[:, q, :], Xr, ALU.mult)
                    V(f_, [PSK(2 + q), "cosT", "sinT", ("Xp", 0, q)], [("xa", q), ("xb", q)])
                    V(lambda e, q=q: e.tensor_tensor(Xp[:, 1, q, :], xa[:, q, :], xb[:, q, :], ALU.subtract), [("xa", q), ("xb", q)], [("Xp", 1, q)])
                    for ri in range(2):
                        V(lambda e, q=q, ri=ri: e.tensor_tensor_scan(Ssc[:, ri, q, :], mag[:, 8, q:q + 1].to_broadcast([128, 256]),
                                                                     Xp[:, ri, q, :], 0.0, ALU.mult, ALU.add),
                          [("Xp", ri, q), "mag"], [("Ssc", ri, q), "tk", "tki"])
                allS = [("Ssc", ri, q) for ri in range(2) for q in range(4)]
                allx = [("xa", q) for q in range(4)] + [("xb", q) for q in range(4)]
                def f_(e):
                    e.tensor_tensor(xa[:], cosT[:], Ssc[:, 0], ALU.mult)
                    return e.tensor_tensor(xb[:], sinT[:], Ssc[:, 1], ALU.mult)
                V(f_, allS + ["cosT", "sinT"], allx + ["xa", "xb"])
                V(lambda e: e.tensor_tensor(Hb[:, 0, :, 1:257], xa[:], xb[:], ALU.subtract), ["xa", "xb"], ["Hb0"])
                def f_(e):
                    e.tensor_tensor(xa[:], cosT[:], Ssc[:, 1], ALU.mult)
                    return e.tensor_tensor(xb[:], sinT[:], Ssc[:, 0], ALU.mult)
                V(f_, allS + ["cosT", "sinT", "Hb0"], allx + ["xa", "xb"])
                V(lambda e: e.tensor_tensor(Hb[:, 1, :, 1:257], xa[:], xb[:], ALU.add), ["xa", "xb"], ["Hb1"])
            def partC(j):
                uv = uT[:, j, :].rearrange("p (k s) -> p s k", s=8)
                for tp in range(7, -1, -1):
                    bi = 6 + (tp % 2)
                    def mmy(e, tp=tp, bi=bi, uv=uv):
                        for s_ in range(tp + 1):
                            e.matmul(ps[bi][:, 0:256], lhsT=Tt[:, tp - s_, :], rhs=uv[:, s_, :], start=(s_ == 0), stop=False)
                        for ri in range(2):
                            for q in range(4):
                                r = e.matmul(ps[bi][32 * q:32 * q + 32, 0:256], lhsT=Qd[:, tp + 1, ri, q, :], rhs=Hb[:, ri, q, 0:256], start=False,
                                             stop=(ri == 1), tile_position=(0, 32 * q))
                        return r
                    P.pe(mmy, reads=[("uT", j, s_) for s_ in range(tp + 1)] + [("Tt", 0), ("Tt", 1), "Qpad", "Hb0", "Hb1"], writes=[PSK(bi)])
                    Aop(lambda e, tp=tp, bi=bi, uv=uv: e.activation(uv[:, tp, :], ps[bi][:, 0:256], AF.Gelu_apprx_tanh),
                        [PSK(bi)], [("uT", j, tp)])
            partA(0)
            for j in range(8):
                partB(j)
                if j < 7:
                    partA(j + 1)
                partC(j)
        P.barrier()
        with SBT(nc, "wg", [128, 2, 2, 8, 512], BF16) as wg, SBT(nc, "sig", [128, 2, 512], F32) as sig, \
                SBT(nc, "mixb", [128, 2, 512], F32) as mixb:
            wglu = T["ssm_w_glu"][0].rearrange("(k p) n -> p k n", p=128)
            it = 0
            for mg in range(2):
                wb = mg % 2
                for vg in range(2):
                    c0 = vg * D + mg * 512
                    P.dma(lambda e, wb=wb, vg=vg, c0=c0: e.dma_start(out=wg[:, wb, vg, :, :], in_=wglu[:, :, c0:c0 + 512]),
                          writes=[("wg", wb, vg)], q="pool")
                for ml in range(4):
                    m = mg * 4 + ml
                    for tq in range(4):
                        pb = it % 2
                        it += 1
                        for vg in range(2):
                            bi = 2 + 2 * vg + pb
                            def mm(e, wb=wb, vg=vg, bi=bi, tq=tq, ml=ml):
                                for k in range(8):
                                    r = e.matmul(ps[bi][:], lhsT=wg[:, wb, vg, k, ml * 128:(ml + 1) * 128], rhs=uT[:, k, tq * 512:(tq + 1) * 512],
                                                 start=(k == 0), stop=(k == 7))
                                return r
                            P.pe(mm, reads=[("wg", wb, vg)], writes=[PSK(bi)])
                        Aop(lambda e, pb=pb: e.activation(sig[:, pb, :], ps[4 + pb][:], AF.Sigmoid), [PSK(4 + pb)], [("sig", pb)])
                        V(lambda e, pb=pb: e.tensor_tensor(mixb[:, pb, :], ps[2 + pb][:], sig[:, pb, :], ALU.mult), [PSK(2 + pb), ("sig", pb)], [("mixb", pb)])
                        V(lambda e, pb=pb, m=m, tq=tq: e.tensor_tensor(xT[:, m, tq * 512:(tq + 1) * 512], xT[:, m, tq * 512:(tq + 1) * 512],
                                                                     mixb[:, pb, :], ALU.add), [("mixb", pb), ("xT", m)], [("xT", m)])


_CACHE = {}


def kernel(**inputs):
    if "prog" not in _CACHE:
        _CACHE["prog"] = build_program()
    nc, _ = _CACHE["prog"]
    consts = make_consts()
    in_maps = []
    for c in range(8):
        m = {}
        for name, shape in INPUT_SPECS:
            if name == "consts":
                m[name] = consts
            elif name in ("x", "mem"):
                m[name] = np.ascontiguousarray(np.asarray(inputs[name], dtype=np.float32)[c * NB:(c + 1) * NB])
            else:
                m[name] = np.ascontiguousarray(np.asarray(inputs[name], dtype=np.float32))
        in_maps.append(m)
    res = run_bass_kernel_spmd(nc, in_maps, core_ids=list(range(8)))
    return np.concatenate([r["out"] for r in res.results], axis=0)
```

```python
import math
import numpy as np
import concourse.bass as bass
from concourse.ap import AP
import concourse.mybir as mybir
from concourse.bass_utils import run_bass_kernel_spmd

F32 = mybir.dt.float32
BF16 = mybir.dt.bfloat16
I32 = mybir.dt.int32
AF = mybir.ActivationFunctionType
ALU = mybir.AluOpType

COMPUTE = ("pe", "act", "dve", "pool")
ALLENG = ("pe", "act", "dve", "pool", "sp")
NDMA_SEMS = 40

S = 2048
D = 1024
NB = 2
DFF = 2816
NF = DFF // 128
MEM = 256


class Op:
    __slots__ = ("eng", "fn", "deps", "idx", "dma", "signal", "cnt", "sem", "clock")


class Prog:
    def __init__(self, nc):
        self.nc = nc
        self.ops = []
        self.last_write = {}
        self.readers = {}
        self.dma_hist = []
        self.n_dma = 0
        self.last_on = {}
        self.dma_since = []

    def add(self, eng, fn, reads=(), writes=(), dma=False, extra_deps=()):
        op = Op()
        op.eng, op.fn, op.dma = eng, fn, dma
        op.idx = len(self.ops)
        op.signal = False
        op.cnt = None
        op.sem = None
        op.clock = None
        deps = set(extra_deps)
        for k in reads:
            w = self.last_write.get(k)
            if w is not None:
                deps.add(w)
        for k in writes:
            w = self.last_write.get(k)
            if w is not None:
                deps.add(w)
            r = self.readers.get(k)
            if r:
                deps.update(r)
        for k in reads:
            self.readers.setdefault(k, []).append(op.idx)
        for k in writes:
            self.last_write[k] = op.idx
            self.readers[k] = []
        if dma:
            j = self.n_dma
            self.n_dma += 1
            op.sem = j % NDMA_SEMS
            if j >= NDMA_SEMS:
                deps.add(self.dma_hist[j - NDMA_SEMS])
            self.dma_hist.append(op.idx)
            self.dma_since.append(op.idx)
        else:
            self.last_on[eng] = op.idx
        deps.discard(op.idx)
        if eng == "pe" and not dma:
            deps = {d_ for d_ in deps if self.ops[d_].eng != "pe" or self.ops[d_].dma}
        op.deps = deps
        self.ops.append(op)
        return op

    def pe(self, fn, reads=(), writes=()):
        return self.add("pe", fn, reads, writes)

    def act(self, fn, reads=(), writes=()):
        return self.add("act", fn, reads, writes)

    def dve(self, fn, reads=(), writes=()):
        return self.add("dve", fn, reads, writes)

    def dma(self, fn, reads=(), writes=(), q="sp"):
        return self.add(q, fn, reads, writes, dma=True)

    def barrier(self):
        deps = set(self.last_on.values()) | set(self.dma_since)
        self.dma_since = []
        for e in ALLENG:
            self.add(e, lambda eng: eng.nop(), extra_deps=deps)
        self.last_write = {}
        self.readers = {}

    def emit(self, final_ops):
        nc = self.nc
        ops = self.ops
        for op in ops:
            for d in op.deps:
                ops[d].signal = True
        for op in final_ops:
            op.signal = True
        eng_cnt = {e: 0 for e in ALLENG}
        dma_cnt = [0] * NDMA_SEMS
        for op in ops:
            if op.dma:
                dma_cnt[op.sem] += 16
                op.cnt = dma_cnt[op.sem]
            elif op.signal:
                eng_cnt[op.eng] += 1
                op.cnt = eng_cnt[op.eng]
        sems = {e: nc.alloc_semaphore("s_" + e) for e in ALLENG}
        dsems = [nc.alloc_semaphore("d_%d" % i) for i in range(NDMA_SEMS)]

        def key_of(o):
            return ("d", o.sem) if o.dma else o.eng

        know = {e: {} for e in ALLENG}
        waits = {}
        for op in ops:
            K = know[op.eng]
            wl = []
            for d in sorted(op.deps, reverse=True):
                dop = ops[d]
                k = key_of(dop)
                if K.get(k, 0) >= dop.cnt:
                    continue
                for kk, vv in dop.clock.items():
                    if K.get(kk, 0) < vv:
                        K[kk] = vv
                K[k] = max(K.get(k, 0), dop.cnt)
                wl.append((dsems[dop.sem] if dop.dma else sems[dop.eng], dop.cnt))
            waits[op.idx] = wl
            if op.signal or op.dma:
                op.clock = dict(K)
        by_eng = {e: [] for e in ALLENG}
        for op in ops:
            by_eng[op.eng].append(op)
        fin = [(dsems[o.sem] if o.dma else sems[o.eng], o.cnt) for o in final_ops]
        self.n_inst = {e: len(by_eng[e]) for e in ALLENG}

        def run(engname, e):
            for op in by_eng[engname]:
                for (s, v) in waits[op.idx]:
                    e.wait_ge(s, v)
                ins = op.fn(e)
                if op.dma:
                    ins.then_inc(dsems[op.sem], 16)
                elif op.signal:
                    ins.then_inc(sems[op.eng], 1)
            if engname == "sp":
                for (s, v) in fin:
                    e.wait_ge(s, v)

        with nc.Block() as block:
            @block.tensor
            def _(e):
                run("pe", e)

            @block.scalar
            def _(e):
                run("act", e)

            @block.vector
            def _(e):
                run("dve", e)

            @block.gpsimd
            def _(e):
                run("pool", e)

            @block.sync
            def _(e):
                run("sp", e)


INPUT_SPECS = [
    ("x", [NB, S, D]), ("mem", [NB, MEM, D]),
    ("norm_mix", [2, D]), ("norm_xattn", [2, D]), ("norm_ffn", [2, D]), ("norm_mem", [D]), ("norm_final", [D]),
    ("ab_w_in", [1, D, 2048]), ("pool_w", [1, 4, 128, 128]), ("pool_scale", [1, 512]), ("ab_w_out", [1, D, D]),
    ("ssm_w_in", [1, D, D]), ("ssm_lam_re", [1, 64, 64]), ("ssm_lam_im", [1, 64, 64]), ("ssm_log_dt", [1, 64]),
    ("ssm_b_re", [1, 64, 64, 16]), ("ssm_b_im", [1, 64, 64, 16]), ("ssm_c_re", [1, 64, 16, 64]),
    ("ssm_c_im", [1, 64, 16, 64]), ("ssm_d", [1, D]), ("ssm_w_glu", [1, D, 2 * D]),
    ("xa_w_q", [2, D, D]), ("xa_w_kv", [2, D, 2 * D]), ("xa_w_o", [2, D, D]),
    ("ffn_w_up", [2, D, 2 * DFF]), ("ffn_conv_w", [2, 3, 2 * DFF]), ("ffn_conv_b", [2, 2 * DFF]),
    ("ffn_w_down", [2, DFF, D]),
    ("consts", [128, 12, 128]),
]


def make_consts():
    c = np.zeros((128, 12, 128), np.float32)
    j = np.arange(128)
    c[:, 0, :] = np.eye(128)
    c[:, 1, :] = -(j[:, None] > j[None, :]).astype(np.float32)
    c[:, 2, :] = -1.0
    c[:, 3, :] = 1.0
    c[:, 4, :] = (j[:, None] < j[None, :]).astype(np.float32)
    c[:, 5, :] = (j[:, None] // 32 == j[None, :] // 32).astype(np.float32)
    c[:, 6, :] = np.arange(128)[None, :]
    c[:, 7, :] = 128 + np.arange(128)[None, :]
    c[:, 8, :] = 1.0 / (1.0 + np.arange(128))[None, :]
    c[:, 9, :] = -(j[:, None] >= j[None, :]).astype(np.float32)
    c[:, 10, :] = -30000.0 * (j[:, None] >= j[None, :])
    return c


class Ctx:
    pass


_uid = [0]


def SBT(nc, name, shape, dt):
    _uid[0] += 1
    return nc.sbuf_tensor("%s_%d" % (name, _uid[0]), shape, dt)


def build_program(stop=None, nb=NB):
    nc = bass.Bass("TRN2", target_bir_lowering=False)
    P = Prog(nc)
    C = Ctx()
    C.nc, C.P = nc, P
    T = {}
    for name, shape in INPUT_SPECS:
        T[name] = nc.dram_tensor(name, shape, F32, kind="ExternalInput").ap()
    out = nc.dram_tensor("out", [NB, S, D], F32, kind="ExternalOutput").ap()
    C.T = T

    def sb(name, shape, dt=F32):
        return nc.alloc_sbuf_tensor(name, shape, dt)

    xT = sb("xT", [128, 8, S])
    cf = sb("cf", [128, 10, 128])
    cb = sb("cb", [128, 6, 128], BF16)
    gains = sb("gains", [128, 8, 8])
    convp = sb("convp", [128, 2, 4, 44])
    pscale = sb("pscale", [128, 4])
    zer = sb("zer", [128, 512], BF16)
    memT = sb("memT", [128, 8, MEM], BF16)
    ps = [nc.alloc_psum_tensor("ps%d" % i, [128, 512], F32) for i in range(8)]
    ident = cf[:, 0, :]
    maskstrict = cf[:, 4, :]

    def PSK(i):
        return ("ps", i)

    P.dma(lambda e: e.dma_start(out=cf[:], in_=T["consts"][:, 0:10, :]), writes=["cf"])
    P.dma(lambda e: e.dma_start(out=cb[:, 0:4, :], in_=T["consts"][:, 0:4, :]), writes=["cb"], q="pool")
    P.dma(lambda e: e.dma_start(out=cb[:, 4:6, :], in_=T["consts"][:, 9:11, :]), writes=["cb2"], q="pool")
    gsrc = [T["norm_mix"][0], T["norm_mix"][1], T["norm_xattn"][0], T["norm_xattn"][1],
            T["norm_ffn"][0], T["norm_ffn"][1], T["norm_mem"], T["norm_final"]]
    for i, g in enumerate(gsrc):
        P.dma(lambda e, i=i, g=g: e.dma_start(out=gains[:, i, :], in_=g.rearrange("(t p) -> p t", p=128),
                                             allow_slow_non_contiguous=True), writes=["gains"], q="act")
    for l in range(2):
        for i in range(3):
            P.dma(lambda e, l=l, i=i: e.dma_start(out=convp[:, l, i, :],
                                                  in_=T["ffn_conv_w"][l, i].rearrange("(t p) -> p t", p=128),
                                                  allow_slow_non_contiguous=True), writes=["convp"], q="act")
        P.dma(lambda e, l=l: e.dma_start(out=convp[:, l, 3, :],
                                         in_=T["ffn_conv_b"][l].rearrange("(t p) -> p t", p=128),
                                         allow_slow_non_contiguous=True), writes=["convp"], q="act")
    P.dma(lambda e: e.dma_start(out=pscale[:], in_=T["pool_scale"][0].rearrange("(t p) -> p t", p=128),
                                allow_slow_non_contiguous=True), writes=["pscale"], q="act")
    P.dve(lambda e: e.memset(zer[:], 0.0), writes=["zer"])

    def load_w(dst, src2d, key, k_tiles, col0, ncols):
        v = src2d.rearrange("(k p) n -> p k n", p=128)
        for k in range(k_tiles):
            P.dma(lambda e, k=k: e.dma_start(out=dst[:, k, :], in_=v[:, k, col0:col0 + ncols]),
                  writes=[(key, k)], q="pool")

    def rmsnorm_tile(hT, hkey, gi, t0, n, sq, rstd, part="all"):
        if part in ("all", "sq"):
            for dt in range(8):
                P.act(lambda e, dt=dt: e.activation(sq[:, dt, 0:n], xT[:, dt, t0:t0 + n], AF.Square),
                      reads=[("xT", dt)], writes=[("sq", dt)])
        if part == "sq":
            return
        def mm(e):
            for dt in range(8):
                r = e.matmul(ps[7][:, 0:n], lhsT=cb[:, 3, :], rhs=sq[:, dt, 0:n], start=(dt == 0), stop=(dt == 7))
            return r
        P.pe(mm, reads=[("sq", dt) for dt in range(8)] + ["cb"], writes=[PSK(7)])
        P.dve(lambda e: e.tensor_scalar(rstd[:, 0:n], ps[7][:, 0:n], 1.0 / D, 1e-6, ALU.mult, ALU.add),
              reads=[PSK(7)], writes=["rstd"])
        P.act(lambda e: e.activation(rstd[:, 0:n], rstd[:, 0:n], AF.Ln), reads=["rstd"], writes=["rstd"])
        P.act(lambda e: e.activation(rstd[:, 0:n], rstd[:, 0:n], AF.Exp, scale=-0.5), reads=["rstd"], writes=["rstd"])
        for dt in range(8):
            P.dve(lambda e, dt=dt: e.scalar_tensor_tensor(hT[:, dt, 0:n], xT[:, dt, t0:t0 + n], gains[:, gi, dt:dt + 1],
                                                          rstd[:, 0:n], ALU.mult, ALU.mult),
                  reads=[("xT", dt), "rstd", "gains"], writes=[(hkey, dt)])

    C.ps_rr = 0

    def linear_fm(w, wkey, act, akey, k_tiles, m_tiles, n, evac, a0=0, banks=(0, 1)):
        for m in range(m_tiles):
            bi = banks[C.ps_rr % len(banks)]
            C.ps_rr += 1
            def mm(e, m=m, bi=bi):
                for k in range(k_tiles):
                    r = e.matmul(ps[bi][:, 0:n], lhsT=w[:, k, m * 128:(m + 1) * 128], rhs=act[:, k, a0:a0 + n],
                                 start=(k == 0), stop=(k == k_tiles - 1))
                return r
            P.pe(mm, reads=[(wkey, k) for k in range(k_tiles)] + [(akey, k) for k in range(k_tiles)], writes=[PSK(bi)])
            evac(m, bi, ps[bi][:, 0:n])

    def resid_add(m, bi, pap, t0, n):
        P.dve(lambda e: e.tensor_tensor(xT[:, m, t0:t0 + n], xT[:, m, t0:t0 + n], pap, ALU.add),
              reads=[PSK(bi), ("xT", m)], writes=[("xT", m)])

    final_ops = []
    for b in range(nb):
        P.barrier()
        with SBT(nc, "xin", [128, 2, D], F32) as xin, SBT(nc, "sq", [128, 8, 512], BF16) as sq, \
                SBT(nc, "rstd", [128, 512], F32) as rstd, SBT(nc, "mn", [128, 2, D], F32) as mn, \
                SBT(nc, "ssq", [128, 4], F32) as ssq:
            for tt in range(16):
                xb = tt % 2
                P.dma(lambda e, tt=tt, xb=xb, b=b: e.dma_start(out=xin[:, xb, :], in_=T["x"][b, tt * 128:(tt + 1) * 128, :]),
                      writes=[("xin", xb)])
                for half in range(2):
                    bi = (tt * 2 + half) % 2
                    def tr(e, xb=xb, half=half, bi=bi):
                        for q in range(4):
                            dt = half * 4 + q
                            r = e.transpose(ps[bi][:, q * 128:(q + 1) * 128], xin[:, xb, dt * 128:(dt + 1) * 128], ident)
                        return r
                    P.pe(tr, reads=[("xin", xb), "cf"], writes=[PSK(bi)])
                    P.act(lambda e, tt=tt, half=half, bi=bi: e.activation(
                        xT[:, half * 4:half * 4 + 4, tt * 128:(tt + 1) * 128],
                        ps[bi][:].rearrange("p (q t) -> p q t", q=4), AF.Copy),
                        reads=[PSK(bi)], writes=[("xT", half * 4 + q) for q in range(4)])
            P.dma(lambda e, b=b: e.dma_start(out=mn[:], in_=T["mem"][b].rearrange("(t p) d -> p t d", p=128)), writes=["mn"])
            for t in range(2):
                P.act(lambda e, t=t: e.activation(xin[:, t, :], mn[:, t, :], AF.Square, accum_out=ssq[:, t:t + 1]),
                      reads=["mn"], writes=[("ssq", t), ("xin", t)])
            P.dve(lambda e: e.tensor_scalar(ssq[:, 2:4], ssq[:, 0:2], 1.0 / D, 1e-6, ALU.mult, ALU.add),
                  reads=[("ssq", 0), ("ssq", 1)], writes=["ssq2"])
            P.act(lambda e: e.activation(ssq[:, 2:4], ssq[:, 2:4], AF.Sqrt), reads=["ssq2"], writes=["ssq2"])
            P.dve(lambda e: e.reciprocal(ssq[:, 2:4], ssq[:, 2:4]), reads=["ssq2"], writes=["ssq2"])
            for t in range(2):
                P.dve(lambda e, t=t: e.tensor_scalar(mn[:, t, :], mn[:, t, :], ssq[:, 2 + t:3 + t], None, ALU.mult),
                      reads=["mn", "ssq2"], writes=["mn"])
            for t in range(2):
                for half in range(2):
                    bi = (t * 2 + half) % 2
                    def tr(e, t=t, half=half, bi=bi):
                        for q in range(4):
                            dt = half * 4 + q
                            r = e.transpose(ps[bi][:, q * 128:(q + 1) * 128], mn[:, t, dt * 128:(dt + 1) * 128], ident)
                        return r
                    P.pe(tr, reads=["mn", "cf"], writes=[PSK(bi)])
                    for q in range(4):
                        dt = half * 4 + q
                        P.dve(lambda e, t=t, q=q, dt=dt, bi=bi: e.tensor_scalar(
                            memT[:, dt, t * 128:(t + 1) * 128], ps[bi][:, q * 128:(q + 1) * 128],
                            gains[:, 6, dt:dt + 1], None, ALU.mult),
                            reads=[PSK(bi), "gains"], writes=[("memT", dt)])
        if stop == "load":
            pass
        else:
            for layer in range(2):
                if layer == 0:
                    stage_mix_ab(C, b, xT, ps, cf, cb, gains, pscale, zer, rmsnorm_tile, load_w, linear_fm, resid_add)
                else:
                    stage_mix_s5(C, b, xT, ps, cf, cb, gains, rmsnorm_tile, load_w, linear_fm, resid_add)
                if stop == "mix%d" % layer:
                    break
                stage_xattn(C, b, layer, xT, ps, cb, gains, memT, rmsnorm_tile, load_w, linear_fm, resid_add)
                if stop == "xa%d" % layer:
                    break
                stage_ffn(C, b, layer, xT, ps, gains, convp, rmsnorm_tile, load_w, linear_fm, resid_add)
                if stop == "ffn%d" % layer:
                    break
        P.barrier()
        with SBT(nc, "sq", [128, 8, 512], BF16) as sq, SBT(nc, "rstd", [128, 512], F32) as rstd, \
                SBT(nc, "yT", [128, 8, 512], F32) as yT, SBT(nc, "yo", [128, 2, D], F32) as yo:
            for tq in range(4):
                t0 = tq * 512
                if stop is None:
                    rmsnorm_tile(yT, "yT", 7, t0, 512, sq, rstd)
                else:
                    for dt in range(8):
                        P.act(lambda e, dt=dt, t0=t0: e.activation(yT[:, dt, :], xT[:, dt, t0:t0 + 512], AF.Copy),
                              reads=[("xT", dt)], writes=[("yT", dt)])
                for ts in range(4):
                    ob = ts % 2
                    for half in range(2):
                        bi = (ts * 2 + half) % 2
                        def tr(e, ts=ts, half=half, bi=bi):
                            for q in range(4):
                                dt = half * 4 + q
                                r = e.transpose(ps[bi][:, q * 128:(q + 1) * 128], yT[:, dt, ts * 128:(ts + 1) * 128], ident)
                            return r
                        P.pe(tr, reads=[("yT", dt) for dt in range(8)] + ["cf"], writes=[PSK(bi)])
                        P.act(lambda e, ob=ob, half=half, bi=bi: e.activation(yo[:, ob, half * 512:(half + 1) * 512],
                                                                               ps[bi][:], AF.Copy),
                              reads=[PSK(bi)], writes=[("yo", ob, half)])
                    tok = t0 + ts * 128
                    o = P.dma(lambda e, ob=ob, tok=tok, b=b: e.dma_start(out=out[b, tok:tok + 128, :], in_=yo[:, ob, :]),
                              reads=[("yo", ob, 0), ("yo", ob, 1)], writes=[("out", b, tok)])
                    final_ops.append(o)
    P.emit(final_ops)
    C.final = final_ops
    return nc, P


def stage_xattn(C, b, layer, xT, ps, cb, gains, memT, rmsnorm_tile, load_w, linear_fm, resid_add):
    nc, P, T = C.nc, C.P, C.T
    P.barrier()

    def PSK(i):
        return ("ps", i)
    with SBT(nc, "wq", [128, 8, D], BF16) as wq, SBT(nc, "wo", [128, 8, D], BF16) as wo, \
            SBT(nc, "wkv", [128, 8, D], BF16) as wkv, \
            SBT(nc, "KT", [128, 8, MEM], BF16) as KT, SBT(nc, "V", [128, 2, D], BF16) as V, \
            SBT(nc, "sq", [128, 8, 512], BF16) as sq, SBT(nc, "rstd", [128, 512], F32) as rstd, \
            SBT(nc, "hT", [128, 2, 8, 512], BF16) as hT, SBT(nc, "qT", [128, 8, 512], BF16) as qT, \
            SBT(nc, "pT", [128, 2, 2, 512], BF16) as pT, SBT(nc, "rs", [128, 2, 512], F32) as rs, \
            SBT(nc, "oT", [128, 8, 512], BF16) as oT:
        load_w(wkv, T["xa_w_kv"][layer], "wkv", 8, 0, D)
        load_w(wq, T["xa_w_q"][layer], "wq", 8, 0, D)

        def evK(m, bi, pap):
            P.act(lambda e: e.activation(KT[:, m, :], pap, AF.Copy), reads=[PSK(bi)], writes=[("KT", m)])
        linear_fm(wkv, "wkv", memT, "memT", 8, 8, MEM, evK)
        load_w(wkv, T["xa_w_kv"][layer], "wkv", 8, D, D)
        load_w(wo, T["xa_w_o"][layer], "wo", 8, 0, D)
        for mt in range(2):
            for nh in range(2):
                bi = (mt * 2 + nh) % 2
                def mm(e, mt=mt, nh=nh, bi=bi):
                    for k in range(8):
                        r = e.matmul(ps[bi][:], lhsT=memT[:, k, mt * 128:(mt + 1) * 128], rhs=wkv[:, k, nh * 512:(nh + 1) * 512],
                                     start=(k == 0), stop=(k == 7))
                    return r
                P.pe(mm, reads=[("wkv", k) for k in range(8)] + [("memT", k) for k in range(8)], writes=[PSK(bi)])
                P.act(lambda e, mt=mt, nh=nh, bi=bi: e.activation(V[:, mt, nh * 512:(nh + 1) * 512], ps[bi][:], AF.Copy),
                      reads=[PSK(bi)], writes=[("V", mt, nh)])
        rmsnorm_tile(hT[:, 0], "hT0", 2 + layer, 0, 512, sq, rstd)
        for tq in range(4):
            t0 = tq * 512
            tb = tq % 2

            def evQ(m, bi, pap):
                P.act(lambda e: e.activation(qT[:, m, :], pap, AF.Copy, scale=1.0 / 16.0), reads=[PSK(bi)], writes=[("qT", m)])
            linear_fm(wq, "wq", hT[:, tb], "hT%d" % tb, 8, 8, 512, evQ)
            def scores(h):
                hb = h % 2
                for mt in range(2):
                    bk = 2 + 2 * hb + mt
                    def mm(e, h=h, mt=mt, bk=bk):
                        for d in range(2):
                            r = e.matmul(ps[bk][:], lhsT=KT[:, 2 * h + d, mt * 128:(mt + 1) * 128], rhs=qT[:, 2 * h + d, :],
                                         start=(d == 0), stop=(d == 1))
                        return r
                    P.pe(mm, reads=[("KT", 2 * h), ("KT", 2 * h + 1), ("qT", 2 * h), ("qT", 2 * h + 1)], writes=[PSK(bk)])
                    P.act(lambda e, mt=mt, hb=hb, bk=bk: e.activation(pT[:, hb, mt, :], ps[bk][:], AF.Exp),
                          reads=[PSK(bk)], writes=[("pT", hb, mt)])

            def rest(h):
                hb = h % 2
                def mms(e, hb=hb):
                    e.matmul(ps[6][:], lhsT=cb[:, 3, :], rhs=pT[:, hb, 0, :], start=True, stop=False)
                    return e.matmul(ps[6][:], lhsT=cb[:, 3, :], rhs=pT[:, hb, 1, :], start=False, stop=True)
                P.pe(mms, reads=[("pT", hb, 0), ("pT", hb, 1), "cb"], writes=[PSK(6)])
                P.act(lambda e, hb=hb: e.activation(rs[:, hb, :], ps[6][:], AF.Ln), reads=[PSK(6)], writes=[("rs", hb)])
                P.act(lambda e, hb=hb: e.activation(rs[:, hb, :], rs[:, hb, :], AF.Exp, scale=-1.0), reads=[("rs", hb)], writes=[("rs", hb)])
                for d in range(2):
                    bi = 7 if d == 0 else 1
                    def mmo(e, h=h, hb=hb, d=d, bi=bi):
                        for mt in range(2):
                            r = e.matmul(ps[bi][:], lhsT=V[:, mt, h * 256 + d * 128:h * 256 + (d + 1) * 128], rhs=pT[:, hb, mt, :],
                                         start=(mt == 0), stop=(mt == 1))
                        return r
                    P.pe(mmo, reads=[("V", 0, h // 2), ("V", 1, h // 2), ("pT", hb, 0), ("pT", hb, 1)], writes=[PSK(bi)])
                    P.dve(lambda e, h=h, hb=hb, d=d, bi=bi: e.tensor_tensor(oT[:, 2 * h + d, :], ps[bi][:], rs[:, hb, :], ALU.mult),
                          reads=[PSK(bi), ("rs", hb)], writes=[("oT", 2 * h + d)])

            scores(0)
            for h in range(4):
                if h < 3:
                    scores(h + 1)
                rest(h)
            if tq + 1 < 4:
                rmsnorm_tile(hT[:, 1 - tb], "hT%d" % (1 - tb), 2 + layer, t0 + 512, 512, sq, rstd, part="sq")
            linear_fm(wo, "wo", oT, "oT", 8, 8, 512, lambda m, bi, pap, t0=t0: resid_add(m, bi, pap, t0, 512))
            if tq + 1 < 4:
                rmsnorm_tile(hT[:, 1 - tb], "hT%d" % (1 - tb), 2 + layer, t0 + 512, 512, sq, rstd, part="rest")


def stage_ffn(C, b, layer, xT, ps, gains, convp, rmsnorm_tile, load_w, linear_fm, resid_add):
    nc, P, T = C.nc, C.P, C.T
    P.barrier()

    def PSK(i):
        return ("ps", i)
    with SBT(nc, "sq", [128, 8, 512], BF16) as sq, SBT(nc, "rstd", [128, 512], F32) as rstd, \
            SBT(nc, "hT", [128, 2, 8, 512], BF16) as hT, SBT(nc, "gT", [128, NF, 512], BF16) as gT, \
            SBT(nc, "wu", [128, 2, 2, 8, 512], BF16) as wu, SBT(nc, "wd", [128, 4, D], BF16) as wd, \
            SBT(nc, "ub", [128, 3, 2, 516], F32) as ub, SBT(nc, "cv", [128, 3, 2, 512], F32) as cv, \
            SBT(nc, "halo", [128, 2 * NF, 2], F32) as halo:
        wup = T["ffn_w_up"][layer].rearrange("(k p) n -> p k n", p=128)
        wdn = T["ffn_w_down"][layer]
        P.dve(lambda e: e.memset(halo[:], 0.0), writes=["halo"])
        groups = [(0, 4), (4, 4), (8, 4), (12, 4), (16, 4), (20, 2)]
        it = 0
        git = 0
        kit = 0
        rmsnorm_tile(hT[:, 0], "hT0", 4 + layer, 0, 512, sq, rstd)
        pend = []

        def tail(pb, fp):
            P.act(lambda e: e.activation(cv[:, pb, 1, :], cv[:, pb, 1, :], AF.Silu),
                  reads=[("cv", pb, 1)], writes=[("cv", pb, 1)])
            P.dve(lambda e: e.tensor_tensor(gT[:, fp, :], cv[:, pb, 0, :], cv[:, pb, 1, :], ALU.mult),
                  reads=[("cv", pb, 0), ("cv", pb, 1)], writes=[("gT", fp)])
        for tq in range(4):
            t0 = tq * 512
            hb = tq % 2
            for gidx, (f0, nf) in enumerate(groups):
                if gidx == 3 and tq + 1 < 4:
                    rmsnorm_tile(hT[:, 1 - hb], "hT%d" % (1 - hb), 4 + layer, t0 + 512, 512, sq, rstd)
                wb = git % 2
                git += 1
                for vg in range(2):
                    col0 = vg * DFF + f0 * 128
                    P.dma(lambda e, wb=wb, vg=vg, col0=col0, nf=nf: e.dma_start(out=wu[:, wb, vg, :, 0:nf * 128],
                                                                              in_=wup[:, :, col0:col0 + nf * 128]),
                          writes=[("wu", wb, vg)], q="pool")
                for fl in range(nf):
                    fp = f0 + fl
                    pb = it % 3
                    it += 1
                    for vg in range(2):
                        bi = 3 * vg + pb
                        f = vg * NF + fp
                        def mm(e, wb=wb, vg=vg, bi=bi, fl=fl, hb=hb):
                            for k in range(8):
                                r = e.matmul(ps[bi][:], lhsT=wu[:, wb, vg, k, fl * 128:(fl + 1) * 128], rhs=hT[:, hb, k, :], start=(k == 0), stop=(k == 7))
                            return r
                        P.pe(mm, reads=[("wu", wb, vg)] + [("hT%d" % hb, k) for k in range(8)], writes=[PSK(bi)])
                        P.act(lambda e, pb=pb, vg=vg, bi=bi: e.activation(ub[:, pb, vg, 2:514], ps[bi][:], AF.Copy),
                              reads=[PSK(bi)], writes=[("ub", pb, vg)])
                        P.act(lambda e, pb=pb, vg=vg, f=f: e.activation(ub[:, pb, vg, 0:2], halo[:, f, :], AF.Copy),
                              reads=["halo%d" % f, "halo"], writes=[("ubh", pb, vg)])
                        P.act(lambda e, pb=pb, vg=vg, f=f, bi=bi: e.activation(cv[:, pb, vg, :], ps[bi][:], AF.Identity,
                                                                               bias=convp[:, layer, 3, f:f + 1],
                                                                               scale=convp[:, layer, 2, f:f + 1]),
                              reads=[PSK(bi), "convp"], writes=[("cv", pb, vg)])
                        P.dve(lambda e, pb=pb, vg=vg, f=f: e.scalar_tensor_tensor(cv[:, pb, vg, :], ub[:, pb, vg, 1:513],
                                                                                  convp[:, layer, 1, f:f + 1], cv[:, pb, vg, :],
                                                                                  ALU.mult, ALU.add),
                              reads=[("ub", pb, vg), ("ubh", pb, vg), ("cv", pb, vg), "convp"], writes=[("cv", pb, vg)])
                        P.dve(lambda e, pb=pb, vg=vg, f=f: e.scalar_tensor_tensor(cv[:, pb, vg, :], ub[:, pb, vg, 0:512],
                                                                                  convp[:, layer, 0, f:f + 1], cv[:, pb, vg, :],
                                                                                  ALU.mult, ALU.add),
                              reads=[("ub", pb, vg), ("ubh", pb, vg), ("cv", pb, vg), "convp"], writes=[("cv", pb, vg)])
                        P.dve(lambda e, pb=pb, vg=vg, f=f: e.tensor_copy(halo[:, f, :], ub[:, pb, vg, 512:514]),
                              reads=[("ub", pb, vg)], writes=["halo%d" % f])
                    if pend:
                        tail(*pend.pop())
                    pend.append((pb, fp))
            if pend:
                tail(*pend.pop())
            for k in range(NF):
                db = kit % 4
                kit += 1
                P.dma(lambda e, db=db, k=k: e.dma_start(out=wd[:, db, :], in_=wdn[k * 128:(k + 1) * 128, :]),
                      writes=[("wd", db)], q="pool")
                def mm(e, db=db, k=k):
                    for m in range(8):
                        r = e.matmul(ps[m][:], lhsT=wd[:, db, m * 128:(m + 1) * 128], rhs=gT[:, k, :], start=(k == 0), stop=(k == NF - 1))
                    return r
                P.pe(mm, reads=[("wd", db), ("gT", k)], writes=[PSK(m) for m in range(8)])
            for m in range(8):
                resid_add(m, m, ps[m][:], t0, 512)


def stage_mix_ab(C, b, xT, ps, cf, cb, gains, pscale, zer, rmsnorm_tile, load_w, linear_fm, resid_add):
    nc, P, T = C.nc, C.P, C.T
    P.barrier()
    maskstrict = cf[:, 4, :]

    def PSK(i):
        return ("ps", i)
    win = T["ab_w_in"][0]
    with SBT(nc, "hT", [128, 8, S], BF16) as hT, SBT(nc, "aT", [128, 4, S], BF16) as aT, \
            SBT(nc, "pTo", [128, 4, S], BF16) as pTo:
        with SBT(nc, "sq", [128, 8, 512], BF16) as sq, SBT(nc, "rstd", [128, 512], F32) as rstd:
            for tq in range(4):
                rmsnorm_tile(hT[:, :, tq * 512:(tq + 1) * 512], "hTn%d" % tq, 0, tq * 512, 512, sq, rstd)
        P.barrier()
        with SBT(nc, "wu4", [128, 8, 512], BF16) as wu4, SBT(nc, "wp", [128, 128], BF16) as wp, \
                SBT(nc, "uA", [128, S], F32) as uA, SBT(nc, "uB", [128, S], F32) as uB, \
                SBT(nc, "u0", [128, S], F32) as u0, SBT(nc, "pb", [128, S], BF16) as pb:
            for g in range(4):
                w_ = 2 ** (g + 1)
                if g == 0:
                    load_w(wu4, win, "wu4", 8, 1536, 512)
                P.dma(lambda e, g=g: e.dma_start(out=wp[:], in_=T["pool_w"][0, g]), writes=["wp"], q="pool")
                for tq in range(4):
                    bi = tq % 2
                    def mm(e, tq=tq, bi=bi, g=g):
                        for k in range(8):
                            r = e.matmul(ps[bi][:], lhsT=wu4[:, k, g * 128:(g + 1) * 128], rhs=hT[:, k, tq * 512:(tq + 1) * 512], start=(k == 0), stop=(k == 7))
                        return r
                    P.pe(mm, reads=[("wu4", k) for k in range(8)] + [("hT", k) for k in range(8)], writes=[PSK(bi)])
                    P.act(lambda e, tq=tq, bi=bi: e.activation(u0[:, tq * 512:(tq + 1) * 512], ps[bi][:], AF.Copy),
                          reads=[PSK(bi)], writes=["u0"])
                src, srck = u0, "u0"
                bufs = [(uA, "uA"), (uB, "uB")]
                for st in range(g + 1):
                    sh = 2 ** st
                    dst, dstk = bufs[st % 2]
                    def stp(e, src=src, dst=dst, sh=sh):
                        e.tensor_copy(dst[:, 0:sh], src[:, 0:sh])
                        return e.tensor_tensor(dst[:, sh:S], src[:, sh:S], src[:, 0:S - sh], ALU.add)
                    P.dve(stp, reads=[srck], writes=[dstk])
                    src, srck = dst, dstk
                def pl(e, src=src, w_=w_):
                    e.scalar_tensor_tensor(pb[:, w_ - 1:S], src[:, w_ - 1:S], 1.0 / w_, u0[:, w_ - 1:S], ALU.mult, ALU.subtract)
                    return e.tensor_tensor(src[:, 0:w_ - 1], src[:, 0:w_ - 1], cf[:, 8, 0:w_ - 1], ALU.mult)
                P.dve(pl, reads=[srck, "u0", "cf"], writes=["pb0", srck])
                P.dve(lambda e, src=src, w_=w_: e.tensor_tensor(pb[:, 0:w_ - 1], src[:, 0:w_ - 1], u0[:, 0:w_ - 1], ALU.subtract),
                      reads=[srck, "u0"], writes=["pb1"])
                for tq in range(4):
                    bi = tq % 2
                    P.pe(lambda e, tq=tq, bi=bi: e.matmul(ps[bi][:], lhsT=wp[:], rhs=pb[:, tq * 512:(tq + 1) * 512], start=True, stop=True),
                         reads=["wp", "pb0", "pb1"], writes=[PSK(bi)])
                    P.act(lambda e, tq=tq, bi=bi, g=g: e.activation(pTo[:, g, tq * 512:(tq + 1) * 512], ps[bi][:], AF.Identity,
                                                                    scale=pscale[:, g:g + 1]),
                          reads=[PSK(bi), "pscale"], writes=[("pTo", g)])
        P.barrier()
        NBUF = 4
        with SBT(nc, "wqkv", [128, 8, 1536], BF16) as wqkv, SBT(nc, "qh", [64, S], BF16) as qh, \
                SBT(nc, "kh", [64, S], BF16) as kh, SBT(nc, "vh", [128, 16, 128], BF16) as vh, \
                SBT(nc, "ex", [128, NBUF, 512], F32) as ex, \
                SBT(nc, "spb", [128, NBUF, 512], BF16) as spb, \
                SBT(nc, "wsb", [128, NBUF, 512], BF16) as wsb, SBT(nc, "Ls", [128, 2, 512], F32) as Ls, \
                SBT(nc, "Lsb", [128, 4, 512], BF16) as Lsb, SBT(nc, "otmp", [64, 2, 512], BF16) as otmp:
            identb = cb[:, 0, :]
            trinc = cb[:, 4, :]
            maskneg = cb[:, 5, :]
            onesneg = cb[:, 2, :]
            git = 0
            for h in range(8):
                hp = h // 2
                if h == 0:
                    load_w(wqkv, win, "wqkv", 8, 0, 1536)
                for j3, (dst, dk, scl) in enumerate([(qh, "qh", 0.125), (kh, "kh", 1.0)]):
                    for tq in range(4):
                        bi = 4 + tq % 2
                        def mm(e, h=h, j3=j3, tq=tq, bi=bi):
                            for k in range(8):
                                r = e.matmul(ps[bi][0:64, :], lhsT=wqkv[:, k, j3 * 512 + h * 64:j3 * 512 + h * 64 + 64],
                                             rhs=hT[:, k, tq * 512:(tq + 1) * 512], start=(k == 0), stop=(k == 7))
                            return r
                        P.pe(mm, reads=[("wqkv", k) for k in range(8)] + [("hT", k) for k in range(8)], writes=[PSK(bi)])
                        P.act(lambda e, dst=dst, tq=tq, bi=bi, scl=scl: e.activation(dst[:, tq * 512:(tq + 1) * 512], ps[bi][0:64, :],
                                                                                   AF.Copy, scale=scl),
                              reads=[PSK(bi)], writes=[(dk, tq)])
                if h % 2 == 0:
                    for t4 in range(4):
                        bi = 4 + t4 % 2
                        def mmv(e, h=h, t4=t4, bi=bi):
                            for tl in range(4):
                                tt = t4 * 4 + tl
                                for k in range(8):
                                    r = e.matmul(ps[bi][:, tl * 128:(tl + 1) * 128], lhsT=hT[:, k, tt * 128:(tt + 1) * 128],
                                                 rhs=wqkv[:, k, 1024 + h * 64:1024 + h * 64 + 128], start=(k == 0), stop=(k == 7))
                            return r
                        P.pe(mmv, reads=[("wqkv", k) for k in range(8)] + [("hT", k) for k in range(8)], writes=[PSK(bi)])
                        P.dve(lambda e, t4=t4, bi=bi: e.tensor_copy(vh[:, t4 * 4:(t4 + 1) * 4, :],
                                                                    ps[bi][:].rearrange("p (t c) -> p t c", t=4)),
                              reads=[PSK(bi)], writes=[("vh", t4)])
                its = []
                for j in range(4):
                    kbs = list(range(4 * j + 3, -1, -1))
                    for ii, kb in enumerate(kbs):
                        diag = kb >= 4 * j
                        qlo = 128 * (kb - 4 * j) if diag else 0
                        its.append(dict(j=j, kb=kb, first=(ii == 0), last=(kb == 0), diag=diag, qlo=qlo, g=git))
                        git += 1

                def zmm(e, dst, it_, stop_after):
                    kb, qlo, j = it_["kb"], it_["qlo"], it_["j"]
                    q0 = 512 * j + qlo
                    r = e.matmul(dst[:, qlo:512], lhsT=kh[:, kb * 128:(kb + 1) * 128], rhs=qh[:, q0:512 * (j + 1)],
                                 start=True, stop=(stop_after and not it_["diag"]))
                    if it_["diag"]:
                        r = e.matmul(dst[:, qlo:qlo + 128], lhsT=identb, rhs=maskneg, start=False, stop=stop_after)
                    return r

                def stageA(it_):
                    r_ = it_["g"] % NBUF
                    j, kb, qlo = it_["j"], it_["kb"], it_["qlo"]
                    lb = j % 2
                    if it_["first"]:
                        P.dve(lambda e, lb=lb: e.memset(Ls[:, lb, :], 0.0), writes=[("Ls", lb)])
                    P.pe(lambda e, it_=it_, r_=r_: zmm(e, ps[r_], it_, True),
                         reads=[("kh", kb // 4), ("qh", j), "cb"], writes=[PSK(r_)])
                    P.act(lambda e, r_=r_, qlo=qlo: e.activation(ex[:, r_, qlo:512], ps[r_][:, qlo:512], AF.Exp),
                          reads=[PSK(r_)], writes=[("ex", r_)])
                    P.act(lambda e, r_=r_, qlo=qlo: e.activation(spb[:, r_, qlo:512], ex[:, r_, qlo:512], AF.Ln, bias=1.0),
                          reads=[("ex", r_)], writes=[("spb", r_)])
                    if not it_["last"]:
                        nqlo = max(0, 128 * (kb - 1 - 4 * j))
                        nr = (it_["g"] + 1) % 4
                        P.dve(lambda e, r_=r_, qlo=qlo, lb=lb: e.tensor_tensor(Ls[:, lb, qlo:512], Ls[:, lb, qlo:512], spb[:, r_, qlo:512], ALU.add),
                              reads=[("Ls", lb), ("spb", r_)], writes=[("Ls", lb)])
                        P.dve(lambda e, nqlo=nqlo, nr=nr, lb=lb: e.tensor_copy(Lsb[:, nr, nqlo:512], Ls[:, lb, nqlo:512]),
                              reads=[("Ls", lb)], writes=[("Lsb", nr)])

                def stageB(it_):
                    r_ = it_["g"] % NBUF
                    j, kb, qlo = it_["j"], it_["kb"], it_["qlo"]
                    ob = j % 2
                    pso = ps[6 + ob]
                    pst = ps[r_]
                    if it_["first"]:
                        P.pe(lambda e, pso=pso: e.matmul(pso[:, :], lhsT=zer[0:1, 0:128], rhs=zer[0:1, 0:512], start=True, stop=False),
                             reads=["zer"], writes=[PSK(6 + ob)])
                    def mmt(e, it_=it_, r_=r_, pst=pst, qlo=qlo):
                        r = e.matmul(pst[:, qlo:512], lhsT=trinc, rhs=spb[:, r_, qlo:512], start=False, stop=it_["first"])
                        if not it_["first"]:
                            r = e.matmul(pst[:, qlo:512], lhsT=onesneg, rhs=Lsb[:, it_["g"] % 4, qlo:512], start=False, stop=True)
                        return r
                    P.pe(mmt, reads=["cb", ("spb", r_), ("Lsb", it_["g"] % 4)], writes=[PSK(r_)])
                    P.act(lambda e, r_=r_, pst=pst, qlo=qlo: e.activation(wsb[:, r_, qlo:512], pst[:, qlo:512], AF.Exp),
                          reads=[PSK(r_)], writes=[("wsb", r_)])
                    P.pe(lambda e, pso=pso, kb=kb, r_=r_, qlo=qlo, last=it_["last"]: e.matmul(
                        pso[:, qlo:512], lhsT=vh[:, kb, :], rhs=wsb[:, r_, qlo:512], start=False, stop=last),
                        reads=[("vh", kb // 4), ("wsb", r_)], writes=[PSK(6 + ob)])
                    if it_["last"]:
                        if h % 2 == 0:
                            P.dve(lambda e, pso=pso, j=j, hp=hp: e.tensor_copy(aT[0:64, hp, 512 * j:512 * (j + 1)], pso[0:64, :]),
                                  reads=[PSK(6 + ob)], writes=[("aT", hp, 0)])
                        else:
                            P.dve(lambda e, pso=pso, j=j, hp=hp: e.tensor_copy(aT[64:128, hp, 512 * j:512 * (j + 1)], pso[64:128, :]),
                                  reads=[PSK(6 + ob)], writes=[("aT", hp, 1)])

                n_it = len(its)
                SK = 2
                for i in range(n_it + SK):
                    if i < n_it:
                        stageA(its[i])
                    if i >= SK:
                        stageB(its[i - SK])
        P.barrier()
        with SBT(nc, "wout", [128, 8, D], BF16) as wout:
            load_w(wout, T["ab_w_out"][0], "wout", 8, 0, D)
            for tq in range(4):
                t0 = tq * 512
                for m in range(8):
                    bi = m % 2
                    def mm(e, m=m, bi=bi, t0=t0):
                        for k in range(8):
                            src = aT if k < 4 else pTo
                            r = e.matmul(ps[bi][:], lhsT=wout[:, k, m * 128:(m + 1) * 128], rhs=src[:, k % 4, t0:t0 + 512],
                                         start=(k == 0), stop=(k == 7))
                        return r
                    P.pe(mm, reads=[("wout", k) for k in range(8)] + ["aTall"], writes=[PSK(bi)])
                    resid_add(m, bi, ps[bi][:], t0, 512)


def stage_mix_s5(C, b, xT, ps, cf, cb, gains, rmsnorm_tile, load_w, linear_fm, resid_add):
    from contextlib import ExitStack
    nc, P, T = C.nc, C.P, C.T
    P.barrier()

    def PSK(i):
        return ("ps", i)
    ident = cf[:, 0, :]
    mask32 = cf[:, 5, :]
    TWO_PI = 6.283185
    INV2PI = 1.0 / (2.0 * math.pi)

    def V(fn, r, w):
        return P.dve(fn, reads=r, writes=w)

    def Aop(fn, r, w):
        return P.act(fn, reads=r, writes=w)

    with SBT(nc, "uT", [128, 8, S], BF16) as uT, SBT(nc, "dcol", [128, 8], F32) as dcol:
        with SBT(nc, "w_in", [128, 8, D], BF16) as w_in, SBT(nc, "sq", [128, 8, 512], BF16) as sq, \
                SBT(nc, "rstd", [128, 512], F32) as rstd, SBT(nc, "hT", [128, 8, 512], BF16) as hT:
            load_w(w_in, T["ssm_w_in"][0], "w_in", 8, 0, D)
            P.dma(lambda e: e.dma_start(out=dcol[:], in_=T["ssm_d"][0].rearrange("(t p) -> p t", p=128),
                                        allow_slow_non_contiguous=True), writes=["dcol"])
            for tq in range(4):
                rmsnorm_tile(hT, "hT", 1, tq * 512, 512, sq, rstd)

                def ev(m, bi, pap, tq=tq):
                    P.act(lambda e: e.activation(uT[:, m, tq * 512:(tq + 1) * 512], pap, AF.Copy),
                          reads=[PSK(bi)], writes=[("uT", m)])
                linear_fm(w_in, "w_in", hT, "hT", 8, 8, 512, ev)
        P.barrier()
        with ExitStack() as es:
            def A(name, shape, dt=F32):
                return es.enter_context(SBT(nc, name, shape, dt))
            lre = A("lre", [128, 4]); lim = A("lim", [128, 4]); ldt = A("ldt", [128, 4])
            dtt = A("dtt", [128, 4]); ar = A("ar", [128, 4]); an = A("an", [128, 4])
            arj = A("arj", [128, 9, 4]); tj = A("tj", [128, 9, 4]); tjc = A("tjc", [128, 9, 4])
            ti = A("ti", [128, 9, 4], I32); fr = A("fr", [128, 9, 4])
            mag = A("mag", [128, 9, 4]); sinj = A("sinj", [128, 9, 4]); cosj = A("cosj", [128, 9, 4])
            Lr = A("Lr", [128, 9, 4]); Li = A("Li", [128, 9, 4])
            nre = A("nre", [128, 4]); den = A("den", [128, 4]); t1 = A("t1", [128, 4]); t2 = A("t2", [128, 4])
            cr = A("cr", [128, 4]); ci = A("ci", [128, 4]); ti8 = A("ti8", [128, 4], I32); t8f = A("t8f", [128, 4])
            Fr = A("Fr", [128, 8, 4]); Fi = A("Fi", [128, 8, 4]); f1 = A("f1", [128, 8, 4]); f2 = A("f2", [128, 8, 4])
            Bst = A("Bst", [128, 2, 4, 16]); Cin = A("Cin", [64, 2, 2, 64]); Cst = A("Cst", [128, 2, 4, 16])
            l1 = A("l1", [128, 9, 4, 16]); l2 = A("l2", [128, 9, 4, 16])
            What = A("What", [128, 8, 2, 128])
            Wt = A("Wt", [128, 8, 2, 128], BF16)
            CL = A("CL", [128, 2, 9, 4, 16])
            LB = CL[:, :, 0:8]
            Qd = A("Qd", [128, 9, 2, 4, 32], BF16)
            Qf = A("Qf", [128, 2, 128])
            TtF = A("TtF", [128, 4, 128]); Tt = A("Tt", [128, 8, 128], BF16)
            cosT = A("cosT", [128, 4, 256]); sinT = A("sinT", [128, 4, 256])
            Xp = A("Xp", [128, 2, 4, 256]); xa = A("xa", [128, 4, 256]); xb = A("xb", [128, 4, 256])
            Ssc = A("Ssc", [128, 2, 4, 256]); tk = Ssc[:, 0]; tki = Ssc[:, 1].bitcast(I32); Hb = A("Hb", [128, 2, 4, 257], BF16)
            iota256 = cf[:, 6:8, :].rearrange("p a b -> p (a b)")
            What6 = What[:].rearrange("p t r (q g c) -> p t r q g c", q=4, g=2)
            Qf5 = Qf[:].rearrange("p r (q g c) -> p r q g c", q=4, g=2)
            Wv = Wt[:].rearrange("p t r n -> p (t r) n")
            V(lambda e: e.memset(What[:], 0.0), [], ["What"])
            V(lambda e: e.memset(Qd[:], 0.0), [], ["Qpad"])
            V(lambda e: e.memset(Qf[:], 0.0), [], ["Qf"])
            V(lambda e: e.memset(Hb[:], 0.0), [], ["Hb"])

            def bc(ap, shape):
                return ap.broadcast_to(shape)

            def partA(j):
                g0 = 8 * j
                P.dma(lambda e, g0=g0: e.dma_start(out=lre[:], in_=T["ssm_lam_re"][0, g0:g0 + 8, :].rearrange("(q g) p -> (g p) q", g=2),
                                                   allow_slow_non_contiguous=True), writes=["lre"])
                P.dma(lambda e, g0=g0: e.dma_start(out=lim[:], in_=T["ssm_lam_im"][0, g0:g0 + 8, :].rearrange("(q g) p -> (g p) q", g=2),
                                                   allow_slow_non_contiguous=True), writes=["lim"])
                for g2 in range(2):
                    P.dma(lambda e, g0=g0, g2=g2: e.dma_start(
                        out=ldt[64 * g2:64 * g2 + 64, :],
                        in_=T["ssm_log_dt"][0, g0:g0 + 8].rearrange("(q g) -> g q", g=2)[g2:g2 + 1, :].broadcast_to([64, 4]),
                        allow_slow_non_contiguous=True), writes=["ldt"])
                for ri, nm in enumerate(["ssm_b_re", "ssm_b_im"]):
                    P.dma(lambda e, g0=g0, ri=ri, nm=nm: e.dma_start(
                        out=Bst[:, ri, :, :], in_=T[nm][0, g0:g0 + 8].rearrange("(q g) p c -> (g p) q c", g=2)),
                        writes=["Bst"])
                for ri, nm in enumerate(["ssm_c_re", "ssm_c_im"]):
                    for q in range(4):
                        P.dma(lambda e, g0=g0, ri=ri, nm=nm, q=q: e.dma_start(
                            out=Cin[16 * q:16 * q + 16, ri, :, :],
                            in_=T[nm][0, g0 + 2 * q:g0 + 2 * q + 2].rearrange("g c p -> c g p")), writes=["Cin"])
                def trc(e):
                    for ri in range(2):
                        r = e.transpose(ps[0][:, ri * 64:(ri + 1) * 64], Cin[:, ri, :, :].rearrange("a g p -> a (g p)"), ident[0:64, 0:64])
                    return r
                P.pe(trc, reads=["Cin", "cf"], writes=[PSK(0)])
                Aop(lambda e: e.activation(Cst[:].rearrange("p r q c -> p (r q c)"), ps[0][:, 0:128], AF.Copy), [PSK(0)], ["Cst"])
                Aop(lambda e: e.activation(dtt[:], ldt[:], AF.Exp), ["ldt"], ["dtt"])
                def f_(e):
                    e.tensor_tensor(ar[:], lre[:], dtt[:], ALU.mult)
                    return e.tensor_tensor(an[:], lim[:], dtt[:], ALU.mult)
                V(f_, ["lre", "lim", "dtt"], ["ar", "an"])
                jv = bc(cf[:, 6, 0:9][:, :, None], [128, 9, 4])
                def f_(e):
                    e.tensor_tensor(arj[:], bc(ar[:, None, :], [128, 9, 4]), jv, ALU.mult)
                    return e.scalar_tensor_tensor(tj[:], bc(an[:, None, :], [128, 9, 4]), INV2PI, jv, ALU.mult, ALU.mult)
                V(f_, ["ar", "an", "cf"], ["arj", "tj"])
                Aop(lambda e: e.activation(mag[:], arj[:], AF.Exp), ["arj"], ["mag"])
                V(lambda e: e.tensor_copy(ti[:], tj[:]), ["tj"], ["ti"])
                V(lambda e: e.tensor_tensor(fr[:], tj[:], ti[:], ALU.subtract), ["tj", "ti"], ["fr"])
                Aop(lambda e: e.activation(sinj[:], fr[:], AF.Sin, scale=TWO_PI), ["fr"], ["sinj"])
                V(lambda e: e.tensor_scalar(tjc[:], tj[:], 0.25, None, ALU.add), ["tj"], ["tjc"])
                V(lambda e: e.tensor_copy(ti[:], tjc[:]), ["tjc"], ["ti"])
                V(lambda e: e.tensor_tensor(fr[:], tjc[:], ti[:], ALU.subtract), ["tjc", "ti"], ["fr"])
                Aop(lambda e: e.activation(cosj[:], fr[:], AF.Sin, scale=TWO_PI), ["fr"], ["cosj"])
                def f_(e):
                    e.tensor_tensor(Lr[:], mag[:], cosj[:], ALU.mult)
                    return e.tensor_tensor(Li[:], mag[:], sinj[:], ALU.mult)
                V(f_, ["mag", "cosj", "sinj"], ["Lr", "Li"])
                def f_(e):
                    e.tensor_scalar(nre[:], Lr[:, 1, :], -1.0, None, ALU.add)
                    e.tensor_tensor(t1[:], lre[:], lre[:], ALU.mult)
                    return e.tensor_tensor(t2[:], lim[:], lim[:], ALU.mult)
                V(f_, ["Lr", "lre", "lim"], ["nre", "t1", "t2"])
                V(lambda e: e.tensor_tensor(den[:], t1[:], t2[:], ALU.add), ["t1", "t2"], ["den"])
                V(lambda e: e.reciprocal(den[:], den[:]), ["den"], ["den"])
                def f_(e):
                    e.tensor_tensor(t1[:], nre[:], lre[:], ALU.mult)
                    return e.tensor_tensor(t2[:], Li[:, 1, :], lim[:], ALU.mult)
                V(f_, ["nre", "lre", "Li", "lim", "den"], ["t1", "t2"])
                V(lambda e: e.tensor_tensor(cr[:], t1[:], t2[:], ALU.add), ["t1", "t2"], ["cr"])
                V(lambda e: e.tensor_tensor(cr[:], cr[:], den[:], ALU.mult), ["cr", "den"], ["cr"])
                def f_(e):
                    e.tensor_tensor(t1[:], Li[:, 1, :], lre[:], ALU.mult)
                    return e.tensor_tensor(t2[:], nre[:], lim[:], ALU.mult)
                V(f_, ["nre", "lre", "Li", "lim", "cr"], ["t1", "t2"])
                V(lambda e: e.tensor_tensor(ci[:], t1[:], t2[:], ALU.subtract), ["t1", "t2"], ["ci"])
                V(lambda e: e.tensor_tensor(ci[:], ci[:], den[:], ALU.mult), ["ci", "den"], ["ci"])
                crb = bc(cr[:, None, :], [128, 8, 4]); cib = bc(ci[:, None, :], [128, 8, 4])
                def f_(e, crb=crb, cib=cib):
                    e.tensor_tensor(f1[:], Lr[:, 0:8, :], crb, ALU.mult)
                    return e.tensor_tensor(f2[:], Li[:, 0:8, :], cib, ALU.mult)
                V(f_, ["Lr", "Li", "cr", "ci"], ["f1", "f2"])
                V(lambda e: e.tensor_tensor(Fr[:], f1[:], f2[:], ALU.subtract), ["f1", "f2"], ["Fr"])
                def f_(e, crb=crb, cib=cib):
                    e.tensor_tensor(f1[:], Lr[:, 0:8, :], cib, ALU.mult)
                    return e.tensor_tensor(f2[:], Li[:, 0:8, :], crb, ALU.mult)
                V(f_, ["Lr", "Li", "cr", "ci", "Fr"], ["f1", "f2"])
                V(lambda e: e.tensor_tensor(Fi[:], f1[:], f2[:], ALU.add), ["f1", "f2"], ["Fi"])
                sh8 = [128, 8, 4, 16]
                Frb = bc(Fr[:, :, :, None], sh8); Fib = bc(Fi[:, :, :, None], sh8)
                B0 = bc(Bst[:, 0, None, :, :], sh8); B1 = bc(Bst[:, 1, None, :, :], sh8)
                def f_(e, Frb=Frb, Fib=Fib, B0=B0, B1=B1):
                    e.tensor_tensor(l1[:, 0:8], Frb, B0, ALU.mult)
                    return e.tensor_tensor(l2[:, 0:8], Fib, B1, ALU.mult)
                V(f_, ["Fr", "Fi", "Bst"], ["l1", "l2"])
                V(lambda e: e.tensor_tensor(LB[:, 0], l1[:, 0:8], l2[:, 0:8], ALU.subtract), ["l1", "l2"], ["LB0", "CL0"])
                def f_(e, Frb=Frb, Fib=Fib, B0=B0, B1=B1):
                    e.tensor_tensor(l1[:, 0:8], Frb, B1, ALU.mult)
                    return e.tensor_tensor(l2[:, 0:8], Fib, B0, ALU.mult)
                V(f_, ["Fr", "Fi", "Bst", "LB0"], ["l1", "l2"])
                V(lambda e: e.tensor_tensor(LB[:, 1], l1[:, 0:8], l2[:, 0:8], ALU.add), ["l1", "l2"], ["LB1", "CL1"])
                def f_(e):
                    for g2 in range(2):
                        for ri in range(2):
                            r = e.tensor_copy(What6[64 * g2:64 * g2 + 64, :, ri, :, g2, :], LB[64 * g2:64 * g2 + 64, ri, :, :, :])
                    return r
                V(f_, ["LB0", "LB1"], ["What"])
            def partB(j):
                g0 = 8 * j
                for c4 in range(4):
                    bi = c4 % 2
                    def trw(e, c4=c4, bi=bi):
                        for i4 in range(4):
                            c = c4 * 4 + i4
                            r = e.transpose(ps[bi][:, i4 * 128:(i4 + 1) * 128], What[:, c // 2, c % 2, :], ident)
                        return r
                    P.pe(trw, reads=["What", "cf"], writes=[PSK(bi)])
                    if c4 % 2 == 0:
                        Aop(lambda e, c4=c4, bi=bi: e.activation(Wv[:, c4 * 4:c4 * 4 + 4, :], ps[bi][:].rearrange("p (a n) -> p a n", a=4), AF.Copy),
                            [PSK(bi)], [("Wpad", c4)])
                    else:
                        V(lambda e, c4=c4, bi=bi: e.tensor_copy(Wv[:, c4 * 4:c4 * 4 + 4, :], ps[bi][:].rearrange("p (a n) -> p a n", a=4)),
                          [PSK(bi)], [("Wpad", c4)])
                sh9 = [128, 9, 4, 16]
                Lrb = bc(Lr[:, :, :, None], sh9); Lib = bc(Li[:, :, :, None], sh9)
                C0 = bc(Cst[:, 0, None, :, :], sh9); C1 = bc(Cst[:, 1, None, :, :], sh9)
                def f_(e, Lrb=Lrb, Lib=Lib, C0=C0, C1=C1):
                    e.tensor_tensor(l1[:], Lrb, C0, ALU.mult)
                    return e.tensor_tensor(l2[:], Lib, C1, ALU.mult)
                V(f_, ["Lr", "Li", "Cst", "LB1"], ["l1", "l2"])
                V(lambda e: e.tensor_tensor(CL[:, 0], l1[:], l2[:], ALU.subtract), ["l1", "l2"], ["CL0", "LB0", "LB1"])
                def f_(e, Lrb=Lrb, Lib=Lib, C0=C0, C1=C1):
                    e.tensor_tensor(l1[:], Lib, C0, ALU.mult)
                    return e.tensor_tensor(l2[:], Lrb, C1, ALU.mult)
                V(f_, ["Lr", "Li", "Cst", "CL0"], ["l1", "l2"])
                V(lambda e: e.scalar_tensor_tensor(CL[:, 1], l1[:], -1.0, l2[:], ALU.mult, ALU.subtract), ["l1", "l2"], ["CL1", "LB0", "LB1"])
                def f_(e):
                    for g2 in range(2):
                        for ri in range(2):
                            e.tensor_copy(Qd[64 * g2:64 * g2 + 64, :, ri, :, 16 * g2:16 * g2 + 16], CL[64 * g2:64 * g2 + 64, ri, :, :, :])
                            r = e.tensor_copy(Qf5[64 * g2:64 * g2 + 64, ri, :, g2, :], CL[64 * g2:64 * g2 + 64, ri, 0, :, :])
                    return r
                V(f_, ["CL0", "CL1"], ["Qpad", "Qf"])
                for half in range(2):
                    bi = half
                    def mmt(e, half=half, bi=bi):
                        for i4 in range(4):
                            tau = half * 4 + i4
                            e.matmul(ps[bi][:, i4 * 128:(i4 + 1) * 128], lhsT=What[:, tau, 0, :], rhs=Qf[:, 0, :], start=True, stop=False)
                            r = e.matmul(ps[bi][:, i4 * 128:(i4 + 1) * 128], lhsT=What[:, tau, 1, :], rhs=Qf[:, 1, :], start=False, stop=True)
                        return r
                    P.pe(mmt, reads=["What", "Qf"], writes=[PSK(bi)])
                    m4 = bc(mask32[:, None, :], [128, 4, 128])
                    if half == 0:
                        V(lambda e, bi=bi, m4=m4: e.tensor_tensor(TtF[:], ps[bi][:].rearrange("p (a n) -> p a n", a=4), m4, ALU.mult),
                          [PSK(bi), "cf"], ["TtF"])
                        V(lambda e, j=j: e.scalar_tensor_tensor(TtF[:, 0, :], ident, dcol[:, j:j + 1], TtF[:, 0, :], ALU.mult, ALU.add),
                          ["TtF", "dcol", "cf"], ["TtF"])
                        V(lambda e: e.tensor_copy(Tt[:, 0:4, :], TtF[:]), ["TtF"], [("Tt", 0)])
                    else:
                        V(lambda e, bi=bi, m4=m4: e.tensor_tensor(Tt[:, 4:8, :], ps[bi][:].rearrange("p (a n) -> p a n", a=4), m4, ALU.mult),
                          [PSK(bi), "cf"], [("Tt", 1)])
                uv = uT[:, j, :].rearrange("p (k s) -> p s k", s=8)
                def mmx(e, uv=uv):
                    for ri in range(2):
                        for tau in range(8):
                            for q in range(4):
                                r = e.matmul(ps[2 + q][:, ri * 256:(ri + 1) * 256], lhsT=Wt[32 * q:32 * q + 32, tau, ri, :],
                                             rhs=uv[32 * q:32 * q + 32, 7 - tau, :], start=(tau == 0), stop=(tau == 7),
                                             tile_position=(32 * q, 0))
                    return r
                P.pe(mmx, reads=[("Wpad", c4) for c4 in range(4)] + [("uT", j, s_) for s_ in range(8)], writes=[PSK(2 + q) for q in range(4)])
                V(lambda e: e.tensor_copy(ti8[:], tj[:, 8, :]), ["tj"], ["ti8"])
                V(lambda e: e.tensor_tensor(t8f[:], tj[:, 8, :], ti8[:], ALU.subtract), ["tj", "ti8"], ["t8f"])
                V(lambda e: e.tensor_tensor(tk, bc(t8f[:, :, None], [128, 4, 256]), bc(iota256[:, None, :], [128, 4, 256]), ALU.mult),
                  ["t8f", "cf"], ["tk"] + [("Ssc", ri_, q_) for ri_ in range(2) for q_ in range(4)])
                V(lambda e: e.tensor_copy(tki, tk), ["tk"], ["tki"])
                V(lambda e: e.tensor_tensor(xa[:], tk, tki, ALU.subtract), ["tk", "tki"], ["xa"])
                Aop(lambda e: e.activation(sinT[:], xa[:], AF.Sin, scale=TWO_PI), ["xa"], ["sinT"])
                V(lambda e: e.tensor_scalar(tk, tk, 0.25, None, ALU.add), ["tk", "tki"], ["tk"])
                V(lambda e: e.tensor_copy(tki, tk), ["tk"], ["tki"])
                V(lambda e: e.tensor_tensor(xb[:], tk, tki, ALU.subtract), ["tk", "tki"], ["xb"])
                Aop(lambda e: e.activation(cosT[:], xb[:], AF.Sin, scale=TWO_PI), ["xb"], ["cosT"])
                for q in range(4):
                    Xr = ps[2 + q][:, 0:256]; Xi = ps[2 + q][:, 256:512]
                    def f_(e, q=q, Xr=Xr, Xi=Xi):
                        e.tensor_tensor(xa[:, q, :], cosT[:, q, :], Xr, ALU.mult)
                        return e.tensor_tensor(xb[:, q, :], sinT[:, q, :], Xi, ALU.mult)
                    V(f_, [PSK(2 + q), "cosT", "sinT", "xa", "xb"], [("xa", q), ("xb", q)])
                    V(lambda e, q=q: e.tensor_tensor(Xp[:, 0, q, :], xa[:, q, :], xb[:, q, :], ALU.add), [("xa", q), ("xb", q)], [("Xp", 0, q)])
                    def f_(e, q=q, Xr=Xr, Xi=Xi):
                        e.tensor_tensor(xa[:, q, :], cosT[:, q, :], Xi, ALU.mult)
                        return e.tensor_tensor(xb[:, q, :], sinT[:, q, :], Xr, ALU.mult)
                    V(f_, [PSK(2 + q), "cosT", "sinT", ("Xp", 0, q)], [("xa", q), ("xb", q)])
                    V(lambda e, q=q: e.tensor_tensor(Xp[:, 1, q, :], xa[:, q, :], xb[:, q, :], ALU.subtract), [("xa", q), ("xb", q)], [("Xp", 1, q)])
                    for ri in range(2):
                        V(lambda e, q=q, ri=ri: e.tensor_tensor_scan(Ssc[:, ri, q, :], mag[:, 8, q:q + 1].to_broadcast([128, 256]),
                                                                     Xp[:, ri, q, :], 0.0, ALU.mult, ALU.add),
                          [("Xp", ri, q), "mag"], [("Ssc", ri, q), "tk", "tki"])
                allS = [("Ssc", ri, q) for ri in range(2) for q in range(4)]
                allx = [("xa", q) for q in range(4)] + [("xb", q) for q in range(4)]
                def f_(e):
                    e.tensor_tensor(xa[:], cosT[:], Ssc[:, 0], ALU.mult)
                    return e.tensor_tensor(xb[:], sinT[:], Ssc[:, 1], ALU.mult)
                V(f_, allS + ["cosT", "sinT"], allx + ["xa", "xb"])
                V(lambda e: e.tensor_tensor(Hb[:, 0, :, 1:257], xa[:], xb[:], ALU.subtract), ["xa", "xb"], ["Hb0"])
                def f_(e):
                    e.tensor_tensor(xa[:], cosT[:], Ssc[:, 1], ALU.mult)
                    return e.tensor_tensor(xb[:], sinT[:], Ssc[:, 0], ALU.mult)
                V(f_, allS + ["cosT", "sinT", "Hb0"], allx + ["xa", "xb"])
                V(lambda e: e.tensor_tensor(Hb[:, 1, :, 1:257], xa[:], xb[:], ALU.add), ["xa", "xb"], ["Hb1"])
            def partC(j):
                uv = uT[:, j, :].rearrange("p (k s) -> p s k", s=8)
                for tp in range(7, -1, -1):
                    bi = 6 + (tp % 2)
                    def mmy(e, tp=tp, bi=bi, uv=uv):
                        for s_ in range(tp + 1):
                            e.matmul(ps[bi][:, 0:256], lhsT=Tt[:, tp - s_, :], rhs=uv[:, s_, :], start=(s_ == 0), stop=False)
                        for ri in range(2):
                            for q in range(4):
                                r = e.matmul(ps[bi][32 * q:32 * q + 32, 0:256], lhsT=Qd[:, tp + 1, ri, q, :], rhs=Hb[:, ri, q, 0:256], start=False,
                                             stop=(ri == 1), tile_position=(0, 32 * q))
                        return r
                    P.pe(mmy, reads=[("uT", j, s_) for s_ in range(tp + 1)] + [("Tt", 0), ("Tt", 1), "Qpad", "Hb0", "Hb1"], writes=[PSK(bi)])
                    Aop(lambda e, tp=tp, bi=bi, uv=uv: e.activation(uv[:, tp, :], ps[bi][:, 0:256], AF.Gelu_apprx_tanh),
                        [PSK(bi)], [("uT", j, tp)])
            partA(0)
            for j in range(8):
                partB(j)
                if j < 7:
                    partA(j + 1)
                partC(j)
        P.barrier()
        with SBT(nc, "wg", [128, 2, 2, 8, 512], BF16) as wg, SBT(nc, "sig", [128, 2, 512], F32) as sig, \
                SBT(nc, "mixb", [128, 2, 512], F32) as mixb:
            wglu = T["ssm_w_glu"][0].rearrange("(k p) n -> p k n", p=128)
            it = 0
            for mg in range(2):
                wb = mg % 2
                for vg in range(2):
                    c0 = vg * D + mg * 512
                    P.dma(lambda e, wb=wb, vg=vg, c0=c0: e.dma_start(out=wg[:, wb, vg, :, :], in_=wglu[:, :, c0:c0 + 512]),
                          writes=[("wg", wb, vg)], q="pool")
                for ml in range(4):
                    m = mg * 4 + ml
                    for tq in range(4):
                        pb = it % 2
                        it += 1
                        for vg in range(2):
                            bi = 2 + 2 * vg + pb
                            def mm(e, wb=wb, vg=vg, bi=bi, tq=tq, ml=ml):
                                for k in range(8):
                                    r = e.matmul(ps[bi][:], lhsT=wg[:, wb, vg, k, ml * 128:(ml + 1) * 128], rhs=uT[:, k, tq * 512:(tq + 1) * 512],
                                                 start=(k == 0), stop=(k == 7))
                                return r
                            P.pe(mm, reads=[("wg", wb, vg)], writes=[PSK(bi)])
                        Aop(lambda e, pb=pb: e.activation(sig[:, pb, :], ps[4 + pb][:], AF.Sigmoid), [PSK(4 + pb)], [("sig", pb)])
                        V(lambda e, pb=pb: e.tensor_tensor(mixb[:, pb, :], ps[2 + pb][:], sig[:, pb, :], ALU.mult), [PSK(2 + pb), ("sig", pb)], [("mixb", pb)])
                        V(lambda e, pb=pb, m=m, tq=tq: e.tensor_tensor(xT[:, m, tq * 512:(tq + 1) * 512], xT[:, m, tq * 512:(tq + 1) * 512],
                                                                     mixb[:, pb, :], ALU.add), [("mixb", pb), ("xT", m)], [("xT", m)])


_CACHE = {}


def kernel(**inputs):
    if "prog" not in _CACHE:
        _CACHE["prog"] = build_program()
    nc, _ = _CACHE["prog"]
    consts = make_consts()
    in_maps = []
    for c in range(8):
        m = {}
        for name, shape in INPUT_SPECS:
            if name == "consts":
                m[name] = consts
            elif name in ("x", "mem"):
                m[name] = np.ascontiguousarray(np.asarray(inputs[name], dtype=np.float32)[c * NB:(c + 1) * NB])
            else:
                m[name] = np.ascontiguousarray(np.asarray(inputs[name], dtype=np.float32))
        in_maps.append(m)
    res = run_bass_kernel_spmd(nc, in_maps, core_ids=list(range(8)))
    return np.concatenate([r["out"] for r in res.results], axis=0)
```

```python
import math
import numpy as np
import concourse.bass as bass
from concourse.ap import AP
import concourse.mybir as mybir
from concourse.bass_utils import run_bass_kernel_spmd

F32 = mybir.dt.float32
BF16 = mybir.dt.bfloat16
I32 = mybir.dt.int32
AF = mybir.ActivationFunctionType
ALU = mybir.AluOpType

COMPUTE = ("pe", "act", "dve", "pool")
ALLENG = ("pe", "act", "dve", "pool", "sp")
NDMA_SEMS = 40

S = 2048
D = 1024
NB = 2
DFF = 2816
NF = DFF // 128
MEM = 256


class Op:
    __slots__ = ("eng", "fn", "deps", "idx", "dma", "signal", "cnt", "sem", "clock")


class Prog:
    def __init__(self, nc):
        self.nc = nc
        self.ops = []
        self.last_write = {}
        self.readers = {}
        self.dma_hist = []
        self.n_dma = 0
        self.last_on = {}
        self.dma_since = []

    def add(self, eng, fn, reads=(), writes=(), dma=False, extra_deps=()):
        op = Op()
        op.eng, op.fn, op.dma = eng, fn, dma
        op.idx = len(self.ops)
        op.signal = False
        op.cnt = None
        op.sem = None
        op.clock = None
        deps = set(extra_deps)
        for k in reads:
            w = self.last_write.get(k)
            if w is not None:
                deps.add(w)
        for k in writes:
            w = self.last_write.get(k)
            if w is not None:
                deps.add(w)
            r = self.readers.get(k)
            if r:
                deps.update(r)
        for k in reads:
            self.readers.setdefault(k, []).append(op.idx)
        for k in writes:
            self.last_write[k] = op.idx
            self.readers[k] = []
        if dma:
            j = self.n_dma
            self.n_dma += 1
            op.sem = j % NDMA_SEMS
            if j >= NDMA_SEMS:
                deps.add(self.dma_hist[j - NDMA_SEMS])
            self.dma_hist.append(op.idx)
            self.dma_since.append(op.idx)
        else:
            self.last_on[eng] = op.idx
        deps.discard(op.idx)
        if eng == "pe" and not dma:
            deps = {d_ for d_ in deps if self.ops[d_].eng != "pe" or self.ops[d_].dma}
        op.deps = deps
        self.ops.append(op)
        return op

    def pe(self, fn, reads=(), writes=()):
        return self.add("pe", fn, reads, writes)

    def act(self, fn, reads=(), writes=()):
        return self.add("act", fn, reads, writes)

    def dve(self, fn, reads=(), writes=()):
        return self.add("dve", fn, reads, writes)

    def dma(self, fn, reads=(), writes=(), q="sp"):
        return self.add(q, fn, reads, writes, dma=True)

    def barrier(self):
        deps = set(self.last_on.values()) | set(self.dma_since)
        self.dma_since = []
        for e in ALLENG:
            self.add(e, lambda eng: eng.nop(), extra_deps=deps)
        self.last_write = {}
        self.readers = {}

    def emit(self, final_ops):
        nc = self.nc
        ops = self.ops
        for op in ops:
            for d in op.deps:
                ops[d].signal = True
        for op in final_ops:
            op.signal = True
        eng_cnt = {e: 0 for e in ALLENG}
        dma_cnt = [0] * NDMA_SEMS
        for op in ops:
            if op.dma:
                dma_cnt[op.sem] += 16
                op.cnt = dma_cnt[op.sem]
            elif op.signal:
                eng_cnt[op.eng] += 1
                op.cnt = eng_cnt[op.eng]
        sems = {e: nc.alloc_semaphore("s_" + e) for e in ALLENG}
        dsems = [nc.alloc_semaphore("d_%d" % i) for i in range(NDMA_SEMS)]

        def key_of(o):
            return ("d", o.sem) if o.dma else o.eng

        know = {e: {} for e in ALLENG}
        waits = {}
        for op in ops:
            K = know[op.eng]
            wl = []
            for d in sorted(op.deps, reverse=True):
                dop = ops[d]
                k = key_of(dop)
                if K.get(k, 0) >= dop.cnt:
                    continue
                for kk, vv in dop.clock.items():
                    if K.get(kk, 0) < vv:
                        K[kk] = vv
                K[k] = max(K.get(k, 0), dop.cnt)
                wl.append((dsems[dop.sem] if dop.dma else sems[dop.eng], dop.cnt))
            waits[op.idx] = wl
            if op.signal or op.dma:
                op.clock = dict(K)
        by_eng = {e: [] for e in ALLENG}
        for op in ops:
            by_eng[op.eng].append(op)
        fin = [(dsems[o.sem] if o.dma else sems[o.eng], o.cnt) for o in final_ops]
        self.n_inst = {e: len(by_eng[e]) for e in ALLENG}

        def run(engname, e):
            for op in by_eng[engname]:
                for (s, v) in waits[op.idx]:
                    e.wait_ge(s, v)
                ins = op.fn(e)
                if op.dma:
                    ins.then_inc(dsems[op.sem], 16)
                elif op.signal:
                    ins.then_inc(sems[op.eng], 1)
            if engname == "sp":
                for (s, v) in fin:
                    e.wait_ge(s, v)

        with nc.Block() as block:
            @block.tensor
            def _(e):
                run("pe", e)

            @block.scalar
            def _(e):
                run("act", e)

            @block.vector
            def _(e):
                run("dve", e)

            @block.gpsimd
            def _(e):
                run("pool", e)

            @block.sync
            def _(e):
                run("sp", e)


INPUT_SPECS = [
    ("x", [NB, S, D]), ("mem", [NB, MEM, D]),
    ("norm_mix", [2, D]), ("norm_xattn", [2, D]), ("norm_ffn", [2, D]), ("norm_mem", [D]), ("norm_final", [D]),
    ("ab_w_in", [1, D, 2048]), ("pool_w", [1, 4, 128, 128]), ("pool_scale", [1, 512]), ("ab_w_out", [1, D, D]),
    ("ssm_w_in", [1, D, D]), ("ssm_lam_re", [1, 64, 64]), ("ssm_lam_im", [1, 64, 64]), ("ssm_log_dt", [1, 64]),
    ("ssm_b_re", [1, 64, 64, 16]), ("ssm_b_im", [1, 64, 64, 16]), ("ssm_c_re", [1, 64, 16, 64]),
    ("ssm_c_im", [1, 64, 16, 64]), ("ssm_d", [1, D]), ("ssm_w_glu", [1, D, 2 * D]),
    ("xa_w_q", [2, D, D]), ("xa_w_kv", [2, D, 2 * D]), ("xa_w_o", [2, D, D]),
    ("ffn_w_up", [2, D, 2 * DFF]), ("ffn_conv_w", [2, 3, 2 * DFF]), ("ffn_conv_b", [2, 2 * DFF]),
    ("ffn_w_down", [2, DFF, D]),
    ("consts", [128, 12, 128]),
]


def make_consts():
    c = np.zeros((128, 12, 128), np.float32)
    j = np.arange(128)
    c[:, 0, :] = np.eye(128)
    c[:, 1, :] = -(j[:, None] > j[None, :]).astype(np.float32)
    c[:, 2, :] = -1.0
    c[:, 3, :] = 1.0
    c[:, 4, :] = (j[:, None] < j[None, :]).astype(np.float32)
    c[:, 5, :] = (j[:, None] // 32 == j[None, :] // 32).astype(np.float32)
    c[:, 6, :] = np.arange(128)[None, :]
    c[:, 7, :] = 128 + np.arange(128)[None, :]
    c[:, 8, :] = 1.0 / (1.0 + np.arange(128))[None, :]
    c[:, 9, :] = -(j[:, None] >= j[None, :]).astype(np.float32)
    c[:, 10, :] = -30000.0 * (j[:, None] >= j[None, :])
    return c


class Ctx:
    pass


_uid = [0]


def SBT(nc, name, shape, dt):
    _uid[0] += 1
    return nc.sbuf_tensor("%s_%d" % (name, _uid[0]), shape, dt)


def build_program(stop=None, nb=NB):
    nc = bass.Bass("TRN2", target_bir_lowering=False)
    P = Prog(nc)
    C = Ctx()
    C.nc, C.P = nc, P
    T = {}
    for name, shape in INPUT_SPECS:
        T[name] = nc.dram_tensor(name, shape, F32, kind="ExternalInput").ap()
    out = nc.dram_tensor("out", [NB, S, D], F32, kind="ExternalOutput").ap()
    C.T = T

    def sb(name, shape, dt=F32):
        return nc.alloc_sbuf_tensor(name, shape, dt)

    xT = sb("xT", [128, 8, S])
    cf = sb("cf", [128, 10, 128])
    cb = sb("cb", [128, 6, 128], BF16)
    gains = sb("gains", [128, 8, 8])
    convp = sb("convp", [128, 2, 4, 44])
    pscale = sb("pscale", [128, 4])
    zer = sb("zer", [128, 512], BF16)
    memT = sb("memT", [128, 8, MEM], BF16)
    ps = [nc.alloc_psum_tensor("ps%d" % i, [128, 512], F32) for i in range(8)]
    ident = cf[:, 0, :]
    maskstrict = cf[:, 4, :]

    def PSK(i):
        return ("ps", i)

    P.dma(lambda e: e.dma_start(out=cf[:], in_=T["consts"][:, 0:10, :]), writes=["cf"])
    P.dma(lambda e: e.dma_start(out=cb[:, 0:4, :], in_=T["consts"][:, 0:4, :]), writes=["cb"], q="pool")
    P.dma(lambda e: e.dma_start(out=cb[:, 4:6, :], in_=T["consts"][:, 9:11, :]), writes=["cb2"], q="pool")
    gsrc = [T["norm_mix"][0], T["norm_mix"][1], T["norm_xattn"][0], T["norm_xattn"][1],
            T["norm_ffn"][0], T["norm_ffn"][1], T["norm_mem"], T["norm_final"]]
    for i, g in enumerate(gsrc):
        P.dma(lambda e, i=i, g=g: e.dma_start(out=gains[:, i, :], in_=g.rearrange("(t p) -> p t", p=128),
                                             allow_slow_non_contiguous=True), writes=["gains"], q="act")
    for l in range(2):
        for i in range(3):
            P.dma(lambda e, l=l, i=i: e.dma_start(out=convp[:, l, i, :],
                                                  in_=T["ffn_conv_w"][l, i].rearrange("(t p) -> p t", p=128),
                                                  allow_slow_non_contiguous=True), writes=["convp"], q="act")
        P.dma(lambda e, l=l: e.dma_start(out=convp[:, l, 3, :],
                                         in_=T["ffn_conv_b"][l].rearrange("(t p) -> p t", p=128),
                                         allow_slow_non_contiguous=True), writes=["convp"], q="act")
    P.dma(lambda e: e.dma_start(out=pscale[:], in_=T["pool_scale"][0].rearrange("(t p) -> p t", p=128),
                                allow_slow_non_contiguous=True), writes=["pscale"], q="act")
    P.dve(lambda e: e.memset(zer[:], 0.0), writes=["zer"])

    def load_w(dst, src2d, key, k_tiles, col0, ncols):
        v = src2d.rearrange("(k p) n -> p k n", p=128)
        for k in range(k_tiles):
            P.dma(lambda e, k=k: e.dma_start(out=dst[:, k, :], in_=v[:, k, col0:col0 + ncols]),
                  writes=[(key, k)], q="pool")

    def rmsnorm_tile(hT, hkey, gi, t0, n, sq, rstd, part="all"):
        if part in ("all", "sq"):
            for dt in range(8):
                P.act(lambda e, dt=dt: e.activation(sq[:, dt, 0:n], xT[:, dt, t0:t0 + n], AF.Square),
                      reads=[("xT", dt)], writes=[("sq", dt)])
        if part == "sq":
            return
        def mm(e):
            for dt in range(8):
                r = e.matmul(ps[7][:, 0:n], lhsT=cb[:, 3, :], rhs=sq[:, dt, 0:n], start=(dt == 0), stop=(dt == 7))
            return r
        P.pe(mm, reads=[("sq", dt) for dt in range(8)] + ["cb"], writes=[PSK(7)])
        P.dve(lambda e: e.tensor_scalar(rstd[:, 0:n], ps[7][:, 0:n], 1.0 / D, 1e-6, ALU.mult, ALU.add),
              reads=[PSK(7)], writes=["rstd"])
        P.act(lambda e: e.activation(rstd[:, 0:n], rstd[:, 0:n], AF.Ln), reads=["rstd"], writes=["rstd"])
        P.act(lambda e: e.activation(rstd[:, 0:n], rstd[:, 0:n], AF.Exp, scale=-0.5), reads=["rstd"], writes=["rstd"])
        for dt in range(8):
            P.dve(lambda e, dt=dt: e.scalar_tensor_tensor(hT[:, dt, 0:n], xT[:, dt, t0:t0 + n], gains[:, gi, dt:dt + 1],
                                                          rstd[:, 0:n], ALU.mult, ALU.mult),
                  reads=[("xT", dt), "rstd", "gains"], writes=[(hkey, dt)])

    C.ps_rr = 0

    def linear_fm(w, wkey, act, akey, k_tiles, m_tiles, n, evac, a0=0, banks=(0, 1)):
        for m in range(m_tiles):
            bi = banks[C.ps_rr % len(banks)]
            C.ps_rr += 1
            def mm(e, m=m, bi=bi):
                for k in range(k_tiles):
                    r = e.matmul(ps[bi][:, 0:n], lhsT=w[:, k, m * 128:(m + 1) * 128], rhs=act[:, k, a0:a0 + n],
                                 start=(k == 0), stop=(k == k_tiles - 1))
                return r
            P.pe(mm, reads=[(wkey, k) for k in range(k_tiles)] + [(akey, k) for k in range(k_tiles)], writes=[PSK(bi)])
            evac(m, bi, ps[bi][:, 0:n])

    def resid_add(m, bi, pap, t0, n):
        P.dve(lambda e: e.tensor_tensor(xT[:, m, t0:t0 + n], xT[:, m, t0:t0 + n], pap, ALU.add),
              reads=[PSK(bi), ("xT", m)], writes=[("xT", m)])

    final_ops = []
    for b in range(nb):
        P.barrier()
        with SBT(nc, "xin", [128, 2, D], F32) as xin, SBT(nc, "sq", [128, 8, 512], BF16) as sq, \
                SBT(nc, "rstd", [128, 512], F32) as rstd, SBT(nc, "mn", [128, 2, D], F32) as mn, \
                SBT(nc, "ssq", [128, 4], F32) as ssq:
            for tt in range(16):
                xb = tt % 2
                P.dma(lambda e, tt=tt, xb=xb, b=b: e.dma_start(out=xin[:, xb, :], in_=T["x"][b, tt * 128:(tt + 1) * 128, :]),
                      writes=[("xin", xb)])
                for half in range(2):
                    bi = (tt * 2 + half) % 2
                    def tr(e, xb=xb, half=half, bi=bi):
                        for q in range(4):
                            dt = half * 4 + q
                            r = e.transpose(ps[bi][:, q * 128:(q + 1) * 128], xin[:, xb, dt * 128:(dt + 1) * 128], ident)
                        return r
                    P.pe(tr, reads=[("xin", xb), "cf"], writes=[PSK(bi)])
                    P.act(lambda e, tt=tt, half=half, bi=bi: e.activation(
                        xT[:, half * 4:half * 4 + 4, tt * 128:(tt + 1) * 128],
                        ps[bi][:].rearrange("p (q t) -> p q t", q=4), AF.Copy),
                        reads=[PSK(bi)], writes=[("xT", half * 4 + q) for q in range(4)])
            P.dma(lambda e, b=b: e.dma_start(out=mn[:], in_=T["mem"][b].rearrange("(t p) d -> p t d", p=128)), writes=["mn"])
            for t in range(2):
                P.act(lambda e, t=t: e.activation(xin[:, t, :], mn[:, t, :], AF.Square, accum_out=ssq[:, t:t + 1]),
                      reads=["mn"], writes=[("ssq", t), ("xin", t)])
            P.dve(lambda e: e.tensor_scalar(ssq[:, 2:4], ssq[:, 0:2], 1.0 / D, 1e-6, ALU.mult, ALU.add),
                  reads=[("ssq", 0), ("ssq", 1)], writes=["ssq2"])
            P.act(lambda e: e.activation(ssq[:, 2:4], ssq[:, 2:4], AF.Sqrt), reads=["ssq2"], writes=["ssq2"])
            P.dve(lambda e: e.reciprocal(ssq[:, 2:4], ssq[:, 2:4]), reads=["ssq2"], writes=["ssq2"])
            for t in range(2):
                P.dve(lambda e, t=t: e.tensor_scalar(mn[:, t, :], mn[:, t, :], ssq[:, 2 + t:3 + t], None, ALU.mult),
                      reads=["mn", "ssq2"], writes=["mn"])
            for t in range(2):
                for half in range(2):
                    bi = (t * 2 + half) % 2
                    def tr(e, t=t, half=half, bi=bi):
                        for q in range(4):
                            dt = half * 4 + q
                            r = e.transpose(ps[bi][:, q * 128:(q + 1) * 128], mn[:, t, dt * 128:(dt + 1) * 128], ident)
                        return r
                    P.pe(tr, reads=["mn", "cf"], writes=[PSK(bi)])
                    for q in range(4):
                        dt = half * 4 + q
                        P.dve(lambda e, t=t, q=q, dt=dt, bi=bi: e.tensor_scalar(
                            memT[:, dt, t * 128:(t + 1) * 128], ps[bi][:, q * 128:(q + 1) * 128],
                            gains[:, 6, dt:dt + 1], None, ALU.mult),
                            reads=[PSK(bi), "gains"], writes=[("memT", dt)])
        if stop == "load":
            pass
        else:
            for layer in range(2):
                if layer == 0:
                    stage_mix_ab(C, b, xT, ps, cf, cb, gains, pscale, zer, rmsnorm_tile, load_w, linear_fm, resid_add)
                else:
                    stage_mix_s5(C, b, xT, ps, cf, cb, gains, rmsnorm_tile, load_w, linear_fm, resid_add)
                if stop == "mix%d" % layer:
                    break
                stage_xattn(C, b, layer, xT, ps, cb, gains, memT, rmsnorm_tile, load_w, linear_fm, resid_add)
                if stop == "xa%d" % layer:
                    break
                stage_ffn(C, b, layer, xT, ps, gains, convp, rmsnorm_tile, load_w, linear_fm, resid_add)
                if stop == "ffn%d" % layer:
                    break
        P.barrier()
        with SBT(nc, "sq", [128, 8, 512], BF16) as sq, SBT(nc, "rstd", [128, 512], F32) as rstd, \
                SBT(nc, "yT", [128, 8, 512], F32) as yT, SBT(nc, "yo", [128, 2, D], F32) as yo:
            for tq in range(4):
                t0 = tq * 512
                if stop is None:
                    rmsnorm_tile(yT, "yT", 7, t0, 512, sq, rstd)
                else:
                    for dt in range(8):
                        P.act(lambda e, dt=dt, t0=t0: e.activation(yT[:, dt, :], xT[:, dt, t0:t0 + 512], AF.Copy),
                              reads=[("xT", dt)], writes=[("yT", dt)])
                for ts in range(4):
                    ob = ts % 2
                    for half in range(2):
                        bi = (ts * 2 + half) % 2
                        def tr(e, ts=ts, half=half, bi=bi):
                            for q in range(4):
                                dt = half * 4 + q
                                r = e.transpose(ps[bi][:, q * 128:(q + 1) * 128], yT[:, dt, ts * 128:(ts + 1) * 128], ident)
                            return r
                        P.pe(tr, reads=[("yT", dt) for dt in range(8)] + ["cf"], writes=[PSK(bi)])
                        P.act(lambda e, ob=ob, half=half, bi=bi: e.activation(yo[:, ob, half * 512:(half + 1) * 512],
                                                                               ps[bi][:], AF.Copy),
                              reads=[PSK(bi)], writes=[("yo", ob, half)])
                    tok = t0 + ts * 128
                    o = P.dma(lambda e, ob=ob, tok=tok, b=b: e.dma_start(out=out[b, tok:tok + 128, :], in_=yo[:, ob, :]),
                              reads=[("yo", ob, 0), ("yo", ob, 1)], writes=[("out", b, tok)])
                    final_ops.append(o)
    P.emit(final_ops)
    C.final = final_ops
    return nc, P


def stage_xattn(C, b, layer, xT, ps, cb, gains, memT, rmsnorm_tile, load_w, linear_fm, resid_add):
    nc, P, T = C.nc, C.P, C.T
    P.barrier()

    def PSK(i):
        return ("ps", i)
    with SBT(nc, "wq", [128, 8, D], BF16) as wq, SBT(nc, "wo", [128, 8, D], BF16) as wo, \
            SBT(nc, "wkv", [128, 8, D], BF16) as wkv, \
            SBT(nc, "KT", [128, 8, MEM], BF16) as KT, SBT(nc, "V", [128, 2, D], BF16) as V, \
            SBT(nc, "sq", [128, 8, 512], BF16) as sq, SBT(nc, "rstd", [128, 512], F32) as rstd, \
            SBT(nc, "hT", [128, 2, 8, 512], BF16) as hT, SBT(nc, "qT", [128, 8, 512], BF16) as qT, \
            SBT(nc, "pT", [128, 2, 2, 512], BF16) as pT, SBT(nc, "rs", [128, 2, 512], F32) as rs, \
            SBT(nc, "oT", [128, 8, 512], BF16) as oT:
        load_w(wkv, T["xa_w_kv"][layer], "wkv", 8, 0, D)
        load_w(wq, T["xa_w_q"][layer], "wq", 8, 0, D)

        def evK(m, bi, pap):
            P.act(lambda e: e.activation(KT[:, m, :], pap, AF.Copy), reads=[PSK(bi)], writes=[("KT", m)])
        linear_fm(wkv, "wkv", memT, "memT", 8, 8, MEM, evK)
        load_w(wkv, T["xa_w_kv"][layer], "wkv", 8, D, D)
        load_w(wo, T["xa_w_o"][layer], "wo", 8, 0, D)
        for mt in range(2):
            for nh in range(2):
                bi = (mt * 2 + nh) % 2
                def mm(e, mt=mt, nh=nh, bi=bi):
                    for k in range(8):
                        r = e.matmul(ps[bi][:], lhsT=memT[:, k, mt * 128:(mt + 1) * 128], rhs=wkv[:, k, nh * 512:(nh + 1) * 512],
                                     start=(k == 0), stop=(k == 7))
                    return r
                P.pe(mm, reads=[("wkv", k) for k in range(8)] + [("memT", k) for k in range(8)], writes=[PSK(bi)])
                P.act(lambda e, mt=mt, nh=nh, bi=bi: e.activation(V[:, mt, nh * 512:(nh + 1) * 512], ps[bi][:], AF.Copy),
                      reads=[PSK(bi)], writes=[("V", mt, nh)])
        rmsnorm_tile(hT[:, 0], "hT0", 2 + layer, 0, 512, sq, rstd)
        for tq in range(4):
            t0 = tq * 512
            tb = tq % 2

            def evQ(m, bi, pap):
                P.act(lambda e: e.activation(qT[:, m, :], pap, AF.Copy, scale=1.0 / 16.0), reads=[PSK(bi)], writes=[("qT", m)])
            linear_fm(wq, "wq", hT[:, tb], "hT%d" % tb, 8, 8, 512, evQ)
            if tq + 1 < 4:
                rmsnorm_tile(hT[:, 1 - tb], "hT%d" % (1 - tb), 2 + layer, t0 + 512, 512, sq, rstd, part="sq")
            def scores(h):
                hb = h % 2
                for mt in range(2):
                    bk = 2 + 2 * hb + mt
                    def mm(e, h=h, mt=mt, bk=bk):
                        for d in range(2):
                            r = e.matmul(ps[bk][:], lhsT=KT[:, 2 * h + d, mt * 128:(mt + 1) * 128], rhs=qT[:, 2 * h + d, :],
                                         start=(d == 0), stop=(d == 1))
                        return r
                    P.pe(mm, reads=[("KT", 2 * h), ("KT", 2 * h + 1), ("qT", 2 * h), ("qT", 2 * h + 1)], writes=[PSK(bk)])
                    P.act(lambda e, mt=mt, hb=hb, bk=bk: e.activation(pT[:, hb, mt, :], ps[bk][:], AF.Exp),
                          reads=[PSK(bk)], writes=[("pT", hb, mt)])

            def rest(h):
                hb = h % 2
                def mms(e, hb=hb):
                    e.matmul(ps[6][:], lhsT=cb[:, 3, :], rhs=pT[:, hb, 0, :], start=True, stop=False)
                    return e.matmul(ps[6][:], lhsT=cb[:, 3, :], rhs=pT[:, hb, 1, :], start=False, stop=True)
                P.pe(mms, reads=[("pT", hb, 0), ("pT", hb, 1), "cb"], writes=[PSK(6)])
                P.act(lambda e, hb=hb: e.activation(rs[:, hb, :], ps[6][:], AF.Ln), reads=[PSK(6)], writes=[("rs", hb)])
                P.act(lambda e, hb=hb: e.activation(rs[:, hb, :], rs[:, hb, :], AF.Exp, scale=-1.0), reads=[("rs", hb)], writes=[("rs", hb)])
                for d in range(2):
                    bi = 7 if d == 0 else 1
                    def mmo(e, h=h, hb=hb, d=d, bi=bi):
                        for mt in range(2):
                            r = e.matmul(ps[bi][:], lhsT=V[:, mt, h * 256 + d * 128:h * 256 + (d + 1) * 128], rhs=pT[:, hb, mt, :],
                                         start=(mt == 0), stop=(mt == 1))
                        return r
                    P.pe(mmo, reads=[("V", 0, h // 2), ("V", 1, h // 2), ("pT", hb, 0), ("pT", hb, 1)], writes=[PSK(bi)])
                    P.dve(lambda e, h=h, hb=hb, d=d, bi=bi: e.tensor_tensor(oT[:, 2 * h + d, :], ps[bi][:], rs[:, hb, :], ALU.mult),
                          reads=[PSK(bi), ("rs", hb)], writes=[("oT", 2 * h + d)])

            scores(0)
            for h in range(4):
                if h < 3:
                    scores(h + 1)
                rest(h)
            if tq + 1 < 4:
                rmsnorm_tile(hT[:, 1 - tb], "hT%d" % (1 - tb), 2 + layer, t0 + 512, 512, sq, rstd, part="rest")
            linear_fm(wo, "wo", oT, "oT", 8, 8, 512, lambda m, bi, pap, t0=t0: resid_add(m, bi, pap, t0, 512))


def stage_ffn(C, b, layer, xT, ps, gains, convp, rmsnorm_tile, load_w, linear_fm, resid_add):
    nc, P, T = C.nc, C.P, C.T
    P.barrier()

    def PSK(i):
        return ("ps", i)
    with SBT(nc, "sq", [128, 8, 512], BF16) as sq, SBT(nc, "rstd", [128, 512], F32) as rstd, \
            SBT(nc, "hT", [128, 2, 8, 512], BF16) as hT, SBT(nc, "gT", [128, NF, 512], BF16) as gT, \
            SBT(nc, "wu", [128, 2, 2, 8, 512], BF16) as wu, SBT(nc, "wd", [128, 4, D], BF16) as wd, \
            SBT(nc, "ub", [128, 3, 2, 516], F32) as ub, SBT(nc, "cv", [128, 3, 2, 512], F32) as cv, \
            SBT(nc, "halo", [128, 2 * NF, 2], F32) as halo:
        wup = T["ffn_w_up"][layer].rearrange("(k p) n -> p k n", p=128)
        wdn = T["ffn_w_down"][layer]
        P.dve(lambda e: e.memset(halo[:], 0.0), writes=["halo"])
        groups = [(0, 4), (4, 4), (8, 4), (12, 4), (16, 4), (20, 2)]
        it = 0
        git = 0
        kit = 0
        rmsnorm_tile(hT[:, 0], "hT0", 4 + layer, 0, 512, sq, rstd)
        pend = []

        def tail(pb, fp):
            P.act(lambda e: e.activation(cv[:, pb, 1, :], cv[:, pb, 1, :], AF.Silu),
                  reads=[("cv", pb, 1)], writes=[("cv", pb, 1)])
            P.dve(lambda e: e.tensor_tensor(gT[:, fp, :], cv[:, pb, 0, :], cv[:, pb, 1, :], ALU.mult),
                  reads=[("cv", pb, 0), ("cv", pb, 1)], writes=[("gT", fp)])
        for tq in range(4):
            t0 = tq * 512
            hb = tq % 2
            if tq + 1 < 4:
                rmsnorm_tile(hT[:, 1 - hb], "hT%d" % (1 - hb), 4 + layer, t0 + 512, 512, sq, rstd, part="sq")
            for (f0, nf) in groups:
                wb = git % 2
                git += 1
                for vg in range(2):
                    col0 = vg * DFF + f0 * 128
                    P.dma(lambda e, wb=wb, vg=vg, col0=col0, nf=nf: e.dma_start(out=wu[:, wb, vg, :, 0:nf * 128],
                                                                              in_=wup[:, :, col0:col0 + nf * 128]),
                          writes=[("wu", wb, vg)], q="pool")
                for fl in range(nf):
                    fp = f0 + fl
                    pb = it % 3
                    it += 1
                    for vg in range(2):
                        bi = 3 * vg + pb
                        f = vg * NF + fp
                        def mm(e, wb=wb, vg=vg, bi=bi, fl=fl, hb=hb):
                            for k in range(8):
                                r = e.matmul(ps[bi][:], lhsT=wu[:, wb, vg, k, fl * 128:(fl + 1) * 128], rhs=hT[:, hb, k, :], start=(k == 0), stop=(k == 7))
                            return r
                        P.pe(mm, reads=[("wu", wb, vg)] + [("hT%d" % hb, k) for k in range(8)], writes=[PSK(bi)])
                        P.act(lambda e, pb=pb, vg=vg, bi=bi: e.activation(ub[:, pb, vg, 2:514], ps[bi][:], AF.Copy),
                              reads=[PSK(bi)], writes=[("ub", pb, vg)])
                        P.act(lambda e, pb=pb, vg=vg, f=f: e.activation(ub[:, pb, vg, 0:2], halo[:, f, :], AF.Copy),
                              reads=["halo%d" % f, "halo"], writes=[("ubh", pb, vg)])
                        P.act(lambda e, pb=pb, vg=vg, f=f, bi=bi: e.activation(cv[:, pb, vg, :], ps[bi][:], AF.Identity,
                                                                               bias=convp[:, layer, 3, f:f + 1],
                                                                               scale=convp[:, layer, 2, f:f + 1]),
                              reads=[PSK(bi), "convp"], writes=[("cv", pb, vg)])
                        P.dve(lambda e, pb=pb, vg=vg, f=f: e.scalar_tensor_tensor(cv[:, pb, vg, :], ub[:, pb, vg, 1:513],
                                                                                  convp[:, layer, 1, f:f + 1], cv[:, pb, vg, :],
                                                                                  ALU.mult, ALU.add),
                              reads=[("ub", pb, vg), ("ubh", pb, vg), ("cv", pb, vg), "convp"], writes=[("cv", pb, vg)])
                        P.dve(lambda e, pb=pb, vg=vg, f=f: e.scalar_tensor_tensor(cv[:, pb, vg, :], ub[:, pb, vg, 0:512],
                                                                                  convp[:, layer, 0, f:f + 1], cv[:, pb, vg, :],
                                                                                  ALU.mult, ALU.add),
                              reads=[("ub", pb, vg), ("ubh", pb, vg), ("cv", pb, vg), "convp"], writes=[("cv", pb, vg)])
                        P.dve(lambda e, pb=pb, vg=vg, f=f: e.tensor_copy(halo[:, f, :], ub[:, pb, vg, 512:514]),
                              reads=[("ub", pb, vg)], writes=["halo%d" % f])
                    if pend:
                        tail(*pend.pop())
                    pend.append((pb, fp))
            if pend:
                tail(*pend.pop())
            if tq + 1 < 4:
                rmsnorm_tile(hT[:, 1 - hb], "hT%d" % (1 - hb), 4 + layer, t0 + 512, 512, sq, rstd, part="rest")
            for k in range(NF):
                db = kit % 4
                kit += 1
                P.dma(lambda e, db=db, k=k: e.dma_start(out=wd[:, db, :], in_=wdn[k * 128:(k + 1) * 128, :]),
                      writes=[("wd", db)], q="pool")
                def mm(e, db=db, k=k):
                    for m in range(8):
                        r = e.matmul(ps[m][:], lhsT=wd[:, db, m * 128:(m + 1) * 128], rhs=gT[:, k, :], start=(k == 0), stop=(k == NF - 1))
                    return r
                P.pe(mm, reads=[("wd", db), ("gT", k)], writes=[PSK(m) for m in range(8)])
            for m in range(8):
                resid_add(m, m, ps[m][:], t0, 512)


def stage_mix_ab(C, b, xT, ps, cf, cb, gains, pscale, zer, rmsnorm_tile, load_w, linear_fm, resid_add):
    nc, P, T = C.nc, C.P, C.T
    P.barrier()
    maskstrict = cf[:, 4, :]

    def PSK(i):
        return ("ps", i)
    win = T["ab_w_in"][0]
    with SBT(nc, "hT", [128, 8, S], BF16) as hT, SBT(nc, "aT", [128, 4, S], BF16) as aT, \
            SBT(nc, "pTo", [128, 4, S], BF16) as pTo:
        with SBT(nc, "sq", [128, 8, 512], BF16) as sq, SBT(nc, "rstd", [128, 512], F32) as rstd, \
                SBT(nc, "hTt", [128, 8, 512], BF16) as hTt:
            for tq in range(4):
                rmsnorm_tile(hTt, "hTt", 0, tq * 512, 512, sq, rstd)
                for dt in range(8):
                    P.act(lambda e, dt=dt, tq=tq: e.activation(hT[:, dt, tq * 512:(tq + 1) * 512], hTt[:, dt, :], AF.Copy),
                          reads=[("hTt", dt)], writes=[("hT", dt)])
        P.barrier()
        with SBT(nc, "wu4", [128, 8, 512], BF16) as wu4, SBT(nc, "wp", [128, 128], BF16) as wp, \
                SBT(nc, "uA", [128, S], F32) as uA, SBT(nc, "uB", [128, S], F32) as uB, \
                SBT(nc, "u0", [128, S], F32) as u0, SBT(nc, "pb", [128, S], BF16) as pb:
            for g in range(4):
                w_ = 2 ** (g + 1)
                if g == 0:
                    load_w(wu4, win, "wu4", 8, 1536, 512)
                P.dma(lambda e, g=g: e.dma_start(out=wp[:], in_=T["pool_w"][0, g]), writes=["wp"], q="pool")
                for tq in range(4):
                    bi = tq % 2
                    def mm(e, tq=tq, bi=bi, g=g):
                        for k in range(8):
                            r = e.matmul(ps[bi][:], lhsT=wu4[:, k, g * 128:(g + 1) * 128], rhs=hT[:, k, tq * 512:(tq + 1) * 512], start=(k == 0), stop=(k == 7))
                        return r
                    P.pe(mm, reads=[("wu4", k) for k in range(8)] + [("hT", k) for k in range(8)], writes=[PSK(bi)])
                    P.act(lambda e, tq=tq, bi=bi: e.activation(u0[:, tq * 512:(tq + 1) * 512], ps[bi][:], AF.Copy),
                          reads=[PSK(bi)], writes=["u0"])
                src, srck = u0, "u0"
                bufs = [(uA, "uA"), (uB, "uB")]
                for st in range(g + 1):
                    sh = 2 ** st
                    dst, dstk = bufs[st % 2]
                    def stp(e, src=src, dst=dst, sh=sh):
                        e.tensor_copy(dst[:, 0:sh], src[:, 0:sh])
                        return e.tensor_tensor(dst[:, sh:S], src[:, sh:S], src[:, 0:S - sh], ALU.add)
                    P.dve(stp, reads=[srck], writes=[dstk])
                    src, srck = dst, dstk
                def pl(e, src=src, w_=w_):
                    e.scalar_tensor_tensor(pb[:, w_ - 1:S], src[:, w_ - 1:S], 1.0 / w_, u0[:, w_ - 1:S], ALU.mult, ALU.subtract)
                    return e.tensor_tensor(src[:, 0:w_ - 1], src[:, 0:w_ - 1], cf[:, 8, 0:w_ - 1], ALU.mult)
                P.dve(pl, reads=[srck, "u0", "cf"], writes=["pb0", srck])
                P.dve(lambda e, src=src, w_=w_: e.tensor_tensor(pb[:, 0:w_ - 1], src[:, 0:w_ - 1], u0[:, 0:w_ - 1], ALU.subtract),
                      reads=[srck, "u0"], writes=["pb1"])
                for tq in range(4):
                    bi = tq % 2
                    P.pe(lambda e, tq=tq, bi=bi: e.matmul(ps[bi][:], lhsT=wp[:], rhs=pb[:, tq * 512:(tq + 1) * 512], start=True, stop=True),
                         reads=["wp", "pb0", "pb1"], writes=[PSK(bi)])
                    P.act(lambda e, tq=tq, bi=bi, g=g: e.activation(pTo[:, g, tq * 512:(tq + 1) * 512], ps[bi][:], AF.Identity,
                                                                    scale=pscale[:, g:g + 1]),
                          reads=[PSK(bi), "pscale"], writes=[("pTo", g)])
        P.barrier()
        NBUF = 4
        with SBT(nc, "wqkv", [128, 8, 1536], BF16) as wqkv, SBT(nc, "qh", [64, S], BF16) as qh, \
                SBT(nc, "kh", [64, S], BF16) as kh, SBT(nc, "vh", [128, 16, 128], BF16) as vh, \
                SBT(nc, "ex", [128, NBUF, 512], F32) as ex, \
                SBT(nc, "spb", [128, NBUF, 512], BF16) as spb, \
                SBT(nc, "wsb", [128, NBUF, 512], BF16) as wsb, SBT(nc, "Ls", [128, 2, 512], F32) as Ls, \
                SBT(nc, "Lsb", [128, 4, 512], BF16) as Lsb, SBT(nc, "otmp", [64, 2, 512], BF16) as otmp:
            identb = cb[:, 0, :]
            trinc = cb[:, 4, :]
            maskneg = cb[:, 5, :]
            onesneg = cb[:, 2, :]
            git = 0
            for h in range(8):
                hp = h // 2
                if h == 0:
                    load_w(wqkv, win, "wqkv", 8, 0, 1536)
                for j3, (dst, dk, scl) in enumerate([(qh, "qh", 0.125), (kh, "kh", 1.0)]):
                    for tq in range(4):
                        bi = 4 + tq % 2
                        def mm(e, h=h, j3=j3, tq=tq, bi=bi):
                            for k in range(8):
                                r = e.matmul(ps[bi][0:64, :], lhsT=wqkv[:, k, j3 * 512 + h * 64:j3 * 512 + h * 64 + 64],
                                             rhs=hT[:, k, tq * 512:(tq + 1) * 512], start=(k == 0), stop=(k == 7))
                            return r
                        P.pe(mm, reads=[("wqkv", k) for k in range(8)] + [("hT", k) for k in range(8)], writes=[PSK(bi)])
                        P.act(lambda e, dst=dst, tq=tq, bi=bi, scl=scl: e.activation(dst[:, tq * 512:(tq + 1) * 512], ps[bi][0:64, :],
                                                                                   AF.Copy, scale=scl),
                              reads=[PSK(bi)], writes=[(dk, tq)])
                if h % 2 == 0:
                    for t4 in range(4):
                        bi = 4 + t4 % 2
                        def mmv(e, h=h, t4=t4, bi=bi):
                            for tl in range(4):
                                tt = t4 * 4 + tl
                                for k in range(8):
                                    r = e.matmul(ps[bi][:, tl * 128:(tl + 1) * 128], lhsT=hT[:, k, tt * 128:(tt + 1) * 128],
                                                 rhs=wqkv[:, k, 1024 + h * 64:1024 + h * 64 + 128], start=(k == 0), stop=(k == 7))
                            return r
                        P.pe(mmv, reads=[("wqkv", k) for k in range(8)] + [("hT", k) for k in range(8)], writes=[PSK(bi)])
                        P.dve(lambda e, t4=t4, bi=bi: e.tensor_copy(vh[:, t4 * 4:(t4 + 1) * 4, :],
                                                                    ps[bi][:].rearrange("p (t c) -> p t c", t=4)),
                              reads=[PSK(bi)], writes=[("vh", t4)])
                its = []
                for j in range(4):
                    kbs = list(range(4 * j + 3, -1, -1))
                    for ii, kb in enumerate(kbs):
                        diag = kb >= 4 * j
                        qlo = 128 * (kb - 4 * j) if diag else 0
                        its.append(dict(j=j, kb=kb, first=(ii == 0), last=(kb == 0), diag=diag, qlo=qlo, g=git))
                        git += 1

                def zmm(e, dst, it_, stop_after):
                    kb, qlo, j = it_["kb"], it_["qlo"], it_["j"]
                    q0 = 512 * j + qlo
                    r = e.matmul(dst[:, qlo:512], lhsT=kh[:, kb * 128:(kb + 1) * 128], rhs=qh[:, q0:512 * (j + 1)],
                                 start=True, stop=(stop_after and not it_["diag"]))
                    if it_["diag"]:
                        r = e.matmul(dst[:, qlo:qlo + 128], lhsT=identb, rhs=maskneg, start=False, stop=stop_after)
                    return r

                def stageA(it_):
                    r_ = it_["g"] % NBUF
                    j, kb, qlo = it_["j"], it_["kb"], it_["qlo"]
                    lb = j % 2
                    if it_["first"]:
                        P.dve(lambda e, lb=lb: e.memset(Ls[:, lb, :], 0.0), writes=[("Ls", lb)])
                    P.pe(lambda e, it_=it_, r_=r_: zmm(e, ps[r_], it_, True),
                         reads=[("kh", kb // 4), ("qh", j), "cb"], writes=[PSK(r_)])
                    P.act(lambda e, r_=r_, qlo=qlo: e.activation(ex[:, r_, qlo:512], ps[r_][:, qlo:512], AF.Exp),
                          reads=[PSK(r_)], writes=[("ex", r_)])
                    P.act(lambda e, r_=r_, qlo=qlo: e.activation(spb[:, r_, qlo:512], ex[:, r_, qlo:512], AF.Ln, bias=1.0),
                          reads=[("ex", r_)], writes=[("spb", r_)])
                    if not it_["last"]:
                        nqlo = max(0, 128 * (kb - 1 - 4 * j))
                        nr = (it_["g"] + 1) % 4
                        P.dve(lambda e, r_=r_, qlo=qlo, lb=lb: e.tensor_tensor(Ls[:, lb, qlo:512], Ls[:, lb, qlo:512], spb[:, r_, qlo:512], ALU.add),
                              reads=[("Ls", lb), ("spb", r_)], writes=[("Ls", lb)])
                        P.dve(lambda e, nqlo=nqlo, nr=nr, lb=lb: e.tensor_copy(Lsb[:, nr, nqlo:512], Ls[:, lb, nqlo:512]),
                              reads=[("Ls", lb)], writes=[("Lsb", nr)])

                def stageB(it_):
                    r_ = it_["g"] % NBUF
                    j, kb, qlo = it_["j"], it_["kb"], it_["qlo"]
                    ob = j % 2
                    pso = ps[6 + ob]
                    pst = ps[r_]
                    if it_["first"]:
                        P.pe(lambda e, pso=pso: e.matmul(pso[:, :], lhsT=zer[0:1, 0:128], rhs=zer[0:1, 0:512], start=True, stop=False),
                             reads=["zer"], writes=[PSK(6 + ob)])
                    def mmt(e, it_=it_, r_=r_, pst=pst, qlo=qlo):
                        r = e.matmul(pst[:, qlo:512], lhsT=trinc, rhs=spb[:, r_, qlo:512], start=False, stop=it_["first"])
                        if not it_["first"]:
                            r = e.matmul(pst[:, qlo:512], lhsT=onesneg, rhs=Lsb[:, it_["g"] % 4, qlo:512], start=False, stop=True)
                        return r
                    P.pe(mmt, reads=["cb", ("spb", r_), ("Lsb", it_["g"] % 4)], writes=[PSK(r_)])
                    P.act(lambda e, r_=r_, pst=pst, qlo=qlo: e.activation(wsb[:, r_, qlo:512], pst[:, qlo:512], AF.Exp),
                          reads=[PSK(r_)], writes=[("wsb", r_)])
                    P.pe(lambda e, pso=pso, kb=kb, r_=r_, qlo=qlo, last=it_["last"]: e.matmul(
                        pso[:, qlo:512], lhsT=vh[:, kb, :], rhs=wsb[:, r_, qlo:512], start=False, stop=last),
                        reads=[("vh", kb // 4), ("wsb", r_)], writes=[PSK(6 + ob)])
                    if it_["last"]:
                        if h % 2 == 0:
                            P.dve(lambda e, pso=pso, j=j, hp=hp: e.tensor_copy(aT[0:64, hp, 512 * j:512 * (j + 1)], pso[0:64, :]),
                                  reads=[PSK(6 + ob)], writes=[("aT", hp, 0)])
                        else:
                            P.dve(lambda e, pso=pso, j=j, hp=hp: e.tensor_copy(aT[64:128, hp, 512 * j:512 * (j + 1)], pso[64:128, :]),
                                  reads=[PSK(6 + ob)], writes=[("aT", hp, 1)])

                n_it = len(its)
                SK = 2
                for i in range(n_it + SK):
                    if i < n_it:
                        stageA(its[i])
                    if i >= SK:
                        stageB(its[i - SK])
        P.barrier()
        with SBT(nc, "wout", [128, 8, D], BF16) as wout:
            load_w(wout, T["ab_w_out"][0], "wout", 8, 0, D)
            for tq in range(4):
                t0 = tq * 512
                for m in range(8):
                    bi = m % 2
                    def mm(e, m=m, bi=bi, t0=t0):
                        for k in range(8):
                            src = aT if k < 4 else pTo
                            r = e.matmul(ps[bi][:], lhsT=wout[:, k, m * 128:(m + 1) * 128], rhs=src[:, k % 4, t0:t0 + 512],
                                         start=(k == 0), stop=(k == 7))
                        return r
                    P.pe(mm, reads=[("wout", k) for k in range(8)] + ["aTall"], writes=[PSK(bi)])
                    resid_add(m, bi, ps[bi][:], t0, 512)


def stage_mix_s5(C, b, xT, ps, cf, cb, gains, rmsnorm_tile, load_w, linear_fm, resid_add):
    from contextlib import ExitStack
    nc, P, T = C.nc, C.P, C.T
    P.barrier()

    def PSK(i):
        return ("ps", i)
    ident = cf[:, 0, :]
    mask32 = cf[:, 5, :]
    TWO_PI = 6.283185
    INV2PI = 1.0 / (2.0 * math.pi)

    def V(fn, r, w):
        return P.dve(fn, reads=r, writes=w)

    def Aop(fn, r, w):
        return P.act(fn, reads=r, writes=w)

    with SBT(nc, "uT", [128, 8, S], BF16) as uT, SBT(nc, "dcol", [128, 8], F32) as dcol:
        with SBT(nc, "w_in", [128, 8, D], BF16) as w_in, SBT(nc, "sq", [128, 8, 512], BF16) as sq, \
                SBT(nc, "rstd", [128, 512], F32) as rstd, SBT(nc, "hT", [128, 8, 512], BF16) as hT:
            load_w(w_in, T["ssm_w_in"][0], "w_in", 8, 0, D)
            P.dma(lambda e: e.dma_start(out=dcol[:], in_=T["ssm_d"][0].rearrange("(t p) -> p t", p=128),
                                        allow_slow_non_contiguous=True), writes=["dcol"])
            for tq in range(4):
                rmsnorm_tile(hT, "hT", 1, tq * 512, 512, sq, rstd)

                def ev(m, bi, pap, tq=tq):
                    P.act(lambda e: e.activation(uT[:, m, tq * 512:(tq + 1) * 512], pap, AF.Copy),
                          reads=[PSK(bi)], writes=[("uT", m)])
                linear_fm(w_in, "w_in", hT, "hT", 8, 8, 512, ev)
        P.barrier()
        with ExitStack() as es:
            def A(name, shape, dt=F32):
                return es.enter_context(SBT(nc, name, shape, dt))
            lre = A("lre", [128, 4]); lim = A("lim", [128, 4]); ldt = A("ldt", [128, 4])
            dtt = A("dtt", [128, 4]); ar = A("ar", [128, 4]); an = A("an", [128, 4])
            arj = A("arj", [128, 9, 4]); tj = A("tj", [128, 9, 4]); tjc = A("tjc", [128, 9, 4])
            ti = A("ti", [128, 9, 4], I32); fr = A("fr", [128, 9, 4])
            mag = A("mag", [128, 9, 4]); sinj = A("sinj", [128, 9, 4]); cosj = A("cosj", [128, 9, 4])
            Lr = A("Lr", [128, 9, 4]); Li = A("Li", [128, 9, 4])
            nre = A("nre", [128, 4]); den = A("den", [128, 4]); t1 = A("t1", [128, 4]); t2 = A("t2", [128, 4])
            cr = A("cr", [128, 4]); ci = A("ci", [128, 4]); ti8 = A("ti8", [128, 4], I32); t8f = A("t8f", [128, 4])
            Fr = A("Fr", [128, 8, 4]); Fi = A("Fi", [128, 8, 4]); f1 = A("f1", [128, 8, 4]); f2 = A("f2", [128, 8, 4])
            Bst = A("Bst", [128, 2, 4, 16]); Cin = A("Cin", [64, 2, 2, 64]); Cst = A("Cst", [128, 2, 4, 16])
            l1 = A("l1", [128, 9, 4, 16]); l2 = A("l2", [128, 9, 4, 16])
            What = A("What", [128, 8, 2, 128])
            Wt = A("Wt", [128, 8, 2, 128], BF16)
            CL = A("CL", [128, 2, 9, 4, 16])
            LB = CL[:, :, 0:8]
            Qd = A("Qd", [128, 9, 2, 4, 32], BF16)
            Qf = A("Qf", [128, 2, 128])
            TtF = A("TtF", [128, 4, 128]); Tt = A("Tt", [128, 8, 128], BF16)
            cosT = A("cosT", [128, 4, 256]); sinT = A("sinT", [128, 4, 256])
            Xp = A("Xp", [128, 2, 4, 256]); xa = A("xa", [128, 4, 256]); xb = A("xb", [128, 4, 256])
            Ssc = A("Ssc", [128, 2, 4, 256]); tk = Ssc[:, 0]; tki = Ssc[:, 1].bitcast(I32); Hb = A("Hb", [128, 2, 4, 257], BF16)
            iota256 = cf[:, 6:8, :].rearrange("p a b -> p (a b)")
            What6 = What[:].rearrange("p t r (q g c) -> p t r q g c", q=4, g=2)
            Qf5 = Qf[:].rearrange("p r (q g c) -> p r q g c", q=4, g=2)
            Wv = Wt[:].rearrange("p t r n -> p (t r) n")
            V(lambda e: e.memset(What[:], 0.0), [], ["What"])
            V(lambda e: e.memset(Qd[:], 0.0), [], ["Qpad"])
            V(lambda e: e.memset(Qf[:], 0.0), [], ["Qf"])
            V(lambda e: e.memset(Hb[:], 0.0), [], ["Hb"])

            def bc(ap, shape):
                return ap.broadcast_to(shape)

            def partA(j):
                g0 = 8 * j
                P.dma(lambda e, g0=g0: e.dma_start(out=lre[:], in_=T["ssm_lam_re"][0, g0:g0 + 8, :].rearrange("(q g) p -> (g p) q", g=2),
                                                   allow_slow_non_contiguous=True), writes=["lre"])
                P.dma(lambda e, g0=g0: e.dma_start(out=lim[:], in_=T["ssm_lam_im"][0, g0:g0 + 8, :].rearrange("(q g) p -> (g p) q", g=2),
                                                   allow_slow_non_contiguous=True), writes=["lim"])
                for g2 in range(2):
                    P.dma(lambda e, g0=g0, g2=g2: e.dma_start(
                        out=ldt[64 * g2:64 * g2 + 64, :],
                        in_=T["ssm_log_dt"][0, g0:g0 + 8].rearrange("(q g) -> g q", g=2)[g2:g2 + 1, :].broadcast_to([64, 4]),
                        allow_slow_non_contiguous=True), writes=["ldt"])
                for ri, nm in enumerate(["ssm_b_re", "ssm_b_im"]):
                    P.dma(lambda e, g0=g0, ri=ri, nm=nm: e.dma_start(
                        out=Bst[:, ri, :, :], in_=T[nm][0, g0:g0 + 8].rearrange("(q g) p c -> (g p) q c", g=2)),
                        writes=["Bst"])
                for ri, nm in enumerate(["ssm_c_re", "ssm_c_im"]):
                    for q in range(4):
                        P.dma(lambda e, g0=g0, ri=ri, nm=nm, q=q: e.dma_start(
                            out=Cin[16 * q:16 * q + 16, ri, :, :],
                            in_=T[nm][0, g0 + 2 * q:g0 + 2 * q + 2].rearrange("g c p -> c g p")), writes=["Cin"])
                def trc(e):
                    for ri in range(2):
                        r = e.transpose(ps[0][:, ri * 64:(ri + 1) * 64], Cin[:, ri, :, :].rearrange("a g p -> a (g p)"), ident[0:64, 0:64])
                    return r
                P.pe(trc, reads=["Cin", "cf"], writes=[PSK(0)])
                Aop(lambda e: e.activation(Cst[:].rearrange("p r q c -> p (r q c)"), ps[0][:, 0:128], AF.Copy), [PSK(0)], ["Cst"])
                Aop(lambda e: e.activation(dtt[:], ldt[:], AF.Exp), ["ldt"], ["dtt"])
                def f_(e):
                    e.tensor_tensor(ar[:], lre[:], dtt[:], ALU.mult)
                    return e.tensor_tensor(an[:], lim[:], dtt[:], ALU.mult)
                V(f_, ["lre", "lim", "dtt"], ["ar", "an"])
                jv = bc(cf[:, 6, 0:9][:, :, None], [128, 9, 4])
                def f_(e):
                    e.tensor_tensor(arj[:], bc(ar[:, None, :], [128, 9, 4]), jv, ALU.mult)
                    return e.scalar_tensor_tensor(tj[:], bc(an[:, None, :], [128, 9, 4]), INV2PI, jv, ALU.mult, ALU.mult)
                V(f_, ["ar", "an", "cf"], ["arj", "tj"])
                Aop(lambda e: e.activation(mag[:], arj[:], AF.Exp), ["arj"], ["mag"])
                V(lambda e: e.tensor_copy(ti[:], tj[:]), ["tj"], ["ti"])
                V(lambda e: e.tensor_tensor(fr[:], tj[:], ti[:], ALU.subtract), ["tj", "ti"], ["fr"])
                Aop(lambda e: e.activation(sinj[:], fr[:], AF.Sin, scale=TWO_PI), ["fr"], ["sinj"])
                V(lambda e: e.tensor_scalar(tjc[:], tj[:], 0.25, None, ALU.add), ["tj"], ["tjc"])
                V(lambda e: e.tensor_copy(ti[:], tjc[:]), ["tjc"], ["ti"])
                V(lambda e: e.tensor_tensor(fr[:], tjc[:], ti[:], ALU.subtract), ["tjc", "ti"], ["fr"])
                Aop(lambda e: e.activation(cosj[:], fr[:], AF.Sin, scale=TWO_PI), ["fr"], ["cosj"])
                def f_(e):
                    e.tensor_tensor(Lr[:], mag[:], cosj[:], ALU.mult)
                    return e.tensor_tensor(Li[:], mag[:], sinj[:], ALU.mult)
                V(f_, ["mag", "cosj", "sinj"], ["Lr", "Li"])
                def f_(e):
                    e.tensor_scalar(nre[:], Lr[:, 1, :], -1.0, None, ALU.add)
                    e.tensor_tensor(t1[:], lre[:], lre[:], ALU.mult)
                    return e.tensor_tensor(t2[:], lim[:], lim[:], ALU.mult)
                V(f_, ["Lr", "lre", "lim"], ["nre", "t1", "t2"])
                V(lambda e: e.tensor_tensor(den[:], t1[:], t2[:], ALU.add), ["t1", "t2"], ["den"])
                V(lambda e: e.reciprocal(den[:], den[:]), ["den"], ["den"])
                def f_(e):
                    e.tensor_tensor(t1[:], nre[:], lre[:], ALU.mult)
                    return e.tensor_tensor(t2[:], Li[:, 1, :], lim[:], ALU.mult)
                V(f_, ["nre", "lre", "Li", "lim", "den"], ["t1", "t2"])
                V(lambda e: e.tensor_tensor(cr[:], t1[:], t2[:], ALU.add), ["t1", "t2"], ["cr"])
                V(lambda e: e.tensor_tensor(cr[:], cr[:], den[:], ALU.mult), ["cr", "den"], ["cr"])
                def f_(e):
                    e.tensor_tensor(t1[:], Li[:, 1, :], lre[:], ALU.mult)
                    return e.tensor_tensor(t2[:], nre[:], lim[:], ALU.mult)
                V(f_, ["nre", "lre", "Li", "lim", "cr"], ["t1", "t2"])
                V(lambda e: e.tensor_tensor(ci[:], t1[:], t2[:], ALU.subtract), ["t1", "t2"], ["ci"])
                V(lambda e: e.tensor_tensor(ci[:], ci[:], den[:], ALU.mult), ["ci", "den"], ["ci"])
                crb = bc(cr[:, None, :], [128, 8, 4]); cib = bc(ci[:, None, :], [128, 8, 4])
                def f_(e, crb=crb, cib=cib):
                    e.tensor_tensor(f1[:], Lr[:, 0:8, :], crb, ALU.mult)
                    return e.tensor_tensor(f2[:], Li[:, 0:8, :], cib, ALU.mult)
                V(f_, ["Lr", "Li", "cr", "ci"], ["f1", "f2"])
                V(lambda e: e.tensor_tensor(Fr[:], f1[:], f2[:], ALU.subtract), ["f1", "f2"], ["Fr"])
                def f_(e, crb=crb, cib=cib):
                    e.tensor_tensor(f1[:], Lr[:, 0:8, :], cib, ALU.mult)
                    return e.tensor_tensor(f2[:], Li[:, 0:8, :], crb, ALU.mult)
                V(f_, ["Lr", "Li", "cr", "ci", "Fr"], ["f1", "f2"])
                V(lambda e: e.tensor_tensor(Fi[:], f1[:], f2[:], ALU.add), ["f1", "f2"], ["Fi"])
                sh8 = [128, 8, 4, 16]
                Frb = bc(Fr[:, :, :, None], sh8); Fib = bc(Fi[:, :, :, None], sh8)
                B0 = bc(Bst[:, 0, None, :, :], sh8); B1 = bc(Bst[:, 1, None, :, :], sh8)
                def f_(e, Frb=Frb, Fib=Fib, B0=B0, B1=B1):
                    e.tensor_tensor(l1[:, 0:8], Frb, B0, ALU.mult)
                    return e.tensor_tensor(l2[:, 0:8], Fib, B1, ALU.mult)
                V(f_, ["Fr", "Fi", "Bst"], ["l1", "l2"])
                V(lambda e: e.tensor_tensor(LB[:, 0], l1[:, 0:8], l2[:, 0:8], ALU.subtract), ["l1", "l2"], ["LB0", "CL0"])
                def f_(e, Frb=Frb, Fib=Fib, B0=B0, B1=B1):
                    e.tensor_tensor(l1[:, 0:8], Frb, B1, ALU.mult)
                    return e.tensor_tensor(l2[:, 0:8], Fib, B0, ALU.mult)
                V(f_, ["Fr", "Fi", "Bst", "LB0"], ["l1", "l2"])
                V(lambda e: e.tensor_tensor(LB[:, 1], l1[:, 0:8], l2[:, 0:8], ALU.add), ["l1", "l2"], ["LB1", "CL1"])
                def f_(e):
                    for g2 in range(2):
                        for ri in range(2):
                            r = e.tensor_copy(What6[64 * g2:64 * g2 + 64, :, ri, :, g2, :], LB[64 * g2:64 * g2 + 64, ri, :, :, :])
                    return r
                V(f_, ["LB0", "LB1"], ["What"])
            def partB(j):
                g0 = 8 * j
                for c4 in range(4):
                    bi = c4 % 2
                    def trw(e, c4=c4, bi=bi):
                        for i4 in range(4):
                            c = c4 * 4 + i4
                            r = e.transpose(ps[bi][:, i4 * 128:(i4 + 1) * 128], What[:, c // 2, c % 2, :], ident)
                        return r
                    P.pe(trw, reads=["What", "cf"], writes=[PSK(bi)])
                    if c4 % 2 == 0:
                        Aop(lambda e, c4=c4, bi=bi: e.activation(Wv[:, c4 * 4:c4 * 4 + 4, :], ps[bi][:].rearrange("p (a n) -> p a n", a=4), AF.Copy),
                            [PSK(bi)], [("Wpad", c4)])
                    else:
                        V(lambda e, c4=c4, bi=bi: e.tensor_copy(Wv[:, c4 * 4:c4 * 4 + 4, :], ps[bi][:].rearrange("p (a n) -> p a n", a=4)),
                          [PSK(bi)], [("Wpad", c4)])
                sh9 = [128, 9, 4, 16]
                Lrb = bc(Lr[:, :, :, None], sh9); Lib = bc(Li[:, :, :, None], sh9)
                C0 = bc(Cst[:, 0, None, :, :], sh9); C1 = bc(Cst[:, 1, None, :, :], sh9)
                def f_(e, Lrb=Lrb, Lib=Lib, C0=C0, C1=C1):
                    e.tensor_tensor(l1[:], Lrb, C0, ALU.mult)
                    return e.tensor_tensor(l2[:], Lib, C1, ALU.mult)
                V(f_, ["Lr", "Li", "Cst", "LB1"], ["l1", "l2"])
                V(lambda e: e.tensor_tensor(CL[:, 0], l1[:], l2[:], ALU.subtract), ["l1", "l2"], ["CL0", "LB0", "LB1"])
                def f_(e, Lrb=Lrb, Lib=Lib, C0=C0, C1=C1):
                    e.tensor_tensor(l1[:], Lib, C0, ALU.mult)
                    return e.tensor_tensor(l2[:], Lrb, C1, ALU.mult)
                V(f_, ["Lr", "Li", "Cst", "CL0"], ["l1", "l2"])
                V(lambda e: e.scalar_tensor_tensor(CL[:, 1], l1[:], -1.0, l2[:], ALU.mult, ALU.subtract), ["l1", "l2"], ["CL1", "LB0", "LB1"])
                def f_(e):
                    for g2 in range(2):
                        for ri in range(2):
                            e.tensor_copy(Qd[64 * g2:64 * g2 + 64, :, ri, :, 16 * g2:16 * g2 + 16], CL[64 * g2:64 * g2 + 64, ri, :, :, :])
                            r = e.tensor_copy(Qf5[64 * g2:64 * g2 + 64, ri, :, g2, :], CL[64 * g2:64 * g2 + 64, ri, 0, :, :])
                    return r
                V(f_, ["CL0", "CL1"], ["Qpad", "Qf"])
                for half in range(2):
                    bi = half
                    def mmt(e, half=half, bi=bi):
                        for i4 in range(4):
                            tau = half * 4 + i4
                            e.matmul(ps[bi][:, i4 * 128:(i4 + 1) * 128], lhsT=What[:, tau, 0, :], rhs=Qf[:, 0, :], start=True, stop=False)
                            r = e.matmul(ps[bi][:, i4 * 128:(i4 + 1) * 128], lhsT=What[:, tau, 1, :], rhs=Qf[:, 1, :], start=False, stop=True)
                        return r
                    P.pe(mmt, reads=["What", "Qf"], writes=[PSK(bi)])
                    m4 = bc(mask32[:, None, :], [128, 4, 128])
                    if half == 0:
                        V(lambda e, bi=bi, m4=m4: e.tensor_tensor(TtF[:], ps[bi][:].rearrange("p (a n) -> p a n", a=4), m4, ALU.mult),
                          [PSK(bi), "cf"], ["TtF"])
                        V(lambda e, j=j: e.scalar_tensor_tensor(TtF[:, 0, :], ident, dcol[:, j:j + 1], TtF[:, 0, :], ALU.mult, ALU.add),
                          ["TtF", "dcol", "cf"], ["TtF"])
                        V(lambda e: e.tensor_copy(Tt[:, 0:4, :], TtF[:]), ["TtF"], [("Tt", 0)])
                    else:
                        V(lambda e, bi=bi, m4=m4: e.tensor_tensor(Tt[:, 4:8, :], ps[bi][:].rearrange("p (a n) -> p a n", a=4), m4, ALU.mult),
                          [PSK(bi), "cf"], [("Tt", 1)])
                uv = uT[:, j, :].rearrange("p (k s) -> p s k", s=8)
                def mmx(e, uv=uv):
                    for ri in range(2):
                        for tau in range(8):
                            for q in range(4):
                                r = e.matmul(ps[2 + q][:, ri * 256:(ri + 1) * 256], lhsT=Wt[32 * q:32 * q + 32, tau, ri, :],
                                             rhs=uv[32 * q:32 * q + 32, 7 - tau, :], start=(tau == 0), stop=(tau == 7),
                                             tile_position=(32 * q, 0))
                    return r
                P.pe(mmx, reads=[("Wpad", c4) for c4 in range(4)] + [("uT", j, s_) for s_ in range(8)], writes=[PSK(2 + q) for q in range(4)])
                V(lambda e: e.tensor_copy(ti8[:], tj[:, 8, :]), ["tj"], ["ti8"])
                V(lambda e: e.tensor_tensor(t8f[:], tj[:, 8, :], ti8[:], ALU.subtract), ["tj", "ti8"], ["t8f"])
                V(lambda e: e.tensor_tensor(tk, bc(t8f[:, :, None], [128, 4, 256]), bc(iota256[:, None, :], [128, 4, 256]), ALU.mult),
                  ["t8f", "cf"], ["tk"] + [("Ssc", ri_, q_) for ri_ in range(2) for q_ in range(4)])
                V(lambda e: e.tensor_copy(tki, tk), ["tk"], ["tki"])
                V(lambda e: e.tensor_tensor(xa[:], tk, tki, ALU.subtract), ["tk", "tki"], ["xa"])
                Aop(lambda e: e.activation(sinT[:], xa[:], AF.Sin, scale=TWO_PI), ["xa"], ["sinT"])
                V(lambda e: e.tensor_scalar(tk, tk, 0.25, None, ALU.add), ["tk", "tki"], ["tk"])
                V(lambda e: e.tensor_copy(tki, tk), ["tk"], ["tki"])
                V(lambda e: e.tensor_tensor(xb[:], tk, tki, ALU.subtract), ["tk", "tki"], ["xb"])
                Aop(lambda e: e.activation(cosT[:], xb[:], AF.Sin, scale=TWO_PI), ["xb"], ["cosT"])
                for q in range(4):
                    Xr = ps[2 + q][:, 0:256]; Xi = ps[2 + q][:, 256:512]
                    def f_(e, q=q, Xr=Xr, Xi=Xi):
                        e.tensor_tensor(xa[:, q, :], cosT[:, q, :], Xr, ALU.mult)
                        return e.tensor_tensor(xb[:, q, :], sinT[:, q, :], Xi, ALU.mult)
                    V(f_, [PSK(2 + q), "cosT", "sinT", "xa", "xb"], [("xa", q), ("xb", q)])
                    V(lambda e, q=q: e.tensor_tensor(Xp[:, 0, q, :], xa[:, q, :], xb[:, q, :], ALU.add), [("xa", q), ("xb", q)], [("Xp", 0, q)])
                    def f_(e, q=q, Xr=Xr, Xi=Xi):
                        e.tensor_tensor(xa[:, q, :], cosT[:, q, :], Xi, ALU.mult)
                        return e.tensor_tensor(xb[:, q, :], sinT[:, q, :], Xr, ALU.mult)
                    V(f_, [PSK(2 + q), "cosT", "sinT", ("Xp", 0, q)], [("xa", q), ("xb", q)])
                    V(lambda e, q=q: e.tensor_tensor(Xp[:, 1, q, :], xa[:, q, :], xb[:, q, :], ALU.subtract), [("xa", q), ("xb", q)], [("Xp", 1, q)])
                    for ri in range(2):
                        V(lambda e, q=q, ri=ri: e.tensor_tensor_scan(Ssc[:, ri, q, :], mag[:, 8, q:q + 1].to_broadcast([128, 256]),
                                                                     Xp[:, ri, q, :], 0.0, ALU.mult, ALU.add),
                          [("Xp", ri, q), "mag"], [("Ssc", ri, q), "tk", "tki"])
                allS = [("Ssc", ri, q) for ri in range(2) for q in range(4)]
                allx = [("xa", q) for q in range(4)] + [("xb", q) for q in range(4)]
                def f_(e):
                    e.tensor_tensor(xa[:], cosT[:], Ssc[:, 0], ALU.mult)
                    return e.tensor_tensor(xb[:], sinT[:], Ssc[:, 1], ALU.mult)
                V(f_, allS + ["cosT", "sinT"], allx + ["xa", "xb"])
                V(lambda e: e.tensor_tensor(Hb[:, 0, :, 1:257], xa[:], xb[:], ALU.subtract), ["xa", "xb"], ["Hb0"])
                def f_(e):
                    e.tensor_tensor(xa[:], cosT[:], Ssc[:, 1], ALU.mult)
                    return e.tensor_tensor(xb[:], sinT[:], Ssc[:, 0], ALU.mult)
                V(f_, allS + ["cosT", "sinT", "Hb0"], allx + ["xa", "xb"])
                V(lambda e: e.tensor_tensor(Hb[:, 1, :, 1:257], xa[:], xb[:], ALU.add), ["xa", "xb"], ["Hb1"])
            def partC(j):
                uv = uT[:, j, :].rearrange("p (k s) -> p s k", s=8)
                for tp in range(7, -1, -1):
                    bi = 6 + (tp % 2)
                    def mmy(e, tp=tp, bi=bi, uv=uv):
                        for s_ in range(tp + 1):
                            e.matmul(ps[bi][:, 0:256], lhsT=Tt[:, tp - s_, :], rhs=uv[:, s_, :], start=(s_ == 0), stop=False)
                        for ri in range(2):
                            for q in range(4):
                                r = e.matmul(ps[bi][32 * q:32 * q + 32, 0:256], lhsT=Qd[:, tp + 1, ri, q, :], rhs=Hb[:, ri, q, 0:256], start=False,
                                             stop=(ri == 1), tile_position=(0, 32 * q))
                        return r
                    P.pe(mmy, reads=[("uT", j, s_) for s_ in range(tp + 1)] + [("Tt", 0), ("Tt", 1), "Qpad", "Hb0", "Hb1"], writes=[PSK(bi)])
                    Aop(lambda e, tp=tp, bi=bi, uv=uv: e.activation(uv[:, tp, :], ps[bi][:, 0:256], AF.Gelu_apprx_tanh),
                        [PSK(bi)], [("uT", j, tp)])
            partA(0)
            for j in range(8):
                partB(j)
                if j < 7:
                    partA(j + 1)
                partC(j)
        P.barrier()
        with SBT(nc, "wg", [128, 2, 2, 8, 512], BF16) as wg, SBT(nc, "sig", [128, 2, 512], F32) as sig, \
                SBT(nc, "mixb", [128, 2, 512], F32) as mixb:
            wglu = T["ssm_w_glu"][0].rearrange("(k p) n -> p k n", p=128)
            it = 0
            for mg in range(2):
                wb = mg % 2
                for vg in range(2):
                    c0 = vg * D + mg * 512
                    P.dma(lambda e, wb=wb, vg=vg, c0=c0: e.dma_start(out=wg[:, wb, vg, :, :], in_=wglu[:, :, c0:c0 + 512]),
                          writes=[("wg", wb, vg)], q="pool")
                for ml in range(4):
                    m = mg * 4 + ml
                    for tq in range(4):
                        pb = it % 2
                        it += 1
                        for vg in range(2):
                            bi = 2 + 2 * vg + pb
                            def mm(e, wb=wb, vg=vg, bi=bi, tq=tq, ml=ml):
                                for k in range(8):
                                    r = e.matmul(ps[bi][:], lhsT=wg[:, wb, vg, k, ml * 128:(ml + 1) * 128], rhs=uT[:, k, tq * 512:(tq + 1) * 512],
                                                 start=(k == 0), stop=(k == 7))
                                return r
                            P.pe(mm, reads=[("wg", wb, vg)], writes=[PSK(bi)])
                        Aop(lambda e, pb=pb: e.activation(sig[:, pb, :], ps[4 + pb][:], AF.Sigmoid), [PSK(4 + pb)], [("sig", pb)])
                        V(lambda e, pb=pb: e.tensor_tensor(mixb[:, pb, :], ps[2 + pb][:], sig[:, pb, :], ALU.mult), [PSK(2 + pb), ("sig", pb)], [("mixb", pb)])
                        V(lambda e, pb=pb, m=m, tq=tq: e.tensor_tensor(xT[:, m, tq * 512:(tq + 1) * 512], xT[:, m, tq * 512:(tq + 1) * 512],
                                                                     mixb[:, pb, :], ALU.add), [("mixb", pb), ("xT", m)], [("xT", m)])


_CACHE = {}


def kernel(**inputs):
    if "prog" not in _CACHE:
        _CACHE["prog"] = build_program()
    nc, _ = _CACHE["prog"]
    consts = make_consts()
    in_maps = []
    for c in range(8):
        m = {}
        for name, shape in INPUT_SPECS:
            if name == "consts":
                m[name] = consts
            elif name in ("x", "mem"):
                m[name] = np.ascontiguousarray(np.asarray(inputs[name], dtype=np.float32)[c * NB:(c + 1) * NB])
            else:
                m[name] = np.ascontiguousarray(np.asarray(inputs[name], dtype=np.float32))
        in_maps.append(m)
    res = run_bass_kernel_spmd(nc, in_maps, core_ids=list(range(8)))
    return np.concatenate([r["out"] for r in res.results], axis=0)
```

```python
import math
import numpy as np
import concourse.bass as bass
from concourse.ap import AP
import concourse.mybir as mybir
from concourse.bass_utils import run_bass_kernel_spmd

F32 = mybir.dt.float32
BF16 = mybir.dt.bfloat16
I32 = mybir.dt.int32
AF = mybir.ActivationFunctionType
ALU = mybir.AluOpType

COMPUTE = ("pe", "act", "dve", "pool")
ALLENG = ("pe", "act", "dve", "pool", "sp")
NDMA_SEMS = 40

S = 2048
D = 1024
NB = 2
DFF = 2816
NF = DFF // 128
MEM = 256


class Op:
    __slots__ = ("eng", "fn", "deps", "idx", "dma", "signal", "cnt", "sem", "clock")


class Prog:
    def __init__(self, nc):
        self.nc = nc
        self.ops = []
        self.last_write = {}
        self.readers = {}
        self.dma_hist = []
        self.n_dma = 0
        self.last_on = {}
        self.dma_since = []

    def add(self, eng, fn, reads=(), writes=(), dma=False, extra_deps=()):
        op = Op()
        op.eng, op.fn, op.dma = eng, fn, dma
        op.idx = len(self.ops)
        op.signal = False
        op.cnt = None
        op.sem = None
        op.clock = None
        deps = set(extra_deps)
        for k in reads:
            w = self.last_write.get(k)
            if w is not None:
                deps.add(w)
        for k in writes:
            w = self.last_write.get(k)
            if w is not None:
                deps.add(w)
            r = self.readers.get(k)
            if r:
                deps.update(r)
        for k in reads:
            self.readers.setdefault(k, []).append(op.idx)
        for k in writes:
            self.last_write[k] = op.idx
            self.readers[k] = []
        if dma:
            j = self.n_dma
            self.n_dma += 1
            op.sem = j % NDMA_SEMS
            if j >= NDMA_SEMS:
                deps.add(self.dma_hist[j - NDMA_SEMS])
            self.dma_hist.append(op.idx)
            self.dma_since.append(op.idx)
        else:
            self.last_on[eng] = op.idx
        deps.discard(op.idx)
        if eng == "pe" and not dma:
            deps = {d_ for d_ in deps if self.ops[d_].eng != "pe" or self.ops[d_].dma}
        op.deps = deps
        self.ops.append(op)
        return op

    def pe(self, fn, reads=(), writes=()):
        return self.add("pe", fn, reads, writes)

    def act(self, fn, reads=(), writes=()):
        return self.add("act", fn, reads, writes)

    def dve(self, fn, reads=(), writes=()):
        return self.add("dve", fn, reads, writes)

    def dma(self, fn, reads=(), writes=(), q="sp"):
        return self.add(q, fn, reads, writes, dma=True)

    def barrier(self):
        deps = set(self.last_on.values()) | set(self.dma_since)
        self.dma_since = []
        for e in ALLENG:
            self.add(e, lambda eng: eng.nop(), extra_deps=deps)
        self.last_write = {}
        self.readers = {}

    def emit(self, final_ops):
        nc = self.nc
        ops = self.ops
        for op in ops:
            for d in op.deps:
                ops[d].signal = True
        for op in final_ops:
            op.signal = True
        eng_cnt = {e: 0 for e in ALLENG}
        dma_cnt = [0] * NDMA_SEMS
        for op in ops:
            if op.dma:
                dma_cnt[op.sem] += 16
                op.cnt = dma_cnt[op.sem]
            elif op.signal:
                eng_cnt[op.eng] += 1
                op.cnt = eng_cnt[op.eng]
        sems = {e: nc.alloc_semaphore("s_" + e) for e in ALLENG}
        dsems = [nc.alloc_semaphore("d_%d" % i) for i in range(NDMA_SEMS)]

        def key_of(o):
            return ("d", o.sem) if o.dma else o.eng

        know = {e: {} for e in ALLENG}
        waits = {}
        for op in ops:
            K = know[op.eng]
            wl = []
            for d in sorted(op.deps, reverse=True):
                dop = ops[d]
                k = key_of(dop)
                if K.get(k, 0) >= dop.cnt:
                    continue
                for kk, vv in dop.clock.items():
                    if K.get(kk, 0) < vv:
                        K[kk] = vv
                K[k] = max(K.get(k, 0), dop.cnt)
                wl.append((dsems[dop.sem] if dop.dma else sems[dop.eng], dop.cnt))
            waits[op.idx] = wl
            if op.signal or op.dma:
                op.clock = dict(K)
        by_eng = {e: [] for e in ALLENG}
        for op in ops:
            by_eng[op.eng].append(op)
        fin = [(dsems[o.sem] if o.dma else sems[o.eng], o.cnt) for o in final_ops]
        self.n_inst = {e: len(by_eng[e]) for e in ALLENG}

        def run(engname, e):
            for op in by_eng[engname]:
                for (s, v) in waits[op.idx]:
                    e.wait_ge(s, v)
                ins = op.fn(e)
                if op.dma:
                    ins.then_inc(dsems[op.sem], 16)
                elif op.signal:
                    ins.then_inc(sems[op.eng], 1)
            if engname == "sp":
                for (s, v) in fin:
                    e.wait_ge(s, v)

        with nc.Block() as block:
            @block.tensor
            def _(e):
                run("pe", e)

            @block.scalar
            def _(e):
                run("act", e)

            @block.vector
            def _(e):
                run("dve", e)

            @block.gpsimd
            def _(e):
                run("pool", e)

            @block.sync
            def _(e):
                run("sp", e)


INPUT_SPECS = [
    ("x", [NB, S, D]), ("mem", [NB, MEM, D]),
    ("norm_mix", [2, D]), ("norm_xattn", [2, D]), ("norm_ffn", [2, D]), ("norm_mem", [D]), ("norm_final", [D]),
    ("ab_w_in", [1, D, 2048]), ("pool_w", [1, 4, 128, 128]), ("pool_scale", [1, 512]), ("ab_w_out", [1, D, D]),
    ("ssm_w_in", [1, D, D]), ("ssm_lam_re", [1, 64, 64]), ("ssm_lam_im", [1, 64, 64]), ("ssm_log_dt", [1, 64]),
    ("ssm_b_re", [1, 64, 64, 16]), ("ssm_b_im", [1, 64, 64, 16]), ("ssm_c_re", [1, 64, 16, 64]),
    ("ssm_c_im", [1, 64, 16, 64]), ("ssm_d", [1, D]), ("ssm_w_glu", [1, D, 2 * D]),
    ("xa_w_q", [2, D, D]), ("xa_w_kv", [2, D, 2 * D]), ("xa_w_o", [2, D, D]),
    ("ffn_w_up", [2, D, 2 * DFF]), ("ffn_conv_w", [2, 3, 2 * DFF]), ("ffn_conv_b", [2, 2 * DFF]),
    ("ffn_w_down", [2, DFF, D]),
    ("consts", [128, 12, 128]),
]


def make_consts():
    c = np.zeros((128, 12, 128), np.float32)
    j = np.arange(128)
    c[:, 0, :] = np.eye(128)
    c[:, 1, :] = -(j[:, None] > j[None, :]).astype(np.float32)
    c[:, 2, :] = -1.0
    c[:, 3, :] = 1.0
    c[:, 4, :] = (j[:, None] < j[None, :]).astype(np.float32)
    c[:, 5, :] = (j[:, None] // 32 == j[None, :] // 32).astype(np.float32)
    c[:, 6, :] = np.arange(128)[None, :]
    c[:, 7, :] = 128 + np.arange(128)[None, :]
    c[:, 8, :] = 1.0 / (1.0 + np.arange(128))[None, :]
    c[:, 9, :] = -(j[:, None] >= j[None, :]).astype(np.float32)
    c[:, 10, :] = -30000.0 * (j[:, None] >= j[None, :])
    return c


class Ctx:
    pass


_uid = [0]


def SBT(nc, name, shape, dt):
    _uid[0] += 1
    return nc.sbuf_tensor("%s_%d" % (name, _uid[0]), shape, dt)


def build_program(stop=None, nb=NB):
    nc = bass.Bass("TRN2", target_bir_lowering=False)
    P = Prog(nc)
    C = Ctx()
    C.nc, C.P = nc, P
    T = {}
    for name, shape in INPUT_SPECS:
        T[name] = nc.dram_tensor(name, shape, F32, kind="ExternalInput").ap()
    out = nc.dram_tensor("out", [NB, S, D], F32, kind="ExternalOutput").ap()
    C.T = T

    def sb(name, shape, dt=F32):
        return nc.alloc_sbuf_tensor(name, shape, dt)

    xT = sb("xT", [128, 8, S])
    cf = sb("cf", [128, 10, 128])
    cb = sb("cb", [128, 6, 128], BF16)
    gains = sb("gains", [128, 8, 8])
    convp = sb("convp", [128, 2, 4, 44])
    pscale = sb("pscale", [128, 4])
    zer = sb("zer", [128, 512], BF16)
    memT = sb("memT", [128, 8, MEM], BF16)
    ps = [nc.alloc_psum_tensor("ps%d" % i, [128, 512], F32) for i in range(8)]
    ident = cf[:, 0, :]
    maskstrict = cf[:, 4, :]

    def PSK(i):
        return ("ps", i)

    P.dma(lambda e: e.dma_start(out=cf[:], in_=T["consts"][:, 0:10, :]), writes=["cf"])
    P.dma(lambda e: e.dma_start(out=cb[:, 0:4, :], in_=T["consts"][:, 0:4, :]), writes=["cb"], q="pool")
    P.dma(lambda e: e.dma_start(out=cb[:, 4:6, :], in_=T["consts"][:, 9:11, :]), writes=["cb2"], q="pool")
    gsrc = [T["norm_mix"][0], T["norm_mix"][1], T["norm_xattn"][0], T["norm_xattn"][1],
            T["norm_ffn"][0], T["norm_ffn"][1], T["norm_mem"], T["norm_final"]]
    for i, g in enumerate(gsrc):
        P.dma(lambda e, i=i, g=g: e.dma_start(out=gains[:, i, :], in_=g.rearrange("(t p) -> p t", p=128),
                                             allow_slow_non_contiguous=True), writes=["gains"], q="act")
    for l in range(2):
        for i in range(3):
            P.dma(lambda e, l=l, i=i: e.dma_start(out=convp[:, l, i, :],
                                                  in_=T["ffn_conv_w"][l, i].rearrange("(t p) -> p t", p=128),
                                                  allow_slow_non_contiguous=True), writes=["convp"], q="act")
        P.dma(lambda e, l=l: e.dma_start(out=convp[:, l, 3, :],
                                         in_=T["ffn_conv_b"][l].rearrange("(t p) -> p t", p=128),
                                         allow_slow_non_contiguous=True), writes=["convp"], q="act")
    P.dma(lambda e: e.dma_start(out=pscale[:], in_=T["pool_scale"][0].rearrange("(t p) -> p t", p=128),
                                allow_slow_non_contiguous=True), writes=["pscale"], q="act")
    P.dve(lambda e: e.memset(zer[:], 0.0), writes=["zer"])

    def load_w(dst, src2d, key, k_tiles, col0, ncols):
        v = src2d.rearrange("(k p) n -> p k n", p=128)
        for k in range(k_tiles):
            P.dma(lambda e, k=k: e.dma_start(out=dst[:, k, :], in_=v[:, k, col0:col0 + ncols]),
                  writes=[(key, k)], q="pool")

    def rmsnorm_tile(hT, hkey, gi, t0, n, sq, rstd, part="all"):
        if part in ("all", "sq"):
            for dt in range(8):
                P.act(lambda e, dt=dt: e.activation(sq[:, dt, 0:n], xT[:, dt, t0:t0 + n], AF.Square),
                      reads=[("xT", dt)], writes=[("sq", dt)])
        if part == "sq":
            return
        def mm(e):
            for dt in range(8):
                r = e.matmul(ps[7][:, 0:n], lhsT=cb[:, 3, :], rhs=sq[:, dt, 0:n], start=(dt == 0), stop=(dt == 7))
            return r
        P.pe(mm, reads=[("sq", dt) for dt in range(8)] + ["cb"], writes=[PSK(7)])
        P.dve(lambda e: e.tensor_scalar(rstd[:, 0:n], ps[7][:, 0:n], 1.0 / D, 1e-6, ALU.mult, ALU.add),
              reads=[PSK(7)], writes=["rstd"])
        P.act(lambda e: e.activation(rstd[:, 0:n], rstd[:, 0:n], AF.Ln), reads=["rstd"], writes=["rstd"])
        P.act(lambda e: e.activation(rstd[:, 0:n], rstd[:, 0:n], AF.Exp, scale=-0.5), reads=["rstd"], writes=["rstd"])
        for dt in range(8):
            P.dve(lambda e, dt=dt: e.scalar_tensor_tensor(hT[:, dt, 0:n], xT[:, dt, t0:t0 + n], gains[:, gi, dt:dt + 1],
                                                          rstd[:, 0:n], ALU.mult, ALU.mult),
                  reads=[("xT", dt), "rstd", "gains"], writes=[(hkey, dt)])

    C.ps_rr = 0

    def linear_fm(w, wkey, act, akey, k_tiles, m_tiles, n, evac, a0=0, banks=(0, 1)):
        for m in range(m_tiles):
            bi = banks[C.ps_rr % len(banks)]
            C.ps_rr += 1
            def mm(e, m=m, bi=bi):
                for k in range(k_tiles):
                    r = e.matmul(ps[bi][:, 0:n], lhsT=w[:, k, m * 128:(m + 1) * 128], rhs=act[:, k, a0:a0 + n],
                                 start=(k == 0), stop=(k == k_tiles - 1))
                return r
            P.pe(mm, reads=[(wkey, k) for k in range(k_tiles)] + [(akey, k) for k in range(k_tiles)], writes=[PSK(bi)])
            evac(m, bi, ps[bi][:, 0:n])

    def resid_add(m, bi, pap, t0, n):
        P.dve(lambda e: e.tensor_tensor(xT[:, m, t0:t0 + n], xT[:, m, t0:t0 + n], pap, ALU.add),
              reads=[PSK(bi), ("xT", m)], writes=[("xT", m)])

    final_ops = []
    for b in range(nb):
        P.barrier()
        with SBT(nc, "xin", [128, 2, D], F32) as xin, SBT(nc, "sq", [128, 8, 512], BF16) as sq, \
                SBT(nc, "rstd", [128, 512], F32) as rstd, SBT(nc, "mn", [128, 2, D], F32) as mn, \
                SBT(nc, "ssq", [128, 4], F32) as ssq:
            for tt in range(16):
                xb = tt % 2
                P.dma(lambda e, tt=tt, xb=xb, b=b: e.dma_start(out=xin[:, xb, :], in_=T["x"][b, tt * 128:(tt + 1) * 128, :]),
                      writes=[("xin", xb)])
                for half in range(2):
                    bi = (tt * 2 + half) % 2
                    def tr(e, xb=xb, half=half, bi=bi):
                        for q in range(4):
                            dt = half * 4 + q
                            r = e.transpose(ps[bi][:, q * 128:(q + 1) * 128], xin[:, xb, dt * 128:(dt + 1) * 128], ident)
                        return r
                    P.pe(tr, reads=[("xin", xb), "cf"], writes=[PSK(bi)])
                    P.act(lambda e, tt=tt, half=half, bi=bi: e.activation(
                        xT[:, half * 4:half * 4 + 4, tt * 128:(tt + 1) * 128],
                        ps[bi][:].rearrange("p (q t) -> p q t", q=4), AF.Copy),
                        reads=[PSK(bi)], writes=[("xT", half * 4 + q) for q in range(4)])
            P.dma(lambda e, b=b: e.dma_start(out=mn[:], in_=T["mem"][b].rearrange("(t p) d -> p t d", p=128)), writes=["mn"])
            for t in range(2):
                P.act(lambda e, t=t: e.activation(xin[:, t, :], mn[:, t, :], AF.Square, accum_out=ssq[:, t:t + 1]),
                      reads=["mn"], writes=[("ssq", t), ("xin", t)])
            P.dve(lambda e: e.tensor_scalar(ssq[:, 2:4], ssq[:, 0:2], 1.0 / D, 1e-6, ALU.mult, ALU.add),
                  reads=[("ssq", 0), ("ssq", 1)], writes=["ssq2"])
            P.act(lambda e: e.activation(ssq[:, 2:4], ssq[:, 2:4], AF.Sqrt), reads=["ssq2"], writes=["ssq2"])
            P.dve(lambda e: e.reciprocal(ssq[:, 2:4], ssq[:, 2:4]), reads=["ssq2"], writes=["ssq2"])
            for t in range(2):
                P.dve(lambda e, t=t: e.tensor_scalar(mn[:, t, :], mn[:, t, :], ssq[:, 2 + t:3 + t], None, ALU.mult),
                      reads=["mn", "ssq2"], writes=["mn"])
            for t in range(2):
                for half in range(2):
                    bi = (t * 2 + half) % 2
                    def tr(e, t=t, half=half, bi=bi):
                        for q in range(4):
                            dt = half * 4 + q
                            r = e.transpose(ps[bi][:, q * 128:(q + 1) * 128], mn[:, t, dt * 128:(dt + 1) * 128], ident)
                        return r
                    P.pe(tr, reads=["mn", "cf"], writes=[PSK(bi)])
                    for q in range(4):
                        dt = half * 4 + q
                        P.dve(lambda e, t=t, q=q, dt=dt, bi=bi: e.tensor_scalar(
                            memT[:, dt, t * 128:(t + 1) * 128], ps[bi][:, q * 128:(q + 1) * 128],
                            gains[:, 6, dt:dt + 1], None, ALU.mult),
                            reads=[PSK(bi), "gains"], writes=[("memT", dt)])
        if stop == "load":
            pass
        else:
            for layer in range(2):
                if layer == 0:
                    stage_mix_ab(C, b, xT, ps, cf, cb, gains, pscale, zer, rmsnorm_tile, load_w, linear_fm, resid_add)
                else:
                    stage_mix_s5(C, b, xT, ps, cf, cb, gains, rmsnorm_tile, load_w, linear_fm, resid_add)
                if stop == "mix%d" % layer:
                    break
                stage_xattn(C, b, layer, xT, ps, cb, gains, memT, rmsnorm_tile, load_w, linear_fm, resid_add)
                if stop == "xa%d" % layer:
                    break
                stage_ffn(C, b, layer, xT, ps, gains, convp, rmsnorm_tile, load_w, linear_fm, resid_add)
                if stop == "ffn%d" % layer:
                    break
        P.barrier()
        with SBT(nc, "sq", [128, 8, 512], BF16) as sq, SBT(nc, "rstd", [128, 512], F32) as rstd, \
                SBT(nc, "yT", [128, 8, 512], F32) as yT, SBT(nc, "yo", [128, 2, D], F32) as yo:
            for tq in range(4):
                t0 = tq * 512
                if stop is None:
                    rmsnorm_tile(yT, "yT", 7, t0, 512, sq, rstd)
                else:
                    for dt in range(8):
                        P.act(lambda e, dt=dt, t0=t0: e.activation(yT[:, dt, :], xT[:, dt, t0:t0 + 512], AF.Copy),
                              reads=[("xT", dt)], writes=[("yT", dt)])
                for ts in range(4):
                    ob = ts % 2
                    for half in range(2):
                        bi = (ts * 2 + half) % 2
                        def tr(e, ts=ts, half=half, bi=bi):
                            for q in range(4):
                                dt = half * 4 + q
                                r = e.transpose(ps[bi][:, q * 128:(q + 1) * 128], yT[:, dt, ts * 128:(ts + 1) * 128], ident)
                            return r
                        P.pe(tr, reads=[("yT", dt) for dt in range(8)] + ["cf"], writes=[PSK(bi)])
                        P.act(lambda e, ob=ob, half=half, bi=bi: e.activation(yo[:, ob, half * 512:(half + 1) * 512],
                                                                               ps[bi][:], AF.Copy),
                              reads=[PSK(bi)], writes=[("yo", ob, half)])
                    tok = t0 + ts * 128
                    o = P.dma(lambda e, ob=ob, tok=tok, b=b: e.dma_start(out=out[b, tok:tok + 128, :], in_=yo[:, ob, :]),
                              reads=[("yo", ob, 0), ("yo", ob, 1)], writes=[("out", b, tok)])
                    final_ops.append(o)
    P.emit(final_ops)
    C.final = final_ops
    return nc, P


def stage_xattn(C, b, layer, xT, ps, cb, gains, memT, rmsnorm_tile, load_w, linear_fm, resid_add):
    nc, P, T = C.nc, C.P, C.T
    P.barrier()

    def PSK(i):
        return ("ps", i)
    with SBT(nc, "wq", [128, 8, D], BF16) as wq, SBT(nc, "wo", [128, 8, D], BF16) as wo, \
            SBT(nc, "wkv", [128, 8, D], BF16) as wkv, \
            SBT(nc, "KT", [128, 8, MEM], BF16) as KT, SBT(nc, "V", [128, 2, D], BF16) as V, \
            SBT(nc, "sq", [128, 8, 512], BF16) as sq, SBT(nc, "rstd", [128, 512], F32) as rstd, \
            SBT(nc, "hT", [128, 2, 8, 512], BF16) as hT, SBT(nc, "qT", [128, 8, 512], BF16) as qT, \
            SBT(nc, "pT", [128, 2, 2, 512], BF16) as pT, SBT(nc, "rs", [128, 2, 512], F32) as rs, \
            SBT(nc, "oT", [128, 8, 512], BF16) as oT:
        load_w(wkv, T["xa_w_kv"][layer], "wkv", 8, 0, D)
        load_w(wq, T["xa_w_q"][layer], "wq", 8, 0, D)

        def evK(m, bi, pap):
            P.act(lambda e: e.activation(KT[:, m, :], pap, AF.Copy), reads=[PSK(bi)], writes=[("KT", m)])
        linear_fm(wkv, "wkv", memT, "memT", 8, 8, MEM, evK)
        load_w(wkv, T["xa_w_kv"][layer], "wkv", 8, D, D)
        load_w(wo, T["xa_w_o"][layer], "wo", 8, 0, D)
        for mt in range(2):
            for nh in range(2):
                bi = (mt * 2 + nh) % 2
                def mm(e, mt=mt, nh=nh, bi=bi):
                    for k in range(8):
                        r = e.matmul(ps[bi][:], lhsT=memT[:, k, mt * 128:(mt + 1) * 128], rhs=wkv[:, k, nh * 512:(nh + 1) * 512],
                                     start=(k == 0), stop=(k == 7))
                    return r
                P.pe(mm, reads=[("wkv", k) for k in range(8)] + [("memT", k) for k in range(8)], writes=[PSK(bi)])
                P.act(lambda e, mt=mt, nh=nh, bi=bi: e.activation(V[:, mt, nh * 512:(nh + 1) * 512], ps[bi][:], AF.Copy),
                      reads=[PSK(bi)], writes=[("V", mt, nh)])
        rmsnorm_tile(hT[:, 0], "hT0", 2 + layer, 0, 512, sq, rstd)
        for tq in range(4):
            t0 = tq * 512
            tb = tq % 2

            def evQ(m, bi, pap):
                P.act(lambda e: e.activation(qT[:, m, :], pap, AF.Copy, scale=1.0 / 16.0), reads=[PSK(bi)], writes=[("qT", m)])
            linear_fm(wq, "wq", hT[:, tb], "hT%d" % tb, 8, 8, 512, evQ)
            def scores(h):
                hb = h % 2
                for mt in range(2):
                    bk = 2 + 2 * hb + mt
                    def mm(e, h=h, mt=mt, bk=bk):
                        for d in range(2):
                            r = e.matmul(ps[bk][:], lhsT=KT[:, 2 * h + d, mt * 128:(mt + 1) * 128], rhs=qT[:, 2 * h + d, :],
                                         start=(d == 0), stop=(d == 1))
                        return r
                    P.pe(mm, reads=[("KT", 2 * h), ("KT", 2 * h + 1), ("qT", 2 * h), ("qT", 2 * h + 1)], writes=[PSK(bk)])
                    P.act(lambda e, mt=mt, hb=hb, bk=bk: e.activation(pT[:, hb, mt, :], ps[bk][:], AF.Exp),
                          reads=[PSK(bk)], writes=[("pT", hb, mt)])

            def rest(h):
                hb = h % 2
                def mms(e, hb=hb):
                    e.matmul(ps[6][:], lhsT=cb[:, 3, :], rhs=pT[:, hb, 0, :], start=True, stop=False)
                    return e.matmul(ps[6][:], lhsT=cb[:, 3, :], rhs=pT[:, hb, 1, :], start=False, stop=True)
                P.pe(mms, reads=[("pT", hb, 0), ("pT", hb, 1), "cb"], writes=[PSK(6)])
                P.act(lambda e, hb=hb: e.activation(rs[:, hb, :], ps[6][:], AF.Ln), reads=[PSK(6)], writes=[("rs", hb)])
                P.act(lambda e, hb=hb: e.activation(rs[:, hb, :], rs[:, hb, :], AF.Exp, scale=-1.0), reads=[("rs", hb)], writes=[("rs", hb)])
                for d in range(2):
                    bi = 7 if d == 0 else 1
                    def mmo(e, h=h, hb=hb, d=d, bi=bi):
                        for mt in range(2):
                            r = e.matmul(ps[bi][:], lhsT=V[:, mt, h * 256 + d * 128:h * 256 + (d + 1) * 128], rhs=pT[:, hb, mt, :],
                                         start=(mt == 0), stop=(mt == 1))
                        return r
                    P.pe(mmo, reads=[("V", 0, h // 2), ("V", 1, h // 2), ("pT", hb, 0), ("pT", hb, 1)], writes=[PSK(bi)])
                    P.dve(lambda e, h=h, hb=hb, d=d, bi=bi: e.tensor_tensor(oT[:, 2 * h + d, :], ps[bi][:], rs[:, hb, :], ALU.mult),
                          reads=[PSK(bi), ("rs", hb)], writes=[("oT", 2 * h + d)])

            scores(0)
            for h in range(4):
                if h < 3:
                    scores(h + 1)
                rest(h)
            if tq + 1 < 4:
                rmsnorm_tile(hT[:, 1 - tb], "hT%d" % (1 - tb), 2 + layer, t0 + 512, 512, sq, rstd, part="sq")
            linear_fm(wo, "wo", oT, "oT", 8, 8, 512, lambda m, bi, pap, t0=t0: resid_add(m, bi, pap, t0, 512))
            if tq + 1 < 4:
                rmsnorm_tile(hT[:, 1 - tb], "hT%d" % (1 - tb), 2 + layer, t0 + 512, 512, sq, rstd, part="rest")


def stage_ffn(C, b, layer, xT, ps, gains, convp, rmsnorm_tile, load_w, linear_fm, resid_add):
    nc, P, T = C.nc, C.P, C.T
    P.barrier()

    def PSK(i):
        return ("ps", i)
    with SBT(nc, "sq", [128, 8, 512], BF16) as sq, SBT(nc, "rstd", [128, 512], F32) as rstd, \
            SBT(nc, "hT", [128, 2, 8, 512], BF16) as hT, SBT(nc, "gT", [128, NF, 512], BF16) as gT, \
            SBT(nc, "wu", [128, 2, 2, 8, 512], BF16) as wu, SBT(nc, "wd", [128, 4, D], BF16) as wd, \
            SBT(nc, "ub", [128, 3, 2, 516], F32) as ub, SBT(nc, "cv", [128, 3, 2, 512], F32) as cv, \
            SBT(nc, "halo", [128, 2 * NF, 2], F32) as halo:
        wup = T["ffn_w_up"][layer].rearrange("(k p) n -> p k n", p=128)
        wdn = T["ffn_w_down"][layer]
        P.dve(lambda e: e.memset(halo[:], 0.0), writes=["halo"])
        groups = [(0, 4), (4, 4), (8, 4), (12, 4), (16, 4), (20, 2)]
        it = 0
        git = 0
        kit = 0
        rmsnorm_tile(hT[:, 0], "hT0", 4 + layer, 0, 512, sq, rstd)
        pend = []

        def tail(pb, fp):
            P.act(lambda e: e.activation(cv[:, pb, 1, :], cv[:, pb, 1, :], AF.Silu),
                  reads=[("cv", pb, 1)], writes=[("cv", pb, 1)])
            P.dve(lambda e: e.tensor_tensor(gT[:, fp, :], cv[:, pb, 0, :], cv[:, pb, 1, :], ALU.mult),
                  reads=[("cv", pb, 0), ("cv", pb, 1)], writes=[("gT", fp)])
        for tq in range(4):
            t0 = tq * 512
            hb = tq % 2
            for (f0, nf) in groups:
                wb = git % 2
                git += 1
                for vg in range(2):
                    col0 = vg * DFF + f0 * 128
                    P.dma(lambda e, wb=wb, vg=vg, col0=col0, nf=nf: e.dma_start(out=wu[:, wb, vg, :, 0:nf * 128],
                                                                              in_=wup[:, :, col0:col0 + nf * 128]),
                          writes=[("wu", wb, vg)], q="pool")
                for fl in range(nf):
                    fp = f0 + fl
                    pb = it % 3
                    it += 1
                    for vg in range(2):
                        bi = 3 * vg + pb
                        f = vg * NF + fp
                        def mm(e, wb=wb, vg=vg, bi=bi, fl=fl, hb=hb):
                            for k in range(8):
                                r = e.matmul(ps[bi][:], lhsT=wu[:, wb, vg, k, fl * 128:(fl + 1) * 128], rhs=hT[:, hb, k, :], start=(k == 0), stop=(k == 7))
                            return r
                        P.pe(mm, reads=[("wu", wb, vg)] + [("hT%d" % hb, k) for k in range(8)], writes=[PSK(bi)])
                        P.act(lambda e, pb=pb, vg=vg, bi=bi: e.activation(ub[:, pb, vg, 2:514], ps[bi][:], AF.Copy),
                              reads=[PSK(bi)], writes=[("ub", pb, vg)])
                        P.act(lambda e, pb=pb, vg=vg, f=f: e.activation(ub[:, pb, vg, 0:2], halo[:, f, :], AF.Copy),
                              reads=["halo%d" % f, "halo"], writes=[("ubh", pb, vg)])
                        P.act(lambda e, pb=pb, vg=vg, f=f, bi=bi: e.activation(cv[:, pb, vg, :], ps[bi][:], AF.Identity,
                                                                               bias=convp[:, layer, 3, f:f + 1],
                                                                               scale=convp[:, layer, 2, f:f + 1]),
                              reads=[PSK(bi), "convp"], writes=[("cv", pb, vg)])
                        P.dve(lambda e, pb=pb, vg=vg, f=f: e.scalar_tensor_tensor(cv[:, pb, vg, :], ub[:, pb, vg, 1:513],
                                                                                  convp[:, layer, 1, f:f + 1], cv[:, pb, vg, :],
                                                                                  ALU.mult, ALU.add),
                              reads=[("ub", pb, vg), ("ubh", pb, vg), ("cv", pb, vg), "convp"], writes=[("cv", pb, vg)])
                        P.dve(lambda e, pb=pb, vg=vg, f=f: e.scalar_tensor_tensor(cv[:, pb, vg, :], ub[:, pb, vg, 0:512],
                                                                                  convp[:, layer, 0, f:f + 1], cv[:, pb, vg, :],
                                                                                  ALU.mult, ALU.add),
                              reads=[("ub", pb, vg), ("ubh", pb, vg), ("cv", pb, vg), "convp"], writes=[("cv", pb, vg)])
                        P.dve(lambda e, pb=pb, vg=vg, f=f: e.tensor_copy(halo[:, f, :], ub[:, pb, vg, 512:514]),
                              reads=[("ub", pb, vg)], writes=["halo%d" % f])
                    if pend:
                        tail(*pend.pop())
                    pend.append((pb, fp))
            if pend:
                tail(*pend.pop())
            if tq + 1 < 4:
                rmsnorm_tile(hT[:, 1 - hb], "hT%d" % (1 - hb), 4 + layer, t0 + 512, 512, sq, rstd, part="sq")
            for k in range(NF):
                db = kit % 4
                kit += 1
                P.dma(lambda e, db=db, k=k: e.dma_start(out=wd[:, db, :], in_=wdn[k * 128:(k + 1) * 128, :]),
                      writes=[("wd", db)], q="pool")
                def mm(e, db=db, k=k):
                    for m in range(8):
                        r = e.matmul(ps[m][:], lhsT=wd[:, db, m * 128:(m + 1) * 128], rhs=gT[:, k, :], start=(k == 0), stop=(k == NF - 1))
                    return r
                P.pe(mm, reads=[("wd", db), ("gT", k)], writes=[PSK(m) for m in range(8)])
            resid_add(7, 7, ps[7][:], t0, 512)
            if tq + 1 < 4:
                rmsnorm_tile(hT[:, 1 - hb], "hT%d" % (1 - hb), 4 + layer, t0 + 512, 512, sq, rstd, part="rest")
            for m in range(7):
                resid_add(m, m, ps[m][:], t0, 512)


def stage_mix_ab(C, b, xT, ps, cf, cb, gains, pscale, zer, rmsnorm_tile, load_w, linear_fm, resid_add):
    nc, P, T = C.nc, C.P, C.T
    P.barrier()
    maskstrict = cf[:, 4, :]

    def PSK(i):
        return ("ps", i)
    win = T["ab_w_in"][0]
    with SBT(nc, "hT", [128, 8, S], BF16) as hT, SBT(nc, "aT", [128, 4, S], BF16) as aT, \
            SBT(nc, "pTo", [128, 4, S], BF16) as pTo:
        with SBT(nc, "sq", [128, 8, 512], BF16) as sq, SBT(nc, "rstd", [128, 512], F32) as rstd, \
                SBT(nc, "hTt", [128, 8, 512], BF16) as hTt:
            for tq in range(4):
                rmsnorm_tile(hTt, "hTt", 0, tq * 512, 512, sq, rstd)
                for dt in range(8):
                    P.act(lambda e, dt=dt, tq=tq: e.activation(hT[:, dt, tq * 512:(tq + 1) * 512], hTt[:, dt, :], AF.Copy),
                          reads=[("hTt", dt)], writes=[("hT", dt)])
        P.barrier()
        with SBT(nc, "wu4", [128, 8, 512], BF16) as wu4, SBT(nc, "wp", [128, 128], BF16) as wp, \
                SBT(nc, "uA", [128, S], F32) as uA, SBT(nc, "uB", [128, S], F32) as uB, \
                SBT(nc, "u0", [128, S], F32) as u0, SBT(nc, "pb", [128, S], BF16) as pb:
            for g in range(4):
                w_ = 2 ** (g + 1)
                if g == 0:
                    load_w(wu4, win, "wu4", 8, 1536, 512)
                P.dma(lambda e, g=g: e.dma_start(out=wp[:], in_=T["pool_w"][0, g]), writes=["wp"], q="pool")
                for tq in range(4):
                    bi = tq % 2
                    def mm(e, tq=tq, bi=bi, g=g):
                        for k in range(8):
                            r = e.matmul(ps[bi][:], lhsT=wu4[:, k, g * 128:(g + 1) * 128], rhs=hT[:, k, tq * 512:(tq + 1) * 512], start=(k == 0), stop=(k == 7))
                        return r
                    P.pe(mm, reads=[("wu4", k) for k in range(8)] + [("hT", k) for k in range(8)], writes=[PSK(bi)])
                    P.act(lambda e, tq=tq, bi=bi: e.activation(u0[:, tq * 512:(tq + 1) * 512], ps[bi][:], AF.Copy),
                          reads=[PSK(bi)], writes=["u0"])
                src, srck = u0, "u0"
                bufs = [(uA, "uA"), (uB, "uB")]
                for st in range(g + 1):
                    sh = 2 ** st
                    dst, dstk = bufs[st % 2]
                    def stp(e, src=src, dst=dst, sh=sh):
                        e.tensor_copy(dst[:, 0:sh], src[:, 0:sh])
                        return e.tensor_tensor(dst[:, sh:S], src[:, sh:S], src[:, 0:S - sh], ALU.add)
                    P.dve(stp, reads=[srck], writes=[dstk])
                    src, srck = dst, dstk
                def pl(e, src=src, w_=w_):
                    e.scalar_tensor_tensor(pb[:, w_ - 1:S], src[:, w_ - 1:S], 1.0 / w_, u0[:, w_ - 1:S], ALU.mult, ALU.subtract)
                    return e.tensor_tensor(src[:, 0:w_ - 1], src[:, 0:w_ - 1], cf[:, 8, 0:w_ - 1], ALU.mult)
                P.dve(pl, reads=[srck, "u0", "cf"], writes=["pb0", srck])
                P.dve(lambda e, src=src, w_=w_: e.tensor_tensor(pb[:, 0:w_ - 1], src[:, 0:w_ - 1], u0[:, 0:w_ - 1], ALU.subtract),
                      reads=[srck, "u0"], writes=["pb1"])
                for tq in range(4):
                    bi = tq % 2
                    P.pe(lambda e, tq=tq, bi=bi: e.matmul(ps[bi][:], lhsT=wp[:], rhs=pb[:, tq * 512:(tq + 1) * 512], start=True, stop=True),
                         reads=["wp", "pb0", "pb1"], writes=[PSK(bi)])
                    P.act(lambda e, tq=tq, bi=bi, g=g: e.activation(pTo[:, g, tq * 512:(tq + 1) * 512], ps[bi][:], AF.Identity,
                                                                    scale=pscale[:, g:g + 1]),
                          reads=[PSK(bi), "pscale"], writes=[("pTo", g)])
        P.barrier()
        NBUF = 4
        with SBT(nc, "wqkv", [128, 8, 1536], BF16) as wqkv, SBT(nc, "qh", [64, S], BF16) as qh, \
                SBT(nc, "kh", [64, S], BF16) as kh, SBT(nc, "vh", [128, 16, 128], BF16) as vh, \
                SBT(nc, "ex", [128, NBUF, 512], F32) as ex, \
                SBT(nc, "spb", [128, NBUF, 512], BF16) as spb, \
                SBT(nc, "wsb", [128, NBUF, 512], BF16) as wsb, SBT(nc, "Ls", [128, 2, 512], F32) as Ls, \
                SBT(nc, "Lsb", [128, 4, 512], BF16) as Lsb, SBT(nc, "otmp", [64, 2, 512], BF16) as otmp:
            identb = cb[:, 0, :]
            trinc = cb[:, 4, :]
            maskneg = cb[:, 5, :]
            onesneg = cb[:, 2, :]
            git = 0
            for h in range(8):
                hp = h // 2
                if h == 0:
                    load_w(wqkv, win, "wqkv", 8, 0, 1536)
                for j3, (dst, dk, scl) in enumerate([(qh, "qh", 0.125), (kh, "kh", 1.0)]):
                    for tq in range(4):
                        bi = 4 + tq % 2
                        def mm(e, h=h, j3=j3, tq=tq, bi=bi):
                            for k in range(8):
                                r = e.matmul(ps[bi][0:64, :], lhsT=wqkv[:, k, j3 * 512 + h * 64:j3 * 512 + h * 64 + 64],
                                             rhs=hT[:, k, tq * 512:(tq + 1) * 512], start=(k == 0), stop=(k == 7))
                            return r
                        P.pe(mm, reads=[("wqkv", k) for k in range(8)] + [("hT", k) for k in range(8)], writes=[PSK(bi)])
                        P.act(lambda e, dst=dst, tq=tq, bi=bi, scl=scl: e.activation(dst[:, tq * 512:(tq + 1) * 512], ps[bi][0:64, :],
                                                                                   AF.Copy, scale=scl),
                              reads=[PSK(bi)], writes=[(dk, tq)])
                if h % 2 == 0:
                    for t4 in range(4):
                        bi = 4 + t4 % 2
                        def mmv(e, h=h, t4=t4, bi=bi):
                            for tl in range(4):
                                tt = t4 * 4 + tl
                                for k in range(8):
                                    r = e.matmul(ps[bi][:, tl * 128:(tl + 1) * 128], lhsT=hT[:, k, tt * 128:(tt + 1) * 128],
                                                 rhs=wqkv[:, k, 1024 + h * 64:1024 + h * 64 + 128], start=(k == 0), stop=(k == 7))
                            return r
                        P.pe(mmv, reads=[("wqkv", k) for k in range(8)] + [("hT", k) for k in range(8)], writes=[PSK(bi)])
                        P.dve(lambda e, t4=t4, bi=bi: e.tensor_copy(vh[:, t4 * 4:(t4 + 1) * 4, :],
                                                                    ps[bi][:].rearrange("p (t c) -> p t c", t=4)),
                              reads=[PSK(bi)], writes=[("vh", t4)])
                its = []
                for j in range(4):
                    kbs = list(range(4 * j + 3, -1, -1))
                    for ii, kb in enumerate(kbs):
                        diag = kb >= 4 * j
                        qlo = 128 * (kb - 4 * j) if diag else 0
                        its.append(dict(j=j, kb=kb, first=(ii == 0), last=(kb == 0), diag=diag, qlo=qlo, g=git))
                        git += 1

                def zmm(e, dst, it_, stop_after):
                    kb, qlo, j = it_["kb"], it_["qlo"], it_["j"]
                    q0 = 512 * j + qlo
                    r = e.matmul(dst[:, qlo:512], lhsT=kh[:, kb * 128:(kb + 1) * 128], rhs=qh[:, q0:512 * (j + 1)],
                                 start=True, stop=(stop_after and not it_["diag"]))
                    if it_["diag"]:
                        r = e.matmul(dst[:, qlo:qlo + 128], lhsT=identb, rhs=maskneg, start=False, stop=stop_after)
                    return r

                def stageA(it_):
                    r_ = it_["g"] % NBUF
                    j, kb, qlo = it_["j"], it_["kb"], it_["qlo"]
                    lb = j % 2
                    if it_["first"]:
                        P.dve(lambda e, lb=lb: e.memset(Ls[:, lb, :], 0.0), writes=[("Ls", lb)])
                    P.pe(lambda e, it_=it_, r_=r_: zmm(e, ps[r_], it_, True),
                         reads=[("kh", kb // 4), ("qh", j), "cb"], writes=[PSK(r_)])
                    P.act(lambda e, r_=r_, qlo=qlo: e.activation(ex[:, r_, qlo:512], ps[r_][:, qlo:512], AF.Exp),
                          reads=[PSK(r_)], writes=[("ex", r_)])
                    P.act(lambda e, r_=r_, qlo=qlo: e.activation(spb[:, r_, qlo:512], ex[:, r_, qlo:512], AF.Ln, bias=1.0),
                          reads=[("ex", r_)], writes=[("spb", r_)])
                    if not it_["last"]:
                        nqlo = max(0, 128 * (kb - 1 - 4 * j))
                        nr = (it_["g"] + 1) % 4
                        P.dve(lambda e, r_=r_, qlo=qlo, lb=lb: e.tensor_tensor(Ls[:, lb, qlo:512], Ls[:, lb, qlo:512], spb[:, r_, qlo:512], ALU.add),
                              reads=[("Ls", lb), ("spb", r_)], writes=[("Ls", lb)])
                        P.dve(lambda e, nqlo=nqlo, nr=nr, lb=lb: e.tensor_copy(Lsb[:, nr, nqlo:512], Ls[:, lb, nqlo:512]),
                              reads=[("Ls", lb)], writes=[("Lsb", nr)])

                def stageB1(it_):
                    r_ = it_["g"] % NBUF
                    qlo = it_["qlo"]
                    pst = ps[r_]
                    def mmt(e, it_=it_, r_=r_, pst=pst, qlo=qlo):
                        r = e.matmul(pst[:, qlo:512], lhsT=trinc, rhs=spb[:, r_, qlo:512], start=False, stop=it_["first"])
                        if not it_["first"]:
                            r = e.matmul(pst[:, qlo:512], lhsT=onesneg, rhs=Lsb[:, it_["g"] % 4, qlo:512], start=False, stop=True)
                        return r
                    P.pe(mmt, reads=["cb", ("spb", r_), ("Lsb", it_["g"] % 4)], writes=[PSK(r_)])
                    P.act(lambda e, r_=r_, pst=pst, qlo=qlo: e.activation(wsb[:, r_, qlo:512], pst[:, qlo:512], AF.Exp),
                          reads=[PSK(r_)], writes=[("wsb", r_)])

                def stageB2(it_):
                    r_ = it_["g"] % NBUF
                    j, kb, qlo = it_["j"], it_["kb"], it_["qlo"]
                    ob = j % 2
                    pso = ps[6 + ob]
                    if it_["first"]:
                        P.pe(lambda e, pso=pso: e.matmul(pso[:, :], lhsT=zer[0:1, 0:128], rhs=zer[0:1, 0:512], start=True, stop=False),
                             reads=["zer"], writes=[PSK(6 + ob)])
                    P.pe(lambda e, pso=pso, kb=kb, r_=r_, qlo=qlo, last=it_["last"]: e.matmul(
                        pso[:, qlo:512], lhsT=vh[:, kb, :], rhs=wsb[:, r_, qlo:512], start=False, stop=last),
                        reads=[("vh", kb // 4), ("wsb", r_)], writes=[PSK(6 + ob)])
                    if it_["last"]:
                        if h % 2 == 0:
                            P.dve(lambda e, pso=pso, j=j, hp=hp: e.tensor_copy(aT[0:64, hp, 512 * j:512 * (j + 1)], pso[0:64, :]),
                                  reads=[PSK(6 + ob)], writes=[("aT", hp, 0)])
                        else:
                            P.dve(lambda e, pso=pso, j=j, hp=hp: e.tensor_copy(aT[64:128, hp, 512 * j:512 * (j + 1)], pso[64:128, :]),
                                  reads=[PSK(6 + ob)], writes=[("aT", hp, 1)])

                n_it = len(its)
                for i in range(n_it + 2):
                    if i < n_it:
                        stageA(its[i])
                    if 1 <= i <= n_it:
                        stageB1(its[i - 1])
                    if i >= 2:
                        stageB2(its[i - 2])
        P.barrier()
        with SBT(nc, "wout", [128, 8, D], BF16) as wout:
            load_w(wout, T["ab_w_out"][0], "wout", 8, 0, D)
            for tq in range(4):
                t0 = tq * 512
                for m in range(8):
                    bi = m % 2
                    def mm(e, m=m, bi=bi, t0=t0):
                        for k in range(8):
                            src = aT if k < 4 else pTo
                            r = e.matmul(ps[bi][:], lhsT=wout[:, k, m * 128:(m + 1) * 128], rhs=src[:, k % 4, t0:t0 + 512],
                                         start=(k == 0), stop=(k == 7))
                        return r
                    P.pe(mm, reads=[("wout", k) for k in range(8)] + ["aTall"], writes=[PSK(bi)])
                    resid_add(m, bi, ps[bi][:], t0, 512)


def stage_mix_s5(C, b, xT, ps, cf, cb, gains, rmsnorm_tile, load_w, linear_fm, resid_add):
    from contextlib import ExitStack
    nc, P, T = C.nc, C.P, C.T
    P.barrier()

    def PSK(i):
        return ("ps", i)
    ident = cf[:, 0, :]
    mask32 = cf[:, 5, :]
    TWO_PI = 6.283185
    INV2PI = 1.0 / (2.0 * math.pi)

    def V(fn, r, w):
        return P.dve(fn, reads=r, writes=w)

    def Aop(fn, r, w):
        return P.act(fn, reads=r, writes=w)

    with SBT(nc, "uT", [128, 8, S], BF16) as uT, SBT(nc, "dcol", [128, 8], F32) as dcol:
        with SBT(nc, "w_in", [128, 8, D], BF16) as w_in, SBT(nc, "sq", [128, 8, 512], BF16) as sq, \
                SBT(nc, "rstd", [128, 512], F32) as rstd, SBT(nc, "hT", [128, 8, 512], BF16) as hT:
            load_w(w_in, T["ssm_w_in"][0], "w_in", 8, 0, D)
            P.dma(lambda e: e.dma_start(out=dcol[:], in_=T["ssm_d"][0].rearrange("(t p) -> p t", p=128),
                                        allow_slow_non_contiguous=True), writes=["dcol"])
            for tq in range(4):
                rmsnorm_tile(hT, "hT", 1, tq * 512, 512, sq, rstd)

                def ev(m, bi, pap, tq=tq):
                    P.act(lambda e: e.activation(uT[:, m, tq * 512:(tq + 1) * 512], pap, AF.Copy),
                          reads=[PSK(bi)], writes=[("uT", m)])
                linear_fm(w_in, "w_in", hT, "hT", 8, 8, 512, ev)
        P.barrier()
        with ExitStack() as es:
            def A(name, shape, dt=F32):
                return es.enter_context(SBT(nc, name, shape, dt))
            lre = A("lre", [128, 4]); lim = A("lim", [128, 4]); ldt = A("ldt", [128, 4])
            dtt = A("dtt", [128, 4]); ar = A("ar", [128, 4]); an = A("an", [128, 4])
            arj = A("arj", [128, 9, 4]); tj = A("tj", [128, 9, 4]); tjc = A("tjc", [128, 9, 4])
            ti = A("ti", [128, 9, 4], I32); fr = A("fr", [128, 9, 4])
            mag = A("mag", [128, 9, 4]); sinj = A("sinj", [128, 9, 4]); cosj = A("cosj", [128, 9, 4])
            Lr = A("Lr", [128, 9, 4]); Li = A("Li", [128, 9, 4])
            nre = A("nre", [128, 4]); den = A("den", [128, 4]); t1 = A("t1", [128, 4]); t2 = A("t2", [128, 4])
            cr = A("cr", [128, 4]); ci = A("ci", [128, 4]); ti8 = A("ti8", [128, 4], I32); t8f = A("t8f", [128, 4])
            Fr = A("Fr", [128, 8, 4]); Fi = A("Fi", [128, 8, 4]); f1 = A("f1", [128, 8, 4]); f2 = A("f2", [128, 8, 4])
            Bst = A("Bst", [128, 2, 4, 16]); Cin = A("Cin", [64, 2, 2, 64]); Cst = A("Cst", [128, 2, 4, 16])
            l1 = A("l1", [128, 9, 4, 16]); l2 = A("l2", [128, 9, 4, 16])
            What = A("What", [128, 8, 2, 128])
            Wt = A("Wt", [128, 8, 2, 128], BF16)
            CL = A("CL", [128, 2, 9, 4, 16])
            LB = CL[:, :, 0:8]
            Qd = A("Qd", [128, 9, 2, 4, 32], BF16)
            Qf = A("Qf", [128, 2, 128])
            TtF = A("TtF", [128, 4, 128]); Tt = A("Tt", [128, 8, 128], BF16)
            cosT = A("cosT", [128, 4, 256]); sinT = A("sinT", [128, 4, 256])
            Xp = A("Xp", [128, 2, 4, 256]); xa = A("xa", [128, 4, 256]); xb = A("xb", [128, 4, 256])
            Ssc = A("Ssc", [128, 2, 4, 256]); tk = Ssc[:, 0]; tki = Ssc[:, 1].bitcast(I32); Hb = A("Hb", [128, 2, 4, 257], BF16)
            iota256 = cf[:, 6:8, :].rearrange("p a b -> p (a b)")
            What6 = What[:].rearrange("p t r (q g c) -> p t r q g c", q=4, g=2)
            Qf5 = Qf[:].rearrange("p r (q g c) -> p r q g c", q=4, g=2)
            Wv = Wt[:].rearrange("p t r n -> p (t r) n")
            V(lambda e: e.memset(What[:], 0.0), [], ["What"])
            V(lambda e: e.memset(Qd[:], 0.0), [], ["Qpad"])
            V(lambda e: e.memset(Qf[:], 0.0), [], ["Qf"])
            V(lambda e: e.memset(Hb[:], 0.0), [], ["Hb"])

            def bc(ap, shape):
                return ap.broadcast_to(shape)

            def partA(j):
                g0 = 8 * j
                P.dma(lambda e, g0=g0: e.dma_start(out=lre[:], in_=T["ssm_lam_re"][0, g0:g0 + 8, :].rearrange("(q g) p -> (g p) q", g=2),
                                                   allow_slow_non_contiguous=True), writes=["lre"])
                P.dma(lambda e, g0=g0: e.dma_start(out=lim[:], in_=T["ssm_lam_im"][0, g0:g0 + 8, :].rearrange("(q g) p -> (g p) q", g=2),
                                                   allow_slow_non_contiguous=True), writes=["lim"])
                for g2 in range(2):
                    P.dma(lambda e, g0=g0, g2=g2: e.dma_start(
                        out=ldt[64 * g2:64 * g2 + 64, :],
                        in_=T["ssm_log_dt"][0, g0:g0 + 8].rearrange("(q g) -> g q", g=2)[g2:g2 + 1, :].broadcast_to([64, 4]),
                        allow_slow_non_contiguous=True), writes=["ldt"])
                for ri, nm in enumerate(["ssm_b_re", "ssm_b_im"]):
                    P.dma(lambda e, g0=g0, ri=ri, nm=nm: e.dma_start(
                        out=Bst[:, ri, :, :], in_=T[nm][0, g0:g0 + 8].rearrange("(q g) p c -> (g p) q c", g=2)),
                        writes=["Bst"])
                for ri, nm in enumerate(["ssm_c_re", "ssm_c_im"]):
                    for q in range(4):
                        P.dma(lambda e, g0=g0, ri=ri, nm=nm, q=q: e.dma_start(
                            out=Cin[16 * q:16 * q + 16, ri, :, :],
                            in_=T[nm][0, g0 + 2 * q:g0 + 2 * q + 2].rearrange("g c p -> c g p")), writes=["Cin"])
                def trc(e):
                    for ri in range(2):
                        r = e.transpose(ps[0][:, ri * 64:(ri + 1) * 64], Cin[:, ri, :, :].rearrange("a g p -> a (g p)"), ident[0:64, 0:64])
                    return r
                P.pe(trc, reads=["Cin", "cf"], writes=[PSK(0)])
                Aop(lambda e: e.activation(Cst[:].rearrange("p r q c -> p (r q c)"), ps[0][:, 0:128], AF.Copy), [PSK(0)], ["Cst"])
                Aop(lambda e: e.activation(dtt[:], ldt[:], AF.Exp), ["ldt"], ["dtt"])
                def f_(e):
                    e.tensor_tensor(ar[:], lre[:], dtt[:], ALU.mult)
                    return e.tensor_tensor(an[:], lim[:], dtt[:], ALU.mult)
                V(f_, ["lre", "lim", "dtt"], ["ar", "an"])
                jv = bc(cf[:, 6, 0:9][:, :, None], [128, 9, 4])
                def f_(e):
                    e.tensor_tensor(arj[:], bc(ar[:, None, :], [128, 9, 4]), jv, ALU.mult)
                    return e.scalar_tensor_tensor(tj[:], bc(an[:, None, :], [128, 9, 4]), INV2PI, jv, ALU.mult, ALU.mult)
                V(f_, ["ar", "an", "cf"], ["arj", "tj"])
                Aop(lambda e: e.activation(mag[:], arj[:], AF.Exp), ["arj"], ["mag"])
                V(lambda e: e.tensor_copy(ti[:], tj[:]), ["tj"], ["ti"])
                V(lambda e: e.tensor_tensor(fr[:], tj[:], ti[:], ALU.subtract), ["tj", "ti"], ["fr"])
                Aop(lambda e: e.activation(sinj[:], fr[:], AF.Sin, scale=TWO_PI), ["fr"], ["sinj"])
                V(lambda e: e.tensor_scalar(tjc[:], tj[:], 0.25, None, ALU.add), ["tj"], ["tjc"])
                V(lambda e: e.tensor_copy(ti[:], tjc[:]), ["tjc"], ["ti"])
                V(lambda e: e.tensor_tensor(fr[:], tjc[:], ti[:], ALU.subtract), ["tjc", "ti"], ["fr"])
                Aop(lambda e: e.activation(cosj[:], fr[:], AF.Sin, scale=TWO_PI), ["fr"], ["cosj"])
                def f_(e):
                    e.tensor_tensor(Lr[:], mag[:], cosj[:], ALU.mult)
                    return e.tensor_tensor(Li[:], mag[:], sinj[:], ALU.mult)
                V(f_, ["mag", "cosj", "sinj"], ["Lr", "Li"])
                def f_(e):
                    e.tensor_scalar(nre[:], Lr[:, 1, :], -1.0, None, ALU.add)
                    e.tensor_tensor(t1[:], lre[:], lre[:], ALU.mult)
                    return e.tensor_tensor(t2[:], lim[:], lim[:], ALU.mult)
                V(f_, ["Lr", "lre", "lim"], ["nre", "t1", "t2"])
                V(lambda e: e.tensor_tensor(den[:], t1[:], t2[:], ALU.add), ["t1", "t2"], ["den"])
                V(lambda e: e.reciprocal(den[:], den[:]), ["den"], ["den"])
                def f_(e):
                    e.tensor_tensor(t1[:], nre[:], lre[:], ALU.mult)
                    return e.tensor_tensor(t2[:], Li[:, 1, :], lim[:], ALU.mult)
                V(f_, ["nre", "lre", "Li", "lim", "den"], ["t1", "t2"])
                V(lambda e: e.tensor_tensor(cr[:], t1[:], t2[:], ALU.add), ["t1", "t2"], ["cr"])
                V(lambda e: e.tensor_tensor(cr[:], cr[:], den[:], ALU.mult), ["cr", "den"], ["cr"])
                def f_(e):
                    e.tensor_tensor(t1[:], Li[:, 1, :], lre[:], ALU.mult)
                    return e.tensor_tensor(t2[:], nre[:], lim[:], ALU.mult)
                V(f_, ["nre", "lre", "Li", "lim", "cr"], ["t1", "t2"])
                V(lambda e: e.tensor_tensor(ci[:], t1[:], t2[:], ALU.subtract), ["t1", "t2"], ["ci"])
                V(lambda e: e.tensor_tensor(ci[:], ci[:], den[:], ALU.mult), ["ci", "den"], ["ci"])
                crb = bc(cr[:, None, :], [128, 8, 4]); cib = bc(ci[:, None, :], [128, 8, 4])
                def f_(e, crb=crb, cib=cib):
                    e.tensor_tensor(f1[:], Lr[:, 0:8, :], crb, ALU.mult)
                    return e.tensor_tensor(f2[:], Li[:, 0:8, :], cib, ALU.mult)
                V(f_, ["Lr", "Li", "cr", "ci"], ["f1", "f2"])
                V(lambda e: e.tensor_tensor(Fr[:], f1[:], f2[:], ALU.subtract), ["f1", "f2"], ["Fr"])
                def f_(e, crb=crb, cib=cib):
                    e.tensor_tensor(f1[:], Lr[:, 0:8, :], cib, ALU.mult)
                    return e.tensor_tensor(f2[:], Li[:, 0:8, :], crb, ALU.mult)
                V(f_, ["Lr", "Li", "cr", "ci", "Fr"], ["f1", "f2"])
                V(lambda e: e.tensor_tensor(Fi[:], f1[:], f2[:], ALU.add), ["f1", "f2"], ["Fi"])
                sh8 = [128, 8, 4, 16]
                Frb = bc(Fr[:, :, :, None], sh8); Fib = bc(Fi[:, :, :, None], sh8)
                B0 = bc(Bst[:, 0, None, :, :], sh8); B1 = bc(Bst[:, 1, None, :, :], sh8)
                def f_(e, Frb=Frb, Fib=Fib, B0=B0, B1=B1):
                    e.tensor_tensor(l1[:, 0:8], Frb, B0, ALU.mult)
                    return e.tensor_tensor(l2[:, 0:8], Fib, B1, ALU.mult)
                V(f_, ["Fr", "Fi", "Bst"], ["l1", "l2"])
                V(lambda e: e.tensor_tensor(LB[:, 0], l1[:, 0:8], l2[:, 0:8], ALU.subtract), ["l1", "l2"], ["LB0", "CL0"])
                def f_(e, Frb=Frb, Fib=Fib, B0=B0, B1=B1):
                    e.tensor_tensor(l1[:, 0:8], Frb, B1, ALU.mult)
                    return e.tensor_tensor(l2[:, 0:8], Fib, B0, ALU.mult)
                V(f_, ["Fr", "Fi", "Bst", "LB0"], ["l1", "l2"])
                V(lambda e: e.tensor_tensor(LB[:, 1], l1[:, 0:8], l2[:, 0:8], ALU.add), ["l1", "l2"], ["LB1", "CL1"])
                def f_(e):
                    for g2 in range(2):
                        for ri in range(2):
                            r = e.tensor_copy(What6[64 * g2:64 * g2 + 64, :, ri, :, g2, :], LB[64 * g2:64 * g2 + 64, ri, :, :, :])
                    return r
                V(f_, ["LB0", "LB1"], ["What"])
            def partB(j):
                g0 = 8 * j
                for c4 in range(4):
                    bi = c4 % 2
                    def trw(e, c4=c4, bi=bi):
                        for i4 in range(4):
                            c = c4 * 4 + i4
                            r = e.transpose(ps[bi][:, i4 * 128:(i4 + 1) * 128], What[:, c // 2, c % 2, :], ident)
                        return r
                    P.pe(trw, reads=["What", "cf"], writes=[PSK(bi)])
                    if c4 % 2 == 0:
                        Aop(lambda e, c4=c4, bi=bi: e.activation(Wv[:, c4 * 4:c4 * 4 + 4, :], ps[bi][:].rearrange("p (a n) -> p a n", a=4), AF.Copy),
                            [PSK(bi)], [("Wpad", c4)])
                    else:
                        V(lambda e, c4=c4, bi=bi: e.tensor_copy(Wv[:, c4 * 4:c4 * 4 + 4, :], ps[bi][:].rearrange("p (a n) -> p a n", a=4)),
                          [PSK(bi)], [("Wpad", c4)])
                sh9 = [128, 9, 4, 16]
                Lrb = bc(Lr[:, :, :, None], sh9); Lib = bc(Li[:, :, :, None], sh9)
                C0 = bc(Cst[:, 0, None, :, :], sh9); C1 = bc(Cst[:, 1, None, :, :], sh9)
                def f_(e, Lrb=Lrb, Lib=Lib, C0=C0, C1=C1):
                    e.tensor_tensor(l1[:], Lrb, C0, ALU.mult)
                    return e.tensor_tensor(l2[:], Lib, C1, ALU.mult)
                V(f_, ["Lr", "Li", "Cst", "LB1"], ["l1", "l2"])
                V(lambda e: e.tensor_tensor(CL[:, 0], l1[:], l2[:], ALU.subtract), ["l1", "l2"], ["CL0", "LB0", "LB1"])
                def f_(e, Lrb=Lrb, Lib=Lib, C0=C0, C1=C1):
                    e.tensor_tensor(l1[:], Lib, C0, ALU.mult)
                    return e.tensor_tensor(l2[:], Lrb, C1, ALU.mult)
                V(f_, ["Lr", "Li", "Cst", "CL0"], ["l1", "l2"])
                V(lambda e: e.scalar_tensor_tensor(CL[:, 1], l1[:], -1.0, l2[:], ALU.mult, ALU.subtract), ["l1", "l2"], ["CL1", "LB0", "LB1"])
                def f_(e):
                    for g2 in range(2):
                        for ri in range(2):
                            e.tensor_copy(Qd[64 * g2:64 * g2 + 64, :, ri, :, 16 * g2:16 * g2 + 16], CL[64 * g2:64 * g2 + 64, ri, :, :, :])
                            r = e.tensor_copy(Qf5[64 * g2:64 * g2 + 64, ri, :, g2, :], CL[64 * g2:64 * g2 + 64, ri, 0, :, :])
                    return r
                V(f_, ["CL0", "CL1"], ["Qpad", "Qf"])
                for half in range(2):
                    bi = half
                    def mmt(e, half=half, bi=bi):
                        for i4 in range(4):
                            tau = half * 4 + i4
                            e.matmul(ps[bi][:, i4 * 128:(i4 + 1) * 128], lhsT=What[:, tau, 0, :], rhs=Qf[:, 0, :], start=True, stop=False)
                            r = e.matmul(ps[bi][:, i4 * 128:(i4 + 1) * 128], lhsT=What[:, tau, 1, :], rhs=Qf[:, 1, :], start=False, stop=True)
                        return r
                    P.pe(mmt, reads=["What", "Qf"], writes=[PSK(bi)])
                    m4 = bc(mask32[:, None, :], [128, 4, 128])
                    if half == 0:
                        V(lambda e, bi=bi, m4=m4: e.tensor_tensor(TtF[:], ps[bi][:].rearrange("p (a n) -> p a n", a=4), m4, ALU.mult),
                          [PSK(bi), "cf"], ["TtF"])
                        V(lambda e, j=j: e.scalar_tensor_tensor(TtF[:, 0, :], ident, dcol[:, j:j + 1], TtF[:, 0, :], ALU.mult, ALU.add),
                          ["TtF", "dcol", "cf"], ["TtF"])
                        V(lambda e: e.tensor_copy(Tt[:, 0:4, :], TtF[:]), ["TtF"], [("Tt", 0)])
                    else:
                        V(lambda e, bi=bi, m4=m4: e.tensor_tensor(Tt[:, 4:8, :], ps[bi][:].rearrange("p (a n) -> p a n", a=4), m4, ALU.mult),
                          [PSK(bi), "cf"], [("Tt", 1)])
                uv = uT[:, j, :].rearrange("p (k s) -> p s k", s=8)
                def mmx(e, uv=uv):
                    for ri in range(2):
                        for tau in range(8):
                            for q in range(4):
                                r = e.matmul(ps[2 + q][:, ri * 256:(ri + 1) * 256], lhsT=Wt[32 * q:32 * q + 32, tau, ri, :],
                                             rhs=uv[32 * q:32 * q + 32, 7 - tau, :], start=(tau == 0), stop=(tau == 7),
                                             tile_position=(32 * q, 0))
                    return r
                P.pe(mmx, reads=[("Wpad", c4) for c4 in range(4)] + [("uT", j, s_) for s_ in range(8)], writes=[PSK(2 + q) for q in range(4)])
                V(lambda e: e.tensor_copy(ti8[:], tj[:, 8, :]), ["tj"], ["ti8"])
                V(lambda e: e.tensor_tensor(t8f[:], tj[:, 8, :], ti8[:], ALU.subtract), ["tj", "ti8"], ["t8f"])
                V(lambda e: e.tensor_tensor(tk, bc(t8f[:, :, None], [128, 4, 256]), bc(iota256[:, None, :], [128, 4, 256]), ALU.mult),
                  ["t8f", "cf"], ["tk"] + [("Ssc", ri_, q_) for ri_ in range(2) for q_ in range(4)])
                V(lambda e: e.tensor_copy(tki, tk), ["tk"], ["tki"])
                V(lambda e: e.tensor_tensor(xa[:], tk, tki, ALU.subtract), ["tk", "tki"], ["xa"])
                Aop(lambda e: e.activation(sinT[:], xa[:], AF.Sin, scale=TWO_PI), ["xa"], ["sinT"])
                V(lambda e: e.tensor_scalar(tk, tk, 0.25, None, ALU.add), ["tk", "tki"], ["tk"])
                V(lambda e: e.tensor_copy(tki, tk), ["tk"], ["tki"])
                V(lambda e: e.tensor_tensor(xb[:], tk, tki, ALU.subtract), ["tk", "tki"], ["xb"])
                Aop(lambda e: e.activation(cosT[:], xb[:], AF.Sin, scale=TWO_PI), ["xb"], ["cosT"])
                for q in range(4):
                    Xr = ps[2 + q][:, 0:256]; Xi = ps[2 + q][:, 256:512]
                    def f_(e, q=q, Xr=Xr, Xi=Xi):
                        e.tensor_tensor(xa[:, q, :], cosT[:, q, :], Xr, ALU.mult)
                        return e.tensor_tensor(xb[:, q, :], sinT[:, q, :], Xi, ALU.mult)
                    V(f_, [PSK(2 + q), "cosT", "sinT", "xa", "xb"], [("xa", q), ("xb", q)])
                    V(lambda e, q=q: e.tensor_tensor(Xp[:, 0, q, :], xa[:, q, :], xb[:, q, :], ALU.add), [("xa", q), ("xb", q)], [("Xp", 0, q)])
                    def f_(e, q=q, Xr=Xr, Xi=Xi):
                        e.tensor_tensor(xa[:, q, :], cosT[:, q, :], Xi, ALU.mult)
                        return e.tensor_tensor(xb[:, q, :], sinT[:, q, :], Xr, ALU.mult)
                    V(f_, [PSK(2 + q), "cosT", "sinT", ("Xp", 0, q)], [("xa", q), ("xb", q)])
                    V(lambda e, q=q: e.tensor_tensor(Xp[:, 1, q, :], xa[:, q, :], xb[:, q, :], ALU.subtract), [("xa", q), ("xb", q)], [("Xp", 1, q)])
                    for ri in range(2):
                        V(lambda e, q=q, ri=ri: e.tensor_tensor_scan(Ssc[:, ri, q, :], mag[:, 8, q:q + 1].to_broadcast([128, 256]),
                                                                     Xp[:, ri, q, :], 0.0, ALU.mult, ALU.add),
                          [("Xp", ri, q), "mag"], [("Ssc", ri, q), "tk", "tki"])
                allS = [("Ssc", ri, q) for ri in range(2) for q in range(4)]
                allx = [("xa", q) for q in range(4)] + [("xb", q) for q in range(4)]
                def f_(e):
                    e.tensor_tensor(xa[:], cosT[:], Ssc[:, 0], ALU.mult)
                    return e.tensor_tensor(xb[:], sinT[:], Ssc[:, 1], ALU.mult)
                V(f_, allS + ["cosT", "sinT"], allx + ["xa", "xb"])
                V(lambda e: e.tensor_tensor(Hb[:, 0, :, 1:257], xa[:], xb[:], ALU.subtract), ["xa", "xb"], ["Hb0"])
                def f_(e):
                    e.tensor_tensor(xa[:], cosT[:], Ssc[:, 1], ALU.mult)
                    return e.tensor_tensor(xb[:], sinT[:], Ssc[:, 0], ALU.mult)
                V(f_, allS + ["cosT", "sinT", "Hb0"], allx + ["xa", "xb"])
                V(lambda e: e.tensor_tensor(Hb[:, 1, :, 1:257], xa[:], xb[:], ALU.add), ["xa", "xb"], ["Hb1"])
            def partC(j):
                uv = uT[:, j, :].rearrange("p (k s) -> p s k", s=8)
                for tp in range(7, -1, -1):
                    bi = 6 + (tp % 2)
                    def mmy(e, tp=tp, bi=bi, uv=uv):
                        for s_ in range(tp + 1):
                            e.matmul(ps[bi][:, 0:256], lhsT=Tt[:, tp - s_, :], rhs=uv[:, s_, :], start=(s_ == 0), stop=False)
                        for ri in range(2):
                            for q in range(4):
                                r = e.matmul(ps[bi][32 * q:32 * q + 32, 0:256], lhsT=Qd[:, tp + 1, ri, q, :], rhs=Hb[:, ri, q, 0:256], start=False,
                                             stop=(ri == 1), tile_position=(0, 32 * q))
                        return r
                    P.pe(mmy, reads=[("uT", j, s_) for s_ in range(tp + 1)] + [("Tt", 0), ("Tt", 1), "Qpad", "Hb0", "Hb1"], writes=[PSK(bi)])
                    Aop(lambda e, tp=tp, bi=bi, uv=uv: e.activation(uv[:, tp, :], ps[bi][:, 0:256], AF.Gelu_apprx_tanh),
                        [PSK(bi)], [("uT", j, tp)])
            partA(0)
            for j in range(8):
                partB(j)
                if j < 7:
                    partA(j + 1)
                partC(j)
        P.barrier()
        with SBT(nc, "wg", [128, 2, 2, 8, 512], BF16) as wg, SBT(nc, "sig", [128, 2, 512], F32) as sig, \
                SBT(nc, "mixb", [128, 2, 512], F32) as mixb:
            wglu = T["ssm_w_glu"][0].rearrange("(k p) n -> p k n", p=128)
            it = 0
            for mg in range(2):
                wb = mg % 2
                for vg in range(2):
                    c0 = vg * D + mg * 512
                    P.dma(lambda e, wb=wb, vg=vg, c0=c0: e.dma_start(out=wg[:, wb, vg, :, :], in_=wglu[:, :, c0:c0 + 512]),
                          writes=[("wg", wb, vg)], q="pool")
                for ml in range(4):
                    m = mg * 4 + ml
                    for tq in range(4):
                        pb = it % 2
                        it += 1
                        for vg in range(2):
                            bi = 2 + 2 * vg + pb
                            def mm(e, wb=wb, vg=vg, bi=bi, tq=tq, ml=ml):
                                for k in range(8):
                                    r = e.matmul(ps[bi][:], lhsT=wg[:, wb, vg, k, ml * 128:(ml + 1) * 128], rhs=uT[:, k, tq * 512:(tq + 1) * 512],
                                                 start=(k == 0), stop=(k == 7))
                                return r
                            P.pe(mm, reads=[("wg", wb, vg)], writes=[PSK(bi)])
                        Aop(lambda e, pb=pb: e.activation(sig[:, pb, :], ps[4 + pb][:], AF.Sigmoid), [PSK(4 + pb)], [("sig", pb)])
                        V(lambda e, pb=pb: e.tensor_tensor(mixb[:, pb, :], ps[2 + pb][:], sig[:, pb, :], ALU.mult), [PSK(2 + pb), ("sig", pb)], [("mixb", pb)])
                        V(lambda e, pb=pb, m=m, tq=tq: e.tensor_tensor(xT[:, m, tq * 512:(tq + 1) * 512], xT[:, m, tq * 512:(tq + 1) * 512],
                                                                     mixb[:, pb, :], ALU.add), [("mixb", pb), ("xT", m)], [("xT", m)])


_CACHE = {}


def kernel(**inputs):
    if "prog" not in _CACHE:
        _CACHE["prog"] = build_program()
    nc, _ = _CACHE["prog"]
    consts = make_consts()
    in_maps = []
    for c in range(8):
        m = {}
        for name, shape in INPUT_SPECS:
            if name == "consts":
                m[name] = consts
            elif name in ("x", "mem"):
                m[name] = np.ascontiguousarray(np.asarray(inputs[name], dtype=np.float32)[c * NB:(c + 1) * NB])
            else:
                m[name] = np.ascontiguousarray(np.asarray(inputs[name], dtype=np.float32))
        in_maps.append(m)
    res = run_bass_kernel_spmd(nc, in_maps, core_ids=list(range(8)))
    return np.concatenate([r["out"] for r in res.results], axis=0)
```

```python
import math
import numpy as np
import concourse.bass as bass
from concourse.ap import AP
import concourse.mybir as mybir
from concourse.bass_utils import run_bass_kernel_spmd

F32 = mybir.dt.float32
BF16 = mybir.dt.bfloat16
I32 = mybir.dt.int32
AF = mybir.ActivationFunctionType
ALU = mybir.AluOpType

COMPUTE = ("pe", "act", "dve", "pool")
ALLENG = ("pe", "act", "dve", "pool", "sp")
NDMA_SEMS = 40

S = 2048
D = 1024
NB = 2
DFF = 2816
NF = DFF // 128
MEM = 256


class Op:
    __slots__ = ("eng", "fn", "deps", "idx", "dma", "signal", "cnt", "sem", "clock")


class Prog:
    def __init__(self, nc):
        self.nc = nc
        self.ops = []
        self.last_write = {}
        self.readers = {}
        self.dma_hist = []
        self.n_dma = 0
        self.last_on = {}
        self.dma_since = []

    def add(self, eng, fn, reads=(), writes=(), dma=False, extra_deps=()):
        op = Op()
        op.eng, op.fn, op.dma = eng, fn, dma
        op.idx = len(self.ops)
        op.signal = False
        op.cnt = None
        op.sem = None
        op.clock = None
        deps = set(extra_deps)
        for k in reads:
            w = self.last_write.get(k)
            if w is not None:
                deps.add(w)
        for k in writes:
            w = self.last_write.get(k)
            if w is not None:
                deps.add(w)
            r = self.readers.get(k)
            if r:
                deps.update(r)
        for k in reads:
            self.readers.setdefault(k, []).append(op.idx)
        for k in writes:
            self.last_write[k] = op.idx
            self.readers[k] = []
        if dma:
            j = self.n_dma
            self.n_dma += 1
            op.sem = j % NDMA_SEMS
            if j >= NDMA_SEMS:
                deps.add(self.dma_hist[j - NDMA_SEMS])
            self.dma_hist.append(op.idx)
            self.dma_since.append(op.idx)
        else:
            self.last_on[eng] = op.idx
        deps.discard(op.idx)
        if eng == "pe" and not dma:
            deps = {d_ for d_ in deps if self.ops[d_].eng != "pe" or self.ops[d_].dma}
        op.deps = deps
        self.ops.append(op)
        return op

    def pe(self, fn, reads=(), writes=()):
        return self.add("pe", fn, reads, writes)

    def act(self, fn, reads=(), writes=()):
        return self.add("act", fn, reads, writes)

    def dve(self, fn, reads=(), writes=()):
        return self.add("dve", fn, reads, writes)

    def dma(self, fn, reads=(), writes=(), q="sp"):
        return self.add(q, fn, reads, writes, dma=True)

    def barrier(self):
        deps = set(self.last_on.values()) | set(self.dma_since)
        self.dma_since = []
        for e in ALLENG:
            self.add(e, lambda eng: eng.nop(), extra_deps=deps)
        self.last_write = {}
        self.readers = {}

    def emit(self, final_ops):
        nc = self.nc
        ops = self.ops
        for op in ops:
            for d in op.deps:
                ops[d].signal = True
        for op in final_ops:
            op.signal = True
        eng_cnt = {e: 0 for e in ALLENG}
        dma_cnt = [0] * NDMA_SEMS
        for op in ops:
            if op.dma:
                dma_cnt[op.sem] += 16
                op.cnt = dma_cnt[op.sem]
            elif op.signal:
                eng_cnt[op.eng] += 1
                op.cnt = eng_cnt[op.eng]
        sems = {e: nc.alloc_semaphore("s_" + e) for e in ALLENG}
        dsems = [nc.alloc_semaphore("d_%d" % i) for i in range(NDMA_SEMS)]

        def key_of(o):
            return ("d", o.sem) if o.dma else o.eng

        know = {e: {} for e in ALLENG}
        waits = {}
        for op in ops:
            K = know[op.eng]
            wl = []
            for d in sorted(op.deps, reverse=True):
                dop = ops[d]
                k = key_of(dop)
                if K.get(k, 0) >= dop.cnt:
                    continue
                for kk, vv in dop.clock.items():
                    if K.get(kk, 0) < vv:
                        K[kk] = vv
                K[k] = max(K.get(k, 0), dop.cnt)
                wl.append((dsems[dop.sem] if dop.dma else sems[dop.eng], dop.cnt))
            waits[op.idx] = wl
            if op.signal or op.dma:
                op.clock = dict(K)
        by_eng = {e: [] for e in ALLENG}
        for op in ops:
            by_eng[op.eng].append(op)
        fin = [(dsems[o.sem] if o.dma else sems[o.eng], o.cnt) for o in final_ops]
        self.n_inst = {e: len(by_eng[e]) for e in ALLENG}

        def run(engname, e):
            for op in by_eng[engname]:
                for (s, v) in waits[op.idx]:
                    e.wait_ge(s, v)
                ins = op.fn(e)
                if op.dma:
                    ins.then_inc(dsems[op.sem], 16)
                elif op.signal:
                    ins.then_inc(sems[op.eng], 1)
            if engname == "sp":
                for (s, v) in fin:
                    e.wait_ge(s, v)

        with nc.Block() as block:
            @block.tensor
            def _(e):
                run("pe", e)

            @block.scalar
            def _(e):
                run("act", e)

            @block.vector
            def _(e):
                run("dve", e)

            @block.gpsimd
            def _(e):
                run("pool", e)

            @block.sync
            def _(e):
                run("sp", e)


INPUT_SPECS = [
    ("x", [NB, S, D]), ("mem", [NB, MEM, D]),
    ("norm_mix", [2, D]), ("norm_xattn", [2, D]), ("norm_ffn", [2, D]), ("norm_mem", [D]), ("norm_final", [D]),
    ("ab_w_in", [1, D, 2048]), ("pool_w", [1, 4, 128, 128]), ("pool_scale", [1, 512]), ("ab_w_out", [1, D, D]),
    ("ssm_w_in", [1, D, D]), ("ssm_lam_re", [1, 64, 64]), ("ssm_lam_im", [1, 64, 64]), ("ssm_log_dt", [1, 64]),
    ("ssm_b_re", [1, 64, 64, 16]), ("ssm_b_im", [1, 64, 64, 16]), ("ssm_c_re", [1, 64, 16, 64]),
    ("ssm_c_im", [1, 64, 16, 64]), ("ssm_d", [1, D]), ("ssm_w_glu", [1, D, 2 * D]),
    ("xa_w_q", [2, D, D]), ("xa_w_kv", [2, D, 2 * D]), ("xa_w_o", [2, D, D]),
    ("ffn_w_up", [2, D, 2 * DFF]), ("ffn_conv_w", [2, 3, 2 * DFF]), ("ffn_conv_b", [2, 2 * DFF]),
    ("ffn_w_down", [2, DFF, D]),
    ("consts", [128, 12, 128]),
]


def make_consts():
    c = np.zeros((128, 12, 128), np.float32)
    j = np.arange(128)
    c[:, 0, :] = np.eye(128)
    c[:, 1, :] = -(j[:, None] > j[None, :]).astype(np.float32)
    c[:, 2, :] = -1.0
    c[:, 3, :] = 1.0
    c[:, 4, :] = (j[:, None] < j[None, :]).astype(np.float32)
    c[:, 5, :] = (j[:, None] // 32 == j[None, :] // 32).astype(np.float32)
    c[:, 6, :] = np.arange(128)[None, :]
    c[:, 7, :] = 128 + np.arange(128)[None, :]
    c[:, 8, :] = 1.0 / (1.0 + np.arange(128))[None, :]
    c[:, 9, :] = -(j[:, None] >= j[None, :]).astype(np.float32)
    c[:, 10, :] = -30000.0 * (j[:, None] >= j[None, :])
    return c


class Ctx:
    pass


_uid = [0]


def SBT(nc, name, shape, dt):
    _uid[0] += 1
    return nc.sbuf_tensor("%s_%d" % (name, _uid[0]), shape, dt)


def build_program(stop=None, nb=NB):
    nc = bass.Bass("TRN2", target_bir_lowering=False)
    P = Prog(nc)
    C = Ctx()
    C.nc, C.P = nc, P
    T = {}
    for name, shape in INPUT_SPECS:
        T[name] = nc.dram_tensor(name, shape, F32, kind="ExternalInput").ap()
    out = nc.dram_tensor("out", [NB, S, D], F32, kind="ExternalOutput").ap()
    C.T = T

    def sb(name, shape, dt=F32):
        return nc.alloc_sbuf_tensor(name, shape, dt)

    xT = sb("xT", [128, 8, S])
    cf = sb("cf", [128, 10, 128])
    cb = sb("cb", [128, 6, 128], BF16)
    gains = sb("gains", [128, 8, 8])
    convp = sb("convp", [128, 2, 4, 44])
    pscale = sb("pscale", [128, 4])
    zer = sb("zer", [128, 512], BF16)
    memT = sb("memT", [128, 8, MEM], BF16)
    ps = [nc.alloc_psum_tensor("ps%d" % i, [128, 512], F32) for i in range(8)]
    ident = cf[:, 0, :]
    maskstrict = cf[:, 4, :]

    def PSK(i):
        return ("ps", i)

    P.dma(lambda e: e.dma_start(out=cf[:], in_=T["consts"][:, 0:10, :]), writes=["cf"])
    P.dma(lambda e: e.dma_start(out=cb[:, 0:4, :], in_=T["consts"][:, 0:4, :]), writes=["cb"], q="pool")
    P.dma(lambda e: e.dma_start(out=cb[:, 4:6, :], in_=T["consts"][:, 9:11, :]), writes=["cb2"], q="pool")
    gsrc = [T["norm_mix"][0], T["norm_mix"][1], T["norm_xattn"][0], T["norm_xattn"][1],
            T["norm_ffn"][0], T["norm_ffn"][1], T["norm_mem"], T["norm_final"]]
    for i, g in enumerate(gsrc):
        P.dma(lambda e, i=i, g=g: e.dma_start(out=gains[:, i, :], in_=g.rearrange("(t p) -> p t", p=128),
                                             allow_slow_non_contiguous=True), writes=["gains"], q="act")
    for l in range(2):
        for i in range(3):
            P.dma(lambda e, l=l, i=i: e.dma_start(out=convp[:, l, i, :],
                                                  in_=T["ffn_conv_w"][l, i].rearrange("(t p) -> p t", p=128),
                                                  allow_slow_non_contiguous=True), writes=["convp"], q="act")
        P.dma(lambda e, l=l: e.dma_start(out=convp[:, l, 3, :],
                                         in_=T["ffn_conv_b"][l].rearrange("(t p) -> p t", p=128),
                                         allow_slow_non_contiguous=True), writes=["convp"], q="act")
    P.dma(lambda e: e.dma_start(out=pscale[:], in_=T["pool_scale"][0].rearrange("(t p) -> p t", p=128),
                                allow_slow_non_contiguous=True), writes=["pscale"], q="act")
    P.dve(lambda e: e.memset(zer[:], 0.0), writes=["zer"])

    def load_w(dst, src2d, key, k_tiles, col0, ncols):
        v = src2d.rearrange("(k p) n -> p k n", p=128)
        for k in range(k_tiles):
            P.dma(lambda e, k=k: e.dma_start(out=dst[:, k, :], in_=v[:, k, col0:col0 + ncols]),
                  writes=[(key, k)], q="pool")

    def rmsnorm_tile(hT, hkey, gi, t0, n, sq, rstd, part="all"):
        if part in ("all", "sq"):
            for dt in range(8):
                P.act(lambda e, dt=dt: e.activation(sq[:, dt, 0:n], xT[:, dt, t0:t0 + n], AF.Square),
                      reads=[("xT", dt)], writes=[("sq", dt)])
        if part == "sq":
            return
        def mm(e):
            for dt in range(8):
                r = e.matmul(ps[7][:, 0:n], lhsT=cb[:, 3, :], rhs=sq[:, dt, 0:n], start=(dt == 0), stop=(dt == 7))
            return r
        P.pe(mm, reads=[("sq", dt) for dt in range(8)] + ["cb"], writes=[PSK(7)])
        P.dve(lambda e: e.tensor_scalar(rstd[:, 0:n], ps[7][:, 0:n], 1.0 / D, 1e-6, ALU.mult, ALU.add),
              reads=[PSK(7)], writes=["rstd"])
        P.act(lambda e: e.activation(rstd[:, 0:n], rstd[:, 0:n], AF.Ln), reads=["rstd"], writes=["rstd"])
        P.act(lambda e: e.activation(rstd[:, 0:n], rstd[:, 0:n], AF.Exp, scale=-0.5), reads=["rstd"], writes=["rstd"])
        for dt in range(8):
            P.dve(lambda e, dt=dt: e.scalar_tensor_tensor(hT[:, dt, 0:n], xT[:, dt, t0:t0 + n], gains[:, gi, dt:dt + 1],
                                                          rstd[:, 0:n], ALU.mult, ALU.mult),
                  reads=[("xT", dt), "rstd", "gains"], writes=[(hkey, dt)])

    C.ps_rr = 0

    def linear_fm(w, wkey, act, akey, k_tiles, m_tiles, n, evac, a0=0, banks=(0, 1)):
        for m in range(m_tiles):
            bi = banks[C.ps_rr % len(banks)]
            C.ps_rr += 1
            def mm(e, m=m, bi=bi):
                for k in range(k_tiles):
                    r = e.matmul(ps[bi][:, 0:n], lhsT=w[:, k, m * 128:(m + 1) * 128], rhs=act[:, k, a0:a0 + n],
                                 start=(k == 0), stop=(k == k_tiles - 1))
                return r
            P.pe(mm, reads=[(wkey, k) for k in range(k_tiles)] + [(akey, k) for k in range(k_tiles)], writes=[PSK(bi)])
            evac(m, bi, ps[bi][:, 0:n])

    def resid_add(m, bi, pap, t0, n):
        P.dve(lambda e: e.tensor_tensor(xT[:, m, t0:t0 + n], xT[:, m, t0:t0 + n], pap, ALU.add),
              reads=[PSK(bi), ("xT", m)], writes=[("xT", m)])

    final_ops = []
    for b in range(nb):
        P.barrier()
        with SBT(nc, "xin", [128, 2, D], F32) as xin, SBT(nc, "sq", [128, 8, 512], BF16) as sq, \
                SBT(nc, "rstd", [128, 512], F32) as rstd, SBT(nc, "mn", [128, 2, D], F32) as mn, \
                SBT(nc, "ssq", [128, 4], F32) as ssq:
            for tt in range(16):
                xb = tt % 2
                P.dma(lambda e, tt=tt, xb=xb, b=b: e.dma_start(out=xin[:, xb, :], in_=T["x"][b, tt * 128:(tt + 1) * 128, :]),
                      writes=[("xin", xb)])
                for half in range(2):
                    bi = (tt * 2 + half) % 2
                    def tr(e, xb=xb, half=half, bi=bi):
                        for q in range(4):
                            dt = half * 4 + q
                            r = e.transpose(ps[bi][:, q * 128:(q + 1) * 128], xin[:, xb, dt * 128:(dt + 1) * 128], ident)
                        return r
                    P.pe(tr, reads=[("xin", xb), "cf"], writes=[PSK(bi)])
                    P.act(lambda e, tt=tt, half=half, bi=bi: e.activation(
                        xT[:, half * 4:half * 4 + 4, tt * 128:(tt + 1) * 128],
                        ps[bi][:].rearrange("p (q t) -> p q t", q=4), AF.Copy),
                        reads=[PSK(bi)], writes=[("xT", half * 4 + q) for q in range(4)])
            P.dma(lambda e, b=b: e.dma_start(out=mn[:], in_=T["mem"][b].rearrange("(t p) d -> p t d", p=128)), writes=["mn"])
            for t in range(2):
                P.act(lambda e, t=t: e.activation(xin[:, t, :], mn[:, t, :], AF.Square, accum_out=ssq[:, t:t + 1]),
                      reads=["mn"], writes=[("ssq", t), ("xin", t)])
            P.dve(lambda e: e.tensor_scalar(ssq[:, 2:4], ssq[:, 0:2], 1.0 / D, 1e-6, ALU.mult, ALU.add),
                  reads=[("ssq", 0), ("ssq", 1)], writes=["ssq2"])
            P.act(lambda e: e.activation(ssq[:, 2:4], ssq[:, 2:4], AF.Sqrt), reads=["ssq2"], writes=["ssq2"])
            P.dve(lambda e: e.reciprocal(ssq[:, 2:4], ssq[:, 2:4]), reads=["ssq2"], writes=["ssq2"])
            for t in range(2):
                P.dve(lambda e, t=t: e.tensor_scalar(mn[:, t, :], mn[:, t, :], ssq[:, 2 + t:3 + t], None, ALU.mult),
                      reads=["mn", "ssq2"], writes=["mn"])
            for t in range(2):
                for half in range(2):
                    bi = (t * 2 + half) % 2
                    def tr(e, t=t, half=half, bi=bi):
                        for q in range(4):
                            dt = half * 4 + q
                            r = e.transpose(ps[bi][:, q * 128:(q + 1) * 128], mn[:, t, dt * 128:(dt + 1) * 128], ident)
                        return r
                    P.pe(tr, reads=["mn", "cf"], writes=[PSK(bi)])
                    for q in range(4):
                        dt = half * 4 + q
                        P.dve(lambda e, t=t, q=q, dt=dt, bi=bi: e.tensor_scalar(
                            memT[:, dt, t * 128:(t + 1) * 128], ps[bi][:, q * 128:(q + 1) * 128],
                            gains[:, 6, dt:dt + 1], None, ALU.mult),
                            reads=[PSK(bi), "gains"], writes=[("memT", dt)])
        if stop == "load":
            pass
        else:
            for layer in range(2):
                if layer == 0:
                    stage_mix_ab(C, b, xT, ps, cf, cb, gains, pscale, zer, rmsnorm_tile, load_w, linear_fm, resid_add)
                else:
                    stage_mix_s5(C, b, xT, ps, cf, cb, gains, rmsnorm_tile, load_w, linear_fm, resid_add)
                if stop == "mix%d" % layer:
                    break
                stage_xattn(C, b, layer, xT, ps, cb, gains, memT, rmsnorm_tile, load_w, linear_fm, resid_add)
                if stop == "xa%d" % layer:
                    break
                stage_ffn(C, b, layer, xT, ps, gains, convp, rmsnorm_tile, load_w, linear_fm, resid_add)
                if stop == "ffn%d" % layer:
                    break
        P.barrier()
        with SBT(nc, "sq", [128, 8, 512], BF16) as sq, SBT(nc, "rstd", [128, 512], F32) as rstd, \
                SBT(nc, "yT", [128, 8, 512], F32) as yT, SBT(nc, "yo", [128, 2, D], F32) as yo:
            for tq in range(4):
                t0 = tq * 512
                if stop is None:
                    rmsnorm_tile(yT, "yT", 7, t0, 512, sq, rstd)
                else:
                    for dt in range(8):
                        P.act(lambda e, dt=dt, t0=t0: e.activation(yT[:, dt, :], xT[:, dt, t0:t0 + 512], AF.Copy),
                              reads=[("xT", dt)], writes=[("yT", dt)])
                for ts in range(4):
                    ob = ts % 2
                    for half in range(2):
                        bi = (ts * 2 + half) % 2
                        def tr(e, ts=ts, half=half, bi=bi):
                            for q in range(4):
                                dt = half * 4 + q
                                r = e.transpose(ps[bi][:, q * 128:(q + 1) * 128], yT[:, dt, ts * 128:(ts + 1) * 128], ident)
                            return r
                        P.pe(tr, reads=[("yT", dt) for dt in range(8)] + ["cf"], writes=[PSK(bi)])
                        P.act(lambda e, ob=ob, half=half, bi=bi: e.activation(yo[:, ob, half * 512:(half + 1) * 512],
                                                                               ps[bi][:], AF.Copy),
                              reads=[PSK(bi)], writes=[("yo", ob, half)])
                    tok = t0 + ts * 128
                    o = P.dma(lambda e, ob=ob, tok=tok, b=b: e.dma_start(out=out[b, tok:tok + 128, :], in_=yo[:, ob, :]),
                              reads=[("yo", ob, 0), ("yo", ob, 1)], writes=[("out", b, tok)])
                    final_ops.append(o)
    P.emit(final_ops)
    C.final = final_ops
    return nc, P


def stage_xattn(C, b, layer, xT, ps, cb, gains, memT, rmsnorm_tile, load_w, linear_fm, resid_add):
    nc, P, T = C.nc, C.P, C.T
    P.barrier()

    def PSK(i):
        return ("ps", i)
    with SBT(nc, "wq", [128, 8, D], BF16) as wq, SBT(nc, "wo", [128, 8, D], BF16) as wo, \
            SBT(nc, "wkv", [128, 8, D], BF16) as wkv, \
            SBT(nc, "KT", [128, 8, MEM], BF16) as KT, SBT(nc, "V", [128, 2, D], BF16) as V, \
            SBT(nc, "sq", [128, 8, 512], BF16) as sq, SBT(nc, "rstd", [128, 512], F32) as rstd, \
            SBT(nc, "hT", [128, 2, 8, 512], BF16) as hT, SBT(nc, "qT", [128, 8, 512], BF16) as qT, \
            SBT(nc, "pT", [128, 2, 2, 512], BF16) as pT, SBT(nc, "rs", [128, 2, 512], F32) as rs, \
            SBT(nc, "oT", [128, 8, 512], BF16) as oT:
        load_w(wkv, T["xa_w_kv"][layer], "wkv", 8, 0, D)
        load_w(wq, T["xa_w_q"][layer], "wq", 8, 0, D)

        def evK(m, bi, pap):
            P.act(lambda e: e.activation(KT[:, m, :], pap, AF.Copy), reads=[PSK(bi)], writes=[("KT", m)])
        linear_fm(wkv, "wkv", memT, "memT", 8, 8, MEM, evK)
        load_w(wkv, T["xa_w_kv"][layer], "wkv", 8, D, D)
        load_w(wo, T["xa_w_o"][layer], "wo", 8, 0, D)
        for mt in range(2):
            for nh in range(2):
                bi = (mt * 2 + nh) % 2
                def mm(e, mt=mt, nh=nh, bi=bi):
                    for k in range(8):
                        r = e.matmul(ps[bi][:], lhsT=memT[:, k, mt * 128:(mt + 1) * 128], rhs=wkv[:, k, nh * 512:(nh + 1) * 512],
                                     start=(k == 0), stop=(k == 7))
                    return r
                P.pe(mm, reads=[("wkv", k) for k in range(8)] + [("memT", k) for k in range(8)], writes=[PSK(bi)])
                P.act(lambda e, mt=mt, nh=nh, bi=bi: e.activation(V[:, mt, nh * 512:(nh + 1) * 512], ps[bi][:], AF.Copy),
                      reads=[PSK(bi)], writes=[("V", mt, nh)])
        rmsnorm_tile(hT[:, 0], "hT0", 2 + layer, 0, 512, sq, rstd)
        for tq in range(4):
            t0 = tq * 512
            tb = tq % 2

            def evQ(m, bi, pap):
                P.act(lambda e: e.activation(qT[:, m, :], pap, AF.Copy, scale=1.0 / 16.0), reads=[PSK(bi)], writes=[("qT", m)])
            linear_fm(wq, "wq", hT[:, tb], "hT%d" % tb, 8, 8, 512, evQ)
            def scores(h):
                hb = h % 2
                for mt in range(2):
                    bk = 2 + 2 * hb + mt
                    def mm(e, h=h, mt=mt, bk=bk):
                        for d in range(2):
                            r = e.matmul(ps[bk][:], lhsT=KT[:, 2 * h + d, mt * 128:(mt + 1) * 128], rhs=qT[:, 2 * h + d, :],
                                         start=(d == 0), stop=(d == 1))
                        return r
                    P.pe(mm, reads=[("KT", 2 * h), ("KT", 2 * h + 1), ("qT", 2 * h), ("qT", 2 * h + 1)], writes=[PSK(bk)])
                    P.act(lambda e, mt=mt, hb=hb, bk=bk: e.activation(pT[:, hb, mt, :], ps[bk][:], AF.Exp),
                          reads=[PSK(bk)], writes=[("pT", hb, mt)])

            def rest(h):
                hb = h % 2
                def mms(e, hb=hb):
                    e.matmul(ps[6][:], lhsT=cb[:, 3, :], rhs=pT[:, hb, 0, :], start=True, stop=False)
                    return e.matmul(ps[6][:], lhsT=cb[:, 3, :], rhs=pT[:, hb, 1, :], start=False, stop=True)
                P.pe(mms, reads=[("pT", hb, 0), ("pT", hb, 1), "cb"], writes=[PSK(6)])
                P.act(lambda e, hb=hb: e.activation(rs[:, hb, :], ps[6][:], AF.Ln), reads=[PSK(6)], writes=[("rs", hb)])
                P.act(lambda e, hb=hb: e.activation(rs[:, hb, :], rs[:, hb, :], AF.Exp, scale=-1.0), reads=[("rs", hb)], writes=[("rs", hb)])
                for d in range(2):
                    bi = 7 if d == 0 else 1
                    def mmo(e, h=h, hb=hb, d=d, bi=bi):
                        for mt in range(2):
                            r = e.matmul(ps[bi][:], lhsT=V[:, mt, h * 256 + d * 128:h * 256 + (d + 1) * 128], rhs=pT[:, hb, mt, :],
                                         start=(mt == 0), stop=(mt == 1))
                        return r
                    P.pe(mmo, reads=[("V", 0, h // 2), ("V", 1, h // 2), ("pT", hb, 0), ("pT", hb, 1)], writes=[PSK(bi)])
                    P.dve(lambda e, h=h, hb=hb, d=d, bi=bi: e.tensor_tensor(oT[:, 2 * h + d, :], ps[bi][:], rs[:, hb, :], ALU.mult),
                          reads=[PSK(bi), ("rs", hb)], writes=[("oT", 2 * h + d)])

            scores(0)
            for h in range(4):
                if h < 3:
                    scores(h + 1)
                rest(h)
            if tq + 1 < 4:
                rmsnorm_tile(hT[:, 1 - tb], "hT%d" % (1 - tb), 2 + layer, t0 + 512, 512, sq, rstd, part="sq")
            linear_fm(wo, "wo", oT, "oT", 8, 8, 512, lambda m, bi, pap, t0=t0: resid_add(m, bi, pap, t0, 512))
            if tq + 1 < 4:
                rmsnorm_tile(hT[:, 1 - tb], "hT%d" % (1 - tb), 2 + layer, t0 + 512, 512, sq, rstd, part="rest")


def stage_ffn(C, b, layer, xT, ps, gains, convp, rmsnorm_tile, load_w, linear_fm, resid_add):
    nc, P, T = C.nc, C.P, C.T
    P.barrier()

    def PSK(i):
        return ("ps", i)
    with SBT(nc, "sq", [128, 8, 512], BF16) as sq, SBT(nc, "rstd", [128, 512], F32) as rstd, \
            SBT(nc, "hT", [128, 2, 8, 512], BF16) as hT, SBT(nc, "gT", [128, NF, 512], BF16) as gT, \
            SBT(nc, "wu", [128, 2, 2, 8, 512], BF16) as wu, SBT(nc, "wd", [128, 4, D], BF16) as wd, \
            SBT(nc, "ub", [128, 3, 2, 516], F32) as ub, SBT(nc, "cv", [128, 3, 2, 512], F32) as cv, \
            SBT(nc, "halo", [128, 2 * NF, 2], F32) as halo:
        wup = T["ffn_w_up"][layer].rearrange("(k p) n -> p k n", p=128)
        wdn = T["ffn_w_down"][layer]
        P.dve(lambda e: e.memset(halo[:], 0.0), writes=["halo"])
        groups = [(0, 4), (4, 4), (8, 4), (12, 4), (16, 4), (20, 2)]
        it = 0
        git = 0
        kit = 0
        rmsnorm_tile(hT[:, 0], "hT0", 4 + layer, 0, 512, sq, rstd)
        pend = []

        def tail(pb, fp):
            P.act(lambda e: e.activation(cv[:, pb, 1, :], cv[:, pb, 1, :], AF.Silu),
                  reads=[("cv", pb, 1)], writes=[("cv", pb, 1)])
            P.dve(lambda e: e.tensor_tensor(gT[:, fp, :], cv[:, pb, 0, :], cv[:, pb, 1, :], ALU.mult),
                  reads=[("cv", pb, 0), ("cv", pb, 1)], writes=[("gT", fp)])
        for tq in range(4):
            t0 = tq * 512
            hb = tq % 2
            for (f0, nf) in groups:
                wb = git % 2
                git += 1
                for vg in range(2):
                    col0 = vg * DFF + f0 * 128
                    P.dma(lambda e, wb=wb, vg=vg, col0=col0, nf=nf: e.dma_start(out=wu[:, wb, vg, :, 0:nf * 128],
                                                                              in_=wup[:, :, col0:col0 + nf * 128]),
                          writes=[("wu", wb, vg)], q="pool")
                for fl in range(nf):
                    fp = f0 + fl
                    pb = it % 3
                    it += 1
                    for vg in range(2):
                        bi = 3 * vg + pb
                        f = vg * NF + fp
                        def mm(e, wb=wb, vg=vg, bi=bi, fl=fl, hb=hb):
                            for k in range(8):
                                r = e.matmul(ps[bi][:], lhsT=wu[:, wb, vg, k, fl * 128:(fl + 1) * 128], rhs=hT[:, hb, k, :], start=(k == 0), stop=(k == 7))
                            return r
                        P.pe(mm, reads=[("wu", wb, vg)] + [("hT%d" % hb, k) for k in range(8)], writes=[PSK(bi)])
                        P.act(lambda e, pb=pb, vg=vg, bi=bi: e.activation(ub[:, pb, vg, 2:514], ps[bi][:], AF.Copy),
                              reads=[PSK(bi)], writes=[("ub", pb, vg)])
                        P.act(lambda e, pb=pb, vg=vg, f=f: e.activation(ub[:, pb, vg, 0:2], halo[:, f, :], AF.Copy),
                              reads=["halo%d" % f, "halo"], writes=[("ubh", pb, vg)])
                        P.act(lambda e, pb=pb, vg=vg, f=f, bi=bi: e.activation(cv[:, pb, vg, :], ps[bi][:], AF.Identity,
                                                                               bias=convp[:, layer, 3, f:f + 1],
                                                                               scale=convp[:, layer, 2, f:f + 1]),
                              reads=[PSK(bi), "convp"], writes=[("cv", pb, vg)])
                        P.dve(lambda e, pb=pb, vg=vg, f=f: e.scalar_tensor_tensor(cv[:, pb, vg, :], ub[:, pb, vg, 1:513],
                                                                                  convp[:, layer, 1, f:f + 1], cv[:, pb, vg, :],
                                                                                  ALU.mult, ALU.add),
                              reads=[("ub", pb, vg), ("ubh", pb, vg), ("cv", pb, vg), "convp"], writes=[("cv", pb, vg)])
                        P.dve(lambda e, pb=pb, vg=vg, f=f: e.scalar_tensor_tensor(cv[:, pb, vg, :], ub[:, pb, vg, 0:512],
                                                                                  convp[:, layer, 0, f:f + 1], cv[:, pb, vg, :],
                                                                                  ALU.mult, ALU.add),
                              reads=[("ub", pb, vg), ("ubh", pb, vg), ("cv", pb, vg), "convp"], writes=[("cv", pb, vg)])
                        P.dve(lambda e, pb=pb, vg=vg, f=f: e.tensor_copy(halo[:, f, :], ub[:, pb, vg, 512:514]),
                              reads=[("ub", pb, vg)], writes=["halo%d" % f])
                    if pend:
                        tail(*pend.pop())
                    pend.append((pb, fp))
            if pend:
                tail(*pend.pop())
            if tq + 1 < 4:
                rmsnorm_tile(hT[:, 1 - hb], "hT%d" % (1 - hb), 4 + layer, t0 + 512, 512, sq, rstd, part="sq")
            for k in range(NF):
                db = kit % 4
                kit += 1
                P.dma(lambda e, db=db, k=k: e.dma_start(out=wd[:, db, :], in_=wdn[k * 128:(k + 1) * 128, :]),
                      writes=[("wd", db)], q="pool")
                def mm(e, db=db, k=k):
                    for m in range(8):
                        r = e.matmul(ps[m][:], lhsT=wd[:, db, m * 128:(m + 1) * 128], rhs=gT[:, k, :], start=(k == 0), stop=(k == NF - 1))
                    return r
                P.pe(mm, reads=[("wd", db), ("gT", k)], writes=[PSK(m) for m in range(8)])
            resid_add(7, 7, ps[7][:], t0, 512)
            if tq + 1 < 4:
                rmsnorm_tile(hT[:, 1 - hb], "hT%d" % (1 - hb), 4 + layer, t0 + 512, 512, sq, rstd, part="rest")
            for m in range(7):
                resid_add(m, m, ps[m][:], t0, 512)


def stage_mix_ab(C, b, xT, ps, cf, cb, gains, pscale, zer, rmsnorm_tile, load_w, linear_fm, resid_add):
    nc, P, T = C.nc, C.P, C.T
    P.barrier()
    maskstrict = cf[:, 4, :]

    def PSK(i):
        return ("ps", i)
    win = T["ab_w_in"][0]
    with SBT(nc, "hT", [128, 8, S], BF16) as hT, SBT(nc, "aT", [128, 4, S], BF16) as aT, \
            SBT(nc, "pTo", [128, 4, S], BF16) as pTo:
        with SBT(nc, "sq", [128, 8, 512], BF16) as sq, SBT(nc, "rstd", [128, 512], F32) as rstd, \
                SBT(nc, "hTt", [128, 8, 512], BF16) as hTt:
            for tq in range(4):
                rmsnorm_tile(hTt, "hTt", 0, tq * 512, 512, sq, rstd)
                for dt in range(8):
                    P.act(lambda e, dt=dt, tq=tq: e.activation(hT[:, dt, tq * 512:(tq + 1) * 512], hTt[:, dt, :], AF.Copy),
                          reads=[("hTt", dt)], writes=[("hT", dt)])
        P.barrier()
        with SBT(nc, "wu4", [128, 8, 512], BF16) as wu4, SBT(nc, "wp", [128, 4, 128], BF16) as wp, \
                SBT(nc, "uA", [128, S], F32) as uA, SBT(nc, "uB", [128, S], F32) as uB, \
                SBT(nc, "u02", [128, 2, S], F32) as u02, SBT(nc, "pb", [128, S], BF16) as pb:
            def uproj(g):
                ub_ = g % 2
                for tq in range(4):
                    bi = tq % 2
                    def mm(e, tq=tq, bi=bi, g=g):
                        for k in range(8):
                            r = e.matmul(ps[bi][:], lhsT=wu4[:, k, g * 128:(g + 1) * 128], rhs=hT[:, k, tq * 512:(tq + 1) * 512], start=(k == 0), stop=(k == 7))
                        return r
                    P.pe(mm, reads=[("wu4", k) for k in range(8)] + [("hT", k) for k in range(8)], writes=[PSK(bi)])
                    P.act(lambda e, tq=tq, bi=bi, ub_=ub_: e.activation(u02[:, ub_, tq * 512:(tq + 1) * 512], ps[bi][:], AF.Copy),
                          reads=[PSK(bi)], writes=[("u0", ub_)])
            load_w(wu4, win, "wu4", 8, 1536, 512)
            uproj(0)
            for g in range(4):
                w_ = 2 ** (g + 1)
                ub_ = g % 2
                u0 = u02[:, ub_]
                u0k = ("u0", ub_)
                P.dma(lambda e, g=g: e.dma_start(out=wp[:, g, :], in_=T["pool_w"][0, g]), writes=[("wp", g)], q="pool")
                if g < 3:
                    uproj(g + 1)
                src, srck = u0, u0k
                bufs = [(uA, "uA"), (uB, "uB")]
                for st in range(g + 1):
                    sh = 2 ** st
                    dst, dstk = bufs[st % 2]
                    def stp(e, src=src, dst=dst, sh=sh):
                        e.tensor_copy(dst[:, 0:sh], src[:, 0:sh])
                        return e.tensor_tensor(dst[:, sh:S], src[:, sh:S], src[:, 0:S - sh], ALU.add)
                    P.dve(stp, reads=[srck], writes=[dstk])
                    src, srck = dst, dstk
                def pl(e, src=src, w_=w_, u0=u0):
                    e.scalar_tensor_tensor(pb[:, w_ - 1:S], src[:, w_ - 1:S], 1.0 / w_, u0[:, w_ - 1:S], ALU.mult, ALU.subtract)
                    return e.tensor_tensor(src[:, 0:w_ - 1], src[:, 0:w_ - 1], cf[:, 8, 0:w_ - 1], ALU.mult)
                P.dve(pl, reads=[srck, u0k, "cf"], writes=["pb0", srck])
                P.dve(lambda e, src=src, w_=w_, u0=u0: e.tensor_tensor(pb[:, 0:w_ - 1], src[:, 0:w_ - 1], u0[:, 0:w_ - 1], ALU.subtract),
                      reads=[srck, u0k], writes=["pb1"])
                for tq in range(4):
                    bi = 2 + tq % 2
                    P.pe(lambda e, tq=tq, bi=bi, g=g: e.matmul(ps[bi][:], lhsT=wp[:, g, :], rhs=pb[:, tq * 512:(tq + 1) * 512], start=True, stop=True),
                         reads=[("wp", g), "pb0", "pb1"], writes=[PSK(bi)])
                    P.act(lambda e, tq=tq, bi=bi, g=g: e.activation(pTo[:, g, tq * 512:(tq + 1) * 512], ps[bi][:], AF.Identity,
                                                                    scale=pscale[:, g:g + 1]),
                          reads=[PSK(bi), "pscale"], writes=[("pTo", g)])
        P.barrier()
        NBUF = 4
        with SBT(nc, "wqkv", [128, 8, 1536], BF16) as wqkv, SBT(nc, "qh", [64, S], BF16) as qh, \
                SBT(nc, "kh", [64, S], BF16) as kh, SBT(nc, "vh", [128, 16, 128], BF16) as vh, \
                SBT(nc, "ex", [128, NBUF, 512], F32) as ex, \
                SBT(nc, "spb", [128, NBUF, 512], BF16) as spb, \
                SBT(nc, "wsb", [128, NBUF, 512], BF16) as wsb, SBT(nc, "Ls", [128, 2, 512], F32) as Ls, \
                SBT(nc, "Lsb", [128, 4, 512], BF16) as Lsb, SBT(nc, "otmp", [64, 2, 512], BF16) as otmp:
            identb = cb[:, 0, :]
            trinc = cb[:, 4, :]
            maskneg = cb[:, 5, :]
            onesneg = cb[:, 2, :]
            git = 0
            for h in range(8):
                hp = h // 2
                if h == 0:
                    load_w(wqkv, win, "wqkv", 8, 0, 1536)
                for j3, (dst, dk, scl) in enumerate([(qh, "qh", 0.125), (kh, "kh", 1.0)]):
                    for tq in range(4):
                        bi = 4 + tq % 2
                        def mm(e, h=h, j3=j3, tq=tq, bi=bi):
                            for k in range(8):
                                r = e.matmul(ps[bi][0:64, :], lhsT=wqkv[:, k, j3 * 512 + h * 64:j3 * 512 + h * 64 + 64],
                                             rhs=hT[:, k, tq * 512:(tq + 1) * 512], start=(k == 0), stop=(k == 7))
                            return r
                        P.pe(mm, reads=[("wqkv", k) for k in range(8)] + [("hT", k) for k in range(8)], writes=[PSK(bi)])
                        P.act(lambda e, dst=dst, tq=tq, bi=bi, scl=scl: e.activation(dst[:, tq * 512:(tq + 1) * 512], ps[bi][0:64, :],
                                                                                   AF.Copy, scale=scl),
                              reads=[PSK(bi)], writes=[(dk, tq)])
                if h % 2 == 0:
                    for t4 in range(4):
                        bi = 4 + t4 % 2
                        def mmv(e, h=h, t4=t4, bi=bi):
                            for tl in range(4):
                                tt = t4 * 4 + tl
                                for k in range(8):
                                    r = e.matmul(ps[bi][:, tl * 128:(tl + 1) * 128], lhsT=hT[:, k, tt * 128:(tt + 1) * 128],
                                                 rhs=wqkv[:, k, 1024 + h * 64:1024 + h * 64 + 128], start=(k == 0), stop=(k == 7))
                            return r
                        P.pe(mmv, reads=[("wqkv", k) for k in range(8)] + [("hT", k) for k in range(8)], writes=[PSK(bi)])
                        P.dve(lambda e, t4=t4, bi=bi: e.tensor_copy(vh[:, t4 * 4:(t4 + 1) * 4, :],
                                                                    ps[bi][:].rearrange("p (t c) -> p t c", t=4)),
                              reads=[PSK(bi)], writes=[("vh", t4)])
                its = []
                for j in range(4):
                    kbs = list(range(4 * j + 3, -1, -1))
                    for ii, kb in enumerate(kbs):
                        diag = kb >= 4 * j
                        qlo = 128 * (kb - 4 * j) if diag else 0
                        its.append(dict(j=j, kb=kb, first=(ii == 0), last=(kb == 0), diag=diag, qlo=qlo, g=git))
                        git += 1

                def zmm(e, dst, it_, stop_after):
                    kb, qlo, j = it_["kb"], it_["qlo"], it_["j"]
                    q0 = 512 * j + qlo
                    r = e.matmul(dst[:, qlo:512], lhsT=kh[:, kb * 128:(kb + 1) * 128], rhs=qh[:, q0:512 * (j + 1)],
                                 start=True, stop=(stop_after and not it_["diag"]))
                    if it_["diag"]:
                        r = e.matmul(dst[:, qlo:qlo + 128], lhsT=identb, rhs=maskneg, start=False, stop=stop_after)
                    return r

                def stageA(it_):
                    r_ = it_["g"] % NBUF
                    j, kb, qlo = it_["j"], it_["kb"], it_["qlo"]
                    lb = j % 2
                    if it_["first"]:
                        P.dve(lambda e, lb=lb: e.memset(Ls[:, lb, :], 0.0), writes=[("Ls", lb)])
                    P.pe(lambda e, it_=it_, r_=r_: zmm(e, ps[r_], it_, True),
                         reads=[("kh", kb // 4), ("qh", j), "cb"], writes=[PSK(r_)])
                    P.act(lambda e, r_=r_, qlo=qlo: e.activation(ex[:, r_, qlo:512], ps[r_][:, qlo:512], AF.Exp),
                          reads=[PSK(r_)], writes=[("ex", r_)])
                    P.act(lambda e, r_=r_, qlo=qlo: e.activation(spb[:, r_, qlo:512], ex[:, r_, qlo:512], AF.Ln, bias=1.0),
                          reads=[("ex", r_)], writes=[("spb", r_)])
                    if not it_["last"]:
                        nqlo = max(0, 128 * (kb - 1 - 4 * j))
                        nr = (it_["g"] + 1) % 4
                        P.dve(lambda e, r_=r_, qlo=qlo, lb=lb: e.tensor_tensor(Ls[:, lb, qlo:512], Ls[:, lb, qlo:512], spb[:, r_, qlo:512], ALU.add),
                              reads=[("Ls", lb), ("spb", r_)], writes=[("Ls", lb)])
                        P.dve(lambda e, nqlo=nqlo, nr=nr, lb=lb: e.tensor_copy(Lsb[:, nr, nqlo:512], Ls[:, lb, nqlo:512]),
                              reads=[("Ls", lb)], writes=[("Lsb", nr)])

                def stageB1(it_):
                    r_ = it_["g"] % NBUF
                    qlo = it_["qlo"]
                    pst = ps[r_]
                    def mmt(e, it_=it_, r_=r_, pst=pst, qlo=qlo):
                        r = e.matmul(pst[:, qlo:512], lhsT=trinc, rhs=spb[:, r_, qlo:512], start=False, stop=it_["first"])
                        if not it_["first"]:
                            r = e.matmul(pst[:, qlo:512], lhsT=onesneg, rhs=Lsb[:, it_["g"] % 4, qlo:512], start=False, stop=True)
                        return r
                    P.pe(mmt, reads=["cb", ("spb", r_), ("Lsb", it_["g"] % 4)], writes=[PSK(r_)])
                    P.act(lambda e, r_=r_, pst=pst, qlo=qlo: e.activation(wsb[:, r_, qlo:512], pst[:, qlo:512], AF.Exp),
                          reads=[PSK(r_)], writes=[("wsb", r_)])

                def stageB2(it_):
                    r_ = it_["g"] % NBUF
                    j, kb, qlo = it_["j"], it_["kb"], it_["qlo"]
                    ob = j % 2
                    pso = ps[6 + ob]
                    if it_["first"]:
                        P.pe(lambda e, pso=pso: e.matmul(pso[:, :], lhsT=zer[0:1, 0:128], rhs=zer[0:1, 0:512], start=True, stop=False),
                             reads=["zer"], writes=[PSK(6 + ob)])
                    P.pe(lambda e, pso=pso, kb=kb, r_=r_, qlo=qlo, last=it_["last"]: e.matmul(
                        pso[:, qlo:512], lhsT=vh[:, kb, :], rhs=wsb[:, r_, qlo:512], start=False, stop=last),
                        reads=[("vh", kb // 4), ("wsb", r_)], writes=[PSK(6 + ob)])
                    if it_["last"]:
                        if h % 2 == 0:
                            P.dve(lambda e, pso=pso, j=j, hp=hp: e.tensor_copy(aT[0:64, hp, 512 * j:512 * (j + 1)], pso[0:64, :]),
                                  reads=[PSK(6 + ob)], writes=[("aT", hp, 0)])
                        else:
                            P.dve(lambda e, pso=pso, j=j, hp=hp: e.tensor_copy(aT[64:128, hp, 512 * j:512 * (j + 1)], pso[64:128, :]),
                                  reads=[PSK(6 + ob)], writes=[("aT", hp, 1)])

                n_it = len(its)
                for i in range(n_it + 2):
                    if i < n_it:
                        stageA(its[i])
                    if 1 <= i <= n_it:
                        stageB1(its[i - 1])
                    if i >= 2:
                        stageB2(its[i - 2])
        P.barrier()
        with SBT(nc, "wout", [128, 8, D], BF16) as wout:
            load_w(wout, T["ab_w_out"][0], "wout", 8, 0, D)
            for tq in range(4):
                t0 = tq * 512
                for m in range(8):
                    bi = m % 2
                    def mm(e, m=m, bi=bi, t0=t0):
                        for k in range(8):
                            src = aT if k < 4 else pTo
                            r = e.matmul(ps[bi][:], lhsT=wout[:, k, m * 128:(m + 1) * 128], rhs=src[:, k % 4, t0:t0 + 512],
                                         start=(k == 0), stop=(k == 7))
                        return r
                    P.pe(mm, reads=[("wout", k) for k in range(8)] + ["aTall"], writes=[PSK(bi)])
                    resid_add(m, bi, ps[bi][:], t0, 512)


def stage_mix_s5(C, b, xT, ps, cf, cb, gains, rmsnorm_tile, load_w, linear_fm, resid_add):
    from contextlib import ExitStack
    nc, P, T = C.nc, C.P, C.T
    P.barrier()

    def PSK(i):
        return ("ps", i)
    ident = cf[:, 0, :]
    mask32 = cf[:, 5, :]
    TWO_PI = 6.283185
    INV2PI = 1.0 / (2.0 * math.pi)

    def V(fn, r, w):
        return P.dve(fn, reads=r, writes=w)

    def Aop(fn, r, w):
        return P.act(fn, reads=r, writes=w)

    with SBT(nc, "uT", [128, 8, S], BF16) as uT, SBT(nc, "dcol", [128, 8], F32) as dcol:
        with SBT(nc, "w_in", [128, 8, D], BF16) as w_in, SBT(nc, "sq", [128, 8, 512], BF16) as sq, \
                SBT(nc, "rstd", [128, 512], F32) as rstd, SBT(nc, "hT", [128, 8, 512], BF16) as hT:
            load_w(w_in, T["ssm_w_in"][0], "w_in", 8, 0, D)
            P.dma(lambda e: e.dma_start(out=dcol[:], in_=T["ssm_d"][0].rearrange("(t p) -> p t", p=128),
                                        allow_slow_non_contiguous=True), writes=["dcol"])
            for tq in range(4):
                rmsnorm_tile(hT, "hT", 1, tq * 512, 512, sq, rstd)

                def ev(m, bi, pap, tq=tq):
                    P.act(lambda e: e.activation(uT[:, m, tq * 512:(tq + 1) * 512], pap, AF.Copy),
                          reads=[PSK(bi)], writes=[("uT", m)])
                linear_fm(w_in, "w_in", hT, "hT", 8, 8, 512, ev)
        P.barrier()
        with ExitStack() as es:
            def A(name, shape, dt=F32):
                return es.enter_context(SBT(nc, name, shape, dt))
            lre = A("lre", [128, 4]); lim = A("lim", [128, 4]); ldt = A("ldt", [128, 4])
            dtt = A("dtt", [128, 4]); ar = A("ar", [128, 4]); an = A("an", [128, 4])
            arj = A("arj", [128, 9, 4]); tj = A("tj", [128, 9, 4]); tjc = A("tjc", [128, 9, 4])
            ti = A("ti", [128, 9, 4], I32); fr = A("fr", [128, 9, 4])
            mag = A("mag", [128, 9, 4]); sinj = A("sinj", [128, 9, 4]); cosj = A("cosj", [128, 9, 4])
            Lr = A("Lr", [128, 9, 4]); Li = A("Li", [128, 9, 4])
            nre = A("nre", [128, 4]); den = A("den", [128, 4]); t1 = A("t1", [128, 4]); t2 = A("t2", [128, 4])
            cr = A("cr", [128, 4]); ci = A("ci", [128, 4]); ti8 = A("ti8", [128, 4], I32); t8f = A("t8f", [128, 4])
            Fr = A("Fr", [128, 8, 4]); Fi = A("Fi", [128, 8, 4]); f1 = A("f1", [128, 8, 4]); f2 = A("f2", [128, 8, 4])
            Bst = A("Bst", [128, 2, 4, 16]); Cin = A("Cin", [64, 2, 2, 64]); Cst = A("Cst", [128, 2, 4, 16])
            l1 = A("l1", [128, 9, 4, 16]); l2 = A("l2", [128, 9, 4, 16])
            What = A("What", [128, 8, 2, 128])
            Wt = A("Wt", [128, 8, 2, 128], BF16)
            CL = A("CL", [128, 2, 9, 4, 16])
            LB = CL[:, :, 0:8]
            Qd = A("Qd", [128, 9, 2, 4, 32], BF16)
            Qf = A("Qf", [128, 2, 128])
            TtF = A("TtF", [128, 4, 128]); Tt = A("Tt", [128, 8, 128], BF16)
            cosT = A("cosT", [128, 4, 256]); sinT = A("sinT", [128, 4, 256])
            Xp = A("Xp", [128, 2, 4, 256]); xa = A("xa", [128, 4, 256]); xb = A("xb", [128, 4, 256])
            Ssc = A("Ssc", [128, 2, 4, 256]); tk = Ssc[:, 0]; tki = Ssc[:, 1].bitcast(I32); Hb = A("Hb", [128, 2, 4, 257], BF16)
            iota256 = cf[:, 6:8, :].rearrange("p a b -> p (a b)")
            What6 = What[:].rearrange("p t r (q g c) -> p t r q g c", q=4, g=2)
            Qf5 = Qf[:].rearrange("p r (q g c) -> p r q g c", q=4, g=2)
            Wv = Wt[:].rearrange("p t r n -> p (t r) n")
            V(lambda e: e.memset(What[:], 0.0), [], ["What"])
            V(lambda e: e.memset(Qd[:], 0.0), [], ["Qpad"])
            V(lambda e: e.memset(Qf[:], 0.0), [], ["Qf"])
            V(lambda e: e.memset(Hb[:], 0.0), [], ["Hb"])

            def bc(ap, shape):
                return ap.broadcast_to(shape)

            def partA(j):
                g0 = 8 * j
                P.dma(lambda e, g0=g0: e.dma_start(out=lre[:], in_=T["ssm_lam_re"][0, g0:g0 + 8, :].rearrange("(q g) p -> (g p) q", g=2),
                                                   allow_slow_non_contiguous=True), writes=["lre"])
                P.dma(lambda e, g0=g0: e.dma_start(out=lim[:], in_=T["ssm_lam_im"][0, g0:g0 + 8, :].rearrange("(q g) p -> (g p) q", g=2),
                                                   allow_slow_non_contiguous=True), writes=["lim"])
                for g2 in range(2):
                    P.dma(lambda e, g0=g0, g2=g2: e.dma_start(
                        out=ldt[64 * g2:64 * g2 + 64, :],
                        in_=T["ssm_log_dt"][0, g0:g0 + 8].rearrange("(q g) -> g q", g=2)[g2:g2 + 1, :].broadcast_to([64, 4]),
                        allow_slow_non_contiguous=True), writes=["ldt"])
                for ri, nm in enumerate(["ssm_b_re", "ssm_b_im"]):
                    P.dma(lambda e, g0=g0, ri=ri, nm=nm: e.dma_start(
                        out=Bst[:, ri, :, :], in_=T[nm][0, g0:g0 + 8].rearrange("(q g) p c -> (g p) q c", g=2)),
                        writes=["Bst"])
                for ri, nm in enumerate(["ssm_c_re", "ssm_c_im"]):
                    for q in range(4):
                        P.dma(lambda e, g0=g0, ri=ri, nm=nm, q=q: e.dma_start(
                            out=Cin[16 * q:16 * q + 16, ri, :, :],
                            in_=T[nm][0, g0 + 2 * q:g0 + 2 * q + 2].rearrange("g c p -> c g p")), writes=["Cin"])
                def trc(e):
                    for ri in range(2):
                        r = e.transpose(ps[0][:, ri * 64:(ri + 1) * 64], Cin[:, ri, :, :].rearrange("a g p -> a (g p)"), ident[0:64, 0:64])
                    return r
                P.pe(trc, reads=["Cin", "cf"], writes=[PSK(0)])
                Aop(lambda e: e.activation(Cst[:].rearrange("p r q c -> p (r q c)"), ps[0][:, 0:128], AF.Copy), [PSK(0)], ["Cst"])
                Aop(lambda e: e.activation(dtt[:], ldt[:], AF.Exp), ["ldt"], ["dtt"])
                def f_(e):
                    e.tensor_tensor(ar[:], lre[:], dtt[:], ALU.mult)
                    return e.tensor_tensor(an[:], lim[:], dtt[:], ALU.mult)
                V(f_, ["lre", "lim", "dtt"], ["ar", "an"])
                jv = bc(cf[:, 6, 0:9][:, :, None], [128, 9, 4])
                def f_(e):
                    e.tensor_tensor(arj[:], bc(ar[:, None, :], [128, 9, 4]), jv, ALU.mult)
                    return e.scalar_tensor_tensor(tj[:], bc(an[:, None, :], [128, 9, 4]), INV2PI, jv, ALU.mult, ALU.mult)
                V(f_, ["ar", "an", "cf"], ["arj", "tj"])
                Aop(lambda e: e.activation(mag[:], arj[:], AF.Exp), ["arj"], ["mag"])
                V(lambda e: e.tensor_copy(ti[:], tj[:]), ["tj"], ["ti"])
                V(lambda e: e.tensor_tensor(fr[:], tj[:], ti[:], ALU.subtract), ["tj", "ti"], ["fr"])
                Aop(lambda e: e.activation(sinj[:], fr[:], AF.Sin, scale=TWO_PI), ["fr"], ["sinj"])
                V(lambda e: e.tensor_scalar(tjc[:], tj[:], 0.25, None, ALU.add), ["tj"], ["tjc"])
                V(lambda e: e.tensor_copy(ti[:], tjc[:]), ["tjc"], ["ti"])
                V(lambda e: e.tensor_tensor(fr[:], tjc[:], ti[:], ALU.subtract), ["tjc", "ti"], ["fr"])
                Aop(lambda e: e.activation(cosj[:], fr[:], AF.Sin, scale=TWO_PI), ["fr"], ["cosj"])
                def f_(e):
                    e.tensor_tensor(Lr[:], mag[:], cosj[:], ALU.mult)
                    return e.tensor_tensor(Li[:], mag[:], sinj[:], ALU.mult)
                V(f_, ["mag", "cosj", "sinj"], ["Lr", "Li"])
                def f_(e):
                    e.tensor_scalar(nre[:], Lr[:, 1, :], -1.0, None, ALU.add)
                    e.tensor_tensor(t1[:], lre[:], lre[:], ALU.mult)
                    return e.tensor_tensor(t2[:], lim[:], lim[:], ALU.mult)
                V(f_, ["Lr", "lre", "lim"], ["nre", "t1", "t2"])
                V(lambda e: e.tensor_tensor(den[:], t1[:], t2[:], ALU.add), ["t1", "t2"], ["den"])
                V(lambda e: e.reciprocal(den[:], den[:]), ["den"], ["den"])
                def f_(e):
                    e.tensor_tensor(t1[:], nre[:], lre[:], ALU.mult)
                    return e.tensor_tensor(t2[:], Li[:, 1, :], lim[:], ALU.mult)
                V(f_, ["nre", "lre", "Li", "lim", "den"], ["t1", "t2"])
                V(lambda e: e.tensor_tensor(cr[:], t1[:], t2[:], ALU.add), ["t1", "t2"], ["cr"])
                V(lambda e: e.tensor_tensor(cr[:], cr[:], den[:], ALU.mult), ["cr", "den"], ["cr"])
                def f_(e):
                    e.tensor_tensor(t1[:], Li[:, 1, :], lre[:], ALU.mult)
                    return e.tensor_tensor(t2[:], nre[:], lim[:], ALU.mult)
                V(f_, ["nre", "lre", "Li", "lim", "cr"], ["t1", "t2"])
                V(lambda e: e.tensor_tensor(ci[:], t1[:], t2[:], ALU.subtract), ["t1", "t2"], ["ci"])
                V(lambda e: e.tensor_tensor(ci[:], ci[:], den[:], ALU.mult), ["ci", "den"], ["ci"])
                crb = bc(cr[:, None, :], [128, 8, 4]); cib = bc(ci[:, None, :], [128, 8, 4])
                def f_(e, crb=crb, cib=cib):
                    e.tensor_tensor(f1[:], Lr[:, 0:8, :], crb, ALU.mult)
                    return e.tensor_tensor(f2[:], Li[:, 0:8, :], cib, ALU.mult)
                V(f_, ["Lr", "Li", "cr", "ci"], ["f1", "f2"])
                V(lambda e: e.tensor_tensor(Fr[:], f1[:], f2[:], ALU.subtract), ["f1", "f2"], ["Fr"])
                def f_(e, crb=crb, cib=cib):
                    e.tensor_tensor(f1[:], Lr[:, 0:8, :], cib, ALU.mult)
                    return e.tensor_tensor(f2[:], Li[:, 0:8, :], crb, ALU.mult)
                V(f_, ["Lr", "Li", "cr", "ci", "Fr"], ["f1", "f2"])
                V(lambda e: e.tensor_tensor(Fi[:], f1[:], f2[:], ALU.add), ["f1", "f2"], ["Fi"])
                sh8 = [128, 8, 4, 16]
                Frb = bc(Fr[:, :, :, None], sh8); Fib = bc(Fi[:, :, :, None], sh8)
                B0 = bc(Bst[:, 0, None, :, :], sh8); B1 = bc(Bst[:, 1, None, :, :], sh8)
                def f_(e, Frb=Frb, Fib=Fib, B0=B0, B1=B1):
                    e.tensor_tensor(l1[:, 0:8], Frb, B0, ALU.mult)
                    return e.tensor_tensor(l2[:, 0:8], Fib, B1, ALU.mult)
                V(f_, ["Fr", "Fi", "Bst"], ["l1", "l2"])
                V(lambda e: e.tensor_tensor(LB[:, 0], l1[:, 0:8], l2[:, 0:8], ALU.subtract), ["l1", "l2"], ["LB0", "CL0"])
                def f_(e, Frb=Frb, Fib=Fib, B0=B0, B1=B1):
                    e.tensor_tensor(l1[:, 0:8], Frb, B1, ALU.mult)
                    return e.tensor_tensor(l2[:, 0:8], Fib, B0, ALU.mult)
                V(f_, ["Fr", "Fi", "Bst", "LB0"], ["l1", "l2"])
                V(lambda e: e.tensor_tensor(LB[:, 1], l1[:, 0:8], l2[:, 0:8], ALU.add), ["l1", "l2"], ["LB1", "CL1"])
                def f_(e):
                    for g2 in range(2):
                        for ri in range(2):
                            r = e.tensor_copy(What6[64 * g2:64 * g2 + 64, :, ri, :, g2, :], LB[64 * g2:64 * g2 + 64, ri, :, :, :])
                    return r
                V(f_, ["LB0", "LB1"], ["What"])
            def partB(j):
                g0 = 8 * j
                for c4 in range(4):
                    bi = c4 % 2
                    def trw(e, c4=c4, bi=bi):
                        for i4 in range(4):
                            c = c4 * 4 + i4
                            r = e.transpose(ps[bi][:, i4 * 128:(i4 + 1) * 128], What[:, c // 2, c % 2, :], ident)
                        return r
                    P.pe(trw, reads=["What", "cf"], writes=[PSK(bi)])
                    if c4 % 2 == 0:
                        Aop(lambda e, c4=c4, bi=bi: e.activation(Wv[:, c4 * 4:c4 * 4 + 4, :], ps[bi][:].rearrange("p (a n) -> p a n", a=4), AF.Copy),
                            [PSK(bi)], [("Wpad", c4)])
                    else:
                        V(lambda e, c4=c4, bi=bi: e.tensor_copy(Wv[:, c4 * 4:c4 * 4 + 4, :], ps[bi][:].rearrange("p (a n) -> p a n", a=4)),
                          [PSK(bi)], [("Wpad", c4)])
                sh9 = [128, 9, 4, 16]
                Lrb = bc(Lr[:, :, :, None], sh9); Lib = bc(Li[:, :, :, None], sh9)
                C0 = bc(Cst[:, 0, None, :, :], sh9); C1 = bc(Cst[:, 1, None, :, :], sh9)
                def f_(e, Lrb=Lrb, Lib=Lib, C0=C0, C1=C1):
                    e.tensor_tensor(l1[:], Lrb, C0, ALU.mult)
                    return e.tensor_tensor(l2[:], Lib, C1, ALU.mult)
                V(f_, ["Lr", "Li", "Cst", "LB1"], ["l1", "l2"])
                V(lambda e: e.tensor_tensor(CL[:, 0], l1[:], l2[:], ALU.subtract), ["l1", "l2"], ["CL0", "LB0", "LB1"])
                def f_(e, Lrb=Lrb, Lib=Lib, C0=C0, C1=C1):
                    e.tensor_tensor(l1[:], Lib, C0, ALU.mult)
                    return e.tensor_tensor(l2[:], Lrb, C1, ALU.mult)
                V(f_, ["Lr", "Li", "Cst", "CL0"], ["l1", "l2"])
                V(lambda e: e.scalar_tensor_tensor(CL[:, 1], l1[:], -1.0, l2[:], ALU.mult, ALU.subtract), ["l1", "l2"], ["CL1", "LB0", "LB1"])
                def f_(e):
                    for g2 in range(2):
                        for ri in range(2):
                            e.tensor_copy(Qd[64 * g2:64 * g2 + 64, :, ri, :, 16 * g2:16 * g2 + 16], CL[64 * g2:64 * g2 + 64, ri, :, :, :])
                            r = e.tensor_copy(Qf5[64 * g2:64 * g2 + 64, ri, :, g2, :], CL[64 * g2:64 * g2 + 64, ri, 0, :, :])
                    return r
                V(f_, ["CL0", "CL1"], ["Qpad", "Qf"])
                for half in range(2):
                    bi = half
                    def mmt(e, half=half, bi=bi):
                        for i4 in range(4):
                            tau = half * 4 + i4
                            e.matmul(ps[bi][:, i4 * 128:(i4 + 1) * 128], lhsT=What[:, tau, 0, :], rhs=Qf[:, 0, :], start=True, stop=False)
                            r = e.matmul(ps[bi][:, i4 * 128:(i4 + 1) * 128], lhsT=What[:, tau, 1, :], rhs=Qf[:, 1, :], start=False, stop=True)
                        return r
                    P.pe(mmt, reads=["What", "Qf"], writes=[PSK(bi)])
                    m4 = bc(mask32[:, None, :], [128, 4, 128])
                    if half == 0:
                        V(lambda e, bi=bi, m4=m4: e.tensor_tensor(TtF[:], ps[bi][:].rearrange("p (a n) -> p a n", a=4), m4, ALU.mult),
                          [PSK(bi), "cf"], ["TtF"])
                        V(lambda e, j=j: e.scalar_tensor_tensor(TtF[:, 0, :], ident, dcol[:, j:j + 1], TtF[:, 0, :], ALU.mult, ALU.add),
                          ["TtF", "dcol", "cf"], ["TtF"])
                        V(lambda e: e.tensor_copy(Tt[:, 0:4, :], TtF[:]), ["TtF"], [("Tt", 0)])
                    else:
                        V(lambda e, bi=bi, m4=m4: e.tensor_tensor(Tt[:, 4:8, :], ps[bi][:].rearrange("p (a n) -> p a n", a=4), m4, ALU.mult),
                          [PSK(bi), "cf"], [("Tt", 1)])
                uv = uT[:, j, :].rearrange("p (k s) -> p s k", s=8)
                def mmx(e, uv=uv):
                    for ri in range(2):
                        for tau in range(8):
                            for q in range(4):
                                r = e.matmul(ps[2 + q][:, ri * 256:(ri + 1) * 256], lhsT=Wt[32 * q:32 * q + 32, tau, ri, :],
                                             rhs=uv[32 * q:32 * q + 32, 7 - tau, :], start=(tau == 0), stop=(tau == 7),
                                             tile_position=(32 * q, 0))
                    return r
                P.pe(mmx, reads=[("Wpad", c4) for c4 in range(4)] + [("uT", j, s_) for s_ in range(8)], writes=[PSK(2 + q) for q in range(4)])
                V(lambda e: e.tensor_copy(ti8[:], tj[:, 8, :]), ["tj"], ["ti8"])
                V(lambda e: e.tensor_tensor(t8f[:], tj[:, 8, :], ti8[:], ALU.subtract), ["tj", "ti8"], ["t8f"])
                V(lambda e: e.tensor_tensor(tk, bc(t8f[:, :, None], [128, 4, 256]), bc(iota256[:, None, :], [128, 4, 256]), ALU.mult),
                  ["t8f", "cf"], ["tk"] + [("Ssc", ri_, q_) for ri_ in range(2) for q_ in range(4)])
                V(lambda e: e.tensor_copy(tki, tk), ["tk"], ["tki"])
                V(lambda e: e.tensor_tensor(xa[:], tk, tki, ALU.subtract), ["tk", "tki"], ["xa"])
                Aop(lambda e: e.activation(sinT[:], xa[:], AF.Sin, scale=TWO_PI), ["xa"], ["sinT"])
                V(lambda e: e.tensor_scalar(tk, tk, 0.25, None, ALU.add), ["tk", "tki"], ["tk"])
                V(lambda e: e.tensor_copy(tki, tk), ["tk"], ["tki"])
                V(lambda e: e.tensor_tensor(xb[:], tk, tki, ALU.subtract), ["tk", "tki"], ["xb"])
                Aop(lambda e: e.activation(cosT[:], xb[:], AF.Sin, scale=TWO_PI), ["xb"], ["cosT"])
                for q in range(4):
                    Xr = ps[2 + q][:, 0:256]; Xi = ps[2 + q][:, 256:512]
                    def f_(e, q=q, Xr=Xr, Xi=Xi):
                        e.tensor_tensor(xa[:, q, :], cosT[:, q, :], Xr, ALU.mult)
                        return e.tensor_tensor(xb[:, q, :], sinT[:, q, :], Xi, ALU.mult)
                    V(f_, [PSK(2 + q), "cosT", "sinT", "xa", "xb"], [("xa", q), ("xb", q)])
                    V(lambda e, q=q: e.tensor_tensor(Xp[:, 0, q, :], xa[:, q, :], xb[:, q, :], ALU.add), [("xa", q), ("xb", q)], [("Xp", 0, q)])
                    def f_(e, q=q, Xr=Xr, Xi=Xi):
                        e.tensor_tensor(xa[:, q, :], cosT[:, q, :], Xi, ALU.mult)
                        return e.tensor_tensor(xb[:, q, :], sinT[:, q, :], Xr, ALU.mult)
                    V(f_, [PSK(2 + q), "cosT", "sinT", ("Xp", 0, q)], [("xa", q), ("xb", q)])
                    V(lambda e, q=q: e.tensor_tensor(Xp[:, 1, q, :], xa[:, q, :], xb[:, q, :], ALU.subtract), [("xa", q), ("xb", q)], [("Xp", 1, q)])
                    for ri in range(2):
                        V(lambda e, q=q, ri=ri: e.tensor_tensor_scan(Ssc[:, ri, q, :], mag[:, 8, q:q + 1].to_broadcast([128, 256]),
                                                                     Xp[:, ri, q, :], 0.0, ALU.mult, ALU.add),
                          [("Xp", ri, q), "mag"], [("Ssc", ri, q), "tk", "tki"])
                allS = [("Ssc", ri, q) for ri in range(2) for q in range(4)]
                allx = [("xa", q) for q in range(4)] + [("xb", q) for q in range(4)]
                def f_(e):
                    e.tensor_tensor(xa[:], cosT[:], Ssc[:, 0], ALU.mult)
                    return e.tensor_tensor(xb[:], sinT[:], Ssc[:, 1], ALU.mult)
                V(f_, allS + ["cosT", "sinT"], allx + ["xa", "xb"])
                V(lambda e: e.tensor_tensor(Hb[:, 0, :, 1:257], xa[:], xb[:], ALU.subtract), ["xa", "xb"], ["Hb0"])
                def f_(e):
                    e.tensor_tensor(xa[:], cosT[:], Ssc[:, 1], ALU.mult)
                    return e.tensor_tensor(xb[:], sinT[:], Ssc[:, 0], ALU.mult)
                V(f_, allS + ["cosT", "sinT", "Hb0"], allx + ["xa", "xb"])
                V(lambda e: e.tensor_tensor(Hb[:, 1, :, 1:257], xa[:], xb[:], ALU.add), ["xa", "xb"], ["Hb1"])
            def partC(j):
                uv = uT[:, j, :].rearrange("p (k s) -> p s k", s=8)
                for tp in range(7, -1, -1):
                    bi = 6 + (tp % 2)
                    def mmy(e, tp=tp, bi=bi, uv=uv):
                        for s_ in range(tp + 1):
                            e.matmul(ps[bi][:, 0:256], lhsT=Tt[:, tp - s_, :], rhs=uv[:, s_, :], start=(s_ == 0), stop=False)
                        for ri in range(2):
                            for q in range(4):
                                r = e.matmul(ps[bi][32 * q:32 * q + 32, 0:256], lhsT=Qd[:, tp + 1, ri, q, :], rhs=Hb[:, ri, q, 0:256], start=False,
                                             stop=(ri == 1), tile_position=(0, 32 * q))
                        return r
                    P.pe(mmy, reads=[("uT", j, s_) for s_ in range(tp + 1)] + [("Tt", 0), ("Tt", 1), "Qpad", "Hb0", "Hb1"], writes=[PSK(bi)])
                    Aop(lambda e, tp=tp, bi=bi, uv=uv: e.activation(uv[:, tp, :], ps[bi][:, 0:256], AF.Gelu_apprx_tanh),
                        [PSK(bi)], [("uT", j, tp)])
            partA(0)
            for j in range(8):
                partB(j)
                if j < 7:
                    partA(j + 1)
                partC(j)
        P.barrier()
        with SBT(nc, "wg", [128, 2, 2, 8, 512], BF16) as wg, SBT(nc, "sig", [128, 2, 512], F32) as sig, \
                SBT(nc, "mixb", [128, 2, 512], F32) as mixb:
            wglu = T["ssm_w_glu"][0].rearrange("(k p) n -> p k n", p=128)
            it = 0
            for mg in range(2):
                wb = mg % 2
                for vg in range(2):
                    c0 = vg * D + mg * 512
                    P.dma(lambda e, wb=wb, vg=vg, c0=c0: e.dma_start(out=wg[:, wb, vg, :, :], in_=wglu[:, :, c0:c0 + 512]),
                          writes=[("wg", wb, vg)], q="pool")
                for ml in range(4):
                    m = mg * 4 + ml
                    for tq in range(4):
                        pb = it % 2
                        it += 1
                        for vg in range(2):
                            bi = 2 + 2 * vg + pb
                            def mm(e, wb=wb, vg=vg, bi=bi, tq=tq, ml=ml):
                                for k in range(8):
                                    r = e.matmul(ps[bi][:], lhsT=wg[:, wb, vg, k, ml * 128:(ml + 1) * 128], rhs=uT[:, k, tq * 512:(tq + 1) * 512],
                                                 start=(k == 0), stop=(k == 7))
                                return r
                            P.pe(mm, reads=[("wg", wb, vg)], writes=[PSK(bi)])
                        Aop(lambda e, pb=pb: e.activation(sig[:, pb, :], ps[4 + pb][:], AF.Sigmoid), [PSK(4 + pb)], [("sig", pb)])
                        V(lambda e, pb=pb: e.tensor_tensor(mixb[:, pb, :], ps[2 + pb][:], sig[:, pb, :], ALU.mult), [PSK(2 + pb), ("sig", pb)], [("mixb", pb)])
                        V(lambda e, pb=pb, m=m, tq=tq: e.tensor_tensor(xT[:, m, tq * 512:(tq + 1) * 512], xT[:, m, tq * 512:(tq + 1) * 512],
                                                                     mixb[:, pb, :], ALU.add), [("mixb", pb), ("xT", m)], [("xT", m)])


_CACHE = {}


def kernel(**inputs):
    if "prog" not in _CACHE:
        _CACHE["prog"] = build_program()
    nc, _ = _CACHE["prog"]
    consts = make_consts()
    in_maps = []
    for c in range(8):
        m = {}
        for name, shape in INPUT_SPECS:
            if name == "consts":
                m[name] = consts
            elif name in ("x", "mem"):
                m[name] = np.ascontiguousarray(np.asarray(inputs[name], dtype=np.float32)[c * NB:(c + 1) * NB])
            else:
                m[name] = np.ascontiguousarray(np.asarray(inputs[name], dtype=np.float32))
        in_maps.append(m)
    res = run_bass_kernel_spmd(nc, in_maps, core_ids=list(range(8)))
    return np.concatenate([r["out"] for r in res.results], axis=0)
```

```python
import math
import numpy as np
import concourse.bass as bass
from concourse.ap import AP
import concourse.mybir as mybir
from concourse.bass_utils import run_bass_kernel_spmd

F32 = mybir.dt.float32
BF16 = mybir.dt.bfloat16
I32 = mybir.dt.int32
AF = mybir.ActivationFunctionType
ALU = mybir.AluOpType

COMPUTE = ("pe", "act", "dve", "pool")
ALLENG = ("pe", "act", "dve", "pool", "sp")
NDMA_SEMS = 40

S = 2048
D = 1024
NB = 2
DFF = 2816
NF = DFF // 128
MEM = 256


class Op:
    __slots__ = ("eng", "fn", "deps", "idx", "dma", "signal", "cnt", "sem", "clock")


class Prog:
    def __init__(self, nc):
        self.nc = nc
        self.ops = []
        self.last_write = {}
        self.readers = {}
        self.dma_hist = []
        self.n_dma = 0
        self.last_on = {}
        self.dma_since = []

    def add(self, eng, fn, reads=(), writes=(), dma=False, extra_deps=()):
        op = Op()
        op.eng, op.fn, op.dma = eng, fn, dma
        op.idx = len(self.ops)
        op.signal = False
        op.cnt = None
        op.sem = None
        op.clock = None
        deps = set(extra_deps)
        for k in reads:
            w = self.last_write.get(k)
            if w is not None:
                deps.add(w)
        for k in writes:
            w = self.last_write.get(k)
            if w is not None:
                deps.add(w)
            r = self.readers.get(k)
            if r:
                deps.update(r)
        for k in reads:
            self.readers.setdefault(k, []).append(op.idx)
        for k in writes:
            self.last_write[k] = op.idx
            self.readers[k] = []
        if dma:
            j = self.n_dma
            self.n_dma += 1
            op.sem = j % NDMA_SEMS
            if j >= NDMA_SEMS:
                deps.add(self.dma_hist[j - NDMA_SEMS])
            self.dma_hist.append(op.idx)
            self.dma_since.append(op.idx)
        else:
            self.last_on[eng] = op.idx
        deps.discard(op.idx)
        if eng == "pe" and not dma:
            deps = {d_ for d_ in deps if self.ops[d_].eng != "pe" or self.ops[d_].dma}
        op.deps = deps
        self.ops.append(op)
        return op

    def pe(self, fn, reads=(), writes=()):
        return self.add("pe", fn, reads, writes)

    def act(self, fn, reads=(), writes=()):
        return self.add("act", fn, reads, writes)

    def dve(self, fn, reads=(), writes=()):
        return self.add("dve", fn, reads, writes)

    def dma(self, fn, reads=(), writes=(), q="sp"):
        return self.add(q, fn, reads, writes, dma=True)

    def barrier(self):
        deps = set(self.last_on.values()) | set(self.dma_since)
        self.dma_since = []
        for e in ALLENG:
            self.add(e, lambda eng: eng.nop(), extra_deps=deps)
        self.last_write = {}
        self.readers = {}

    def emit(self, final_ops):
        nc = self.nc
        ops = self.ops
        for op in ops:
            for d in op.deps:
                ops[d].signal = True
        for op in final_ops:
            op.signal = True
        eng_cnt = {e: 0 for e in ALLENG}
        dma_cnt = [0] * NDMA_SEMS
        for op in ops:
            if op.dma:
                dma_cnt[op.sem] += 16
                op.cnt = dma_cnt[op.sem]
            elif op.signal:
                eng_cnt[op.eng] += 1
                op.cnt = eng_cnt[op.eng]
        sems = {e: nc.alloc_semaphore("s_" + e) for e in ALLENG}
        dsems = [nc.alloc_semaphore("d_%d" % i) for i in range(NDMA_SEMS)]

        def key_of(o):
            return ("d", o.sem) if o.dma else o.eng

        know = {e: {} for e in ALLENG}
        waits = {}
        for op in ops:
            K = know[op.eng]
            wl = []
            for d in sorted(op.deps, reverse=True):
                dop = ops[d]
                k = key_of(dop)
                if K.get(k, 0) >= dop.cnt:
                    continue
                for kk, vv in dop.clock.items():
                    if K.get(kk, 0) < vv:
                        K[kk] = vv
                K[k] = max(K.get(k, 0), dop.cnt)
                wl.append((dsems[dop.sem] if dop.dma else sems[dop.eng], dop.cnt))
            waits[op.idx] = wl
            if op.signal or op.dma:
                op.clock = dict(K)
        by_eng = {e: [] for e in ALLENG}
        for op in ops:
            by_eng[op.eng].append(op)
        fin = [(dsems[o.sem] if o.dma else sems[o.eng], o.cnt) for o in final_ops]
        self.n_inst = {e: len(by_eng[e]) for e in ALLENG}

        def run(engname, e):
            for op in by_eng[engname]:
                for (s, v) in waits[op.idx]:
                    e.wait_ge(s, v)
                ins = op.fn(e)
                if op.dma:
                    ins.then_inc(dsems[op.sem], 16)
                elif op.signal:
                    ins.then_inc(sems[op.eng], 1)
            if engname == "sp":
                for (s, v) in fin:
                    e.wait_ge(s, v)

        with nc.Block() as block:
            @block.tensor
            def _(e):
                run("pe", e)

            @block.scalar
            def _(e):
                run("act", e)

            @block.vector
            def _(e):
                run("dve", e)

            @block.gpsimd
            def _(e):
                run("pool", e)

            @block.sync
            def _(e):
                run("sp", e)


INPUT_SPECS = [
    ("x", [NB, S, D]), ("mem", [NB, MEM, D]),
    ("norm_mix", [2, D]), ("norm_xattn", [2, D]), ("norm_ffn", [2, D]), ("norm_mem", [D]), ("norm_final", [D]),
    ("ab_w_in", [1, D, 2048]), ("pool_w", [1, 4, 128, 128]), ("pool_scale", [1, 512]), ("ab_w_out", [1, D, D]),
    ("ssm_w_in", [1, D, D]), ("ssm_lam_re", [1, 64, 64]), ("ssm_lam_im", [1, 64, 64]), ("ssm_log_dt", [1, 64]),
    ("ssm_b_re", [1, 64, 64, 16]), ("ssm_b_im", [1, 64, 64, 16]), ("ssm_c_re", [1, 64, 16, 64]),
    ("ssm_c_im", [1, 64, 16, 64]), ("ssm_d", [1, D]), ("ssm_w_glu", [1, D, 2 * D]),
    ("xa_w_q", [2, D, D]), ("xa_w_kv", [2, D, 2 * D]), ("xa_w_o", [2, D, D]),
    ("ffn_w_up", [2, D, 2 * DFF]), ("ffn_conv_w", [2, 3, 2 * DFF]), ("ffn_conv_b", [2, 2 * DFF]),
    ("ffn_w_down", [2, DFF, D]),
    ("consts", [128, 12, 128]),
]


def make_consts():
    c = np.zeros((128, 12, 128), np.float32)
    j = np.arange(128)
    c[:, 0, :] = np.eye(128)
    c[:, 1, :] = -(j[:, None] > j[None, :]).astype(np.float32)
    c[:, 2, :] = -1.0
    c[:, 3, :] = 1.0
    c[:, 4, :] = (j[:, None] < j[None, :]).astype(np.float32)
    c[:, 5, :] = (j[:, None] // 32 == j[None, :] // 32).astype(np.float32)
    c[:, 6, :] = np.arange(128)[None, :]
    c[:, 7, :] = 128 + np.arange(128)[None, :]
    c[:, 8, :] = 1.0 / (1.0 + np.arange(128))[None, :]
    c[:, 9, :] = -(j[:, None] >= j[None, :]).astype(np.float32)
    c[:, 10, :] = -30000.0 * (j[:, None] >= j[None, :])
    return c


class Ctx:
    pass


_uid = [0]


def SBT(nc, name, shape, dt):
    _uid[0] += 1
    return nc.sbuf_tensor("%s_%d" % (name, _uid[0]), shape, dt)


def build_program(stop=None, nb=NB):
    nc = bass.Bass("TRN2", target_bir_lowering=False)
    P = Prog(nc)
    C = Ctx()
    C.nc, C.P = nc, P
    T = {}
    for name, shape in INPUT_SPECS:
        T[name] = nc.dram_tensor(name, shape, F32, kind="ExternalInput").ap()
    out = nc.dram_tensor("out", [NB, S, D], F32, kind="ExternalOutput").ap()
    C.T = T

    def sb(name, shape, dt=F32):
        return nc.alloc_sbuf_tensor(name, shape, dt)

    xT = sb("xT", [128, 8, S])
    cf = sb("cf", [128, 10, 128])
    cb = sb("cb", [128, 6, 128], BF16)
    gains = sb("gains", [128, 8, 8])
    convp = sb("convp", [128, 2, 4, 44])
    pscale = sb("pscale", [128, 4])
    zer = sb("zer", [128, 512], BF16)
    memT = sb("memT", [128, 8, MEM], BF16)
    ps = [nc.alloc_psum_tensor("ps%d" % i, [128, 512], F32) for i in range(8)]
    ident = cf[:, 0, :]
    maskstrict = cf[:, 4, :]

    def PSK(i):
        return ("ps", i)

    P.dma(lambda e: e.dma_start(out=cf[:], in_=T["consts"][:, 0:10, :]), writes=["cf"])
    P.dma(lambda e: e.dma_start(out=cb[:, 0:4, :], in_=T["consts"][:, 0:4, :]), writes=["cb"], q="pool")
    P.dma(lambda e: e.dma_start(out=cb[:, 4:6, :], in_=T["consts"][:, 9:11, :]), writes=["cb2"], q="pool")
    gsrc = [T["norm_mix"][0], T["norm_mix"][1], T["norm_xattn"][0], T["norm_xattn"][1],
            T["norm_ffn"][0], T["norm_ffn"][1], T["norm_mem"], T["norm_final"]]
    for i, g in enumerate(gsrc):
        P.dma(lambda e, i=i, g=g: e.dma_start(out=gains[:, i, :], in_=g.rearrange("(t p) -> p t", p=128),
                                             allow_slow_non_contiguous=True), writes=["gains"], q="act")
    for l in range(2):
        for i in range(3):
            P.dma(lambda e, l=l, i=i: e.dma_start(out=convp[:, l, i, :],
                                                  in_=T["ffn_conv_w"][l, i].rearrange("(t p) -> p t", p=128),
                                                  allow_slow_non_contiguous=True), writes=["convp"], q="act")
        P.dma(lambda e, l=l: e.dma_start(out=convp[:, l, 3, :],
                                         in_=T["ffn_conv_b"][l].rearrange("(t p) -> p t", p=128),
                                         allow_slow_non_contiguous=True), writes=["convp"], q="act")
    P.dma(lambda e: e.dma_start(out=pscale[:], in_=T["pool_scale"][0].rearrange("(t p) -> p t", p=128),
                                allow_slow_non_contiguous=True), writes=["pscale"], q="act")
    P.dve(lambda e: e.memset(zer[:], 0.0), writes=["zer"])

    def load_w(dst, src2d, key, k_tiles, col0, ncols):
        v = src2d.rearrange("(k p) n -> p k n", p=128)
        for k in range(k_tiles):
            P.dma(lambda e, k=k: e.dma_start(out=dst[:, k, :], in_=v[:, k, col0:col0 + ncols]),
                  writes=[(key, k)], q="pool")

    def rmsnorm_tile(hT, hkey, gi, t0, n, sq, rstd, part="all"):
        if part in ("all", "sq"):
            for dt in range(8):
                P.act(lambda e, dt=dt: e.activation(sq[:, dt, 0:n], xT[:, dt, t0:t0 + n], AF.Square),
                      reads=[("xT", dt)], writes=[("sq", dt)])
        if part == "sq":
            return
        def mm(e):
            for dt in range(8):
                r = e.matmul(ps[7][:, 0:n], lhsT=cb[:, 3, :], rhs=sq[:, dt, 0:n], start=(dt == 0), stop=(dt == 7))
            return r
        P.pe(mm, reads=[("sq", dt) for dt in range(8)] + ["cb"], writes=[PSK(7)])
        P.dve(lambda e: e.tensor_scalar(rstd[:, 0:n], ps[7][:, 0:n], 1.0 / D, 1e-6, ALU.mult, ALU.add),
              reads=[PSK(7)], writes=["rstd"])
        P.act(lambda e: e.activation(rstd[:, 0:n], rstd[:, 0:n], AF.Ln), reads=["rstd"], writes=["rstd"])
        P.act(lambda e: e.activation(rstd[:, 0:n], rstd[:, 0:n], AF.Exp, scale=-0.5), reads=["rstd"], writes=["rstd"])
        for dt in range(8):
            P.dve(lambda e, dt=dt: e.scalar_tensor_tensor(hT[:, dt, 0:n], xT[:, dt, t0:t0 + n], gains[:, gi, dt:dt + 1],
                                                          rstd[:, 0:n], ALU.mult, ALU.mult),
                  reads=[("xT", dt), "rstd", "gains"], writes=[(hkey, dt)])

    C.ps_rr = 0

    def linear_fm(w, wkey, act, akey, k_tiles, m_tiles, n, evac, a0=0, banks=(0, 1)):
        for m in range(m_tiles):
            bi = banks[C.ps_rr % len(banks)]
            C.ps_rr += 1
            def mm(e, m=m, bi=bi):
                for k in range(k_tiles):
                    r = e.matmul(ps[bi][:, 0:n], lhsT=w[:, k, m * 128:(m + 1) * 128], rhs=act[:, k, a0:a0 + n],
                                 start=(k == 0), stop=(k == k_tiles - 1))
                return r
            P.pe(mm, reads=[(wkey, k) for k in range(k_tiles)] + [(akey, k) for k in range(k_tiles)], writes=[PSK(bi)])
            evac(m, bi, ps[bi][:, 0:n])

    def resid_add(m, bi, pap, t0, n):
        P.dve(lambda e: e.tensor_tensor(xT[:, m, t0:t0 + n], xT[:, m, t0:t0 + n], pap, ALU.add),
              reads=[PSK(bi), ("xT", m)], writes=[("xT", m)])

    final_ops = []
    for b in range(nb):
        P.barrier()
        with SBT(nc, "xin", [128, 2, D], F32) as xin, SBT(nc, "sq", [128, 8, 512], BF16) as sq, \
                SBT(nc, "rstd", [128, 512], F32) as rstd, SBT(nc, "mn", [128, 2, D], F32) as mn, \
                SBT(nc, "ssq", [128, 4], F32) as ssq:
            for tt in range(16):
                xb = tt % 2
                P.dma(lambda e, tt=tt, xb=xb, b=b: e.dma_start(out=xin[:, xb, :], in_=T["x"][b, tt * 128:(tt + 1) * 128, :]),
                      writes=[("xin", xb)])
                for half in range(2):
                    bi = (tt * 2 + half) % 2
                    def tr(e, xb=xb, half=half, bi=bi):
                        for q in range(4):
                            dt = half * 4 + q
                            r = e.transpose(ps[bi][:, q * 128:(q + 1) * 128], xin[:, xb, dt * 128:(dt + 1) * 128], ident)
                        return r
                    P.pe(tr, reads=[("xin", xb), "cf"], writes=[PSK(bi)])
                    P.act(lambda e, tt=tt, half=half, bi=bi: e.activation(
                        xT[:, half * 4:half * 4 + 4, tt * 128:(tt + 1) * 128],
                        ps[bi][:].rearrange("p (q t) -> p q t", q=4), AF.Copy),
                        reads=[PSK(bi)], writes=[("xT", half * 4 + q) for q in range(4)])
            P.dma(lambda e, b=b: e.dma_start(out=mn[:], in_=T["mem"][b].rearrange("(t p) d -> p t d", p=128)), writes=["mn"])
            for t in range(2):
                P.act(lambda e, t=t: e.activation(xin[:, t, :], mn[:, t, :], AF.Square, accum_out=ssq[:, t:t + 1]),
                      reads=["mn"], writes=[("ssq", t), ("xin", t)])
            P.dve(lambda e: e.tensor_scalar(ssq[:, 2:4], ssq[:, 0:2], 1.0 / D, 1e-6, ALU.mult, ALU.add),
                  reads=[("ssq", 0), ("ssq", 1)], writes=["ssq2"])
            P.act(lambda e: e.activation(ssq[:, 2:4], ssq[:, 2:4], AF.Sqrt), reads=["ssq2"], writes=["ssq2"])
            P.dve(lambda e: e.reciprocal(ssq[:, 2:4], ssq[:, 2:4]), reads=["ssq2"], writes=["ssq2"])
            for t in range(2):
                P.dve(lambda e, t=t: e.tensor_scalar(mn[:, t, :], mn[:, t, :], ssq[:, 2 + t:3 + t], None, ALU.mult),
                      reads=["mn", "ssq2"], writes=["mn"])
            for t in range(2):
                for half in range(2):
                    bi = (t * 2 + half) % 2
                    def tr(e, t=t, half=half, bi=bi):
                        for q in range(4):
                            dt = half * 4 + q
                            r = e.transpose(ps[bi][:, q * 128:(q + 1) * 128], mn[:, t, dt * 128:(dt + 1) * 128], ident)
                        return r
                    P.pe(tr, reads=["mn", "cf"], writes=[PSK(bi)])
                    for q in range(4):
                        dt = half * 4 + q
                        P.dve(lambda e, t=t, q=q, dt=dt, bi=bi: e.tensor_scalar(
                            memT[:, dt, t * 128:(t + 1) * 128], ps[bi][:, q * 128:(q + 1) * 128],
                            gains[:, 6, dt:dt + 1], None, ALU.mult),
                            reads=[PSK(bi), "gains"], writes=[("memT", dt)])
        if stop == "load":
            pass
        else:
            for layer in range(2):
                if layer == 0:
                    stage_mix_ab(C, b, xT, ps, cf, cb, gains, pscale, zer, rmsnorm_tile, load_w, linear_fm, resid_add)
                else:
                    stage_mix_s5(C, b, xT, ps, cf, cb, gains, rmsnorm_tile, load_w, linear_fm, resid_add)
                if stop == "mix%d" % layer:
                    break
                stage_xattn(C, b, layer, xT, ps, cb, gains, memT, rmsnorm_tile, load_w, linear_fm, resid_add)
                if stop == "xa%d" % layer:
                    break
                stage_ffn(C, b, layer, xT, ps, gains, convp, rmsnorm_tile, load_w, linear_fm, resid_add)
                if stop == "ffn%d" % layer:
                    break
        P.barrier()
        with SBT(nc, "sq", [128, 8, 512], BF16) as sq, SBT(nc, "rstd", [128, 512], F32) as rstd, \
                SBT(nc, "yT", [128, 8, 512], F32) as yT, SBT(nc, "yo", [128, 2, D], F32) as yo:
            for tq in range(4):
                t0 = tq * 512
                if stop is None:
                    rmsnorm_tile(yT, "yT", 7, t0, 512, sq, rstd)
                else:
                    for dt in range(8):
                        P.act(lambda e, dt=dt, t0=t0: e.activation(yT[:, dt, :], xT[:, dt, t0:t0 + 512], AF.Copy),
                              reads=[("xT", dt)], writes=[("yT", dt)])
                for ts in range(4):
                    ob = ts % 2
                    for half in range(2):
                        bi = (ts * 2 + half) % 2
                        def tr(e, ts=ts, half=half, bi=bi):
                            for q in range(4):
                                dt = half * 4 + q
                                r = e.transpose(ps[bi][:, q * 128:(q + 1) * 128], yT[:, dt, ts * 128:(ts + 1) * 128], ident)
                            return r
                        P.pe(tr, reads=[("yT", dt) for dt in range(8)] + ["cf"], writes=[PSK(bi)])
                        P.act(lambda e, ob=ob, half=half, bi=bi: e.activation(yo[:, ob, half * 512:(half + 1) * 512],
                                                                               ps[bi][:], AF.Copy),
                              reads=[PSK(bi)], writes=[("yo", ob, half)])
                    tok = t0 + ts * 128
                    o = P.dma(lambda e, ob=ob, tok=tok, b=b: e.dma_start(out=out[b, tok:tok + 128, :], in_=yo[:, ob, :]),
                              reads=[("yo", ob, 0), ("yo", ob, 1)], writes=[("out", b, tok)])
                    final_ops.append(o)
    P.emit(final_ops)
    C.final = final_ops
    return nc, P


def stage_xattn(C, b, layer, xT, ps, cb, gains, memT, rmsnorm_tile, load_w, linear_fm, resid_add):
    nc, P, T = C.nc, C.P, C.T
    P.barrier()

    def PSK(i):
        return ("ps", i)
    with SBT(nc, "wq", [128, 8, D], BF16) as wq, SBT(nc, "wo", [128, 8, D], BF16) as wo, \
            SBT(nc, "wkv", [128, 8, D], BF16) as wkv, \
            SBT(nc, "KT", [128, 8, MEM], BF16) as KT, SBT(nc, "V", [128, 2, D], BF16) as V, \
            SBT(nc, "sq", [128, 8, 512], BF16) as sq, SBT(nc, "rstd", [128, 512], F32) as rstd, \
            SBT(nc, "hT", [128, 2, 8, 512], BF16) as hT, SBT(nc, "qT", [128, 8, 512], BF16) as qT, \
            SBT(nc, "pT", [128, 2, 2, 512], BF16) as pT, SBT(nc, "rs", [128, 2, 512], F32) as rs, \
            SBT(nc, "oT", [128, 8, 512], BF16) as oT:
        load_w(wkv, T["xa_w_kv"][layer], "wkv", 8, 0, D)
        load_w(wq, T["xa_w_q"][layer], "wq", 8, 0, D)

        def evK(m, bi, pap):
            P.act(lambda e: e.activation(KT[:, m, :], pap, AF.Copy), reads=[PSK(bi)], writes=[("KT", m)])
        linear_fm(wkv, "wkv", memT, "memT", 8, 8, MEM, evK)
        load_w(wkv, T["xa_w_kv"][layer], "wkv", 8, D, D)
        load_w(wo, T["xa_w_o"][layer], "wo", 8, 0, D)
        for mt in range(2):
            for nh in range(2):
                bi = (mt * 2 + nh) % 2
                def mm(e, mt=mt, nh=nh, bi=bi):
                    for k in range(8):
                        r = e.matmul(ps[bi][:], lhsT=memT[:, k, mt * 128:(mt + 1) * 128], rhs=wkv[:, k, nh * 512:(nh + 1) * 512],
                                     start=(k == 0), stop=(k == 7))
                    return r
                P.pe(mm, reads=[("wkv", k) for k in range(8)] + [("memT", k) for k in range(8)], writes=[PSK(bi)])
                P.act(lambda e, mt=mt, nh=nh, bi=bi: e.activation(V[:, mt, nh * 512:(nh + 1) * 512], ps[bi][:], AF.Copy),
                      reads=[PSK(bi)], writes=[("V", mt, nh)])
        rmsnorm_tile(hT[:, 0], "hT0", 2 + layer, 0, 512, sq, rstd)
        for tq in range(4):
            t0 = tq * 512
            tb = tq % 2

            def evQ(m, bi, pap):
                P.act(lambda e: e.activation(qT[:, m, :], pap, AF.Copy, scale=1.0 / 16.0), reads=[PSK(bi)], writes=[("qT", m)])
            linear_fm(wq, "wq", hT[:, tb], "hT%d" % tb, 8, 8, 512, evQ)
            def scores(h):
                hb = h % 2
                for mt in range(2):
                    bk = 2 + 2 * hb + mt
                    def mm(e, h=h, mt=mt, bk=bk):
                        for d in range(2):
                            r = e.matmul(ps[bk][:], lhsT=KT[:, 2 * h + d, mt * 128:(mt + 1) * 128], rhs=qT[:, 2 * h + d, :],
                                         start=(d == 0), stop=(d == 1))
                        return r
                    P.pe(mm, reads=[("KT", 2 * h), ("KT", 2 * h + 1), ("qT", 2 * h), ("qT", 2 * h + 1)], writes=[PSK(bk)])
                    P.act(lambda e, mt=mt, hb=hb, bk=bk: e.activation(pT[:, hb, mt, :], ps[bk][:], AF.Exp),
                          reads=[PSK(bk)], writes=[("pT", hb, mt)])

            def rest(h):
                hb = h % 2
                def mms(e, hb=hb):
                    e.matmul(ps[6][:], lhsT=cb[:, 3, :], rhs=pT[:, hb, 0, :], start=True, stop=False)
                    return e.matmul(ps[6][:], lhsT=cb[:, 3, :], rhs=pT[:, hb, 1, :], start=False, stop=True)
                P.pe(mms, reads=[("pT", hb, 0), ("pT", hb, 1), "cb"], writes=[PSK(6)])
                P.act(lambda e, hb=hb: e.activation(rs[:, hb, :], ps[6][:], AF.Ln), reads=[PSK(6)], writes=[("rs", hb)])
                P.act(lambda e, hb=hb: e.activation(rs[:, hb, :], rs[:, hb, :], AF.Exp, scale=-1.0), reads=[("rs", hb)], writes=[("rs", hb)])
                for d in range(2):
                    bi = 7 if d == 0 else 1
                    def mmo(e, h=h, hb=hb, d=d, bi=bi):
                        for mt in range(2):
                            r = e.matmul(ps[bi][:], lhsT=V[:, mt, h * 256 + d * 128:h * 256 + (d + 1) * 128], rhs=pT[:, hb, mt, :],
                                         start=(mt == 0), stop=(mt == 1))
                        return r
                    P.pe(mmo, reads=[("V", 0, h // 2), ("V", 1, h // 2), ("pT", hb, 0), ("pT", hb, 1)], writes=[PSK(bi)])
                    P.dve(lambda e, h=h, hb=hb, d=d, bi=bi: e.tensor_tensor(oT[:, 2 * h + d, :], ps[bi][:], rs[:, hb, :], ALU.mult),
                          reads=[PSK(bi), ("rs", hb)], writes=[("oT", 2 * h + d)])

            scores(0)
            for h in range(4):
                if h < 3:
                    scores(h + 1)
                rest(h)
            if tq + 1 < 4:
                rmsnorm_tile(hT[:, 1 - tb], "hT%d" % (1 - tb), 2 + layer, t0 + 512, 512, sq, rstd, part="sq")
            linear_fm(wo, "wo", oT, "oT", 8, 8, 512, lambda m, bi, pap, t0=t0: resid_add(m, bi, pap, t0, 512))
            if tq + 1 < 4:
                rmsnorm_tile(hT[:, 1 - tb], "hT%d" % (1 - tb), 2 + layer, t0 + 512, 512, sq, rstd, part="rest")


def stage_ffn(C, b, layer, xT, ps, gains, convp, rmsnorm_tile, load_w, linear_fm, resid_add):
    nc, P, T = C.nc, C.P, C.T
    P.barrier()

    def PSK(i):
        return ("ps", i)
    with SBT(nc, "sq", [128, 8, 512], BF16) as sq, SBT(nc, "rstd", [128, 512], F32) as rstd, \
            SBT(nc, "hT", [128, 2, 8, 512], BF16) as hT, SBT(nc, "gT", [128, NF, 512], BF16) as gT, \
            SBT(nc, "wu", [128, 2, 2, 8, 512], BF16) as wu, SBT(nc, "wd", [128, 4, D], BF16) as wd, \
            SBT(nc, "ub", [128, 3, 2, 516], F32) as ub, SBT(nc, "cv", [128, 3, 2, 512], F32) as cv, \
            SBT(nc, "halo", [128, 2 * NF, 2], F32) as halo:
        wup = T["ffn_w_up"][layer].rearrange("(k p) n -> p k n", p=128)
        wdn = T["ffn_w_down"][layer]
        P.dve(lambda e: e.memset(halo[:], 0.0), writes=["halo"])
        groups = [(0, 4), (4, 4), (8, 4), (12, 4), (16, 4), (20, 2)]
        it = 0
        git = 0
        kit = 0
        rmsnorm_tile(hT[:, 0], "hT0", 4 + layer, 0, 512, sq, rstd)
        pend = []

        def tail(pb, fp):
            P.act(lambda e: e.activation(cv[:, pb, 1, :], cv[:, pb, 1, :], AF.Silu),
                  reads=[("cv", pb, 1)], writes=[("cv", pb, 1)])
            P.dve(lambda e: e.tensor_tensor(gT[:, fp, :], cv[:, pb, 0, :], cv[:, pb, 1, :], ALU.mult),
                  reads=[("cv", pb, 0), ("cv", pb, 1)], writes=[("gT", fp)])
        for tq in range(4):
            t0 = tq * 512
            hb = tq % 2
            for (f0, nf) in groups:
                wb = git % 2
                git += 1
                for vg in range(2):
                    col0 = vg * DFF + f0 * 128
                    P.dma(lambda e, wb=wb, vg=vg, col0=col0, nf=nf: e.dma_start(out=wu[:, wb, vg, :, 0:nf * 128],
                                                                              in_=wup[:, :, col0:col0 + nf * 128]),
                          writes=[("wu", wb, vg)], q="pool")
                for fl in range(nf):
                    fp = f0 + fl
                    pb = it % 3
                    it += 1
                    for vg in range(2):
                        bi = 3 * vg + pb
                        f = vg * NF + fp
                        def mm(e, wb=wb, vg=vg, bi=bi, fl=fl, hb=hb):
                            for k in range(8):
                                r = e.matmul(ps[bi][:], lhsT=wu[:, wb, vg, k, fl * 128:(fl + 1) * 128], rhs=hT[:, hb, k, :], start=(k == 0), stop=(k == 7))
                            return r
                        P.pe(mm, reads=[("wu", wb, vg)] + [("hT%d" % hb, k) for k in range(8)], writes=[PSK(bi)])
                        P.act(lambda e, pb=pb, vg=vg, bi=bi: e.activation(ub[:, pb, vg, 2:514], ps[bi][:], AF.Copy),
                              reads=[PSK(bi)], writes=[("ub", pb, vg)])
                        P.act(lambda e, pb=pb, vg=vg, f=f: e.activation(ub[:, pb, vg, 0:2], halo[:, f, :], AF.Copy),
                              reads=["halo%d" % f, "halo"], writes=[("ubh", pb, vg)])
                        P.act(lambda e, pb=pb, vg=vg, f=f, bi=bi: e.activation(cv[:, pb, vg, :], ps[bi][:], AF.Identity,
                                                                               bias=convp[:, layer, 3, f:f + 1],
                                                                               scale=convp[:, layer, 2, f:f + 1]),
                              reads=[PSK(bi), "convp"], writes=[("cv", pb, vg)])
                        P.dve(lambda e, pb=pb, vg=vg, f=f: e.scalar_tensor_tensor(cv[:, pb, vg, :], ub[:, pb, vg, 1:513],
                                                                                  convp[:, layer, 1, f:f + 1], cv[:, pb, vg, :],
                                                                                  ALU.mult, ALU.add),
                              reads=[("ub", pb, vg), ("ubh", pb, vg), ("cv", pb, vg), "convp"], writes=[("cv", pb, vg)])
                        P.dve(lambda e, pb=pb, vg=vg, f=f: e.scalar_tensor_tensor(cv[:, pb, vg, :], ub[:, pb, vg, 0:512],
                                                                                  convp[:, layer, 0, f:f + 1], cv[:, pb, vg, :],
                                                                                  ALU.mult, ALU.add),
                              reads=[("ub", pb, vg), ("ubh", pb, vg), ("cv", pb, vg), "convp"], writes=[("cv", pb, vg)])
                        P.dve(lambda e, pb=pb, vg=vg, f=f: e.tensor_copy(halo[:, f, :], ub[:, pb, vg, 512:514]),
                              reads=[("ub", pb, vg)], writes=["halo%d" % f])
                    if pend:
                        tail(*pend.pop())
                    pend.append((pb, fp))
            if pend:
                tail(*pend.pop())
            if tq + 1 < 4:
                rmsnorm_tile(hT[:, 1 - hb], "hT%d" % (1 - hb), 4 + layer, t0 + 512, 512, sq, rstd, part="sq")
            for k in range(NF):
                db = kit % 4
                kit += 1
                P.dma(lambda e, db=db, k=k: e.dma_start(out=wd[:, db, :], in_=wdn[k * 128:(k + 1) * 128, :]),
                      writes=[("wd", db)], q="pool")
                def mm(e, db=db, k=k):
                    for m in range(8):
                        r = e.matmul(ps[m][:], lhsT=wd[:, db, m * 128:(m + 1) * 128], rhs=gT[:, k, :], start=(k == 0), stop=(k == NF - 1))
                    return r
                P.pe(mm, reads=[("wd", db), ("gT", k)], writes=[PSK(m) for m in range(8)])
            resid_add(7, 7, ps[7][:], t0, 512)
            if tq + 1 < 4:
                rmsnorm_tile(hT[:, 1 - hb], "hT%d" % (1 - hb), 4 + layer, t0 + 512, 512, sq, rstd, part="rest")
            for m in range(7):
                resid_add(m, m, ps[m][:], t0, 512)


def stage_mix_ab(C, b, xT, ps, cf, cb, gains, pscale, zer, rmsnorm_tile, load_w, linear_fm, resid_add):
    nc, P, T = C.nc, C.P, C.T
    P.barrier()
    maskstrict = cf[:, 4, :]

    def PSK(i):
        return ("ps", i)
    win = T["ab_w_in"][0]
    with SBT(nc, "hT", [128, 8, S], BF16) as hT, SBT(nc, "aT", [128, 4, S], BF16) as aT, \
            SBT(nc, "pTo", [128, 4, S], BF16) as pTo:
        with SBT(nc, "sq", [128, 8, 512], BF16) as sq, SBT(nc, "rstd", [128, 512], F32) as rstd, \
                SBT(nc, "hTt", [128, 8, 512], BF16) as hTt:
            for tq in range(4):
                rmsnorm_tile(hTt, "hTt", 0, tq * 512, 512, sq, rstd)
                for dt in range(8):
                    P.act(lambda e, dt=dt, tq=tq: e.activation(hT[:, dt, tq * 512:(tq + 1) * 512], hTt[:, dt, :], AF.Copy),
                          reads=[("hTt", dt)], writes=[("hT", dt)])
        P.barrier()
        with SBT(nc, "wu4", [128, 8, 512], BF16) as wu4, SBT(nc, "wp", [128, 4, 128], BF16) as wp, \
                SBT(nc, "uA", [128, S], F32) as uA, SBT(nc, "uB", [128, S], F32) as uB, \
                SBT(nc, "u02", [128, 2, S], F32) as u02, SBT(nc, "pb", [128, S], BF16) as pb:
            def uproj(g):
                ub_ = g % 2
                for tq in range(4):
                    bi = tq % 2
                    def mm(e, tq=tq, bi=bi, g=g):
                        for k in range(8):
                            r = e.matmul(ps[bi][:], lhsT=wu4[:, k, g * 128:(g + 1) * 128], rhs=hT[:, k, tq * 512:(tq + 1) * 512], start=(k == 0), stop=(k == 7))
                        return r
                    P.pe(mm, reads=[("wu4", k) for k in range(8)] + [("hT", k) for k in range(8)], writes=[PSK(bi)])
                    P.act(lambda e, tq=tq, bi=bi, ub_=ub_: e.activation(u02[:, ub_, tq * 512:(tq + 1) * 512], ps[bi][:], AF.Copy),
                          reads=[PSK(bi)], writes=[("u0", ub_)])
            load_w(wu4, win, "wu4", 8, 1536, 512)
            uproj(0)
            for g in range(4):
                w_ = 2 ** (g + 1)
                ub_ = g % 2
                u0 = u02[:, ub_]
                u0k = ("u0", ub_)
                P.dma(lambda e, g=g: e.dma_start(out=wp[:, g, :], in_=T["pool_w"][0, g]), writes=[("wp", g)], q="pool")
                if g < 3:
                    uproj(g + 1)
                src, srck = u0, u0k
                bufs = [(uA, "uA"), (uB, "uB")]
                for st in range(g + 1):
                    sh = 2 ** st
                    dst, dstk = bufs[st % 2]
                    def stp(e, src=src, dst=dst, sh=sh):
                        e.tensor_copy(dst[:, 0:sh], src[:, 0:sh])
                        return e.tensor_tensor(dst[:, sh:S], src[:, sh:S], src[:, 0:S - sh], ALU.add)
                    P.dve(stp, reads=[srck], writes=[dstk])
                    src, srck = dst, dstk
                def pl(e, src=src, w_=w_, u0=u0):
                    e.scalar_tensor_tensor(pb[:, w_ - 1:S], src[:, w_ - 1:S], 1.0 / w_, u0[:, w_ - 1:S], ALU.mult, ALU.subtract)
                    return e.tensor_tensor(src[:, 0:w_ - 1], src[:, 0:w_ - 1], cf[:, 8, 0:w_ - 1], ALU.mult)
                P.dve(pl, reads=[srck, u0k, "cf"], writes=["pb0", srck])
                P.dve(lambda e, src=src, w_=w_, u0=u0: e.tensor_tensor(pb[:, 0:w_ - 1], src[:, 0:w_ - 1], u0[:, 0:w_ - 1], ALU.subtract),
                      reads=[srck, u0k], writes=["pb1"])
                for tq in range(4):
                    bi = 2 + tq % 2
                    P.pe(lambda e, tq=tq, bi=bi, g=g: e.matmul(ps[bi][:], lhsT=wp[:, g, :], rhs=pb[:, tq * 512:(tq + 1) * 512], start=True, stop=True),
                         reads=[("wp", g), "pb0", "pb1"], writes=[PSK(bi)])
                    P.act(lambda e, tq=tq, bi=bi, g=g: e.activation(pTo[:, g, tq * 512:(tq + 1) * 512], ps[bi][:], AF.Identity,
                                                                    scale=pscale[:, g:g + 1]),
                          reads=[PSK(bi), "pscale"], writes=[("pTo", g)])
        P.barrier()
        NBUF = 4
        with SBT(nc, "wqkv", [128, 8, 1536], BF16) as wqkv, SBT(nc, "qh", [128, S], BF16) as qh, \
                SBT(nc, "kh", [128, S], BF16) as kh, SBT(nc, "vh", [128, 16, 128], BF16) as vh, \
                SBT(nc, "ex", [128, NBUF, 512], F32) as ex, \
                SBT(nc, "spb", [128, NBUF, 512], BF16) as spb, \
                SBT(nc, "wsb", [128, NBUF, 512], BF16) as wsb, SBT(nc, "Ls", [128, 2, 512], F32) as Ls, \
                SBT(nc, "Lsb", [128, 4, 512], BF16) as Lsb, SBT(nc, "otmp", [64, 2, 512], BF16) as otmp:
            identb = cb[:, 0, :]
            trinc = cb[:, 4, :]
            maskneg = cb[:, 5, :]
            onesneg = cb[:, 2, :]
            git = 0
            for h in range(8):
                hp = h // 2
                if h == 0:
                    load_w(wqkv, win, "wqkv", 8, 0, 1536)
                hb64 = 64 * (h % 2)
                for j3, (dst, dk, scl) in enumerate([(qh, "qh", 0.125), (kh, "kh", 1.0)]):
                    if h % 2 == 1:
                        break
                    for tq in range(4):
                        bi = 4 + tq % 2
                        def mm(e, h=h, j3=j3, tq=tq, bi=bi):
                            for k in range(8):
                                r = e.matmul(ps[bi][:, :], lhsT=wqkv[:, k, j3 * 512 + h * 64:j3 * 512 + h * 64 + 128],
                                             rhs=hT[:, k, tq * 512:(tq + 1) * 512], start=(k == 0), stop=(k == 7))
                            return r
                        P.pe(mm, reads=[("wqkv", k) for k in range(8)] + [("hT", k) for k in range(8)], writes=[PSK(bi)])
                        P.act(lambda e, dst=dst, tq=tq, bi=bi, scl=scl: e.activation(dst[:, tq * 512:(tq + 1) * 512], ps[bi][:, :],
                                                                                   AF.Copy, scale=scl),
                              reads=[PSK(bi)], writes=[(dk, tq)])
                if h % 2 == 0:
                    for t4 in range(4):
                        bi = 4 + t4 % 2
                        def mmv(e, h=h, t4=t4, bi=bi):
                            for tl in range(4):
                                tt = t4 * 4 + tl
                                for k in range(8):
                                    r = e.matmul(ps[bi][:, tl * 128:(tl + 1) * 128], lhsT=hT[:, k, tt * 128:(tt + 1) * 128],
                                                 rhs=wqkv[:, k, 1024 + h * 64:1024 + h * 64 + 128], start=(k == 0), stop=(k == 7))
                            return r
                        P.pe(mmv, reads=[("wqkv", k) for k in range(8)] + [("hT", k) for k in range(8)], writes=[PSK(bi)])
                        P.dve(lambda e, t4=t4, bi=bi: e.tensor_copy(vh[:, t4 * 4:(t4 + 1) * 4, :],
                                                                    ps[bi][:].rearrange("p (t c) -> p t c", t=4)),
                              reads=[PSK(bi)], writes=[("vh", t4)])
                its = []
                for j in range(4):
                    kbs = list(range(4 * j + 3, -1, -1))
                    for ii, kb in enumerate(kbs):
                        diag = kb >= 4 * j
                        qlo = 128 * (kb - 4 * j) if diag else 0
                        its.append(dict(j=j, kb=kb, first=(ii == 0), last=(kb == 0), diag=diag, qlo=qlo, g=git))
                        git += 1

                def zmm(e, dst, it_, stop_after, hb64=hb64):
                    kb, qlo, j = it_["kb"], it_["qlo"], it_["j"]
                    q0 = 512 * j + qlo
                    r = e.matmul(dst[:, qlo:512], lhsT=kh[hb64:hb64 + 64, kb * 128:(kb + 1) * 128], rhs=qh[hb64:hb64 + 64, q0:512 * (j + 1)],
                                 start=True, stop=(stop_after and not it_["diag"]))
                    if it_["diag"]:
                        r = e.matmul(dst[:, qlo:qlo + 128], lhsT=identb, rhs=maskneg, start=False, stop=stop_after)
                    return r

                def stageA(it_):
                    r_ = it_["g"] % NBUF
                    j, kb, qlo = it_["j"], it_["kb"], it_["qlo"]
                    lb = j % 2
                    if it_["first"]:
                        P.dve(lambda e, lb=lb: e.memset(Ls[:, lb, :], 0.0), writes=[("Ls", lb)])
                    P.pe(lambda e, it_=it_, r_=r_, zmm=zmm: zmm(e, ps[r_], it_, True),
                         reads=[("kh", kb // 4), ("qh", j), "cb"], writes=[PSK(r_)])
                    P.act(lambda e, r_=r_, qlo=qlo: e.activation(ex[:, r_, qlo:512], ps[r_][:, qlo:512], AF.Exp),
                          reads=[PSK(r_)], writes=[("ex", r_)])
                    P.act(lambda e, r_=r_, qlo=qlo: e.activation(spb[:, r_, qlo:512], ex[:, r_, qlo:512], AF.Ln, bias=1.0),
                          reads=[("ex", r_)], writes=[("spb", r_)])
                    if not it_["last"]:
                        nqlo = max(0, 128 * (kb - 1 - 4 * j))
                        nr = (it_["g"] + 1) % 4
                        P.dve(lambda e, r_=r_, qlo=qlo, lb=lb: e.tensor_tensor(Ls[:, lb, qlo:512], Ls[:, lb, qlo:512], spb[:, r_, qlo:512], ALU.add),
                              reads=[("Ls", lb), ("spb", r_)], writes=[("Ls", lb)])
                        P.dve(lambda e, nqlo=nqlo, nr=nr, lb=lb: e.tensor_copy(Lsb[:, nr, nqlo:512], Ls[:, lb, nqlo:512]),
                              reads=[("Ls", lb)], writes=[("Lsb", nr)])

                def stageB1(it_):
                    r_ = it_["g"] % NBUF
                    qlo = it_["qlo"]
                    pst = ps[r_]
                    def mmt(e, it_=it_, r_=r_, pst=pst, qlo=qlo):
                        r = e.matmul(pst[:, qlo:512], lhsT=trinc, rhs=spb[:, r_, qlo:512], start=False, stop=it_["first"])
                        if not it_["first"]:
                            r = e.matmul(pst[:, qlo:512], lhsT=onesneg, rhs=Lsb[:, it_["g"] % 4, qlo:512], start=False, stop=True)
                        return r
                    P.pe(mmt, reads=["cb", ("spb", r_), ("Lsb", it_["g"] % 4)], writes=[PSK(r_)])
                    P.act(lambda e, r_=r_, pst=pst, qlo=qlo: e.activation(wsb[:, r_, qlo:512], pst[:, qlo:512], AF.Exp),
                          reads=[PSK(r_)], writes=[("wsb", r_)])

                def stageB2(it_):
                    r_ = it_["g"] % NBUF
                    j, kb, qlo = it_["j"], it_["kb"], it_["qlo"]
                    ob = j % 2
                    pso = ps[6 + ob]
                    if it_["first"]:
                        P.pe(lambda e, pso=pso: e.matmul(pso[:, :], lhsT=zer[0:1, 0:128], rhs=zer[0:1, 0:512], start=True, stop=False),
                             reads=["zer"], writes=[PSK(6 + ob)])
                    P.pe(lambda e, pso=pso, kb=kb, r_=r_, qlo=qlo, last=it_["last"]: e.matmul(
                        pso[:, qlo:512], lhsT=vh[:, kb, :], rhs=wsb[:, r_, qlo:512], start=False, stop=last),
                        reads=[("vh", kb // 4), ("wsb", r_)], writes=[PSK(6 + ob)])
                    if it_["last"]:
                        if h % 2 == 0:
                            P.dve(lambda e, pso=pso, j=j, hp=hp: e.tensor_copy(aT[0:64, hp, 512 * j:512 * (j + 1)], pso[0:64, :]),
                                  reads=[PSK(6 + ob)], writes=[("aT", hp, 0)])
                        else:
                            P.dve(lambda e, pso=pso, j=j, hp=hp: e.tensor_copy(aT[64:128, hp, 512 * j:512 * (j + 1)], pso[64:128, :]),
                                  reads=[PSK(6 + ob)], writes=[("aT", hp, 1)])

                n_it = len(its)
                for i in range(n_it + 2):
                    if i < n_it:
                        stageA(its[i])
                    if 1 <= i <= n_it:
                        stageB1(its[i - 1])
                    if i >= 2:
                        stageB2(its[i - 2])
        P.barrier()
        with SBT(nc, "wout", [128, 8, D], BF16) as wout:
            load_w(wout, T["ab_w_out"][0], "wout", 8, 0, D)
            for tq in range(4):
                t0 = tq * 512
                for m in range(8):
                    bi = m % 2
                    def mm(e, m=m, bi=bi, t0=t0):
                        for k in range(8):
                            src = aT if k < 4 else pTo
                            r = e.matmul(ps[bi][:], lhsT=wout[:, k, m * 128:(m + 1) * 128], rhs=src[:, k % 4, t0:t0 + 512],
                                         start=(k == 0), stop=(k == 7))
                        return r
                    P.pe(mm, reads=[("wout", k) for k in range(8)] + ["aTall"], writes=[PSK(bi)])
                    resid_add(m, bi, ps[bi][:], t0, 512)


def stage_mix_s5(C, b, xT, ps, cf, cb, gains, rmsnorm_tile, load_w, linear_fm, resid_add):
    from contextlib import ExitStack
    nc, P, T = C.nc, C.P, C.T
    P.barrier()

    def PSK(i):
        return ("ps", i)
    ident = cf[:, 0, :]
    mask32 = cf[:, 5, :]
    TWO_PI = 6.283185
    INV2PI = 1.0 / (2.0 * math.pi)

    def V(fn, r, w):
        return P.dve(fn, reads=r, writes=w)

    def Aop(fn, r, w):
        return P.act(fn, reads=r, writes=w)

    with SBT(nc, "uT", [128, 8, S], BF16) as uT, SBT(nc, "dcol", [128, 8], F32) as dcol:
        with SBT(nc, "w_in", [128, 8, D], BF16) as w_in, SBT(nc, "sq", [128, 8, 512], BF16) as sq, \
                SBT(nc, "rstd", [128, 512], F32) as rstd, SBT(nc, "hT", [128, 8, 512], BF16) as hT:
            load_w(w_in, T["ssm_w_in"][0], "w_in", 8, 0, D)
            P.dma(lambda e: e.dma_start(out=dcol[:], in_=T["ssm_d"][0].rearrange("(t p) -> p t", p=128),
                                        allow_slow_non_contiguous=True), writes=["dcol"])
            for tq in range(4):
                rmsnorm_tile(hT, "hT", 1, tq * 512, 512, sq, rstd)

                def ev(m, bi, pap, tq=tq):
                    P.act(lambda e: e.activation(uT[:, m, tq * 512:(tq + 1) * 512], pap, AF.Copy),
                          reads=[PSK(bi)], writes=[("uT", m)])
                linear_fm(w_in, "w_in", hT, "hT", 8, 8, 512, ev)
        P.barrier()
        with ExitStack() as es:
            def A(name, shape, dt=F32):
                return es.enter_context(SBT(nc, name, shape, dt))
            lre = A("lre", [128, 4]); lim = A("lim", [128, 4]); ldt = A("ldt", [128, 4])
            dtt = A("dtt", [128, 4]); ar = A("ar", [128, 4]); an = A("an", [128, 4])
            arj = A("arj", [128, 9, 4]); tj = A("tj", [128, 9, 4]); tjc = A("tjc", [128, 9, 4])
            ti = A("ti", [128, 9, 4], I32); fr = A("fr", [128, 9, 4])
            mag = A("mag", [128, 9, 4]); sinj = A("sinj", [128, 9, 4]); cosj = A("cosj", [128, 9, 4])
            Lr = A("Lr", [128, 9, 4]); Li = A("Li", [128, 9, 4])
            nre = A("nre", [128, 4]); den = A("den", [128, 4]); t1 = A("t1", [128, 4]); t2 = A("t2", [128, 4])
            cr = A("cr", [128, 4]); ci = A("ci", [128, 4]); ti8 = A("ti8", [128, 4], I32); t8f = A("t8f", [128, 4])
            Fr = A("Fr", [128, 8, 4]); Fi = A("Fi", [128, 8, 4]); f1 = A("f1", [128, 8, 4]); f2 = A("f2", [128, 8, 4])
            Bst = A("Bst", [128, 2, 4, 16]); Cin = A("Cin", [64, 2, 2, 64]); Cst = A("Cst", [128, 2, 4, 16])
            l1 = A("l1", [128, 9, 4, 16]); l2 = A("l2", [128, 9, 4, 16])
            What = A("What", [128, 8, 2, 128])
            Wt = A("Wt", [128, 8, 2, 128], BF16)
            CL = A("CL", [128, 2, 9, 4, 16])
            LB = CL[:, :, 0:8]
            Qd = A("Qd", [128, 9, 2, 4, 32], BF16)
            Qf = A("Qf", [128, 2, 128])
            TtF = A("TtF", [128, 4, 128]); Tt = A("Tt", [128, 8, 128], BF16)
            cosT = A("cosT", [128, 4, 256]); sinT = A("sinT", [128, 4, 256])
            Xp = A("Xp", [128, 2, 4, 256]); xa = A("xa", [128, 4, 256]); xb = A("xb", [128, 4, 256])
            Ssc = A("Ssc", [128, 2, 4, 256]); tk = Ssc[:, 0]; tki = Ssc[:, 1].bitcast(I32); Hb = A("Hb", [128, 2, 4, 257], BF16)
            iota256 = cf[:, 6:8, :].rearrange("p a b -> p (a b)")
            What6 = What[:].rearrange("p t r (q g c) -> p t r q g c", q=4, g=2)
            Qf5 = Qf[:].rearrange("p r (q g c) -> p r q g c", q=4, g=2)
            Wv = Wt[:].rearrange("p t r n -> p (t r) n")
            V(lambda e: e.memset(What[:], 0.0), [], ["What"])
            V(lambda e: e.memset(Qd[:], 0.0), [], ["Qpad"])
            V(lambda e: e.memset(Qf[:], 0.0), [], ["Qf"])
            V(lambda e: e.memset(Hb[:], 0.0), [], ["Hb"])

            def bc(ap, shape):
                return ap.broadcast_to(shape)

            def partA(j):
                g0 = 8 * j
                P.dma(lambda e, g0=g0: e.dma_start(out=lre[:], in_=T["ssm_lam_re"][0, g0:g0 + 8, :].rearrange("(q g) p -> (g p) q", g=2),
                                                   allow_slow_non_contiguous=True), writes=["lre"])
                P.dma(lambda e, g0=g0: e.dma_start(out=lim[:], in_=T["ssm_lam_im"][0, g0:g0 + 8, :].rearrange("(q g) p -> (g p) q", g=2),
                                                   allow_slow_non_contiguous=True), writes=["lim"])
                for g2 in range(2):
                    P.dma(lambda e, g0=g0, g2=g2: e.dma_start(
                        out=ldt[64 * g2:64 * g2 + 64, :],
                        in_=T["ssm_log_dt"][0, g0:g0 + 8].rearrange("(q g) -> g q", g=2)[g2:g2 + 1, :].broadcast_to([64, 4]),
                        allow_slow_non_contiguous=True), writes=["ldt"])
                for ri, nm in enumerate(["ssm_b_re", "ssm_b_im"]):
                    P.dma(lambda e, g0=g0, ri=ri, nm=nm: e.dma_start(
                        out=Bst[:, ri, :, :], in_=T[nm][0, g0:g0 + 8].rearrange("(q g) p c -> (g p) q c", g=2)),
                        writes=["Bst"])
                for ri, nm in enumerate(["ssm_c_re", "ssm_c_im"]):
                    for q in range(4):
                        P.dma(lambda e, g0=g0, ri=ri, nm=nm, q=q: e.dma_start(
                            out=Cin[16 * q:16 * q + 16, ri, :, :],
                            in_=T[nm][0, g0 + 2 * q:g0 + 2 * q + 2].rearrange("g c p -> c g p")), writes=["Cin"])
                def trc(e):
                    for ri in range(2):
                        r = e.transpose(ps[0][:, ri * 64:(ri + 1) * 64], Cin[:, ri, :, :].rearrange("a g p -> a (g p)"), ident[0:64, 0:64])
                    return r
                P.pe(trc, reads=["Cin", "cf"], writes=[PSK(0)])
                Aop(lambda e: e.activation(Cst[:].rearrange("p r q c -> p (r q c)"), ps[0][:, 0:128], AF.Copy), [PSK(0)], ["Cst"])
                Aop(lambda e: e.activation(dtt[:], ldt[:], AF.Exp), ["ldt"], ["dtt"])
                def f_(e):
                    e.tensor_tensor(ar[:], lre[:], dtt[:], ALU.mult)
                    return e.tensor_tensor(an[:], lim[:], dtt[:], ALU.mult)
                V(f_, ["lre", "lim", "dtt"], ["ar", "an"])
                jv = bc(cf[:, 6, 0:9][:, :, None], [128, 9, 4])
                def f_(e):
                    e.tensor_tensor(arj[:], bc(ar[:, None, :], [128, 9, 4]), jv, ALU.mult)
                    return e.scalar_tensor_tensor(tj[:], bc(an[:, None, :], [128, 9, 4]), INV2PI, jv, ALU.mult, ALU.mult)
                V(f_, ["ar", "an", "cf"], ["arj", "tj"])
                Aop(lambda e: e.activation(mag[:], arj[:], AF.Exp), ["arj"], ["mag"])
                V(lambda e: e.tensor_copy(ti[:], tj[:]), ["tj"], ["ti"])
                V(lambda e: e.tensor_tensor(fr[:], tj[:], ti[:], ALU.subtract), ["tj", "ti"], ["fr"])
                Aop(lambda e: e.activation(sinj[:], fr[:], AF.Sin, scale=TWO_PI), ["fr"], ["sinj"])
                V(lambda e: e.tensor_scalar(tjc[:], tj[:], 0.25, None, ALU.add), ["tj"], ["tjc"])
                V(lambda e: e.tensor_copy(ti[:], tjc[:]), ["tjc"], ["ti"])
                V(lambda e: e.tensor_tensor(fr[:], tjc[:], ti[:], ALU.subtract), ["tjc", "ti"], ["fr"])
                Aop(lambda e: e.activation(cosj[:], fr[:], AF.Sin, scale=TWO_PI), ["fr"], ["cosj"])
                def f_(e):
                    e.tensor_tensor(Lr[:], mag[:], cosj[:], ALU.mult)
                    return e.tensor_tensor(Li[:], mag[:], sinj[:], ALU.mult)
                V(f_, ["mag", "cosj", "sinj"], ["Lr", "Li"])
                def f_(e):
                    e.tensor_scalar(nre[:], Lr[:, 1, :], -1.0, None, ALU.add)
                    e.tensor_tensor(t1[:], lre[:], lre[:], ALU.mult)
                    return e.tensor_tensor(t2[:], lim[:], lim[:], ALU.mult)
                V(f_, ["Lr", "lre", "lim"], ["nre", "t1", "t2"])
                V(lambda e: e.tensor_tensor(den[:], t1[:], t2[:], ALU.add), ["t1", "t2"], ["den"])
                V(lambda e: e.reciprocal(den[:], den[:]), ["den"], ["den"])
                def f_(e):
                    e.tensor_tensor(t1[:], nre[:], lre[:], ALU.mult)
                    return e.tensor_tensor(t2[:], Li[:, 1, :], lim[:], ALU.mult)
                V(f_, ["nre", "lre", "Li", "lim", "den"], ["t1", "t2"])
                V(lambda e: e.tensor_tensor(cr[:], t1[:], t2[:], ALU.add), ["t1", "t2"], ["cr"])
                V(lambda e: e.tensor_tensor(cr[:], cr[:], den[:], ALU.mult), ["cr", "den"], ["cr"])
                def f_(e):
                    e.tensor_tensor(t1[:], Li[:, 1, :], lre[:], ALU.mult)
                    return e.tensor_tensor(t2[:], nre[:], lim[:], ALU.mult)
                V(f_, ["nre", "lre", "Li", "lim", "cr"], ["t1", "t2"])
                V(lambda e: e.tensor_tensor(ci[:], t1[:], t2[:], ALU.subtract), ["t1", "t2"], ["ci"])
                V(lambda e: e.tensor_tensor(ci[:], ci[:], den[:], ALU.mult), ["ci", "den"], ["ci"])
                crb = bc(cr[:, None, :], [128, 8, 4]); cib = bc(ci[:, None, :], [128, 8, 4])
                def f_(e, crb=crb, cib=cib):
                    e.tensor_tensor(f1[:], Lr[:, 0:8, :], crb, ALU.mult)
                    return e.tensor_tensor(f2[:], Li[:, 0:8, :], cib, ALU.mult)
                V(f_, ["Lr", "Li", "cr", "ci"], ["f1", "f2"])
                V(lambda e: e.tensor_tensor(Fr[:], f1[:], f2[:], ALU.subtract), ["f1", "f2"], ["Fr"])
                def f_(e, crb=crb, cib=cib):
                    e.tensor_tensor(f1[:], Lr[:, 0:8, :], cib, ALU.mult)
                    return e.tensor_tensor(f2[:], Li[:, 0:8, :], crb, ALU.mult)
                V(f_, ["Lr", "Li", "cr", "ci", "Fr"], ["f1", "f2"])
                V(lambda e: e.tensor_tensor(Fi[:], f1[:], f2[:], ALU.add), ["f1", "f2"], ["Fi"])
                sh8 = [128, 8, 4, 16]
                Frb = bc(Fr[:, :, :, None], sh8); Fib = bc(Fi[:, :, :, None], sh8)
                B0 = bc(Bst[:, 0, None, :, :], sh8); B1 = bc(Bst[:, 1, None, :, :], sh8)
                def f_(e, Frb=Frb, Fib=Fib, B0=B0, B1=B1):
                    e.tensor_tensor(l1[:, 0:8], Frb, B0, ALU.mult)
                    return e.tensor_tensor(l2[:, 0:8], Fib, B1, ALU.mult)
                V(f_, ["Fr", "Fi", "Bst"], ["l1", "l2"])
                V(lambda e: e.tensor_tensor(LB[:, 0], l1[:, 0:8], l2[:, 0:8], ALU.subtract), ["l1", "l2"], ["LB0", "CL0"])
                def f_(e, Frb=Frb, Fib=Fib, B0=B0, B1=B1):
                    e.tensor_tensor(l1[:, 0:8], Frb, B1, ALU.mult)
                    return e.tensor_tensor(l2[:, 0:8], Fib, B0, ALU.mult)
                V(f_, ["Fr", "Fi", "Bst", "LB0"], ["l1", "l2"])
                V(lambda e: e.tensor_tensor(LB[:, 1], l1[:, 0:8], l2[:, 0:8], ALU.add), ["l1", "l2"], ["LB1", "CL1"])
                def f_(e):
                    for g2 in range(2):
                        for ri in range(2):
                            r = e.tensor_copy(What6[64 * g2:64 * g2 + 64, :, ri, :, g2, :], LB[64 * g2:64 * g2 + 64, ri, :, :, :])
                    return r
                V(f_, ["LB0", "LB1"], ["What"])
            def partB(j):
                g0 = 8 * j
                for c4 in range(4):
                    bi = c4 % 2
                    def trw(e, c4=c4, bi=bi):
                        for i4 in range(4):
                            c = c4 * 4 + i4
                            r = e.transpose(ps[bi][:, i4 * 128:(i4 + 1) * 128], What[:, c // 2, c % 2, :], ident)
                        return r
                    P.pe(trw, reads=["What", "cf"], writes=[PSK(bi)])
                    if c4 % 2 == 0:
                        Aop(lambda e, c4=c4, bi=bi: e.activation(Wv[:, c4 * 4:c4 * 4 + 4, :], ps[bi][:].rearrange("p (a n) -> p a n", a=4), AF.Copy),
                            [PSK(bi)], [("Wpad", c4)])
                    else:
                        V(lambda e, c4=c4, bi=bi: e.tensor_copy(Wv[:, c4 * 4:c4 * 4 + 4, :], ps[bi][:].rearrange("p (a n) -> p a n", a=4)),
                          [PSK(bi)], [("Wpad", c4)])
                sh9 = [128, 9, 4, 16]
                Lrb = bc(Lr[:, :, :, None], sh9); Lib = bc(Li[:, :, :, None], sh9)
                C0 = bc(Cst[:, 0, None, :, :], sh9); C1 = bc(Cst[:, 1, None, :, :], sh9)
                def f_(e, Lrb=Lrb, Lib=Lib, C0=C0, C1=C1):
                    e.tensor_tensor(l1[:], Lrb, C0, ALU.mult)
                    return e.tensor_tensor(l2[:], Lib, C1, ALU.mult)
                V(f_, ["Lr", "Li", "Cst", "LB1"], ["l1", "l2"])
                V(lambda e: e.tensor_tensor(CL[:, 0], l1[:], l2[:], ALU.subtract), ["l1", "l2"], ["CL0", "LB0", "LB1"])
                def f_(e, Lrb=Lrb, Lib=Lib, C0=C0, C1=C1):
                    e.tensor_tensor(l1[:], Lib, C0, ALU.mult)
                    return e.tensor_tensor(l2[:], Lrb, C1, ALU.mult)
                V(f_, ["Lr", "Li", "Cst", "CL0"], ["l1", "l2"])
                V(lambda e: e.scalar_tensor_tensor(CL[:, 1], l1[:], -1.0, l2[:], ALU.mult, ALU.subtract), ["l1", "l2"], ["CL1", "LB0", "LB1"])
                def f_(e):
                    for g2 in range(2):
                        for ri in range(2):
                            e.tensor_copy(Qd[64 * g2:64 * g2 + 64, :, ri, :, 16 * g2:16 * g2 + 16], CL[64 * g2:64 * g2 + 64, ri, :, :, :])
                            r = e.tensor_copy(Qf5[64 * g2:64 * g2 + 64, ri, :, g2, :], CL[64 * g2:64 * g2 + 64, ri, 0, :, :])
                    return r
                V(f_, ["CL0", "CL1"], ["Qpad", "Qf"])
                for half in range(2):
                    bi = half
                    def mmt(e, half=half, bi=bi):
                        for i4 in range(4):
                            tau = half * 4 + i4
                            e.matmul(ps[bi][:, i4 * 128:(i4 + 1) * 128], lhsT=What[:, tau, 0, :], rhs=Qf[:, 0, :], start=True, stop=False)
                            r = e.matmul(ps[bi][:, i4 * 128:(i4 + 1) * 128], lhsT=What[:, tau, 1, :], rhs=Qf[:, 1, :], start=False, stop=True)
                        return r
                    P.pe(mmt, reads=["What", "Qf"], writes=[PSK(bi)])
                    m4 = bc(mask32[:, None, :], [128, 4, 128])
                    if half == 0:
                        V(lambda e, bi=bi, m4=m4: e.tensor_tensor(TtF[:], ps[bi][:].rearrange("p (a n) -> p a n", a=4), m4, ALU.mult),
                          [PSK(bi), "cf"], ["TtF"])
                        V(lambda e, j=j: e.scalar_tensor_tensor(TtF[:, 0, :], ident, dcol[:, j:j + 1], TtF[:, 0, :], ALU.mult, ALU.add),
                          ["TtF", "dcol", "cf"], ["TtF"])
                        V(lambda e: e.tensor_copy(Tt[:, 0:4, :], TtF[:]), ["TtF"], [("Tt", 0)])
                    else:
                        V(lambda e, bi=bi, m4=m4: e.tensor_tensor(Tt[:, 4:8, :], ps[bi][:].rearrange("p (a n) -> p a n", a=4), m4, ALU.mult),
                          [PSK(bi), "cf"], [("Tt", 1)])
                uv = uT[:, j, :].rearrange("p (k s) -> p s k", s=8)
                def mmx(e, uv=uv):
                    for ri in range(2):
                        for tau in range(8):
                            for q in range(4):
                                r = e.matmul(ps[2 + q][:, ri * 256:(ri + 1) * 256], lhsT=Wt[32 * q:32 * q + 32, tau, ri, :],
                                             rhs=uv[32 * q:32 * q + 32, 7 - tau, :], start=(tau == 0), stop=(tau == 7),
                                             tile_position=(32 * q, 0))
                    return r
                P.pe(mmx, reads=[("Wpad", c4) for c4 in range(4)] + [("uT", j, s_) for s_ in range(8)], writes=[PSK(2 + q) for q in range(4)])
                V(lambda e: e.tensor_copy(ti8[:], tj[:, 8, :]), ["tj"], ["ti8"])
                V(lambda e: e.tensor_tensor(t8f[:], tj[:, 8, :], ti8[:], ALU.subtract), ["tj", "ti8"], ["t8f"])
                V(lambda e: e.tensor_tensor(tk, bc(t8f[:, :, None], [128, 4, 256]), bc(iota256[:, None, :], [128, 4, 256]), ALU.mult),
                  ["t8f", "cf"], ["tk"] + [("Ssc", ri_, q_) for ri_ in range(2) for q_ in range(4)])
                V(lambda e: e.tensor_copy(tki, tk), ["tk"], ["tki"])
                V(lambda e: e.tensor_tensor(xa[:], tk, tki, ALU.subtract), ["tk", "tki"], ["xa"])
                Aop(lambda e: e.activation(sinT[:], xa[:], AF.Sin, scale=TWO_PI), ["xa"], ["sinT"])
                V(lambda e: e.tensor_scalar(tk, tk, 0.25, None, ALU.add), ["tk", "tki"], ["tk"])
                V(lambda e: e.tensor_copy(tki, tk), ["tk"], ["tki"])
                V(lambda e: e.tensor_tensor(xb[:], tk, tki, ALU.subtract), ["tk", "tki"], ["xb"])
                Aop(lambda e: e.activation(cosT[:], xb[:], AF.Sin, scale=TWO_PI), ["xb"], ["cosT"])
                for q in range(4):
                    Xr = ps[2 + q][:, 0:256]; Xi = ps[2 + q][:, 256:512]
                    def f_(e, q=q, Xr=Xr, Xi=Xi):
                        e.tensor_tensor(xa[:, q, :], cosT[:, q, :], Xr, ALU.mult)
                        return e.tensor_tensor(xb[:, q, :], sinT[:, q, :], Xi, ALU.mult)
                    V(f_, [PSK(2 + q), "cosT", "sinT", "xa", "xb"], [("xa", q), ("xb", q)])
                    V(lambda e, q=q: e.tensor_tensor(Xp[:, 0, q, :], xa[:, q, :], xb[:, q, :], ALU.add), [("xa", q), ("xb", q)], [("Xp", 0, q)])
                    def f_(e, q=q, Xr=Xr, Xi=Xi):
                        e.tensor_tensor(xa[:, q, :], cosT[:, q, :], Xi, ALU.mult)
                        return e.tensor_tensor(xb[:, q, :], sinT[:, q, :], Xr, ALU.mult)
                    V(f_, [PSK(2 + q), "cosT", "sinT", ("Xp", 0, q)], [("xa", q), ("xb", q)])
                    V(lambda e, q=q: e.tensor_tensor(Xp[:, 1, q, :], xa[:, q, :], xb[:, q, :], ALU.subtract), [("xa", q), ("xb", q)], [("Xp", 1, q)])
                    for ri in range(2):
                        V(lambda e, q=q, ri=ri: e.tensor_tensor_scan(Ssc[:, ri, q, :], mag[:, 8, q:q + 1].to_broadcast([128, 256]),
                                                                     Xp[:, ri, q, :], 0.0, ALU.mult, ALU.add),
                          [("Xp", ri, q), "mag"], [("Ssc", ri, q), "tk", "tki"])
                allS = [("Ssc", ri, q) for ri in range(2) for q in range(4)]
                allx = [("xa", q) for q in range(4)] + [("xb", q) for q in range(4)]
                def f_(e):
                    e.tensor_tensor(xa[:], cosT[:], Ssc[:, 0], ALU.mult)
                    return e.tensor_tensor(xb[:], sinT[:], Ssc[:, 1], ALU.mult)
                V(f_, allS + ["cosT", "sinT"], allx + ["xa", "xb"])
                V(lambda e: e.tensor_tensor(Hb[:, 0, :, 1:257], xa[:], xb[:], ALU.subtract), ["xa", "xb"], ["Hb0"])
                def f_(e):
                    e.tensor_tensor(xa[:], cosT[:], Ssc[:, 1], ALU.mult)
                    return e.tensor_tensor(xb[:], sinT[:], Ssc[:, 0], ALU.mult)
                V(f_, allS + ["cosT", "sinT", "Hb0"], allx + ["xa", "xb"])
                V(lambda e: e.tensor_tensor(Hb[:, 1, :, 1:257], xa[:], xb[:], ALU.add), ["xa", "xb"], ["Hb1"])
            def partC(j):
                uv = uT[:, j, :].rearrange("p (k s) -> p s k", s=8)
                for tp in range(7, -1, -1):
                    bi = 6 + (tp % 2)
                    def mmy(e, tp=tp, bi=bi, uv=uv):
                        for s_ in range(tp + 1):
                            e.matmul(ps[bi][:, 0:256], lhsT=Tt[:, tp - s_, :], rhs=uv[:, s_, :], start=(s_ == 0), stop=False)
                        for ri in range(2):
                            for q in range(4):
                                r = e.matmul(ps[bi][32 * q:32 * q + 32, 0:256], lhsT=Qd[:, tp + 1, ri, q, :], rhs=Hb[:, ri, q, 0:256], start=False,
                                             stop=(ri == 1), tile_position=(0, 32 * q))
                        return r
                    P.pe(mmy, reads=[("uT", j, s_) for s_ in range(tp + 1)] + [("Tt", 0), ("Tt", 1), "Qpad", "Hb0", "Hb1"], writes=[PSK(bi)])
                    Aop(lambda e, tp=tp, bi=bi, uv=uv: e.activation(uv[:, tp, :], ps[bi][:, 0:256], AF.Gelu_apprx_tanh),
                        [PSK(bi)], [("uT", j, tp)])
            partA(0)
            for j in range(8):
                partB(j)
                if j < 7:
                    partA(j + 1)
                partC(j)
        P.barrier()
        with SBT(nc, "wg", [128, 2, 2, 8, 512], BF16) as wg, SBT(nc, "sig", [128, 2, 512], F32) as sig, \
                SBT(nc, "mixb", [128, 2, 512], F32) as mixb:
            wglu = T["ssm_w_glu"][0].rearrange("(k p) n -> p k n", p=128)
            it = 0
            for mg in range(2):
                wb = mg % 2
                for vg in range(2):
                    c0 = vg * D + mg * 512
                    P.dma(lambda e, wb=wb, vg=vg, c0=c0: e.dma_start(out=wg[:, wb, vg, :, :], in_=wglu[:, :, c0:c0 + 512]),
                          writes=[("wg", wb, vg)], q="pool")
                for ml in range(4):
                    m = mg * 4 + ml
                    for tq in range(4):
                        pb = it % 2
                        it += 1
                        for vg in range(2):
                            bi = 2 + 2 * vg + pb
                            def mm(e, wb=wb, vg=vg, bi=bi, tq=tq, ml=ml):
                                for k in range(8):
                                    r = e.matmul(ps[bi][:], lhsT=wg[:, wb, vg, k, ml * 128:(ml + 1) * 128], rhs=uT[:, k, tq * 512:(tq + 1) * 512],
                                                 start=(k == 0), stop=(k == 7))
                                return r
                            P.pe(mm, reads=[("wg", wb, vg)], writes=[PSK(bi)])
                        Aop(lambda e, pb=pb: e.activation(sig[:, pb, :], ps[4 + pb][:], AF.Sigmoid), [PSK(4 + pb)], [("sig", pb)])
                        V(lambda e, pb=pb: e.tensor_tensor(mixb[:, pb, :], ps[2 + pb][:], sig[:, pb, :], ALU.mult), [PSK(2 + pb), ("sig", pb)], [("mixb", pb)])
                        V(lambda e, pb=pb, m=m, tq=tq: e.tensor_tensor(xT[:, m, tq * 512:(tq + 1) * 512], xT[:, m, tq * 512:(tq + 1) * 512],
                                                                     mixb[:, pb, :], ALU.add), [("mixb", pb), ("xT", m)], [("xT", m)])


_CACHE = {}


def kernel(**inputs):
    if "prog" not in _CACHE:
        _CACHE["prog"] = build_program()
    nc, _ = _CACHE["prog"]
    consts = make_consts()
    in_maps = []
    for c in range(8):
        m = {}
        for name, shape in INPUT_SPECS:
            if name == "consts":
                m[name] = consts
            elif name in ("x", "mem"):
                m[name] = np.ascontiguousarray(np.asarray(inputs[name], dtype=np.float32)[c * NB:(c + 1) * NB])
            else:
                m[name] = np.ascontiguousarray(np.asarray(inputs[name], dtype=np.float32))
        in_maps.append(m)
    res = run_bass_kernel_spmd(nc, in_maps, core_ids=list(range(8)))
    return np.concatenate([r["out"] for r in res.results], axis=0)
```

```python
import math
import numpy as np
import concourse.bass as bass
from concourse.ap import AP
import concourse.mybir as mybir
from concourse.bass_utils import run_bass_kernel_spmd

F32 = mybir.dt.float32
BF16 = mybir.dt.bfloat16
I32 = mybir.dt.int32
AF = mybir.ActivationFunctionType
ALU = mybir.AluOpType

COMPUTE = ("pe", "act", "dve", "pool")
ALLENG = ("pe", "act", "dve", "pool", "sp")
NDMA_SEMS = 40

S = 2048
D = 1024
NB = 2
DFF = 2816
NF = DFF // 128
MEM = 256


class Op:
    __slots__ = ("eng", "fn", "deps", "idx", "dma", "signal", "cnt", "sem", "clock")


class Prog:
    def __init__(self, nc):
        self.nc = nc
        self.ops = []
        self.last_write = {}
        self.readers = {}
        self.dma_hist = []
        self.n_dma = 0
        self.last_on = {}
        self.dma_since = []

    def add(self, eng, fn, reads=(), writes=(), dma=False, extra_deps=()):
        op = Op()
        op.eng, op.fn, op.dma = eng, fn, dma
        op.idx = len(self.ops)
        op.signal = False
        op.cnt = None
        op.sem = None
        op.clock = None
        deps = set(extra_deps)
        for k in reads:
            w = self.last_write.get(k)
            if w is not None:
                deps.add(w)
        for k in writes:
            w = self.last_write.get(k)
            if w is not None:
                deps.add(w)
            r = self.readers.get(k)
            if r:
                deps.update(r)
        for k in reads:
            self.readers.setdefault(k, []).append(op.idx)
        for k in writes:
            self.last_write[k] = op.idx
            self.readers[k] = []
        if dma:
            j = self.n_dma
            self.n_dma += 1
            op.sem = j % NDMA_SEMS
            if j >= NDMA_SEMS:
                deps.add(self.dma_hist[j - NDMA_SEMS])
            self.dma_hist.append(op.idx)
            self.dma_since.append(op.idx)
        else:
            self.last_on[eng] = op.idx
        deps.discard(op.idx)
        if eng == "pe" and not dma:
            deps = {d_ for d_ in deps if self.ops[d_].eng != "pe" or self.ops[d_].dma}
        op.deps = deps
        self.ops.append(op)
        return op

    def pe(self, fn, reads=(), writes=()):
        return self.add("pe", fn, reads, writes)

    def act(self, fn, reads=(), writes=()):
        return self.add("act", fn, reads, writes)

    def dve(self, fn, reads=(), writes=()):
        return self.add("dve", fn, reads, writes)

    def dma(self, fn, reads=(), writes=(), q="sp"):
        return self.add(q, fn, reads, writes, dma=True)

    def barrier(self):
        deps = set(self.last_on.values()) | set(self.dma_since)
        self.dma_since = []
        for e in ALLENG:
            self.add(e, lambda eng: eng.nop(), extra_deps=deps)
        self.last_write = {}
        self.readers = {}

    def emit(self, final_ops):
        nc = self.nc
        ops = self.ops
        for op in ops:
            for d in op.deps:
                ops[d].signal = True
        for op in final_ops:
            op.signal = True
        eng_cnt = {e: 0 for e in ALLENG}
        dma_cnt = [0] * NDMA_SEMS
        for op in ops:
            if op.dma:
                dma_cnt[op.sem] += 16
                op.cnt = dma_cnt[op.sem]
            elif op.signal:
                eng_cnt[op.eng] += 1
                op.cnt = eng_cnt[op.eng]
        sems = {e: nc.alloc_semaphore("s_" + e) for e in ALLENG}
        dsems = [nc.alloc_semaphore("d_%d" % i) for i in range(NDMA_SEMS)]

        def key_of(o):
            return ("d", o.sem) if o.dma else o.eng

        know = {e: {} for e in ALLENG}
        waits = {}
        for op in ops:
            K = know[op.eng]
            wl = []
            for d in sorted(op.deps, reverse=True):
                dop = ops[d]
                k = key_of(dop)
                if K.get(k, 0) >= dop.cnt:
                    continue
                for kk, vv in dop.clock.items():
                    if K.get(kk, 0) < vv:
                        K[kk] = vv
                K[k] = max(K.get(k, 0), dop.cnt)
                wl.append((dsems[dop.sem] if dop.dma else sems[dop.eng], dop.cnt))
            waits[op.idx] = wl
            if op.signal or op.dma:
                op.clock = dict(K)
        by_eng = {e: [] for e in ALLENG}
        for op in ops:
            by_eng[op.eng].append(op)
        fin = [(dsems[o.sem] if o.dma else sems[o.eng], o.cnt) for o in final_ops]
        self.n_inst = {e: len(by_eng[e]) for e in ALLENG}

        def run(engname, e):
            for op in by_eng[engname]:
                for (s, v) in waits[op.idx]:
                    e.wait_ge(s, v)
                ins = op.fn(e)
                if op.dma:
                    ins.then_inc(dsems[op.sem], 16)
                elif op.signal:
                    ins.then_inc(sems[op.eng], 1)
            if engname == "sp":
                for (s, v) in fin:
                    e.wait_ge(s, v)

        with nc.Block() as block:
            @block.tensor
            def _(e):
                run("pe", e)

            @block.scalar
            def _(e):
                run("act", e)

            @block.vector
            def _(e):
                run("dve", e)

            @block.gpsimd
            def _(e):
                run("pool", e)

            @block.sync
            def _(e):
                run("sp", e)


INPUT_SPECS = [
    ("x", [NB, S, D]), ("mem", [NB, MEM, D]),
    ("norm_mix", [2, D]), ("norm_xattn", [2, D]), ("norm_ffn", [2, D]), ("norm_mem", [D]), ("norm_final", [D]),
    ("ab_w_in", [1, D, 2048]), ("pool_w", [1, 4, 128, 128]), ("pool_scale", [1, 512]), ("ab_w_out", [1, D, D]),
    ("ssm_w_in", [1, D, D]), ("ssm_lam_re", [1, 64, 64]), ("ssm_lam_im", [1, 64, 64]), ("ssm_log_dt", [1, 64]),
    ("ssm_b_re", [1, 64, 64, 16]), ("ssm_b_im", [1, 64, 64, 16]), ("ssm_c_re", [1, 64, 16, 64]),
    ("ssm_c_im", [1, 64, 16, 64]), ("ssm_d", [1, D]), ("ssm_w_glu", [1, D, 2 * D]),
    ("xa_w_q", [2, D, D]), ("xa_w_kv", [2, D, 2 * D]), ("xa_w_o", [2, D, D]),
    ("ffn_w_up", [2, D, 2 * DFF]), ("ffn_conv_w", [2, 3, 2 * DFF]), ("ffn_conv_b", [2, 2 * DFF]),
    ("ffn_w_down", [2, DFF, D]),
    ("consts", [128, 12, 128]),
]


def make_consts():
    c = np.zeros((128, 12, 128), np.float32)
    j = np.arange(128)
    c[:, 0, :] = np.eye(128)
    c[:, 1, :] = -(j[:, None] > j[None, :]).astype(np.float32)
    c[:, 2, :] = -1.0
    c[:, 3, :] = 1.0
    c[:, 4, :] = (j[:, None] < j[None, :]).astype(np.float32)
    c[:, 5, :] = (j[:, None] // 32 == j[None, :] // 32).astype(np.float32)
    c[:, 6, :] = np.arange(128)[None, :]
    c[:, 7, :] = 128 + np.arange(128)[None, :]
    c[:, 8, :] = 1.0 / (1.0 + np.arange(128))[None, :]
    c[:, 9, :] = -(j[:, None] >= j[None, :]).astype(np.float32)
    c[:, 10, :] = -30000.0 * (j[:, None] >= j[None, :])
    return c


class Ctx:
    pass


_uid = [0]


def SBT(nc, name, shape, dt):
    _uid[0] += 1
    return nc.sbuf_tensor("%s_%d" % (name, _uid[0]), shape, dt)


def build_program(stop=None, nb=NB):
    nc = bass.Bass("TRN2", target_bir_lowering=False)
    P = Prog(nc)
    C = Ctx()
    C.nc, C.P = nc, P
    T = {}
    for name, shape in INPUT_SPECS:
        T[name] = nc.dram_tensor(name, shape, F32, kind="ExternalInput").ap()
    out = nc.dram_tensor("out", [NB, S, D], F32, kind="ExternalOutput").ap()
    C.T = T

    def sb(name, shape, dt=F32):
        return nc.alloc_sbuf_tensor(name, shape, dt)

    xT = sb("xT", [128, 8, S])
    cf = sb("cf", [128, 10, 128])
    cb = sb("cb", [128, 6, 128], BF16)
    gains = sb("gains", [128, 8, 8])
    convp = sb("convp", [128, 2, 4, 44])
    pscale = sb("pscale", [128, 4])
    zer = sb("zer", [128, 512], BF16)
    memT = sb("memT", [128, 8, MEM], BF16)
    ps = [nc.alloc_psum_tensor("ps%d" % i, [128, 512], F32) for i in range(8)]
    ident = cf[:, 0, :]
    maskstrict = cf[:, 4, :]

    def PSK(i):
        return ("ps", i)

    P.dma(lambda e: e.dma_start(out=cf[:], in_=T["consts"][:, 0:10, :]), writes=["cf"])
    P.dma(lambda e: e.dma_start(out=cb[:, 0:4, :], in_=T["consts"][:, 0:4, :]), writes=["cb"], q="pool")
    P.dma(lambda e: e.dma_start(out=cb[:, 4:6, :], in_=T["consts"][:, 9:11, :]), writes=["cb2"], q="pool")
    gsrc = [T["norm_mix"][0], T["norm_mix"][1], T["norm_xattn"][0], T["norm_xattn"][1],
            T["norm_ffn"][0], T["norm_ffn"][1], T["norm_mem"], T["norm_final"]]
    for i, g in enumerate(gsrc):
        P.dma(lambda e, i=i, g=g: e.dma_start(out=gains[:, i, :], in_=g.rearrange("(t p) -> p t", p=128),
                                             allow_slow_non_contiguous=True), writes=["gains"], q="act")
    for l in range(2):
        for i in range(3):
            P.dma(lambda e, l=l, i=i: e.dma_start(out=convp[:, l, i, :],
                                                  in_=T["ffn_conv_w"][l, i].rearrange("(t p) -> p t", p=128),
                                                  allow_slow_non_contiguous=True), writes=["convp"], q="act")
        P.dma(lambda e, l=l: e.dma_start(out=convp[:, l, 3, :],
                                         in_=T["ffn_conv_b"][l].rearrange("(t p) -> p t", p=128),
                                         allow_slow_non_contiguous=True), writes=["convp"], q="act")
    P.dma(lambda e: e.dma_start(out=pscale[:], in_=T["pool_scale"][0].rearrange("(t p) -> p t", p=128),
                                allow_slow_non_contiguous=True), writes=["pscale"], q="act")
    P.dve(lambda e: e.memset(zer[:], 0.0), writes=["zer"])

    def load_w(dst, src2d, key, k_tiles, col0, ncols):
        v = src2d.rearrange("(k p) n -> p k n", p=128)
        for k in range(k_tiles):
            P.dma(lambda e, k=k: e.dma_start(out=dst[:, k, :], in_=v[:, k, col0:col0 + ncols]),
                  writes=[(key, k)], q="pool")

    def rmsnorm_tile(hT, hkey, gi, t0, n, sq, rstd, part="all"):
        if part in ("all", "sq"):
            for dt in range(8):
                P.act(lambda e, dt=dt: e.activation(sq[:, dt, 0:n], xT[:, dt, t0:t0 + n], AF.Square),
                      reads=[("xT", dt)], writes=[("sq", dt)])
        if part == "sq":
            return
        def mm(e):
            for dt in range(8):
                r = e.matmul(ps[7][:, 0:n], lhsT=cb[:, 3, :], rhs=sq[:, dt, 0:n], start=(dt == 0), stop=(dt == 7))
            return r
        P.pe(mm, reads=[("sq", dt) for dt in range(8)] + ["cb"], writes=[PSK(7)])
        P.dve(lambda e: e.tensor_scalar(rstd[:, 0:n], ps[7][:, 0:n], 1.0 / D, 1e-6, ALU.mult, ALU.add),
              reads=[PSK(7)], writes=["rstd"])
        P.act(lambda e: e.activation(rstd[:, 0:n], rstd[:, 0:n], AF.Ln), reads=["rstd"], writes=["rstd"])
        P.act(lambda e: e.activation(rstd[:, 0:n], rstd[:, 0:n], AF.Exp, scale=-0.5), reads=["rstd"], writes=["rstd"])
        for dt in range(8):
            P.dve(lambda e, dt=dt: e.scalar_tensor_tensor(hT[:, dt, 0:n], xT[:, dt, t0:t0 + n], gains[:, gi, dt:dt + 1],
                                                          rstd[:, 0:n], ALU.mult, ALU.mult),
                  reads=[("xT", dt), "rstd", "gains"], writes=[(hkey, dt)])

    C.ps_rr = 0

    def linear_fm(w, wkey, act, akey, k_tiles, m_tiles, n, evac, a0=0, banks=(0, 1)):
        for m in range(m_tiles):
            bi = banks[C.ps_rr % len(banks)]
            C.ps_rr += 1
            def mm(e, m=m, bi=bi):
                for k in range(k_tiles):
                    r = e.matmul(ps[bi][:, 0:n], lhsT=w[:, k, m * 128:(m + 1) * 128], rhs=act[:, k, a0:a0 + n],
                                 start=(k == 0), stop=(k == k_tiles - 1))
                return r
            P.pe(mm, reads=[(wkey, k) for k in range(k_tiles)] + [(akey, k) for k in range(k_tiles)], writes=[PSK(bi)])
            evac(m, bi, ps[bi][:, 0:n])

    def resid_add(m, bi, pap, t0, n):
        P.dve(lambda e: e.tensor_tensor(xT[:, m, t0:t0 + n], xT[:, m, t0:t0 + n], pap, ALU.add),
              reads=[PSK(bi), ("xT", m)], writes=[("xT", m)])

    final_ops = []
    for b in range(nb):
        P.barrier()
        with SBT(nc, "xin", [128, 2, D], F32) as xin, SBT(nc, "sq", [128, 8, 512], BF16) as sq, \
                SBT(nc, "rstd", [128, 512], F32) as rstd, SBT(nc, "mn", [128, 2, D], F32) as mn, \
                SBT(nc, "ssq", [128, 4], F32) as ssq:
            for tt in range(16):
                xb = tt % 2
                P.dma(lambda e, tt=tt, xb=xb, b=b: e.dma_start(out=xin[:, xb, :], in_=T["x"][b, tt * 128:(tt + 1) * 128, :]),
                      writes=[("xin", xb)])
                for half in range(2):
                    bi = (tt * 2 + half) % 2
                    def tr(e, xb=xb, half=half, bi=bi):
                        for q in range(4):
                            dt = half * 4 + q
                            r = e.transpose(ps[bi][:, q * 128:(q + 1) * 128], xin[:, xb, dt * 128:(dt + 1) * 128], ident)
                        return r
                    P.pe(tr, reads=[("xin", xb), "cf"], writes=[PSK(bi)])
                    P.act(lambda e, tt=tt, half=half, bi=bi: e.activation(
                        xT[:, half * 4:half * 4 + 4, tt * 128:(tt + 1) * 128],
                        ps[bi][:].rearrange("p (q t) -> p q t", q=4), AF.Copy),
                        reads=[PSK(bi)], writes=[("xT", half * 4 + q) for q in range(4)])
            P.dma(lambda e, b=b: e.dma_start(out=mn[:], in_=T["mem"][b].rearrange("(t p) d -> p t d", p=128)), writes=["mn"])
            for t in range(2):
                P.act(lambda e, t=t: e.activation(xin[:, t, :], mn[:, t, :], AF.Square, accum_out=ssq[:, t:t + 1]),
                      reads=["mn"], writes=[("ssq", t), ("xin", t)])
            P.dve(lambda e: e.tensor_scalar(ssq[:, 2:4], ssq[:, 0:2], 1.0 / D, 1e-6, ALU.mult, ALU.add),
                  reads=[("ssq", 0), ("ssq", 1)], writes=["ssq2"])
            P.act(lambda e: e.activation(ssq[:, 2:4], ssq[:, 2:4], AF.Sqrt), reads=["ssq2"], writes=["ssq2"])
            P.dve(lambda e: e.reciprocal(ssq[:, 2:4], ssq[:, 2:4]), reads=["ssq2"], writes=["ssq2"])
            for t in range(2):
                P.dve(lambda e, t=t: e.tensor_scalar(mn[:, t, :], mn[:, t, :], ssq[:, 2 + t:3 + t], None, ALU.mult),
                      reads=["mn", "ssq2"], writes=["mn"])
            for t in range(2):
                for half in range(2):
                    bi = (t * 2 + half) % 2
                    def tr(e, t=t, half=half, bi=bi):
                        for q in range(4):
                            dt = half * 4 + q
                            r = e.transpose(ps[bi][:, q * 128:(q + 1) * 128], mn[:, t, dt * 128:(dt + 1) * 128], ident)
                        return r
                    P.pe(tr, reads=["mn", "cf"], writes=[PSK(bi)])
                    for q in range(4):
                        dt = half * 4 + q
                        P.dve(lambda e, t=t, q=q, dt=dt, bi=bi: e.tensor_scalar(
                            memT[:, dt, t * 128:(t + 1) * 128], ps[bi][:, q * 128:(q + 1) * 128],
                            gains[:, 6, dt:dt + 1], None, ALU.mult),
                            reads=[PSK(bi), "gains"], writes=[("memT", dt)])
        if stop == "load":
            pass
        else:
            for layer in range(2):
                if layer == 0:
                    stage_mix_ab(C, b, xT, ps, cf, cb, gains, pscale, zer, rmsnorm_tile, load_w, linear_fm, resid_add)
                else:
                    stage_mix_s5(C, b, xT, ps, cf, cb, gains, rmsnorm_tile, load_w, linear_fm, resid_add)
                if stop == "mix%d" % layer:
                    break
                stage_xattn(C, b, layer, xT, ps, cb, gains, memT, rmsnorm_tile, load_w, linear_fm, resid_add)
                if stop == "xa%d" % layer:
                    break
                stage_ffn(C, b, layer, xT, ps, gains, convp, rmsnorm_tile, load_w, linear_fm, resid_add)
                if stop == "ffn%d" % layer:
                    break
        P.barrier()
        with SBT(nc, "sq", [128, 8, 512], BF16) as sq, SBT(nc, "rstd", [128, 512], F32) as rstd, \
                SBT(nc, "yT", [128, 8, 512], F32) as yT, SBT(nc, "yo", [128, 2, D], F32) as yo:
            for tq in range(4):
                t0 = tq * 512
                if stop is None:
                    rmsnorm_tile(yT, "yT", 7, t0, 512, sq, rstd)
                else:
                    for dt in range(8):
                        P.act(lambda e, dt=dt, t0=t0: e.activation(yT[:, dt, :], xT[:, dt, t0:t0 + 512], AF.Copy),
                              reads=[("xT", dt)], writes=[("yT", dt)])
                for ts in range(4):
                    ob = ts % 2
                    for half in range(2):
                        bi = (ts * 2 + half) % 2
                        def tr(e, ts=ts, half=half, bi=bi):
                            for q in range(4):
                                dt = half * 4 + q
                                r = e.transpose(ps[bi][:, q * 128:(q + 1) * 128], yT[:, dt, ts * 128:(ts + 1) * 128], ident)
                            return r
                        P.pe(tr, reads=[("yT", dt) for dt in range(8)] + ["cf"], writes=[PSK(bi)])
                        P.act(lambda e, ob=ob, half=half, bi=bi: e.activation(yo[:, ob, half * 512:(half + 1) * 512],
                                                                               ps[bi][:], AF.Copy),
                              reads=[PSK(bi)], writes=[("yo", ob, half)])
                    tok = t0 + ts * 128
                    o = P.dma(lambda e, ob=ob, tok=tok, b=b: e.dma_start(out=out[b, tok:tok + 128, :], in_=yo[:, ob, :]),
                              reads=[("yo", ob, 0), ("yo", ob, 1)], writes=[("out", b, tok)])
                    final_ops.append(o)
    P.emit(final_ops)
    C.final = final_ops
    return nc, P


def stage_xattn(C, b, layer, xT, ps, cb, gains, memT, rmsnorm_tile, load_w, linear_fm, resid_add):
    nc, P, T = C.nc, C.P, C.T
    P.barrier()

    def PSK(i):
        return ("ps", i)
    with SBT(nc, "wq", [128, 8, D], BF16) as wq, SBT(nc, "wo", [128, 8, D], BF16) as wo, \
            SBT(nc, "wkv", [128, 8, D], BF16) as wkv, \
            SBT(nc, "KT", [128, 8, MEM], BF16) as KT, SBT(nc, "V", [128, 2, D], BF16) as V, \
            SBT(nc, "sq", [128, 8, 512], BF16) as sq, SBT(nc, "rstd", [128, 512], F32) as rstd, \
            SBT(nc, "hT", [128, 2, 8, 512], BF16) as hT, SBT(nc, "qT", [128, 8, 512], BF16) as qT, \
            SBT(nc, "pT", [128, 2, 2, 512], BF16) as pT, SBT(nc, "rs", [128, 2, 512], F32) as rs, \
            SBT(nc, "oT", [128, 8, 512], BF16) as oT:
        load_w(wkv, T["xa_w_kv"][layer], "wkv", 8, 0, D)
        load_w(wq, T["xa_w_q"][layer], "wq", 8, 0, D)

        def evK(m, bi, pap):
            P.act(lambda e: e.activation(KT[:, m, :], pap, AF.Copy), reads=[PSK(bi)], writes=[("KT", m)])
        linear_fm(wkv, "wkv", memT, "memT", 8, 8, MEM, evK)
        load_w(wkv, T["xa_w_kv"][layer], "wkv", 8, D, D)
        load_w(wo, T["xa_w_o"][layer], "wo", 8, 0, D)
        for mt in range(2):
            for nh in range(2):
                bi = (mt * 2 + nh) % 2
                def mm(e, mt=mt, nh=nh, bi=bi):
                    for k in range(8):
                        r = e.matmul(ps[bi][:], lhsT=memT[:, k, mt * 128:(mt + 1) * 128], rhs=wkv[:, k, nh * 512:(nh + 1) * 512],
                                     start=(k == 0), stop=(k == 7))
                    return r
                P.pe(mm, reads=[("wkv", k) for k in range(8)] + [("memT", k) for k in range(8)], writes=[PSK(bi)])
                P.act(lambda e, mt=mt, nh=nh, bi=bi: e.activation(V[:, mt, nh * 512:(nh + 1) * 512], ps[bi][:], AF.Copy),
                      reads=[PSK(bi)], writes=[("V", mt, nh)])
        rmsnorm_tile(hT[:, 0], "hT0", 2 + layer, 0, 512, sq, rstd)
        for tq in range(4):
            t0 = tq * 512
            tb = tq % 2

            def evQ(m, bi, pap):
                P.act(lambda e: e.activation(qT[:, m, :], pap, AF.Copy, scale=1.0 / 16.0), reads=[PSK(bi)], writes=[("qT", m)])
            linear_fm(wq, "wq", hT[:, tb], "hT%d" % tb, 8, 8, 512, evQ)
            def scores(h):
                hb = h % 2
                for mt in range(2):
                    bk = 2 + 2 * hb + mt
                    def mm(e, h=h, mt=mt, bk=bk):
                        for d in range(2):
                            r = e.matmul(ps[bk][:], lhsT=KT[:, 2 * h + d, mt * 128:(mt + 1) * 128], rhs=qT[:, 2 * h + d, :],
                                         start=(d == 0), stop=(d == 1))
                        return r
                    P.pe(mm, reads=[("KT", 2 * h), ("KT", 2 * h + 1), ("qT", 2 * h), ("qT", 2 * h + 1)], writes=[PSK(bk)])
                    P.act(lambda e, mt=mt, hb=hb, bk=bk: e.activation(pT[:, hb, mt, :], ps[bk][:], AF.Exp),
                          reads=[PSK(bk)], writes=[("pT", hb, mt)])

            def rest(h):
                hb = h % 2
                def mms(e, hb=hb):
                    e.matmul(ps[6][:], lhsT=cb[:, 3, :], rhs=pT[:, hb, 0, :], start=True, stop=False)
                    return e.matmul(ps[6][:], lhsT=cb[:, 3, :], rhs=pT[:, hb, 1, :], start=False, stop=True)
                P.pe(mms, reads=[("pT", hb, 0), ("pT", hb, 1), "cb"], writes=[PSK(6)])
                P.act(lambda e, hb=hb: e.activation(rs[:, hb, :], ps[6][:], AF.Ln), reads=[PSK(6)], writes=[("rs", hb)])
                P.act(lambda e, hb=hb: e.activation(rs[:, hb, :], rs[:, hb, :], AF.Exp, scale=-1.0), reads=[("rs", hb)], writes=[("rs", hb)])
                for d in range(2):
                    bi = 7 if d == 0 else 1
                    def mmo(e, h=h, hb=hb, d=d, bi=bi):
                        for mt in range(2):
                            r = e.matmul(ps[bi][:], lhsT=V[:, mt, h * 256 + d * 128:h * 256 + (d + 1) * 128], rhs=pT[:, hb, mt, :],
                                         start=(mt == 0), stop=(mt == 1))
                        return r
                    P.pe(mmo, reads=[("V", 0, h // 2), ("V", 1, h // 2), ("pT", hb, 0), ("pT", hb, 1)], writes=[PSK(bi)])
                    P.dve(lambda e, h=h, hb=hb, d=d, bi=bi: e.tensor_tensor(oT[:, 2 * h + d, :], ps[bi][:], rs[:, hb, :], ALU.mult),
                          reads=[PSK(bi), ("rs", hb)], writes=[("oT", 2 * h + d)])

            scores(0)
            for h in range(4):
                if h < 3:
                    scores(h + 1)
                rest(h)
                if h == 1 and tq + 1 < 4:
                    rmsnorm_tile(hT[:, 1 - tb], "hT%d" % (1 - tb), 2 + layer, t0 + 512, 512, sq, rstd, part="sq")
            if tq + 1 < 4:
                rmsnorm_tile(hT[:, 1 - tb], "hT%d" % (1 - tb), 2 + layer, t0 + 512, 512, sq, rstd, part="rest")
            linear_fm(wo, "wo", oT, "oT", 8, 8, 512, lambda m, bi, pap, t0=t0: resid_add(m, bi, pap, t0, 512))


def stage_ffn(C, b, layer, xT, ps, gains, convp, rmsnorm_tile, load_w, linear_fm, resid_add):
    nc, P, T = C.nc, C.P, C.T
    P.barrier()

    def PSK(i):
        return ("ps", i)
    with SBT(nc, "sq", [128, 8, 512], BF16) as sq, SBT(nc, "rstd", [128, 512], F32) as rstd, \
            SBT(nc, "hT", [128, 2, 8, 512], BF16) as hT, SBT(nc, "gT", [128, NF, 512], BF16) as gT, \
            SBT(nc, "wu", [128, 2, 2, 8, 512], BF16) as wu, SBT(nc, "wd", [128, 4, D], BF16) as wd, \
            SBT(nc, "ub", [128, 3, 2, 516], F32) as ub, SBT(nc, "cv", [128, 3, 2, 512], F32) as cv, \
            SBT(nc, "halo", [128, 2 * NF, 2], F32) as halo:
        wup = T["ffn_w_up"][layer].rearrange("(k p) n -> p k n", p=128)
        wdn = T["ffn_w_down"][layer]
        P.dve(lambda e: e.memset(halo[:], 0.0), writes=["halo"])
        groups = [(0, 4), (4, 4), (8, 4), (12, 4), (16, 4), (20, 2)]
        it = 0
        git = 0
        kit = 0
        rmsnorm_tile(hT[:, 0], "hT0", 4 + layer, 0, 512, sq, rstd)
        pend = []

        def tail(pb, fp):
            P.act(lambda e: e.activation(cv[:, pb, 1, :], cv[:, pb, 1, :], AF.Silu),
                  reads=[("cv", pb, 1)], writes=[("cv", pb, 1)])
            P.dve(lambda e: e.tensor_tensor(gT[:, fp, :], cv[:, pb, 0, :], cv[:, pb, 1, :], ALU.mult),
                  reads=[("cv", pb, 0), ("cv", pb, 1)], writes=[("gT", fp)])
        for tq in range(4):
            t0 = tq * 512
            hb = tq % 2
            for (f0, nf) in groups:
                wb = git % 2
                git += 1
                for vg in range(2):
                    col0 = vg * DFF + f0 * 128
                    P.dma(lambda e, wb=wb, vg=vg, col0=col0, nf=nf: e.dma_start(out=wu[:, wb, vg, :, 0:nf * 128],
                                                                              in_=wup[:, :, col0:col0 + nf * 128]),
                          writes=[("wu", wb, vg)], q="pool")
                for fl in range(nf):
                    fp = f0 + fl
                    pb = it % 3
                    it += 1
                    for vg in range(2):
                        bi = 3 * vg + pb
                        f = vg * NF + fp
                        def mm(e, wb=wb, vg=vg, bi=bi, fl=fl, hb=hb):
                            for k in range(8):
                                r = e.matmul(ps[bi][:], lhsT=wu[:, wb, vg, k, fl * 128:(fl + 1) * 128], rhs=hT[:, hb, k, :], start=(k == 0), stop=(k == 7))
                            return r
                        P.pe(mm, reads=[("wu", wb, vg)] + [("hT%d" % hb, k) for k in range(8)], writes=[PSK(bi)])
                        P.act(lambda e, pb=pb, vg=vg, bi=bi: e.activation(ub[:, pb, vg, 2:514], ps[bi][:], AF.Copy),
                              reads=[PSK(bi)], writes=[("ub", pb, vg)])
                        P.act(lambda e, pb=pb, vg=vg, f=f: e.activation(ub[:, pb, vg, 0:2], halo[:, f, :], AF.Copy),
                              reads=["halo%d" % f, "halo"], writes=[("ubh", pb, vg)])
                        P.act(lambda e, pb=pb, vg=vg, f=f, bi=bi: e.activation(cv[:, pb, vg, :], ps[bi][:], AF.Identity,
                                                                               bias=convp[:, layer, 3, f:f + 1],
                                                                               scale=convp[:, layer, 2, f:f + 1]),
                              reads=[PSK(bi), "convp"], writes=[("cv", pb, vg)])
                        P.dve(lambda e, pb=pb, vg=vg, f=f: e.scalar_tensor_tensor(cv[:, pb, vg, :], ub[:, pb, vg, 1:513],
                                                                                  convp[:, layer, 1, f:f + 1], cv[:, pb, vg, :],
                                                                                  ALU.mult, ALU.add),
                              reads=[("ub", pb, vg), ("ubh", pb, vg), ("cv", pb, vg), "convp"], writes=[("cv", pb, vg)])
                        P.dve(lambda e, pb=pb, vg=vg, f=f: e.scalar_tensor_tensor(cv[:, pb, vg, :], ub[:, pb, vg, 0:512],
                                                                                  convp[:, layer, 0, f:f + 1], cv[:, pb, vg, :],
                                                                                  ALU.mult, ALU.add),
                              reads=[("ub", pb, vg), ("ubh", pb, vg), ("cv", pb, vg), "convp"], writes=[("cv", pb, vg)])
                        P.dve(lambda e, pb=pb, vg=vg, f=f: e.tensor_copy(halo[:, f, :], ub[:, pb, vg, 512:514]),
                              reads=[("ub", pb, vg)], writes=["halo%d" % f])
                    if pend:
                        tail(*pend.pop())
                    pend.append((pb, fp))
            if pend:
                tail(*pend.pop())
            if tq + 1 < 4:
                rmsnorm_tile(hT[:, 1 - hb], "hT%d" % (1 - hb), 4 + layer, t0 + 512, 512, sq, rstd, part="sq")
            for k in range(NF):
                db = kit % 4
                kit += 1
                P.dma(lambda e, db=db, k=k: e.dma_start(out=wd[:, db, :], in_=wdn[k * 128:(k + 1) * 128, :]),
                      writes=[("wd", db)], q="pool")
                def mm(e, db=db, k=k):
                    for m in range(8):
                        r = e.matmul(ps[m][:], lhsT=wd[:, db, m * 128:(m + 1) * 128], rhs=gT[:, k, :], start=(k == 0), stop=(k == NF - 1))
                    return r
                P.pe(mm, reads=[("wd", db), ("gT", k)], writes=[PSK(m) for m in range(8)])
            resid_add(7, 7, ps[7][:], t0, 512)
            if tq + 1 < 4:
                rmsnorm_tile(hT[:, 1 - hb], "hT%d" % (1 - hb), 4 + layer, t0 + 512, 512, sq, rstd, part="rest")
            for m in range(7):
                resid_add(m, m, ps[m][:], t0, 512)


def stage_mix_ab(C, b, xT, ps, cf, cb, gains, pscale, zer, rmsnorm_tile, load_w, linear_fm, resid_add):
    nc, P, T = C.nc, C.P, C.T
    P.barrier()
    maskstrict = cf[:, 4, :]

    def PSK(i):
        return ("ps", i)
    win = T["ab_w_in"][0]
    with SBT(nc, "hT", [128, 8, S], BF16) as hT, SBT(nc, "aT", [128, 4, S], BF16) as aT, \
            SBT(nc, "pTo", [128, 4, S], BF16) as pTo:
        with SBT(nc, "sq", [128, 8, 512], BF16) as sq, SBT(nc, "rstd", [128, 512], F32) as rstd, \
                SBT(nc, "hTt", [128, 8, 512], BF16) as hTt:
            for tq in range(4):
                rmsnorm_tile(hTt, "hTt", 0, tq * 512, 512, sq, rstd)
                for dt in range(8):
                    P.act(lambda e, dt=dt, tq=tq: e.activation(hT[:, dt, tq * 512:(tq + 1) * 512], hTt[:, dt, :], AF.Copy),
                          reads=[("hTt", dt)], writes=[("hT", dt)])
        P.barrier()
        with SBT(nc, "wu4", [128, 8, 512], BF16) as wu4, SBT(nc, "wp", [128, 4, 128], BF16) as wp, \
                SBT(nc, "uA", [128, S], F32) as uA, SBT(nc, "uB", [128, S], F32) as uB, \
                SBT(nc, "u02", [128, 2, S], F32) as u02, SBT(nc, "pb", [128, S], BF16) as pb:
            def uproj(g):
                ub_ = g % 2
                for tq in range(4):
                    bi = tq % 2
                    def mm(e, tq=tq, bi=bi, g=g):
                        for k in range(8):
                            r = e.matmul(ps[bi][:], lhsT=wu4[:, k, g * 128:(g + 1) * 128], rhs=hT[:, k, tq * 512:(tq + 1) * 512], start=(k == 0), stop=(k == 7))
                        return r
                    P.pe(mm, reads=[("wu4", k) for k in range(8)] + [("hT", k) for k in range(8)], writes=[PSK(bi)])
                    P.act(lambda e, tq=tq, bi=bi, ub_=ub_: e.activation(u02[:, ub_, tq * 512:(tq + 1) * 512], ps[bi][:], AF.Copy),
                          reads=[PSK(bi)], writes=[("u0", ub_)])
            load_w(wu4, win, "wu4", 8, 1536, 512)
            uproj(0)
            for g in range(4):
                w_ = 2 ** (g + 1)
                ub_ = g % 2
                u0 = u02[:, ub_]
                u0k = ("u0", ub_)
                P.dma(lambda e, g=g: e.dma_start(out=wp[:, g, :], in_=T["pool_w"][0, g]), writes=[("wp", g)], q="pool")
                if g < 3:
                    uproj(g + 1)
                src, srck = u0, u0k
                bufs = [(uA, "uA"), (uB, "uB")]
                for st in range(g + 1):
                    sh = 2 ** st
                    dst, dstk = bufs[st % 2]
                    def stp(e, src=src, dst=dst, sh=sh):
                        e.tensor_copy(dst[:, 0:sh], src[:, 0:sh])
                        return e.tensor_tensor(dst[:, sh:S], src[:, sh:S], src[:, 0:S - sh], ALU.add)
                    P.dve(stp, reads=[srck], writes=[dstk])
                    src, srck = dst, dstk
                def pl(e, src=src, w_=w_, u0=u0):
                    e.scalar_tensor_tensor(pb[:, w_ - 1:S], src[:, w_ - 1:S], 1.0 / w_, u0[:, w_ - 1:S], ALU.mult, ALU.subtract)
                    return e.tensor_tensor(src[:, 0:w_ - 1], src[:, 0:w_ - 1], cf[:, 8, 0:w_ - 1], ALU.mult)
                P.dve(pl, reads=[srck, u0k, "cf"], writes=["pb0", srck])
                P.dve(lambda e, src=src, w_=w_, u0=u0: e.tensor_tensor(pb[:, 0:w_ - 1], src[:, 0:w_ - 1], u0[:, 0:w_ - 1], ALU.subtract),
                      reads=[srck, u0k], writes=["pb1"])
                for tq in range(4):
                    bi = 2 + tq % 2
                    P.pe(lambda e, tq=tq, bi=bi, g=g: e.matmul(ps[bi][:], lhsT=wp[:, g, :], rhs=pb[:, tq * 512:(tq + 1) * 512], start=True, stop=True),
                         reads=[("wp", g), "pb0", "pb1"], writes=[PSK(bi)])
                    P.act(lambda e, tq=tq, bi=bi, g=g: e.activation(pTo[:, g, tq * 512:(tq + 1) * 512], ps[bi][:], AF.Identity,
                                                                    scale=pscale[:, g:g + 1]),
                          reads=[PSK(bi), "pscale"], writes=[("pTo", g)])
        P.barrier()
        NBUF = 4
        with SBT(nc, "wqkv", [128, 8, 1536], BF16) as wqkv, SBT(nc, "qh", [128, S], BF16) as qh, \
                SBT(nc, "kh", [128, S], BF16) as kh, SBT(nc, "vh", [128, 16, 128], BF16) as vh, \
                SBT(nc, "ex", [128, NBUF, 512], F32) as ex, \
                SBT(nc, "spb", [128, NBUF, 512], BF16) as spb, \
                SBT(nc, "wsb", [128, NBUF, 512], BF16) as wsb, SBT(nc, "Ls", [128, 2, 512], F32) as Ls, \
                SBT(nc, "Lsb", [128, 4, 512], BF16) as Lsb, SBT(nc, "otmp", [64, 2, 512], BF16) as otmp:
            identb = cb[:, 0, :]
            trinc = cb[:, 4, :]
            maskneg = cb[:, 5, :]
            onesneg = cb[:, 2, :]
            git = 0
            for h in range(8):
                hp = h // 2
                if h == 0:
                    load_w(wqkv, win, "wqkv", 8, 0, 1536)
                hb64 = 64 * (h % 2)
                for j3, (dst, dk, scl) in enumerate([(qh, "qh", 0.125), (kh, "kh", 1.0)]):
                    if h % 2 == 1:
                        break
                    for tq in range(4):
                        bi = 4 + tq % 2
                        def mm(e, h=h, j3=j3, tq=tq, bi=bi):
                            for k in range(8):
                                r = e.matmul(ps[bi][:, :], lhsT=wqkv[:, k, j3 * 512 + h * 64:j3 * 512 + h * 64 + 128],
                                             rhs=hT[:, k, tq * 512:(tq + 1) * 512], start=(k == 0), stop=(k == 7))
                            return r
                        P.pe(mm, reads=[("wqkv", k) for k in range(8)] + [("hT", k) for k in range(8)], writes=[PSK(bi)])
                        P.act(lambda e, dst=dst, tq=tq, bi=bi, scl=scl: e.activation(dst[:, tq * 512:(tq + 1) * 512], ps[bi][:, :],
                                                                                   AF.Copy, scale=scl),
                              reads=[PSK(bi)], writes=[(dk, tq)])
                if h % 2 == 0:
                    for t4 in range(4):
                        bi = 4 + t4 % 2
                        def mmv(e, h=h, t4=t4, bi=bi):
                            for tl in range(4):
                                tt = t4 * 4 + tl
                                for k in range(8):
                                    r = e.matmul(ps[bi][:, tl * 128:(tl + 1) * 128], lhsT=hT[:, k, tt * 128:(tt + 1) * 128],
                                                 rhs=wqkv[:, k, 1024 + h * 64:1024 + h * 64 + 128], start=(k == 0), stop=(k == 7))
                            return r
                        P.pe(mmv, reads=[("wqkv", k) for k in range(8)] + [("hT", k) for k in range(8)], writes=[PSK(bi)])
                        P.dve(lambda e, t4=t4, bi=bi: e.tensor_copy(vh[:, t4 * 4:(t4 + 1) * 4, :],
                                                                    ps[bi][:].rearrange("p (t c) -> p t c", t=4)),
                              reads=[PSK(bi)], writes=[("vh", t4)])
                its = []
                for j in range(4):
                    kbs = list(range(4 * j + 3, -1, -1))
                    for ii, kb in enumerate(kbs):
                        diag = kb >= 4 * j
                        qlo = 128 * (kb - 4 * j) if diag else 0
                        its.append(dict(j=j, kb=kb, first=(ii == 0), last=(kb == 0), diag=diag, qlo=qlo, g=git))
                        git += 1

                def zmm(e, dst, it_, stop_after, hb64=hb64):
                    kb, qlo, j = it_["kb"], it_["qlo"], it_["j"]
                    q0 = 512 * j + qlo
                    r = e.matmul(dst[:, qlo:512], lhsT=kh[hb64:hb64 + 64, kb * 128:(kb + 1) * 128], rhs=qh[hb64:hb64 + 64, q0:512 * (j + 1)],
                                 start=True, stop=(stop_after and not it_["diag"]))
                    if it_["diag"]:
                        r = e.matmul(dst[:, qlo:qlo + 128], lhsT=identb, rhs=maskneg, start=False, stop=stop_after)
                    return r

                def stageA(it_):
                    r_ = it_["g"] % NBUF
                    j, kb, qlo = it_["j"], it_["kb"], it_["qlo"]
                    lb = j % 2
                    if it_["first"]:
                        P.dve(lambda e, lb=lb: e.memset(Ls[:, lb, :], 0.0), writes=[("Ls", lb)])
                    P.pe(lambda e, it_=it_, r_=r_, zmm=zmm: zmm(e, ps[r_], it_, True),
                         reads=[("kh", kb // 4), ("qh", j), "cb"], writes=[PSK(r_)])
                    P.act(lambda e, r_=r_, qlo=qlo: e.activation(ex[:, r_, qlo:512], ps[r_][:, qlo:512], AF.Exp),
                          reads=[PSK(r_)], writes=[("ex", r_)])
                    P.act(lambda e, r_=r_, qlo=qlo: e.activation(spb[:, r_, qlo:512], ex[:, r_, qlo:512], AF.Ln, bias=1.0),
                          reads=[("ex", r_)], writes=[("spb", r_)])
                    if not it_["last"]:
                        nqlo = max(0, 128 * (kb - 1 - 4 * j))
                        nr = (it_["g"] + 1) % 4
                        P.dve(lambda e, r_=r_, qlo=qlo, lb=lb: e.tensor_tensor(Ls[:, lb, qlo:512], Ls[:, lb, qlo:512], spb[:, r_, qlo:512], ALU.add),
                              reads=[("Ls", lb), ("spb", r_)], writes=[("Ls", lb)])
                        P.dve(lambda e, nqlo=nqlo, nr=nr, lb=lb: e.tensor_copy(Lsb[:, nr, nqlo:512], Ls[:, lb, nqlo:512]),
                              reads=[("Ls", lb)], writes=[("Lsb", nr)])

                def stageB1(it_):
                    r_ = it_["g"] % NBUF
                    qlo = it_["qlo"]
                    pst = ps[r_]
                    def mmt(e, it_=it_, r_=r_, pst=pst, qlo=qlo):
                        r = e.matmul(pst[:, qlo:512], lhsT=trinc, rhs=spb[:, r_, qlo:512], start=False, stop=it_["first"])
                        if not it_["first"]:
                            r = e.matmul(pst[:, qlo:512], lhsT=onesneg, rhs=Lsb[:, it_["g"] % 4, qlo:512], start=False, stop=True)
                        return r
                    P.pe(mmt, reads=["cb", ("spb", r_), ("Lsb", it_["g"] % 4)], writes=[PSK(r_)])
                    P.act(lambda e, r_=r_, pst=pst, qlo=qlo: e.activation(wsb[:, r_, qlo:512], pst[:, qlo:512], AF.Exp),
                          reads=[PSK(r_)], writes=[("wsb", r_)])

                def stageB2(it_):
                    r_ = it_["g"] % NBUF
                    j, kb, qlo = it_["j"], it_["kb"], it_["qlo"]
                    ob = j % 2
                    pso = ps[6 + ob]
                    if it_["first"]:
                        P.pe(lambda e, pso=pso: e.matmul(pso[:, :], lhsT=zer[0:1, 0:128], rhs=zer[0:1, 0:512], start=True, stop=False),
                             reads=["zer"], writes=[PSK(6 + ob)])
                    P.pe(lambda e, pso=pso, kb=kb, r_=r_, qlo=qlo, last=it_["last"]: e.matmul(
                        pso[:, qlo:512], lhsT=vh[:, kb, :], rhs=wsb[:, r_, qlo:512], start=False, stop=last),
                        reads=[("vh", kb // 4), ("wsb", r_)], writes=[PSK(6 + ob)])
                    if it_["last"]:
                        if h % 2 == 0:
                            P.dve(lambda e, pso=pso, j=j, hp=hp: e.tensor_copy(aT[0:64, hp, 512 * j:512 * (j + 1)], pso[0:64, :]),
                                  reads=[PSK(6 + ob)], writes=[("aT", hp, 0)])
                        else:
                            P.dve(lambda e, pso=pso, j=j, hp=hp: e.tensor_copy(aT[64:128, hp, 512 * j:512 * (j + 1)], pso[64:128, :]),
                                  reads=[PSK(6 + ob)], writes=[("aT", hp, 1)])

                n_it = len(its)
                for i in range(n_it + 2):
                    if i < n_it:
                        stageA(its[i])
                    if 1 <= i <= n_it:
                        stageB1(its[i - 1])
                    if i >= 2:
                        stageB2(its[i - 2])
        P.barrier()
        with SBT(nc, "wout", [128, 8, D], BF16) as wout:
            load_w(wout, T["ab_w_out"][0], "wout", 8, 0, D)
            for tq in range(4):
                t0 = tq * 512
                for m in range(8):
                    bi = m % 2
                    def mm(e, m=m, bi=bi, t0=t0):
                        for k in range(8):
                            src = aT if k < 4 else pTo
                            r = e.matmul(ps[bi][:], lhsT=wout[:, k, m * 128:(m + 1) * 128], rhs=src[:, k % 4, t0:t0 + 512],
                                         start=(k == 0), stop=(k == 7))
                        return r
                    P.pe(mm, reads=[("wout", k) for k in range(8)] + ["aTall"], writes=[PSK(bi)])
                    resid_add(m, bi, ps[bi][:], t0, 512)


def stage_mix_s5(C, b, xT, ps, cf, cb, gains, rmsnorm_tile, load_w, linear_fm, resid_add):
    from contextlib import ExitStack
    nc, P, T = C.nc, C.P, C.T
    P.barrier()

    def PSK(i):
        return ("ps", i)
    ident = cf[:, 0, :]
    mask32 = cf[:, 5, :]
    TWO_PI = 6.283185
    INV2PI = 1.0 / (2.0 * math.pi)

    def V(fn, r, w):
        return P.dve(fn, reads=r, writes=w)

    def Aop(fn, r, w):
        return P.act(fn, reads=r, writes=w)

    with SBT(nc, "uT", [128, 8, S], BF16) as uT, SBT(nc, "dcol", [128, 8], F32) as dcol:
        with SBT(nc, "w_in", [128, 8, D], BF16) as w_in, SBT(nc, "sq", [128, 8, 512], BF16) as sq, \
                SBT(nc, "rstd", [128, 512], F32) as rstd, SBT(nc, "hT", [128, 8, 512], BF16) as hT:
            load_w(w_in, T["ssm_w_in"][0], "w_in", 8, 0, D)
            P.dma(lambda e: e.dma_start(out=dcol[:], in_=T["ssm_d"][0].rearrange("(t p) -> p t", p=128),
                                        allow_slow_non_contiguous=True), writes=["dcol"])
            for tq in range(4):
                rmsnorm_tile(hT, "hT", 1, tq * 512, 512, sq, rstd)

                def ev(m, bi, pap, tq=tq):
                    P.act(lambda e: e.activation(uT[:, m, tq * 512:(tq + 1) * 512], pap, AF.Copy),
                          reads=[PSK(bi)], writes=[("uT", m)])
                linear_fm(w_in, "w_in", hT, "hT", 8, 8, 512, ev)
        P.barrier()
        with ExitStack() as es:
            def A(name, shape, dt=F32):
                return es.enter_context(SBT(nc, name, shape, dt))
            lre = A("lre", [128, 4]); lim = A("lim", [128, 4]); ldt = A("ldt", [128, 4])
            dtt = A("dtt", [128, 4]); ar = A("ar", [128, 4]); an = A("an", [128, 4])
            arj = A("arj", [128, 9, 4]); tj = A("tj", [128, 9, 4]); tjc = A("tjc", [128, 9, 4])
            ti = A("ti", [128, 9, 4], I32); fr = A("fr", [128, 9, 4])
            mag = A("mag", [128, 9, 4]); sinj = A("sinj", [128, 9, 4]); cosj = A("cosj", [128, 9, 4])
            Lr = A("Lr", [128, 9, 4]); Li = A("Li", [128, 9, 4])
            nre = A("nre", [128, 4]); den = A("den", [128, 4]); t1 = A("t1", [128, 4]); t2 = A("t2", [128, 4])
            cr = A("cr", [128, 4]); ci = A("ci", [128, 4]); ti8 = A("ti8", [128, 4], I32); t8f = A("t8f", [128, 4])
            Fr = A("Fr", [128, 8, 4]); Fi = A("Fi", [128, 8, 4]); f1 = A("f1", [128, 8, 4]); f2 = A("f2", [128, 8, 4])
            Bst = A("Bst", [128, 2, 4, 16]); Cin = A("Cin", [64, 2, 2, 64]); Cst = A("Cst", [128, 2, 4, 16])
            l1 = A("l1", [128, 9, 4, 16]); l2 = A("l2", [128, 9, 4, 16])
            What = A("What", [128, 8, 2, 128])
            Wt = A("Wt", [128, 8, 2, 128], BF16)
            CL = A("CL", [128, 2, 9, 4, 16])
            LB = CL[:, :, 0:8]
            Qd = A("Qd", [128, 9, 2, 4, 32], BF16)
            Qf = A("Qf", [128, 2, 128])
            TtF = A("TtF", [128, 4, 128]); Tt = A("Tt", [128, 8, 128], BF16)
            cosT = A("cosT", [128, 4, 256]); sinT = A("sinT", [128, 4, 256])
            Xp = A("Xp", [128, 2, 4, 256]); xa = A("xa", [128, 4, 256]); xb = A("xb", [128, 4, 256])
            Ssc = A("Ssc", [128, 2, 4, 256]); tk = Ssc[:, 0]; tki = Ssc[:, 1].bitcast(I32); Hb = A("Hb", [128, 2, 4, 257], BF16)
            iota256 = cf[:, 6:8, :].rearrange("p a b -> p (a b)")
            What6 = What[:].rearrange("p t r (q g c) -> p t r q g c", q=4, g=2)
            Qf5 = Qf[:].rearrange("p r (q g c) -> p r q g c", q=4, g=2)
            Wv = Wt[:].rearrange("p t r n -> p (t r) n")
            V(lambda e: e.memset(What[:], 0.0), [], ["What"])
            V(lambda e: e.memset(Qd[:], 0.0), [], ["Qpad"])
            V(lambda e: e.memset(Qf[:], 0.0), [], ["Qf"])
            V(lambda e: e.memset(Hb[:], 0.0), [], ["Hb"])

            def bc(ap, shape):
                return ap.broadcast_to(shape)

            def partA(j):
                g0 = 8 * j
                P.dma(lambda e, g0=g0: e.dma_start(out=lre[:], in_=T["ssm_lam_re"][0, g0:g0 + 8, :].rearrange("(q g) p -> (g p) q", g=2),
                                                   allow_slow_non_contiguous=True), writes=["lre"])
                P.dma(lambda e, g0=g0: e.dma_start(out=lim[:], in_=T["ssm_lam_im"][0, g0:g0 + 8, :].rearrange("(q g) p -> (g p) q", g=2),
                                                   allow_slow_non_contiguous=True), writes=["lim"])
                for g2 in range(2):
                    P.dma(lambda e, g0=g0, g2=g2: e.dma_start(
                        out=ldt[64 * g2:64 * g2 + 64, :],
                        in_=T["ssm_log_dt"][0, g0:g0 + 8].rearrange("(q g) -> g q", g=2)[g2:g2 + 1, :].broadcast_to([64, 4]),
                        allow_slow_non_contiguous=True), writes=["ldt"])
                for ri, nm in enumerate(["ssm_b_re", "ssm_b_im"]):
                    P.dma(lambda e, g0=g0, ri=ri, nm=nm: e.dma_start(
                        out=Bst[:, ri, :, :], in_=T[nm][0, g0:g0 + 8].rearrange("(q g) p c -> (g p) q c", g=2)),
                        writes=["Bst"])
                for ri, nm in enumerate(["ssm_c_re", "ssm_c_im"]):
                    for q in range(4):
                        P.dma(lambda e, g0=g0, ri=ri, nm=nm, q=q: e.dma_start(
                            out=Cin[16 * q:16 * q + 16, ri, :, :],
                            in_=T[nm][0, g0 + 2 * q:g0 + 2 * q + 2].rearrange("g c p -> c g p")), writes=["Cin"])
                def trc(e):
                    for ri in range(2):
                        r = e.transpose(ps[0][:, ri * 64:(ri + 1) * 64], Cin[:, ri, :, :].rearrange("a g p -> a (g p)"), ident[0:64, 0:64])
                    return r
                P.pe(trc, reads=["Cin", "cf"], writes=[PSK(0)])
                Aop(lambda e: e.activation(Cst[:].rearrange("p r q c -> p (r q c)"), ps[0][:, 0:128], AF.Copy), [PSK(0)], ["Cst"])
                Aop(lambda e: e.activation(dtt[:], ldt[:], AF.Exp), ["ldt"], ["dtt"])
                def f_(e):
                    e.tensor_tensor(ar[:], lre[:], dtt[:], ALU.mult)
                    return e.tensor_tensor(an[:], lim[:], dtt[:], ALU.mult)
                V(f_, ["lre", "lim", "dtt"], ["ar", "an"])
                jv = bc(cf[:, 6, 0:9][:, :, None], [128, 9, 4])
                def f_(e):
                    e.tensor_tensor(arj[:], bc(ar[:, None, :], [128, 9, 4]), jv, ALU.mult)
                    return e.scalar_tensor_tensor(tj[:], bc(an[:, None, :], [128, 9, 4]), INV2PI, jv, ALU.mult, ALU.mult)
                V(f_, ["ar", "an", "cf"], ["arj", "tj"])
                Aop(lambda e: e.activation(mag[:], arj[:], AF.Exp), ["arj"], ["mag"])
                V(lambda e: e.tensor_copy(ti[:], tj[:]), ["tj"], ["ti"])
                V(lambda e: e.tensor_tensor(fr[:], tj[:], ti[:], ALU.subtract), ["tj", "ti"], ["fr"])
                Aop(lambda e: e.activation(sinj[:], fr[:], AF.Sin, scale=TWO_PI), ["fr"], ["sinj"])
                V(lambda e: e.tensor_scalar(tjc[:], tj[:], 0.25, None, ALU.add), ["tj"], ["tjc"])
                V(lambda e: e.tensor_copy(ti[:], tjc[:]), ["tjc"], ["ti"])
                V(lambda e: e.tensor_tensor(fr[:], tjc[:], ti[:], ALU.subtract), ["tjc", "ti"], ["fr"])
                Aop(lambda e: e.activation(cosj[:], fr[:], AF.Sin, scale=TWO_PI), ["fr"], ["cosj"])
                def f_(e):
                    e.tensor_tensor(Lr[:], mag[:], cosj[:], ALU.mult)
                    return e.tensor_tensor(Li[:], mag[:], sinj[:], ALU.mult)
                V(f_, ["mag", "cosj", "sinj"], ["Lr", "Li"])
                def f_(e):
                    e.tensor_scalar(nre[:], Lr[:, 1, :], -1.0, None, ALU.add)
                    e.tensor_tensor(t1[:], lre[:], lre[:], ALU.mult)
                    return e.tensor_tensor(t2[:], lim[:], lim[:], ALU.mult)
                V(f_, ["Lr", "lre", "lim"], ["nre", "t1", "t2"])
                V(lambda e: e.tensor_tensor(den[:], t1[:], t2[:], ALU.add), ["t1", "t2"], ["den"])
                V(lambda e: e.reciprocal(den[:], den[:]), ["den"], ["den"])
                def f_(e):
                    e.tensor_tensor(t1[:], nre[:], lre[:], ALU.mult)
                    return e.tensor_tensor(t2[:], Li[:, 1, :], lim[:], ALU.mult)
                V(f_, ["nre", "lre", "Li", "lim", "den"], ["t1", "t2"])
                V(lambda e: e.tensor_tensor(cr[:], t1[:], t2[:], ALU.add), ["t1", "t2"], ["cr"])
                V(lambda e: e.tensor_tensor(cr[:], cr[:], den[:], ALU.mult), ["cr", "den"], ["cr"])
                def f_(e):
                    e.tensor_tensor(t1[:], Li[:, 1, :], lre[:], ALU.mult)
                    return e.tensor_tensor(t2[:], nre[:], lim[:], ALU.mult)
                V(f_, ["nre", "lre", "Li", "lim", "cr"], ["t1", "t2"])
                V(lambda e: e.tensor_tensor(ci[:], t1[:], t2[:], ALU.subtract), ["t1", "t2"], ["ci"])
                V(lambda e: e.tensor_tensor(ci[:], ci[:], den[:], ALU.mult), ["ci", "den"], ["ci"])
                crb = bc(cr[:, None, :], [128, 8, 4]); cib = bc(ci[:, None, :], [128, 8, 4])
                def f_(e, crb=crb, cib=cib):
                    e.tensor_tensor(f1[:], Lr[:, 0:8, :], crb, ALU.mult)
                    return e.tensor_tensor(f2[:], Li[:, 0:8, :], cib, ALU.mult)
                V(f_, ["Lr", "Li", "cr", "ci"], ["f1", "f2"])
                V(lambda e: e.tensor_tensor(Fr[:], f1[:], f2[:], ALU.subtract), ["f1", "f2"], ["Fr"])
                def f_(e, crb=crb, cib=cib):
                    e.tensor_tensor(f1[:], Lr[:, 0:8, :], cib, ALU.mult)
                    return e.tensor_tensor(f2[:], Li[:, 0:8, :], crb, ALU.mult)
                V(f_, ["Lr", "Li", "cr", "ci", "Fr"], ["f1", "f2"])
                V(lambda e: e.tensor_tensor(Fi[:], f1[:], f2[:], ALU.add), ["f1", "f2"], ["Fi"])
                sh8 = [128, 8, 4, 16]
                Frb = bc(Fr[:, :, :, None], sh8); Fib = bc(Fi[:, :, :, None], sh8)
                B0 = bc(Bst[:, 0, None, :, :], sh8); B1 = bc(Bst[:, 1, None, :, :], sh8)
                def f_(e, Frb=Frb, Fib=Fib, B0=B0, B1=B1):
                    e.tensor_tensor(l1[:, 0:8], Frb, B0, ALU.mult)
                    return e.tensor_tensor(l2[:, 0:8], Fib, B1, ALU.mult)
                V(f_, ["Fr", "Fi", "Bst"], ["l1", "l2"])
                V(lambda e: e.tensor_tensor(LB[:, 0], l1[:, 0:8], l2[:, 0:8], ALU.subtract), ["l1", "l2"], ["LB0", "CL0"])
                def f_(e, Frb=Frb, Fib=Fib, B0=B0, B1=B1):
                    e.tensor_tensor(l1[:, 0:8], Frb, B1, ALU.mult)
                    return e.tensor_tensor(l2[:, 0:8], Fib, B0, ALU.mult)
                V(f_, ["Fr", "Fi", "Bst", "LB0"], ["l1", "l2"])
                V(lambda e: e.tensor_tensor(LB[:, 1], l1[:, 0:8], l2[:, 0:8], ALU.add), ["l1", "l2"], ["LB1", "CL1"])
                def f_(e):
                    for g2 in range(2):
                        for ri in range(2):
                            r = e.tensor_copy(What6[64 * g2:64 * g2 + 64, :, ri, :, g2, :], LB[64 * g2:64 * g2 + 64, ri, :, :, :])
                    return r
                V(f_, ["LB0", "LB1"], ["What"])
            def partB(j):
                g0 = 8 * j
                for c4 in range(4):
                    bi = c4 % 2
                    def trw(e, c4=c4, bi=bi):
                        for i4 in range(4):
                            c = c4 * 4 + i4
                            r = e.transpose(ps[bi][:, i4 * 128:(i4 + 1) * 128], What[:, c // 2, c % 2, :], ident)
                        return r
                    P.pe(trw, reads=["What", "cf"], writes=[PSK(bi)])
                    if c4 % 2 == 0:
                        Aop(lambda e, c4=c4, bi=bi: e.activation(Wv[:, c4 * 4:c4 * 4 + 4, :], ps[bi][:].rearrange("p (a n) -> p a n", a=4), AF.Copy),
                            [PSK(bi)], [("Wpad", c4)])
                    else:
                        V(lambda e, c4=c4, bi=bi: e.tensor_copy(Wv[:, c4 * 4:c4 * 4 + 4, :], ps[bi][:].rearrange("p (a n) -> p a n", a=4)),
                          [PSK(bi)], [("Wpad", c4)])
                sh9 = [128, 9, 4, 16]
                Lrb = bc(Lr[:, :, :, None], sh9); Lib = bc(Li[:, :, :, None], sh9)
                C0 = bc(Cst[:, 0, None, :, :], sh9); C1 = bc(Cst[:, 1, None, :, :], sh9)
                def f_(e, Lrb=Lrb, Lib=Lib, C0=C0, C1=C1):
                    e.tensor_tensor(l1[:], Lrb, C0, ALU.mult)
                    return e.tensor_tensor(l2[:], Lib, C1, ALU.mult)
                V(f_, ["Lr", "Li", "Cst", "LB1"], ["l1", "l2"])
                V(lambda e: e.tensor_tensor(CL[:, 0], l1[:], l2[:], ALU.subtract), ["l1", "l2"], ["CL0", "LB0", "LB1"])
                def f_(e, Lrb=Lrb, Lib=Lib, C0=C0, C1=C1):
                    e.tensor_tensor(l1[:], Lib, C0, ALU.mult)
                    return e.tensor_tensor(l2[:], Lrb, C1, ALU.mult)
                V(f_, ["Lr", "Li", "Cst", "CL0"], ["l1", "l2"])
                V(lambda e: e.scalar_tensor_tensor(CL[:, 1], l1[:], -1.0, l2[:], ALU.mult, ALU.subtract), ["l1", "l2"], ["CL1", "LB0", "LB1"])
                def f_(e):
                    for g2 in range(2):
                        for ri in range(2):
                            e.tensor_copy(Qd[64 * g2:64 * g2 + 64, :, ri, :, 16 * g2:16 * g2 + 16], CL[64 * g2:64 * g2 + 64, ri, :, :, :])
                            r = e.tensor_copy(Qf5[64 * g2:64 * g2 + 64, ri, :, g2, :], CL[64 * g2:64 * g2 + 64, ri, 0, :, :])
                    return r
                V(f_, ["CL0", "CL1"], ["Qpad", "Qf"])
                for half in range(2):
                    bi = half
                    def mmt(e, half=half, bi=bi):
                        for i4 in range(4):
                            tau = half * 4 + i4
                            e.matmul(ps[bi][:, i4 * 128:(i4 + 1) * 128], lhsT=What[:, tau, 0, :], rhs=Qf[:, 0, :], start=True, stop=False)
                            r = e.matmul(ps[bi][:, i4 * 128:(i4 + 1) * 128], lhsT=What[:, tau, 1, :], rhs=Qf[:, 1, :], start=False, stop=True)
                        return r
                    P.pe(mmt, reads=["What", "Qf"], writes=[PSK(bi)])
                    m4 = bc(mask32[:, None, :], [128, 4, 128])
                    if half == 0:
                        V(lambda e, bi=bi, m4=m4: e.tensor_tensor(TtF[:], ps[bi][:].rearrange("p (a n) -> p a n", a=4), m4, ALU.mult),
                          [PSK(bi), "cf"], ["TtF"])
                        V(lambda e, j=j: e.scalar_tensor_tensor(TtF[:, 0, :], ident, dcol[:, j:j + 1], TtF[:, 0, :], ALU.mult, ALU.add),
                          ["TtF", "dcol", "cf"], ["TtF"])
                        V(lambda e: e.tensor_copy(Tt[:, 0:4, :], TtF[:]), ["TtF"], [("Tt", 0)])
                    else:
                        V(lambda e, bi=bi, m4=m4: e.tensor_tensor(Tt[:, 4:8, :], ps[bi][:].rearrange("p (a n) -> p a n", a=4), m4, ALU.mult),
                          [PSK(bi), "cf"], [("Tt", 1)])
                uv = uT[:, j, :].rearrange("p (k s) -> p s k", s=8)
                def mmx(e, uv=uv):
                    for ri in range(2):
                        for tau in range(8):
                            for q in range(4):
                                r = e.matmul(ps[2 + q][:, ri * 256:(ri + 1) * 256], lhsT=Wt[32 * q:32 * q + 32, tau, ri, :],
                                             rhs=uv[32 * q:32 * q + 32, 7 - tau, :], start=(tau == 0), stop=(tau == 7),
                                             tile_position=(32 * q, 0))
                    return r
                P.pe(mmx, reads=[("Wpad", c4) for c4 in range(4)] + [("uT", j, s_) for s_ in range(8)], writes=[PSK(2 + q) for q in range(4)])
                V(lambda e: e.tensor_copy(ti8[:], tj[:, 8, :]), ["tj"], ["ti8"])
                V(lambda e: e.tensor_tensor(t8f[:], tj[:, 8, :], ti8[:], ALU.subtract), ["tj", "ti8"], ["t8f"])
                V(lambda e: e.tensor_tensor(tk, bc(t8f[:, :, None], [128, 4, 256]), bc(iota256[:, None, :], [128, 4, 256]), ALU.mult),
                  ["t8f", "cf"], ["tk"] + [("Ssc", ri_, q_) for ri_ in range(2) for q_ in range(4)])
                V(lambda e: e.tensor_copy(tki, tk), ["tk"], ["tki"])
                V(lambda e: e.tensor_tensor(xa[:], tk, tki, ALU.subtract), ["tk", "tki"], ["xa"])
                Aop(lambda e: e.activation(sinT[:], xa[:], AF.Sin, scale=TWO_PI), ["xa"], ["sinT"])
                V(lambda e: e.tensor_scalar(tk, tk, 0.25, None, ALU.add), ["tk", "tki"], ["tk"])
                V(lambda e: e.tensor_copy(tki, tk), ["tk"], ["tki"])
                V(lambda e: e.tensor_tensor(xb[:], tk, tki, ALU.subtract), ["tk", "tki"], ["xb"])
                Aop(lambda e: e.activation(cosT[:], xb[:], AF.Sin, scale=TWO_PI), ["xb"], ["cosT"])
                for q in range(4):
                    Xr = ps[2 + q][:, 0:256]; Xi = ps[2 + q][:, 256:512]
                    def f_(e, q=q, Xr=Xr, Xi=Xi):
                        e.tensor_tensor(xa[:, q, :], cosT[:, q, :], Xr, ALU.mult)
                        return e.tensor_tensor(xb[:, q, :], sinT[:, q, :], Xi, ALU.mult)
                    V(f_, [PSK(2 + q), "cosT", "sinT", "xa", "xb"], [("xa", q), ("xb", q)])
                    V(lambda e, q=q: e.tensor_tensor(Xp[:, 0, q, :], xa[:, q, :], xb[:, q, :], ALU.add), [("xa", q), ("xb", q)], [("Xp", 0, q)])
                    def f_(e, q=q, Xr=Xr, Xi=Xi):
                        e.tensor_tensor(xa[:, q, :], cosT[:, q, :], Xi, ALU.mult)
                        return e.tensor_tensor(xb[:, q, :], sinT[:, q, :], Xr, ALU.mult)
                    V(f_, [PSK(2 + q), "cosT", "sinT", ("Xp", 0, q)], [("xa", q), ("xb", q)])
                    V(lambda e, q=q: e.tensor_tensor(Xp[:, 1, q, :], xa[:, q, :], xb[:, q, :], ALU.subtract), [("xa", q), ("xb", q)], [("Xp", 1, q)])
                    for ri in range(2):
                        V(lambda e, q=q, ri=ri: e.tensor_tensor_scan(Ssc[:, ri, q, :], mag[:, 8, q:q + 1].to_broadcast([128, 256]),
                                                                     Xp[:, ri, q, :], 0.0, ALU.mult, ALU.add),
                          [("Xp", ri, q), "mag"], [("Ssc", ri, q), "tk", "tki"])
                allS = [("Ssc", ri, q) for ri in range(2) for q in range(4)]
                allx = [("xa", q) for q in range(4)] + [("xb", q) for q in range(4)]
                def f_(e):
                    e.tensor_tensor(xa[:], cosT[:], Ssc[:, 0], ALU.mult)
                    return e.tensor_tensor(xb[:], sinT[:], Ssc[:, 1], ALU.mult)
                V(f_, allS + ["cosT", "sinT"], allx + ["xa", "xb"])
                V(lambda e: e.tensor_tensor(Hb[:, 0, :, 1:257], xa[:], xb[:], ALU.subtract), ["xa", "xb"], ["Hb0"])
                def f_(e):
                    e.tensor_tensor(xa[:], cosT[:], Ssc[:, 1], ALU.mult)
                    return e.tensor_tensor(xb[:], sinT[:], Ssc[:, 0], ALU.mult)
                V(f_, allS + ["cosT", "sinT", "Hb0"], allx + ["xa", "xb"])
                V(lambda e: e.tensor_tensor(Hb[:, 1, :, 1:257], xa[:], xb[:], ALU.add), ["xa", "xb"], ["Hb1"])
            def partC(j):
                uv = uT[:, j, :].rearrange("p (k s) -> p s k", s=8)
                for tp in range(7, -1, -1):
                    bi = 6 + (tp % 2)
                    def mmy(e, tp=tp, bi=bi, uv=uv):
                        for s_ in range(tp + 1):
                            e.matmul(ps[bi][:, 0:256], lhsT=Tt[:, tp - s_, :], rhs=uv[:, s_, :], start=(s_ == 0), stop=False)
                        for ri in range(2):
                            for q in range(4):
                                r = e.matmul(ps[bi][32 * q:32 * q + 32, 0:256], lhsT=Qd[:, tp + 1, ri, q, :], rhs=Hb[:, ri, q, 0:256], start=False,
                                             stop=(ri == 1), tile_position=(0, 32 * q))
                        return r
                    P.pe(mmy, reads=[("uT", j, s_) for s_ in range(tp + 1)] + [("Tt", 0), ("Tt", 1), "Qpad", "Hb0", "Hb1"], writes=[PSK(bi)])
                    Aop(lambda e, tp=tp, bi=bi, uv=uv: e.activation(uv[:, tp, :], ps[bi][:, 0:256], AF.Gelu_apprx_tanh),
                        [PSK(bi)], [("uT", j, tp)])
            partA(0)
            for j in range(8):
                partB(j)
                if j < 7:
                    partA(j + 1)
                partC(j)
        P.barrier()
        with SBT(nc, "wg", [128, 2, 2, 8, 512], BF16) as wg, SBT(nc, "sig", [128, 2, 512], F32) as sig, \
                SBT(nc, "mixb", [128, 2, 512], F32) as mixb:
            wglu = T["ssm_w_glu"][0].rearrange("(k p) n -> p k n", p=128)
            it = 0
            for mg in range(2):
                wb = mg % 2
                for vg in range(2):
                    c0 = vg * D + mg * 512
                    P.dma(lambda e, wb=wb, vg=vg, c0=c0: e.dma_start(out=wg[:, wb, vg, :, :], in_=wglu[:, :, c0:c0 + 512]),
                          writes=[("wg", wb, vg)], q="pool")
                for ml in range(4):
                    m = mg * 4 + ml
                    for tq in range(4):
                        pb = it % 2
                        it += 1
                        for vg in range(2):
                            bi = 2 + 2 * vg + pb
                            def mm(e, wb=wb, vg=vg, bi=bi, tq=tq, ml=ml):
                                for k in range(8):
                                    r = e.matmul(ps[bi][:], lhsT=wg[:, wb, vg, k, ml * 128:(ml + 1) * 128], rhs=uT[:, k, tq * 512:(tq + 1) * 512],
                                                 start=(k == 0), stop=(k == 7))
                                return r
                            P.pe(mm, reads=[("wg", wb, vg)], writes=[PSK(bi)])
                        Aop(lambda e, pb=pb: e.activation(sig[:, pb, :], ps[4 + pb][:], AF.Sigmoid), [PSK(4 + pb)], [("sig", pb)])
                        V(lambda e, pb=pb: e.tensor_tensor(mixb[:, pb, :], ps[2 + pb][:], sig[:, pb, :], ALU.mult), [PSK(2 + pb), ("sig", pb)], [("mixb", pb)])
                        V(lambda e, pb=pb, m=m, tq=tq: e.tensor_tensor(xT[:, m, tq * 512:(tq + 1) * 512], xT[:, m, tq * 512:(tq + 1) * 512],
                                                                     mixb[:, pb, :], ALU.add), [("mixb", pb), ("xT", m)], [("xT", m)])


_CACHE = {}


def kernel(**inputs):
    if "prog" not in _CACHE:
        _CACHE["prog"] = build_program()
    nc, _ = _CACHE["prog"]
    consts = make_consts()
    in_maps = []
    for c in range(8):
        m = {}
        for name, shape in INPUT_SPECS:
            if name == "consts":
                m[name] = consts
            elif name in ("x", "mem"):
                m[name] = np.ascontiguousarray(np.asarray(inputs[name], dtype=np.float32)[c * NB:(c + 1) * NB])
            else:
                m[name] = np.ascontiguousarray(np.asarray(inputs[name], dtype=np.float32))
        in_maps.append(m)
    res = run_bass_kernel_spmd(nc, in_maps, core_ids=list(range(8)))
    return np.concatenate([r["out"] for r in res.results], axis=0)
```

```python
import math
import numpy as np
import concourse.bass as bass
from concourse.ap import AP
import concourse.mybir as mybir
from concourse.bass_utils import run_bass_kernel_spmd

F32 = mybir.dt.float32
BF16 = mybir.dt.bfloat16
I32 = mybir.dt.int32
AF = mybir.ActivationFunctionType
ALU = mybir.AluOpType

COMPUTE = ("pe", "act", "dve", "pool")
ALLENG = ("pe", "act", "dve", "pool", "sp")
NDMA_SEMS = 40

S = 2048
D = 1024
NB = 2
DFF = 2816
NF = DFF // 128
MEM = 256


class Op:
    __slots__ = ("eng", "fn", "deps", "idx", "dma", "signal", "cnt", "sem", "clock")


class Prog:
    def __init__(self, nc):
        self.nc = nc
        self.ops = []
        self.last_write = {}
        self.readers = {}
        self.dma_hist = []
        self.n_dma = 0
        self.last_on = {}
        self.dma_since = []

    def add(self, eng, fn, reads=(), writes=(), dma=False, extra_deps=()):
        op = Op()
        op.eng, op.fn, op.dma = eng, fn, dma
        op.idx = len(self.ops)
        op.signal = False
        op.cnt = None
        op.sem = None
        op.clock = None
        deps = set(extra_deps)
        for k in reads:
            w = self.last_write.get(k)
            if w is not None:
                deps.add(w)
        for k in writes:
            w = self.last_write.get(k)
            if w is not None:
                deps.add(w)
            r = self.readers.get(k)
            if r:
                deps.update(r)
        for k in reads:
            self.readers.setdefault(k, []).append(op.idx)
        for k in writes:
            self.last_write[k] = op.idx
            self.readers[k] = []
        if dma:
            j = self.n_dma
            self.n_dma += 1
            op.sem = j % NDMA_SEMS
            if j >= NDMA_SEMS:
                deps.add(self.dma_hist[j - NDMA_SEMS])
            self.dma_hist.append(op.idx)
            self.dma_since.append(op.idx)
        else:
            self.last_on[eng] = op.idx
        deps.discard(op.idx)
        if eng == "pe" and not dma:
            deps = {d_ for d_ in deps if self.ops[d_].eng != "pe" or self.ops[d_].dma}
        op.deps = deps
        self.ops.append(op)
        return op

    def pe(self, fn, reads=(), writes=()):
        return self.add("pe", fn, reads, writes)

    def act(self, fn, reads=(), writes=()):
        return self.add("act", fn, reads, writes)

    def dve(self, fn, reads=(), writes=()):
        return self.add("dve", fn, reads, writes)

    def dma(self, fn, reads=(), writes=(), q="sp"):
        return self.add(q, fn, reads, writes, dma=True)

    def barrier(self):
        deps = set(self.last_on.values()) | set(self.dma_since)
        self.dma_since = []
        for e in ALLENG:
            self.add(e, lambda eng: eng.nop(), extra_deps=deps)
        self.last_write = {}
        self.readers = {}

    def emit(self, final_ops):
        nc = self.nc
        ops = self.ops
        for op in ops:
            for d in op.deps:
                ops[d].signal = True
        for op in final_ops:
            op.signal = True
        eng_cnt = {e: 0 for e in ALLENG}
        dma_cnt = [0] * NDMA_SEMS
        for op in ops:
            if op.dma:
                dma_cnt[op.sem] += 16
                op.cnt = dma_cnt[op.sem]
            elif op.signal:
                eng_cnt[op.eng] += 1
                op.cnt = eng_cnt[op.eng]
        sems = {e: nc.alloc_semaphore("s_" + e) for e in ALLENG}
        dsems = [nc.alloc_semaphore("d_%d" % i) for i in range(NDMA_SEMS)]

        def key_of(o):
            return ("d", o.sem) if o.dma else o.eng

        know = {e: {} for e in ALLENG}
        waits = {}
        for op in ops:
            K = know[op.eng]
            wl = []
            for d in sorted(op.deps, reverse=True):
                dop = ops[d]
                k = key_of(dop)
                if K.get(k, 0) >= dop.cnt:
                    continue
                for kk, vv in dop.clock.items():
                    if K.get(kk, 0) < vv:
                        K[kk] = vv
                K[k] = max(K.get(k, 0), dop.cnt)
                wl.append((dsems[dop.sem] if dop.dma else sems[dop.eng], dop.cnt))
            waits[op.idx] = wl
            if op.signal or op.dma:
                op.clock = dict(K)
        by_eng = {e: [] for e in ALLENG}
        for op in ops:
            by_eng[op.eng].append(op)
        fin = [(dsems[o.sem] if o.dma else sems[o.eng], o.cnt) for o in final_ops]
        self.n_inst = {e: len(by_eng[e]) for e in ALLENG}

        def run(engname, e):
            for op in by_eng[engname]:
                for (s, v) in waits[op.idx]:
                    e.wait_ge(s, v)
                ins = op.fn(e)
                if op.dma:
                    ins.then_inc(dsems[op.sem], 16)
                elif op.signal:
                    ins.then_inc(sems[op.eng], 1)
            if engname == "sp":
                for (s, v) in fin:
                    e.wait_ge(s, v)

        with nc.Block() as block:
            @block.tensor
            def _(e):
                run("pe", e)

            @block.scalar
            def _(e):
                run("act", e)

            @block.vector
            def _(e):
                run("dve", e)

            @block.gpsimd
            def _(e):
                run("pool", e)

            @block.sync
            def _(e):
                run("sp", e)


INPUT_SPECS = [
    ("x", [NB, S, D]), ("mem", [NB, MEM, D]),
    ("norm_mix", [2, D]), ("norm_xattn", [2, D]), ("norm_ffn", [2, D]), ("norm_mem", [D]), ("norm_final", [D]),
    ("ab_w_in", [1, D, 2048]), ("pool_w", [1, 4, 128, 128]), ("pool_scale", [1, 512]), ("ab_w_out", [1, D, D]),
    ("ssm_w_in", [1, D, D]), ("ssm_lam_re", [1, 64, 64]), ("ssm_lam_im", [1, 64, 64]), ("ssm_log_dt", [1, 64]),
    ("ssm_b_re", [1, 64, 64, 16]), ("ssm_b_im", [1, 64, 64, 16]), ("ssm_c_re", [1, 64, 16, 64]),
    ("ssm_c_im", [1, 64, 16, 64]), ("ssm_d", [1, D]), ("ssm_w_glu", [1, D, 2 * D]),
    ("xa_w_q", [2, D, D]), ("xa_w_kv", [2, D, 2 * D]), ("xa_w_o", [2, D, D]),
    ("ffn_w_up", [2, D, 2 * DFF]), ("ffn_conv_w", [2, 3, 2 * DFF]), ("ffn_conv_b", [2, 2 * DFF]),
    ("ffn_w_down", [2, DFF, D]),
    ("consts", [128, 12, 128]),
]


def make_consts():
    c = np.zeros((128, 12, 128), np.float32)
    j = np.arange(128)
    c[:, 0, :] = np.eye(128)
    c[:, 1, :] = -(j[:, None] > j[None, :]).astype(np.float32)
    c[:, 2, :] = -1.0
    c[:, 3, :] = 1.0
    c[:, 4, :] = (j[:, None] < j[None, :]).astype(np.float32)
    c[:, 5, :] = (j[:, None] // 32 == j[None, :] // 32).astype(np.float32)
    c[:, 6, :] = np.arange(128)[None, :]
    c[:, 7, :] = 128 + np.arange(128)[None, :]
    c[:, 8, :] = 1.0 / (1.0 + np.arange(128))[None, :]
    c[:, 9, :] = -(j[:, None] >= j[None, :]).astype(np.float32)
    c[:, 10, :] = -30000.0 * (j[:, None] >= j[None, :])
    return c


class Ctx:
    pass


_uid = [0]


def SBT(nc, name, shape, dt):
    _uid[0] += 1
    return nc.sbuf_tensor("%s_%d" % (name, _uid[0]), shape, dt)


def build_program(stop=None, nb=NB):
    nc = bass.Bass("TRN2", target_bir_lowering=False)
    P = Prog(nc)
    C = Ctx()
    C.nc, C.P = nc, P
    T = {}
    for name, shape in INPUT_SPECS:
        T[name] = nc.dram_tensor(name, shape, F32, kind="ExternalInput").ap()
    out = nc.dram_tensor("out", [NB, S, D], F32, kind="ExternalOutput").ap()
    C.T = T

    def sb(name, shape, dt=F32):
        return nc.alloc_sbuf_tensor(name, shape, dt)

    xT = sb("xT", [128, 8, S])
    cf = sb("cf", [128, 10, 128])
    cb = sb("cb", [128, 6, 128], BF16)
    gains = sb("gains", [128, 8, 8])
    convp = sb("convp", [128, 2, 4, 44])
    pscale = sb("pscale", [128, 4])
    zer = sb("zer", [128, 512], BF16)
    memT = sb("memT", [128, 8, MEM], BF16)
    ps = [nc.alloc_psum_tensor("ps%d" % i, [128, 512], F32) for i in range(8)]
    ident = cf[:, 0, :]
    maskstrict = cf[:, 4, :]

    def PSK(i):
        return ("ps", i)

    P.dma(lambda e: e.dma_start(out=cf[:], in_=T["consts"][:, 0:10, :]), writes=["cf"])
    P.dma(lambda e: e.dma_start(out=cb[:, 0:4, :], in_=T["consts"][:, 0:4, :]), writes=["cb"], q="pool")
    P.dma(lambda e: e.dma_start(out=cb[:, 4:6, :], in_=T["consts"][:, 9:11, :]), writes=["cb2"], q="pool")
    gsrc = [T["norm_mix"][0], T["norm_mix"][1], T["norm_xattn"][0], T["norm_xattn"][1],
            T["norm_ffn"][0], T["norm_ffn"][1], T["norm_mem"], T["norm_final"]]
    for i, g in enumerate(gsrc):
        P.dma(lambda e, i=i, g=g: e.dma_start(out=gains[:, i, :], in_=g.rearrange("(t p) -> p t", p=128),
                                             allow_slow_non_contiguous=True), writes=["gains"], q="act")
    for l in range(2):
        for i in range(3):
            P.dma(lambda e, l=l, i=i: e.dma_start(out=convp[:, l, i, :],
                                                  in_=T["ffn_conv_w"][l, i].rearrange("(t p) -> p t", p=128),
                                                  allow_slow_non_contiguous=True), writes=["convp"], q="act")
        P.dma(lambda e, l=l: e.dma_start(out=convp[:, l, 3, :],
                                         in_=T["ffn_conv_b"][l].rearrange("(t p) -> p t", p=128),
                                         allow_slow_non_contiguous=True), writes=["convp"], q="act")
    P.dma(lambda e: e.dma_start(out=pscale[:], in_=T["pool_scale"][0].rearrange("(t p) -> p t", p=128),
                                allow_slow_non_contiguous=True), writes=["pscale"], q="act")
    P.dve(lambda e: e.memset(zer[:], 0.0), writes=["zer"])

    def load_w(dst, src2d, key, k_tiles, col0, ncols):
        v = src2d.rearrange("(k p) n -> p k n", p=128)
        for k in range(k_tiles):
            P.dma(lambda e, k=k: e.dma_start(out=dst[:, k, :], in_=v[:, k, col0:col0 + ncols]),
                  writes=[(key, k)], q="pool")

    def rmsnorm_tile(hT, hkey, gi, t0, n, sq, rstd, part="all"):
        if part in ("all", "sq"):
            for dt in range(8):
                P.act(lambda e, dt=dt: e.activation(sq[:, dt, 0:n], xT[:, dt, t0:t0 + n], AF.Square),
                      reads=[("xT", dt)], writes=[("sq", dt)])
        if part == "sq":
            return
        def mm(e):
            for dt in range(8):
                r = e.matmul(ps[7][:, 0:n], lhsT=cb[:, 3, :], rhs=sq[:, dt, 0:n], start=(dt == 0), stop=(dt == 7))
            return r
        P.pe(mm, reads=[("sq", dt) for dt in range(8)] + ["cb"], writes=[PSK(7)])
        P.dve(lambda e: e.tensor_scalar(rstd[:, 0:n], ps[7][:, 0:n], 1.0 / D, 1e-6, ALU.mult, ALU.add),
              reads=[PSK(7)], writes=["rstd"])
        P.act(lambda e: e.activation(rstd[:, 0:n], rstd[:, 0:n], AF.Ln), reads=["rstd"], writes=["rstd"])
        P.act(lambda e: e.activation(rstd[:, 0:n], rstd[:, 0:n], AF.Exp, scale=-0.5), reads=["rstd"], writes=["rstd"])
        for dt in range(8):
            P.dve(lambda e, dt=dt: e.scalar_tensor_tensor(hT[:, dt, 0:n], xT[:, dt, t0:t0 + n], gains[:, gi, dt:dt + 1],
                                                          rstd[:, 0:n], ALU.mult, ALU.mult),
                  reads=[("xT", dt), "rstd", "gains"], writes=[(hkey, dt)])

    C.ps_rr = 0

    def linear_fm(w, wkey, act, akey, k_tiles, m_tiles, n, evac, a0=0, banks=(0, 1)):
        for m in range(m_tiles):
            bi = banks[C.ps_rr % len(banks)]
            C.ps_rr += 1
            def mm(e, m=m, bi=bi):
                for k in range(k_tiles):
                    r = e.matmul(ps[bi][:, 0:n], lhsT=w[:, k, m * 128:(m + 1) * 128], rhs=act[:, k, a0:a0 + n],
                                 start=(k == 0), stop=(k == k_tiles - 1))
                return r
            P.pe(mm, reads=[(wkey, k) for k in range(k_tiles)] + [(akey, k) for k in range(k_tiles)], writes=[PSK(bi)])
            evac(m, bi, ps[bi][:, 0:n])

    def resid_add(m, bi, pap, t0, n):
        P.dve(lambda e: e.tensor_tensor(xT[:, m, t0:t0 + n], xT[:, m, t0:t0 + n], pap, ALU.add),
              reads=[PSK(bi), ("xT", m)], writes=[("xT", m)])

    final_ops = []
    for b in range(nb):
        if b > 0:
            P.barrier()
        with SBT(nc, "xin", [128, 2, D], F32) as xin, SBT(nc, "sq", [128, 8, 512], BF16) as sq, \
                SBT(nc, "rstd", [128, 512], F32) as rstd, SBT(nc, "mn", [128, 2, D], F32) as mn, \
                SBT(nc, "ssq", [128, 4], F32) as ssq:
            for tt in range(16):
                xb = tt % 2
                P.dma(lambda e, tt=tt, xb=xb, b=b: e.dma_start(out=xin[:, xb, :], in_=T["x"][b, tt * 128:(tt + 1) * 128, :]),
                      writes=[("xin", xb)])
                for half in range(2):
                    bi = (tt * 2 + half) % 2
                    def tr(e, xb=xb, half=half, bi=bi):
                        for q in range(4):
                            dt = half * 4 + q
                            r = e.transpose(ps[bi][:, q * 128:(q + 1) * 128], xin[:, xb, dt * 128:(dt + 1) * 128], ident)
                        return r
                    P.pe(tr, reads=[("xin", xb), "cf"], writes=[PSK(bi)])
                    P.act(lambda e, tt=tt, half=half, bi=bi: e.activation(
                        xT[:, half * 4:half * 4 + 4, tt * 128:(tt + 1) * 128],
                        ps[bi][:].rearrange("p (q t) -> p q t", q=4), AF.Copy),
                        reads=[PSK(bi)], writes=[("xT", half * 4 + q) for q in range(4)])
            P.dma(lambda e, b=b: e.dma_start(out=mn[:], in_=T["mem"][b].rearrange("(t p) d -> p t d", p=128)), writes=["mn"])
            for t in range(2):
                P.act(lambda e, t=t: e.activation(xin[:, t, :], mn[:, t, :], AF.Square, accum_out=ssq[:, t:t + 1]),
                      reads=["mn"], writes=[("ssq", t), ("xin", t)])
            P.dve(lambda e: e.tensor_scalar(ssq[:, 2:4], ssq[:, 0:2], 1.0 / D, 1e-6, ALU.mult, ALU.add),
                  reads=[("ssq", 0), ("ssq", 1)], writes=["ssq2"])
            P.act(lambda e: e.activation(ssq[:, 2:4], ssq[:, 2:4], AF.Sqrt), reads=["ssq2"], writes=["ssq2"])
            P.dve(lambda e: e.reciprocal(ssq[:, 2:4], ssq[:, 2:4]), reads=["ssq2"], writes=["ssq2"])
            for t in range(2):
                P.dve(lambda e, t=t: e.tensor_scalar(mn[:, t, :], mn[:, t, :], ssq[:, 2 + t:3 + t], None, ALU.mult),
                      reads=["mn", "ssq2"], writes=["mn"])
            for t in range(2):
                for half in range(2):
                    bi = (t * 2 + half) % 2
                    def tr(e, t=t, half=half, bi=bi):
                        for q in range(4):
                            dt = half * 4 + q
                            r = e.transpose(ps[bi][:, q * 128:(q + 1) * 128], mn[:, t, dt * 128:(dt + 1) * 128], ident)
                        return r
                    P.pe(tr, reads=["mn", "cf"], writes=[PSK(bi)])
                    for q in range(4):
                        dt = half * 4 + q
                        P.dve(lambda e, t=t, q=q, dt=dt, bi=bi: e.tensor_scalar(
                            memT[:, dt, t * 128:(t + 1) * 128], ps[bi][:, q * 128:(q + 1) * 128],
                            gains[:, 6, dt:dt + 1], None, ALU.mult),
                            reads=[PSK(bi), "gains"], writes=[("memT", dt)])
        if stop == "load":
            pass
        else:
            for layer in range(2):
                if layer == 0:
                    stage_mix_ab(C, b, xT, ps, cf, cb, gains, pscale, zer, rmsnorm_tile, load_w, linear_fm, resid_add)
                else:
                    stage_mix_s5(C, b, xT, ps, cf, cb, gains, rmsnorm_tile, load_w, linear_fm, resid_add)
                if stop == "mix%d" % layer:
                    break
                stage_xattn(C, b, layer, xT, ps, cb, gains, memT, rmsnorm_tile, load_w, linear_fm, resid_add)
                if stop == "xa%d" % layer:
                    break
                stage_ffn(C, b, layer, xT, ps, gains, convp, rmsnorm_tile, load_w, linear_fm, resid_add)
                if stop == "ffn%d" % layer:
                    break
        P.barrier()
        with SBT(nc, "sq", [128, 8, 512], BF16) as sq, SBT(nc, "rstd", [128, 512], F32) as rstd, \
                SBT(nc, "yT", [128, 8, 512], F32) as yT, SBT(nc, "yo", [128, 2, D], F32) as yo:
            for tq in range(4):
                t0 = tq * 512
                if stop is None:
                    rmsnorm_tile(yT, "yT", 7, t0, 512, sq, rstd)
                else:
                    for dt in range(8):
                        P.act(lambda e, dt=dt, t0=t0: e.activation(yT[:, dt, :], xT[:, dt, t0:t0 + 512], AF.Copy),
                              reads=[("xT", dt)], writes=[("yT", dt)])
                for ts in range(4):
                    ob = ts % 2
                    for half in range(2):
                        bi = (ts * 2 + half) % 2
                        def tr(e, ts=ts, half=half, bi=bi):
                            for q in range(4):
                                dt = half * 4 + q
                                r = e.transpose(ps[bi][:, q * 128:(q + 1) * 128], yT[:, dt, ts * 128:(ts + 1) * 128], ident)
                            return r
                        P.pe(tr, reads=[("yT", dt) for dt in range(8)] + ["cf"], writes=[PSK(bi)])
                        P.act(lambda e, ob=ob, half=half, bi=bi: e.activation(yo[:, ob, half * 512:(half + 1) * 512],
                                                                               ps[bi][:], AF.Copy),
                              reads=[PSK(bi)], writes=[("yo", ob, half)])
                    tok = t0 + ts * 128
                    o = P.dma(lambda e, ob=ob, tok=tok, b=b: e.dma_start(out=out[b, tok:tok + 128, :], in_=yo[:, ob, :]),
                              reads=[("yo", ob, 0), ("yo", ob, 1)], writes=[("out", b, tok)])
                    final_ops.append(o)
    P.emit(final_ops)
    C.final = final_ops
    return nc, P


def stage_xattn(C, b, layer, xT, ps, cb, gains, memT, rmsnorm_tile, load_w, linear_fm, resid_add):
    nc, P, T = C.nc, C.P, C.T
    P.barrier()

    def PSK(i):
        return ("ps", i)
    with SBT(nc, "wq", [128, 8, D], BF16) as wq, SBT(nc, "wo", [128, 8, D], BF16) as wo, \
            SBT(nc, "wkv", [128, 8, D], BF16) as wkv, \
            SBT(nc, "KT", [128, 8, MEM], BF16) as KT, SBT(nc, "V", [128, 2, D], BF16) as V, \
            SBT(nc, "sq", [128, 8, 512], BF16) as sq, SBT(nc, "rstd", [128, 512], F32) as rstd, \
            SBT(nc, "hT", [128, 2, 8, 512], BF16) as hT, SBT(nc, "qT", [128, 8, 512], BF16) as qT, \
            SBT(nc, "pT", [128, 2, 2, 512], BF16) as pT, SBT(nc, "rs", [128, 2, 512], F32) as rs, \
            SBT(nc, "oT", [128, 8, 512], BF16) as oT:
        load_w(wkv, T["xa_w_kv"][layer], "wkv", 8, 0, D)
        load_w(wq, T["xa_w_q"][layer], "wq", 8, 0, D)

        def evK(m, bi, pap):
            P.act(lambda e: e.activation(KT[:, m, :], pap, AF.Copy), reads=[PSK(bi)], writes=[("KT", m)])
        linear_fm(wkv, "wkv", memT, "memT", 8, 8, MEM, evK)
        load_w(wkv, T["xa_w_kv"][layer], "wkv", 8, D, D)
        load_w(wo, T["xa_w_o"][layer], "wo", 8, 0, D)
        for mt in range(2):
            for nh in range(2):
                bi = (mt * 2 + nh) % 2
                def mm(e, mt=mt, nh=nh, bi=bi):
                    for k in range(8):
                        r = e.matmul(ps[bi][:], lhsT=memT[:, k, mt * 128:(mt + 1) * 128], rhs=wkv[:, k, nh * 512:(nh + 1) * 512],
                                     start=(k == 0), stop=(k == 7))
                    return r
                P.pe(mm, reads=[("wkv", k) for k in range(8)] + [("memT", k) for k in range(8)], writes=[PSK(bi)])
                P.act(lambda e, mt=mt, nh=nh, bi=bi: e.activation(V[:, mt, nh * 512:(nh + 1) * 512], ps[bi][:], AF.Copy),
                      reads=[PSK(bi)], writes=[("V", mt, nh)])
        rmsnorm_tile(hT[:, 0], "hT0", 2 + layer, 0, 512, sq, rstd)
        for tq in range(4):
            t0 = tq * 512
            tb = tq % 2

            def evQ(m, bi, pap):
                P.act(lambda e: e.activation(qT[:, m, :], pap, AF.Copy, scale=1.0 / 16.0), reads=[PSK(bi)], writes=[("qT", m)])
            linear_fm(wq, "wq", hT[:, tb], "hT%d" % tb, 8, 8, 512, evQ)
            def scores(h):
                hb = h % 2
                for mt in range(2):
                    bk = 2 + 2 * hb + mt
                    def mm(e, h=h, mt=mt, bk=bk):
                        for d in range(2):
                            r = e.matmul(ps[bk][:], lhsT=KT[:, 2 * h + d, mt * 128:(mt + 1) * 128], rhs=qT[:, 2 * h + d, :],
                                         start=(d == 0), stop=(d == 1))
                        return r
                    P.pe(mm, reads=[("KT", 2 * h), ("KT", 2 * h + 1), ("qT", 2 * h), ("qT", 2 * h + 1)], writes=[PSK(bk)])
                    P.act(lambda e, mt=mt, hb=hb, bk=bk: e.activation(pT[:, hb, mt, :], ps[bk][:], AF.Exp),
                          reads=[PSK(bk)], writes=[("pT", hb, mt)])

            def rest(h):
                hb = h % 2
                def mms(e, hb=hb):
                    e.matmul(ps[6][:], lhsT=cb[:, 3, :], rhs=pT[:, hb, 0, :], start=True, stop=False)
                    return e.matmul(ps[6][:], lhsT=cb[:, 3, :], rhs=pT[:, hb, 1, :], start=False, stop=True)
                P.pe(mms, reads=[("pT", hb, 0), ("pT", hb, 1), "cb"], writes=[PSK(6)])
                P.act(lambda e, hb=hb: e.activation(rs[:, hb, :], ps[6][:], AF.Ln), reads=[PSK(6)], writes=[("rs", hb)])
                P.act(lambda e, hb=hb: e.activation(rs[:, hb, :], rs[:, hb, :], AF.Exp, scale=-1.0), reads=[("rs", hb)], writes=[("rs", hb)])
                for d in range(2):
                    bi = 7 if d == 0 else 1
                    def mmo(e, h=h, hb=hb, d=d, bi=bi):
                        for mt in range(2):
                            r = e.matmul(ps[bi][:], lhsT=V[:, mt, h * 256 + d * 128:h * 256 + (d + 1) * 128], rhs=pT[:, hb, mt, :],
                                         start=(mt == 0), stop=(mt == 1))
                        return r
                    P.pe(mmo, reads=[("V", 0, h // 2), ("V", 1, h // 2), ("pT", hb, 0), ("pT", hb, 1)], writes=[PSK(bi)])
                    P.dve(lambda e, h=h, hb=hb, d=d, bi=bi: e.tensor_tensor(oT[:, 2 * h + d, :], ps[bi][:], rs[:, hb, :], ALU.mult),
                          reads=[PSK(bi), ("rs", hb)], writes=[("oT", 2 * h + d)])

            scores(0)
            for h in range(4):
                if h < 3:
                    scores(h + 1)
                rest(h)
            if tq + 1 < 4:
                rmsnorm_tile(hT[:, 1 - tb], "hT%d" % (1 - tb), 2 + layer, t0 + 512, 512, sq, rstd, part="sq")
            linear_fm(wo, "wo", oT, "oT", 8, 8, 512, lambda m, bi, pap, t0=t0: resid_add(m, bi, pap, t0, 512))
            if tq + 1 < 4:
                rmsnorm_tile(hT[:, 1 - tb], "hT%d" % (1 - tb), 2 + layer, t0 + 512, 512, sq, rstd, part="rest")


def stage_ffn(C, b, layer, xT, ps, gains, convp, rmsnorm_tile, load_w, linear_fm, resid_add):
    nc, P, T = C.nc, C.P, C.T
    P.barrier()

    def PSK(i):
        return ("ps", i)
    with SBT(nc, "sq", [128, 8, 512], BF16) as sq, SBT(nc, "rstd", [128, 512], F32) as rstd, \
            SBT(nc, "hT", [128, 2, 8, 512], BF16) as hT, SBT(nc, "gT", [128, NF, 512], BF16) as gT, \
            SBT(nc, "wu", [128, 2, 2, 8, 512], BF16) as wu, SBT(nc, "wd", [128, 4, D], BF16) as wd, \
            SBT(nc, "ub", [128, 3, 2, 516], F32) as ub, SBT(nc, "cv", [128, 3, 2, 512], F32) as cv, \
            SBT(nc, "halo", [128, 2 * NF, 2], F32) as halo:
        wup = T["ffn_w_up"][layer].rearrange("(k p) n -> p k n", p=128)
        wdn = T["ffn_w_down"][layer]
        P.dve(lambda e: e.memset(halo[:], 0.0), writes=["halo"])
        groups = [(0, 4), (4, 4), (8, 4), (12, 4), (16, 4), (20, 2)]
        it = 0
        git = 0
        kit = 0
        rmsnorm_tile(hT[:, 0], "hT0", 4 + layer, 0, 512, sq, rstd)
        pend = []

        def tail(pb, fp):
            P.act(lambda e: e.activation(cv[:, pb, 1, :], cv[:, pb, 1, :], AF.Silu),
                  reads=[("cv", pb, 1)], writes=[("cv", pb, 1)])
            P.dve(lambda e: e.tensor_tensor(gT[:, fp, :], cv[:, pb, 0, :], cv[:, pb, 1, :], ALU.mult),
                  reads=[("cv", pb, 0), ("cv", pb, 1)], writes=[("gT", fp)])
        for tq in range(4):
            t0 = tq * 512
            hb = tq % 2
            for (f0, nf) in groups:
                wb = git % 2
                git += 1
                for vg in range(2):
                    col0 = vg * DFF + f0 * 128
                    P.dma(lambda e, wb=wb, vg=vg, col0=col0, nf=nf: e.dma_start(out=wu[:, wb, vg, :, 0:nf * 128],
                                                                              in_=wup[:, :, col0:col0 + nf * 128]),
                          writes=[("wu", wb, vg)], q="pool")
                for fl in range(nf):
                    fp = f0 + fl
                    pb = it % 3
                    it += 1
                    for vg in range(2):
                        bi = 3 * vg + pb
                        f = vg * NF + fp
                        def mm(e, wb=wb, vg=vg, bi=bi, fl=fl, hb=hb):
                            for k in range(8):
                                r = e.matmul(ps[bi][:], lhsT=wu[:, wb, vg, k, fl * 128:(fl + 1) * 128], rhs=hT[:, hb, k, :], start=(k == 0), stop=(k == 7))
                            return r
                        P.pe(mm, reads=[("wu", wb, vg)] + [("hT%d" % hb, k) for k in range(8)], writes=[PSK(bi)])
                        P.act(lambda e, pb=pb, vg=vg, bi=bi: e.activation(ub[:, pb, vg, 2:514], ps[bi][:], AF.Copy),
                              reads=[PSK(bi)], writes=[("ub", pb, vg)])
                        P.act(lambda e, pb=pb, vg=vg, f=f: e.activation(ub[:, pb, vg, 0:2], halo[:, f, :], AF.Copy),
                              reads=["halo%d" % f, "halo"], writes=[("ubh", pb, vg)])
                        P.act(lambda e, pb=pb, vg=vg, f=f, bi=bi: e.activation(cv[:, pb, vg, :], ps[bi][:], AF.Identity,
                                                                               bias=convp[:, layer, 3, f:f + 1],
                                                                               scale=convp[:, layer, 2, f:f + 1]),
                              reads=[PSK(bi), "convp"], writes=[("cv", pb, vg)])
                        P.dve(lambda e, pb=pb, vg=vg, f=f: e.scalar_tensor_tensor(cv[:, pb, vg, :], ub[:, pb, vg, 1:513],
                                                                                  convp[:, layer, 1, f:f + 1], cv[:, pb, vg, :],
                                                                                  ALU.mult, ALU.add),
                              reads=[("ub", pb, vg), ("ubh", pb, vg), ("cv", pb, vg), "convp"], writes=[("cv", pb, vg)])
                        P.dve(lambda e, pb=pb, vg=vg, f=f: e.scalar_tensor_tensor(cv[:, pb, vg, :], ub[:, pb, vg, 0:512],
                                                                                  convp[:, layer, 0, f:f + 1], cv[:, pb, vg, :],
                                                                                  ALU.mult, ALU.add),
                              reads=[("ub", pb, vg), ("ubh", pb, vg), ("cv", pb, vg), "convp"], writes=[("cv", pb, vg)])
                        P.dve(lambda e, pb=pb, vg=vg, f=f: e.tensor_copy(halo[:, f, :], ub[:, pb, vg, 512:514]),
                              reads=[("ub", pb, vg)], writes=["halo%d" % f])
                    if pend:
                        tail(*pend.pop())
                    pend.append((pb, fp))
            if pend:
                tail(*pend.pop())
            if tq + 1 < 4:
                rmsnorm_tile(hT[:, 1 - hb], "hT%d" % (1 - hb), 4 + layer, t0 + 512, 512, sq, rstd, part="sq")
            for k in range(NF):
                db = kit % 4
                kit += 1
                P.dma(lambda e, db=db, k=k: e.dma_start(out=wd[:, db, :], in_=wdn[k * 128:(k + 1) * 128, :]),
                      writes=[("wd", db)], q="pool")
                def mm(e, db=db, k=k):
                    for m in range(8):
                        r = e.matmul(ps[m][:], lhsT=wd[:, db, m * 128:(m + 1) * 128], rhs=gT[:, k, :], start=(k == 0), stop=(k == NF - 1))
                    return r
                P.pe(mm, reads=[("wd", db), ("gT", k)], writes=[PSK(m) for m in range(8)])
            resid_add(7, 7, ps[7][:], t0, 512)
            if tq + 1 < 4:
                rmsnorm_tile(hT[:, 1 - hb], "hT%d" % (1 - hb), 4 + layer, t0 + 512, 512, sq, rstd, part="rest")
            for m in range(7):
                resid_add(m, m, ps[m][:], t0, 512)


def stage_mix_ab(C, b, xT, ps, cf, cb, gains, pscale, zer, rmsnorm_tile, load_w, linear_fm, resid_add):
    nc, P, T = C.nc, C.P, C.T
    P.barrier()
    maskstrict = cf[:, 4, :]

    def PSK(i):
        return ("ps", i)
    win = T["ab_w_in"][0]
    with SBT(nc, "hT", [128, 8, S], BF16) as hT, SBT(nc, "aT", [128, 4, S], BF16) as aT, \
            SBT(nc, "pTo", [128, 4, S], BF16) as pTo:
        with SBT(nc, "sq", [128, 8, 512], BF16) as sq, SBT(nc, "rstd", [128, 512], F32) as rstd, \
                SBT(nc, "hTt", [128, 8, 512], BF16) as hTt:
            for tq in range(4):
                rmsnorm_tile(hTt, "hTt", 0, tq * 512, 512, sq, rstd)
                for dt in range(8):
                    P.act(lambda e, dt=dt, tq=tq: e.activation(hT[:, dt, tq * 512:(tq + 1) * 512], hTt[:, dt, :], AF.Copy),
                          reads=[("hTt", dt)], writes=[("hT", dt)])
        P.barrier()
        with SBT(nc, "wu4", [128, 8, 512], BF16) as wu4, SBT(nc, "wp", [128, 4, 128], BF16) as wp, \
                SBT(nc, "uA", [128, S], F32) as uA, SBT(nc, "uB", [128, S], F32) as uB, \
                SBT(nc, "u02", [128, 2, S], F32) as u02, SBT(nc, "pb", [128, S], BF16) as pb:
            def uproj(g):
                ub_ = g % 2
                for tq in range(4):
                    bi = tq % 2
                    def mm(e, tq=tq, bi=bi, g=g):
                        for k in range(8):
                            r = e.matmul(ps[bi][:], lhsT=wu4[:, k, g * 128:(g + 1) * 128], rhs=hT[:, k, tq * 512:(tq + 1) * 512], start=(k == 0), stop=(k == 7))
                        return r
                    P.pe(mm, reads=[("wu4", k) for k in range(8)] + [("hT", k) for k in range(8)], writes=[PSK(bi)])
                    P.act(lambda e, tq=tq, bi=bi, ub_=ub_: e.activation(u02[:, ub_, tq * 512:(tq + 1) * 512], ps[bi][:], AF.Copy),
                          reads=[PSK(bi)], writes=[("u0", ub_)])
            load_w(wu4, win, "wu4", 8, 1536, 512)
            uproj(0)
            for g in range(4):
                w_ = 2 ** (g + 1)
                ub_ = g % 2
                u0 = u02[:, ub_]
                u0k = ("u0", ub_)
                P.dma(lambda e, g=g: e.dma_start(out=wp[:, g, :], in_=T["pool_w"][0, g]), writes=[("wp", g)], q="pool")
                if g < 3:
                    uproj(g + 1)
                src, srck = u0, u0k
                bufs = [(uA, "uA"), (uB, "uB")]
                for st in range(g + 1):
                    sh = 2 ** st
                    dst, dstk = bufs[st % 2]
                    def stp(e, src=src, dst=dst, sh=sh):
                        e.tensor_copy(dst[:, 0:sh], src[:, 0:sh])
                        return e.tensor_tensor(dst[:, sh:S], src[:, sh:S], src[:, 0:S - sh], ALU.add)
                    P.dve(stp, reads=[srck], writes=[dstk])
                    src, srck = dst, dstk
                def pl(e, src=src, w_=w_, u0=u0):
                    e.scalar_tensor_tensor(pb[:, w_ - 1:S], src[:, w_ - 1:S], 1.0 / w_, u0[:, w_ - 1:S], ALU.mult, ALU.subtract)
                    return e.tensor_tensor(src[:, 0:w_ - 1], src[:, 0:w_ - 1], cf[:, 8, 0:w_ - 1], ALU.mult)
                P.dve(pl, reads=[srck, u0k, "cf"], writes=["pb0", srck])
                P.dve(lambda e, src=src, w_=w_, u0=u0: e.tensor_tensor(pb[:, 0:w_ - 1], src[:, 0:w_ - 1], u0[:, 0:w_ - 1], ALU.subtract),
                      reads=[srck, u0k], writes=["pb1"])
                for tq in range(4):
                    bi = 2 + tq % 2
                    P.pe(lambda e, tq=tq, bi=bi, g=g: e.matmul(ps[bi][:], lhsT=wp[:, g, :], rhs=pb[:, tq * 512:(tq + 1) * 512], start=True, stop=True),
                         reads=[("wp", g), "pb0", "pb1"], writes=[PSK(bi)])
                    P.act(lambda e, tq=tq, bi=bi, g=g: e.activation(pTo[:, g, tq * 512:(tq + 1) * 512], ps[bi][:], AF.Identity,
                                                                    scale=pscale[:, g:g + 1]),
                          reads=[PSK(bi), "pscale"], writes=[("pTo", g)])
        P.barrier()
        NBUF = 4
        with SBT(nc, "wqkv", [128, 8, 1536], BF16) as wqkv, SBT(nc, "qh", [128, S], BF16) as qh, \
                SBT(nc, "kh", [128, S], BF16) as kh, SBT(nc, "vh", [128, 16, 128], BF16) as vh, \
                SBT(nc, "ex", [128, NBUF, 512], F32) as ex, \
                SBT(nc, "spb", [128, NBUF, 512], BF16) as spb, \
                SBT(nc, "wsb", [128, NBUF, 512], BF16) as wsb, SBT(nc, "Ls", [128, 2, 512], F32) as Ls, \
                SBT(nc, "Lsb", [128, 4, 512], BF16) as Lsb, SBT(nc, "otmp", [64, 2, 512], BF16) as otmp:
            identb = cb[:, 0, :]
            trinc = cb[:, 4, :]
            maskneg = cb[:, 5, :]
            onesneg = cb[:, 2, :]
            git = 0
            for h in range(8):
                hp = h // 2
                if h == 0:
                    load_w(wqkv, win, "wqkv", 8, 0, 1536)
                hb64 = 64 * (h % 2)
                for j3, (dst, dk, scl) in enumerate([(qh, "qh", 0.125), (kh, "kh", 1.0)]):
                    if h % 2 == 1:
                        break
                    for tq in range(4):
                        bi = 4 + tq % 2
                        def mm(e, h=h, j3=j3, tq=tq, bi=bi):
                            for k in range(8):
                                r = e.matmul(ps[bi][:, :], lhsT=wqkv[:, k, j3 * 512 + h * 64:j3 * 512 + h * 64 + 128],
                                             rhs=hT[:, k, tq * 512:(tq + 1) * 512], start=(k == 0), stop=(k == 7))
                            return r
                        P.pe(mm, reads=[("wqkv", k) for k in range(8)] + [("hT", k) for k in range(8)], writes=[PSK(bi)])
                        P.act(lambda e, dst=dst, tq=tq, bi=bi, scl=scl: e.activation(dst[:, tq * 512:(tq + 1) * 512], ps[bi][:, :],
                                                                                   AF.Copy, scale=scl),
                              reads=[PSK(bi)], writes=[(dk, tq)])
                if h % 2 == 0:
                    for t4 in range(4):
                        bi = 4 + t4 % 2
                        def mmv(e, h=h, t4=t4, bi=bi):
                            for tl in range(4):
                                tt = t4 * 4 + tl
                                for k in range(8):
                                    r = e.matmul(ps[bi][:, tl * 128:(tl + 1) * 128], lhsT=hT[:, k, tt * 128:(tt + 1) * 128],
                                                 rhs=wqkv[:, k, 1024 + h * 64:1024 + h * 64 + 128], start=(k == 0), stop=(k == 7))
                            return r
                        P.pe(mmv, reads=[("wqkv", k) for k in range(8)] + [("hT", k) for k in range(8)], writes=[PSK(bi)])
                        P.dve(lambda e, t4=t4, bi=bi: e.tensor_copy(vh[:, t4 * 4:(t4 + 1) * 4, :],
                                                                    ps[bi][:].rearrange("p (t c) -> p t c", t=4)),
                              reads=[PSK(bi)], writes=[("vh", t4)])
                its = []
                for j in range(4):
                    kbs = list(range(4 * j + 3, -1, -1))
                    for ii, kb in enumerate(kbs):
                        diag = kb >= 4 * j
                        qlo = 128 * (kb - 4 * j) if diag else 0
                        its.append(dict(j=j, kb=kb, first=(ii == 0), last=(kb == 0), diag=diag, qlo=qlo, g=git))
                        git += 1

                def zmm(e, dst, it_, stop_after, hb64=hb64):
                    kb, qlo, j = it_["kb"], it_["qlo"], it_["j"]
                    q0 = 512 * j + qlo
                    r = e.matmul(dst[:, qlo:512], lhsT=kh[hb64:hb64 + 64, kb * 128:(kb + 1) * 128], rhs=qh[hb64:hb64 + 64, q0:512 * (j + 1)],
                                 start=True, stop=(stop_after and not it_["diag"]))
                    if it_["diag"]:
                        r = e.matmul(dst[:, qlo:qlo + 128], lhsT=identb, rhs=maskneg, start=False, stop=stop_after)
                    return r

                def stageA(it_):
                    r_ = it_["g"] % NBUF
                    j, kb, qlo = it_["j"], it_["kb"], it_["qlo"]
                    lb = j % 2
                    if it_["first"]:
                        P.dve(lambda e, lb=lb: e.memset(Ls[:, lb, :], 0.0), writes=[("Ls", lb)])
                    P.pe(lambda e, it_=it_, r_=r_, zmm=zmm: zmm(e, ps[r_], it_, True),
                         reads=[("kh", kb // 4), ("qh", j), "cb"], writes=[PSK(r_)])
                    P.act(lambda e, r_=r_, qlo=qlo: e.activation(ex[:, r_, qlo:512], ps[r_][:, qlo:512], AF.Exp),
                          reads=[PSK(r_)], writes=[("ex", r_)])
                    P.act(lambda e, r_=r_, qlo=qlo: e.activation(spb[:, r_, qlo:512], ex[:, r_, qlo:512], AF.Ln, bias=1.0),
                          reads=[("ex", r_)], writes=[("spb", r_)])
                    if not it_["last"]:
                        nqlo = max(0, 128 * (kb - 1 - 4 * j))
                        nr = (it_["g"] + 1) % 4
                        P.dve(lambda e, r_=r_, qlo=qlo, lb=lb: e.tensor_tensor(Ls[:, lb, qlo:512], Ls[:, lb, qlo:512], spb[:, r_, qlo:512], ALU.add),
                              reads=[("Ls", lb), ("spb", r_)], writes=[("Ls", lb)])
                        P.dve(lambda e, nqlo=nqlo, nr=nr, lb=lb: e.tensor_copy(Lsb[:, nr, nqlo:512], Ls[:, lb, nqlo:512]),
                              reads=[("Ls", lb)], writes=[("Lsb", nr)])

                def stageB1(it_):
                    r_ = it_["g"] % NBUF
                    qlo = it_["qlo"]
                    pst = ps[r_]
                    def mmt(e, it_=it_, r_=r_, pst=pst, qlo=qlo):
                        r = e.matmul(pst[:, qlo:512], lhsT=trinc, rhs=spb[:, r_, qlo:512], start=False, stop=it_["first"])
                        if not it_["first"]:
                            r = e.matmul(pst[:, qlo:512], lhsT=onesneg, rhs=Lsb[:, it_["g"] % 4, qlo:512], start=False, stop=True)
                        return r
                    P.pe(mmt, reads=["cb", ("spb", r_), ("Lsb", it_["g"] % 4)], writes=[PSK(r_)])
                    P.act(lambda e, r_=r_, pst=pst, qlo=qlo: e.activation(wsb[:, r_, qlo:512], pst[:, qlo:512], AF.Exp),
                          reads=[PSK(r_)], writes=[("wsb", r_)])

                def stageB2(it_):
                    r_ = it_["g"] % NBUF
                    j, kb, qlo = it_["j"], it_["kb"], it_["qlo"]
                    ob = j % 2
                    pso = ps[6 + ob]
                    if it_["first"]:
                        P.pe(lambda e, pso=pso: e.matmul(pso[:, :], lhsT=zer[0:1, 0:128], rhs=zer[0:1, 0:512], start=True, stop=False),
                             reads=["zer"], writes=[PSK(6 + ob)])
                    P.pe(lambda e, pso=pso, kb=kb, r_=r_, qlo=qlo, last=it_["last"]: e.matmul(
                        pso[:, qlo:512], lhsT=vh[:, kb, :], rhs=wsb[:, r_, qlo:512], start=False, stop=last),
                        reads=[("vh", kb // 4), ("wsb", r_)], writes=[PSK(6 + ob)])
                    if it_["last"]:
                        if h % 2 == 0:
                            P.dve(lambda e, pso=pso, j=j, hp=hp: e.tensor_copy(aT[0:64, hp, 512 * j:512 * (j + 1)], pso[0:64, :]),
                                  reads=[PSK(6 + ob)], writes=[("aT", hp, 0)])
                        else:
                            P.dve(lambda e, pso=pso, j=j, hp=hp: e.tensor_copy(aT[64:128, hp, 512 * j:512 * (j + 1)], pso[64:128, :]),
                                  reads=[PSK(6 + ob)], writes=[("aT", hp, 1)])

                n_it = len(its)
                for i in range(n_it + 2):
                    if i < n_it:
                        stageA(its[i])
                    if 1 <= i <= n_it:
                        stageB1(its[i - 1])
                    if i >= 2:
                        stageB2(its[i - 2])
        P.barrier()
        with SBT(nc, "wout", [128, 8, D], BF16) as wout:
            load_w(wout, T["ab_w_out"][0], "wout", 8, 0, D)
            for tq in range(4):
                t0 = tq * 512
                for m in range(8):
                    bi = m % 2
                    def mm(e, m=m, bi=bi, t0=t0):
                        for k in range(8):
                            src = aT if k < 4 else pTo
                            r = e.matmul(ps[bi][:], lhsT=wout[:, k, m * 128:(m + 1) * 128], rhs=src[:, k % 4, t0:t0 + 512],
                                         start=(k == 0), stop=(k == 7))
                        return r
                    P.pe(mm, reads=[("wout", k) for k in range(8)] + ["aTall"], writes=[PSK(bi)])
                    resid_add(m, bi, ps[bi][:], t0, 512)


def stage_mix_s5(C, b, xT, ps, cf, cb, gains, rmsnorm_tile, load_w, linear_fm, resid_add):
    from contextlib import ExitStack
    nc, P, T = C.nc, C.P, C.T
    P.barrier()

    def PSK(i):
        return ("ps", i)
    ident = cf[:, 0, :]
    mask32 = cf[:, 5, :]
    TWO_PI = 6.283185
    INV2PI = 1.0 / (2.0 * math.pi)

    def V(fn, r, w):
        return P.dve(fn, reads=r, writes=w)

    def Aop(fn, r, w):
        return P.act(fn, reads=r, writes=w)

    with SBT(nc, "uT", [128, 8, S], BF16) as uT, SBT(nc, "dcol", [128, 8], F32) as dcol:
        with SBT(nc, "w_in", [128, 8, D], BF16) as w_in, SBT(nc, "sq", [128, 8, 512], BF16) as sq, \
                SBT(nc, "rstd", [128, 512], F32) as rstd, SBT(nc, "hT", [128, 8, 512], BF16) as hT:
            load_w(w_in, T["ssm_w_in"][0], "w_in", 8, 0, D)
            P.dma(lambda e: e.dma_start(out=dcol[:], in_=T["ssm_d"][0].rearrange("(t p) -> p t", p=128),
                                        allow_slow_non_contiguous=True), writes=["dcol"])
            for tq in range(4):
                rmsnorm_tile(hT, "hT", 1, tq * 512, 512, sq, rstd)

                def ev(m, bi, pap, tq=tq):
                    P.act(lambda e: e.activation(uT[:, m, tq * 512:(tq + 1) * 512], pap, AF.Copy),
                          reads=[PSK(bi)], writes=[("uT", m)])
                linear_fm(w_in, "w_in", hT, "hT", 8, 8, 512, ev)
        P.barrier()
        with ExitStack() as es:
            def A(name, shape, dt=F32):
                return es.enter_context(SBT(nc, name, shape, dt))
            lre = A("lre", [128, 4]); lim = A("lim", [128, 4]); ldt = A("ldt", [128, 4])
            dtt = A("dtt", [128, 4]); ar = A("ar", [128, 4]); an = A("an", [128, 4])
            arj = A("arj", [128, 9, 4]); tj = A("tj", [128, 9, 4]); tjc = A("tjc", [128, 9, 4])
            ti = A("ti", [128, 9, 4], I32); fr = A("fr", [128, 9, 4])
            mag = A("mag", [128, 9, 4]); sinj = A("sinj", [128, 9, 4]); cosj = A("cosj", [128, 9, 4])
            Lr = A("Lr", [128, 9, 4]); Li = A("Li", [128, 9, 4])
            nre = A("nre", [128, 4]); den = A("den", [128, 4]); t1 = A("t1", [128, 4]); t2 = A("t2", [128, 4])
            cr = A("cr", [128, 4]); ci = A("ci", [128, 4]); ti8 = A("ti8", [128, 4], I32); t8f = A("t8f", [128, 4])
            Fr = A("Fr", [128, 8, 4]); Fi = A("Fi", [128, 8, 4]); f1 = A("f1", [128, 8, 4]); f2 = A("f2", [128, 8, 4])
            Bst = A("Bst", [128, 2, 4, 16]); Cin = A("Cin", [64, 2, 2, 64]); Cst = A("Cst", [128, 2, 4, 16])
            l1 = A("l1", [128, 9, 4, 16]); l2 = A("l2", [128, 9, 4, 16])
            What = A("What", [128, 8, 2, 128])
            Wt = A("Wt", [128, 8, 2, 128], BF16)
            CL = A("CL", [128, 2, 9, 4, 16])
            LB = CL[:, :, 0:8]
            Qd = A("Qd", [128, 9, 2, 4, 32], BF16)
            Qf = A("Qf", [128, 2, 128])
            TtF = A("TtF", [128, 4, 128]); Tt = A("Tt", [128, 8, 128], BF16)
            cosT = A("cosT", [128, 4, 256]); sinT = A("sinT", [128, 4, 256])
            Xp = A("Xp", [128, 2, 4, 256]); xa = A("xa", [128, 4, 256]); xb = A("xb", [128, 4, 256])
            Ssc = A("Ssc", [128, 2, 4, 256]); tk = Ssc[:, 0]; tki = Ssc[:, 1].bitcast(I32); Hb = A("Hb", [128, 2, 4, 257], BF16)
            iota256 = cf[:, 6:8, :].rearrange("p a b -> p (a b)")
            What6 = What[:].rearrange("p t r (q g c) -> p t r q g c", q=4, g=2)
            Qf5 = Qf[:].rearrange("p r (q g c) -> p r q g c", q=4, g=2)
            Wv = Wt[:].rearrange("p t r n -> p (t r) n")
            V(lambda e: e.memset(What[:], 0.0), [], ["What"])
            V(lambda e: e.memset(Qd[:], 0.0), [], ["Qpad"])
            V(lambda e: e.memset(Qf[:], 0.0), [], ["Qf"])
            V(lambda e: e.memset(Hb[:], 0.0), [], ["Hb"])

            def bc(ap, shape):
                return ap.broadcast_to(shape)

            def partA(j):
                g0 = 8 * j
                P.dma(lambda e, g0=g0: e.dma_start(out=lre[:], in_=T["ssm_lam_re"][0, g0:g0 + 8, :].rearrange("(q g) p -> (g p) q", g=2),
                                                   allow_slow_non_contiguous=True), writes=["lre"])
                P.dma(lambda e, g0=g0: e.dma_start(out=lim[:], in_=T["ssm_lam_im"][0, g0:g0 + 8, :].rearrange("(q g) p -> (g p) q", g=2),
                                                   allow_slow_non_contiguous=True), writes=["lim"])
                for g2 in range(2):
                    P.dma(lambda e, g0=g0, g2=g2: e.dma_start(
                        out=ldt[64 * g2:64 * g2 + 64, :],
                        in_=T["ssm_log_dt"][0, g0:g0 + 8].rearrange("(q g) -> g q", g=2)[g2:g2 + 1, :].broadcast_to([64, 4]),
                        allow_slow_non_contiguous=True), writes=["ldt"])
                for ri, nm in enumerate(["ssm_b_re", "ssm_b_im"]):
                    P.dma(lambda e, g0=g0, ri=ri, nm=nm: e.dma_start(
                        out=Bst[:, ri, :, :], in_=T[nm][0, g0:g0 + 8].rearrange("(q g) p c -> (g p) q c", g=2)),
                        writes=["Bst"])
                for ri, nm in enumerate(["ssm_c_re", "ssm_c_im"]):
                    for q in range(4):
                        P.dma(lambda e, g0=g0, ri=ri, nm=nm, q=q: e.dma_start(
                            out=Cin[16 * q:16 * q + 16, ri, :, :],
                            in_=T[nm][0, g0 + 2 * q:g0 + 2 * q + 2].rearrange("g c p -> c g p")), writes=["Cin"])
                def trc(e):
                    for ri in range(2):
                        r = e.transpose(ps[0][:, ri * 64:(ri + 1) * 64], Cin[:, ri, :, :].rearrange("a g p -> a (g p)"), ident[0:64, 0:64])
                    return r
                P.pe(trc, reads=["Cin", "cf"], writes=[PSK(0)])
                Aop(lambda e: e.activation(Cst[:].rearrange("p r q c -> p (r q c)"), ps[0][:, 0:128], AF.Copy), [PSK(0)], ["Cst"])
                Aop(lambda e: e.activation(dtt[:], ldt[:], AF.Exp), ["ldt"], ["dtt"])
                def f_(e):
                    e.tensor_tensor(ar[:], lre[:], dtt[:], ALU.mult)
                    return e.tensor_tensor(an[:], lim[:], dtt[:], ALU.mult)
                V(f_, ["lre", "lim", "dtt"], ["ar", "an"])
                jv = bc(cf[:, 6, 0:9][:, :, None], [128, 9, 4])
                def f_(e):
                    e.tensor_tensor(arj[:], bc(ar[:, None, :], [128, 9, 4]), jv, ALU.mult)
                    return e.scalar_tensor_tensor(tj[:], bc(an[:, None, :], [128, 9, 4]), INV2PI, jv, ALU.mult, ALU.mult)
                V(f_, ["ar", "an", "cf"], ["arj", "tj"])
                Aop(lambda e: e.activation(mag[:], arj[:], AF.Exp), ["arj"], ["mag"])
                V(lambda e: e.tensor_copy(ti[:], tj[:]), ["tj"], ["ti"])
                V(lambda e: e.tensor_tensor(fr[:], tj[:], ti[:], ALU.subtract), ["tj", "ti"], ["fr"])
                Aop(lambda e: e.activation(sinj[:], fr[:], AF.Sin, scale=TWO_PI), ["fr"], ["sinj"])
                V(lambda e: e.tensor_scalar(tjc[:], tj[:], 0.25, None, ALU.add), ["tj"], ["tjc"])
                V(lambda e: e.tensor_copy(ti[:], tjc[:]), ["tjc"], ["ti"])
                V(lambda e: e.tensor_tensor(fr[:], tjc[:], ti[:], ALU.subtract), ["tjc", "ti"], ["fr"])
                Aop(lambda e: e.activation(cosj[:], fr[:], AF.Sin, scale=TWO_PI), ["fr"], ["cosj"])
                def f_(e):
                    e.tensor_tensor(Lr[:], mag[:], cosj[:], ALU.mult)
                    return e.tensor_tensor(Li[:], mag[:], sinj[:], ALU.mult)
                V(f_, ["mag", "cosj", "sinj"], ["Lr", "Li"])
                def f_(e):
                    e.tensor_scalar(nre[:], Lr[:, 1, :], -1.0, None, ALU.add)
                    e.tensor_tensor(t1[:], lre[:], lre[:], ALU.mult)
                    return e.tensor_tensor(t2[:], lim[:], lim[:], ALU.mult)
                V(f_, ["Lr", "lre", "lim"], ["nre", "t1", "t2"])
                V(lambda e: e.tensor_tensor(den[:], t1[:], t2[:], ALU.add), ["t1", "t2"], ["den"])
                V(lambda e: e.reciprocal(den[:], den[:]), ["den"], ["den"])
                def f_(e):
                    e.tensor_tensor(t1[:], nre[:], lre[:], ALU.mult)
                    return e.tensor_tensor(t2[:], Li[:, 1, :], lim[:], ALU.mult)
                V(f_, ["nre", "lre", "Li", "lim", "den"], ["t1", "t2"])
                V(lambda e: e.tensor_tensor(cr[:], t1[:], t2[:], ALU.add), ["t1", "t2"], ["cr"])
                V(lambda e: e.tensor_tensor(cr[:], cr[:], den[:], ALU.mult), ["cr", "den"], ["cr"])
                def f_(e):
                    e.tensor_tensor(t1[:], Li[:, 1, :], lre[:], ALU.mult)
                    return e.tensor_tensor(t2[:], nre[:], lim[:], ALU.mult)
                V(f_, ["nre", "lre", "Li", "lim", "cr"], ["t1", "t2"])
                V(lambda e: e.tensor_tensor(ci[:], t1[:], t2[:], ALU.subtract), ["t1", "t2"], ["ci"])
                V(lambda e: e.tensor_tensor(ci[:], ci[:], den[:], ALU.mult), ["ci", "den"], ["ci"])
                crb = bc(cr[:, None, :], [128, 8, 4]); cib = bc(ci[:, None, :], [128, 8, 4])
                def f_(e, crb=crb, cib=cib):
                    e.tensor_tensor(f1[:], Lr[:, 0:8, :], crb, ALU.mult)
                    return e.tensor_tensor(f2[:], Li[:, 0:8, :], cib, ALU.mult)
                V(f_, ["Lr", "Li", "cr", "ci"], ["f1", "f2"])
                V(lambda e: e.tensor_tensor(Fr[:], f1[:], f2[:], ALU.subtract), ["f1", "f2"], ["Fr"])
                def f_(e, crb=crb, cib=cib):
                    e.tensor_tensor(f1[:], Lr[:, 0:8, :], cib, ALU.mult)
                    return e.tensor_tensor(f2[:], Li[:, 0:8, :], crb, ALU.mult)
                V(f_, ["Lr", "Li", "cr", "ci", "Fr"], ["f1", "f2"])
                V(lambda e: e.tensor_tensor(Fi[:], f1[:], f2[:], ALU.add), ["f1", "f2"], ["Fi"])
                sh8 = [128, 8, 4, 16]
                Frb = bc(Fr[:, :, :, None], sh8); Fib = bc(Fi[:, :, :, None], sh8)
                B0 = bc(Bst[:, 0, None, :, :], sh8); B1 = bc(Bst[:, 1, None, :, :], sh8)
                def f_(e, Frb=Frb, Fib=Fib, B0=B0, B1=B1):
                    e.tensor_tensor(l1[:, 0:8], Frb, B0, ALU.mult)
                    return e.tensor_tensor(l2[:, 0:8], Fib, B1, ALU.mult)
                V(f_, ["Fr", "Fi", "Bst"], ["l1", "l2"])
                V(lambda e: e.tensor_tensor(LB[:, 0], l1[:, 0:8], l2[:, 0:8], ALU.subtract), ["l1", "l2"], ["LB0", "CL0"])
                def f_(e, Frb=Frb, Fib=Fib, B0=B0, B1=B1):
                    e.tensor_tensor(l1[:, 0:8], Frb, B1, ALU.mult)
                    return e.tensor_tensor(l2[:, 0:8], Fib, B0, ALU.mult)
                V(f_, ["Fr", "Fi", "Bst", "LB0"], ["l1", "l2"])
                V(lambda e: e.tensor_tensor(LB[:, 1], l1[:, 0:8], l2[:, 0:8], ALU.add), ["l1", "l2"], ["LB1", "CL1"])
                def f_(e):
                    for g2 in range(2):
                        for ri in range(2):
                            r = e.tensor_copy(What6[64 * g2:64 * g2 + 64, :, ri, :, g2, :], LB[64 * g2:64 * g2 + 64, ri, :, :, :])
                    return r
                V(f_, ["LB0", "LB1"], ["What"])
            def partB(j):
                g0 = 8 * j
                for c4 in range(4):
                    bi = c4 % 2
                    def trw(e, c4=c4, bi=bi):
                        for i4 in range(4):
                            c = c4 * 4 + i4
                            r = e.transpose(ps[bi][:, i4 * 128:(i4 + 1) * 128], What[:, c // 2, c % 2, :], ident)
                        return r
                    P.pe(trw, reads=["What", "cf"], writes=[PSK(bi)])
                    if c4 % 2 == 0:
                        Aop(lambda e, c4=c4, bi=bi: e.activation(Wv[:, c4 * 4:c4 * 4 + 4, :], ps[bi][:].rearrange("p (a n) -> p a n", a=4), AF.Copy),
                            [PSK(bi)], [("Wpad", c4)])
                    else:
                        V(lambda e, c4=c4, bi=bi: e.tensor_copy(Wv[:, c4 * 4:c4 * 4 + 4, :], ps[bi][:].rearrange("p (a n) -> p a n", a=4)),
                          [PSK(bi)], [("Wpad", c4)])
                sh9 = [128, 9, 4, 16]
                Lrb = bc(Lr[:, :, :, None], sh9); Lib = bc(Li[:, :, :, None], sh9)
                C0 = bc(Cst[:, 0, None, :, :], sh9); C1 = bc(Cst[:, 1, None, :, :], sh9)
                def f_(e, Lrb=Lrb, Lib=Lib, C0=C0, C1=C1):
                    e.tensor_tensor(l1[:], Lrb, C0, ALU.mult)
                    return e.tensor_tensor(l2[:], Lib, C1, ALU.mult)
                V(f_, ["Lr", "Li", "Cst", "LB1"], ["l1", "l2"])
                V(lambda e: e.tensor_tensor(CL[:, 0], l1[:], l2[:], ALU.subtract), ["l1", "l2"], ["CL0", "LB0", "LB1"])
                def f_(e, Lrb=Lrb, Lib=Lib, C0=C0, C1=C1):
                    e.tensor_tensor(l1[:], Lib, C0, ALU.mult)
                    return e.tensor_tensor(l2[:], Lrb, C1, ALU.mult)
                V(f_, ["Lr", "Li", "Cst", "CL0"], ["l1", "l2"])
                V(lambda e: e.scalar_tensor_tensor(CL[:, 1], l1[:], -1.0, l2[:], ALU.mult, ALU.subtract), ["l1", "l2"], ["CL1", "LB0", "LB1"])
                def f_(e):
                    for g2 in range(2):
                        for ri in range(2):
                            e.tensor_copy(Qd[64 * g2:64 * g2 + 64, :, ri, :, 16 * g2:16 * g2 + 16], CL[64 * g2:64 * g2 + 64, ri, :, :, :])
                            r = e.tensor_copy(Qf5[64 * g2:64 * g2 + 64, ri, :, g2, :], CL[64 * g2:64 * g2 + 64, ri, 0, :, :])
                    return r
                V(f_, ["CL0", "CL1"], ["Qpad", "Qf"])
                for half in range(2):
                    bi = half
                    def mmt(e, half=half, bi=bi):
                        for i4 in range(4):
                            tau = half * 4 + i4
                            e.matmul(ps[bi][:, i4 * 128:(i4 + 1) * 128], lhsT=What[:, tau, 0, :], rhs=Qf[:, 0, :], start=True, stop=False)
                            r = e.matmul(ps[bi][:, i4 * 128:(i4 + 1) * 128], lhsT=What[:, tau, 1, :], rhs=Qf[:, 1, :], start=False, stop=True)
                        return r
                    P.pe(mmt, reads=["What", "Qf"], writes=[PSK(bi)])
                    m4 = bc(mask32[:, None, :], [128, 4, 128])
                    if half == 0:
                        V(lambda e, bi=bi, m4=m4: e.tensor_tensor(TtF[:], ps[bi][:].rearrange("p (a n) -> p a n", a=4), m4, ALU.mult),
                          [PSK(bi), "cf"], ["TtF"])
                        V(lambda e, j=j: e.scalar_tensor_tensor(TtF[:, 0, :], ident, dcol[:, j:j + 1], TtF[:, 0, :], ALU.mult, ALU.add),
                          ["TtF", "dcol", "cf"], ["TtF"])
                        V(lambda e: e.tensor_copy(Tt[:, 0:4, :], TtF[:]), ["TtF"], [("Tt", 0)])
                    else:
                        V(lambda e, bi=bi, m4=m4: e.tensor_tensor(Tt[:, 4:8, :], ps[bi][:].rearrange("p (a n) -> p a n", a=4), m4, ALU.mult),
                          [PSK(bi), "cf"], [("Tt", 1)])
                uv = uT[:, j, :].rearrange("p (k s) -> p s k", s=8)
                def mmx(e, uv=uv):
                    for ri in range(2):
                        for tau in range(8):
                            for q in range(4):
                                r = e.matmul(ps[2 + q][:, ri * 256:(ri + 1) * 256], lhsT=Wt[32 * q:32 * q + 32, tau, ri, :],
                                             rhs=uv[32 * q:32 * q + 32, 7 - tau, :], start=(tau == 0), stop=(tau == 7),
                                             tile_position=(32 * q, 0))
                    return r
                P.pe(mmx, reads=[("Wpad", c4) for c4 in range(4)] + [("uT", j, s_) for s_ in range(8)], writes=[PSK(2 + q) for q in range(4)])
                V(lambda e: e.tensor_copy(ti8[:], tj[:, 8, :]), ["tj"], ["ti8"])
                V(lambda e: e.tensor_tensor(t8f[:], tj[:, 8, :], ti8[:], ALU.subtract), ["tj", "ti8"], ["t8f"])
                V(lambda e: e.tensor_tensor(tk, bc(t8f[:, :, None], [128, 4, 256]), bc(iota256[:, None, :], [128, 4, 256]), ALU.mult),
                  ["t8f", "cf"], ["tk"] + [("Ssc", ri_, q_) for ri_ in range(2) for q_ in range(4)])
                V(lambda e: e.tensor_copy(tki, tk), ["tk"], ["tki"])
                V(lambda e: e.tensor_tensor(xa[:], tk, tki, ALU.subtract), ["tk", "tki"], ["xa"])
                Aop(lambda e: e.activation(sinT[:], xa[:], AF.Sin, scale=TWO_PI), ["xa"], ["sinT"])
                V(lambda e: e.tensor_scalar(tk, tk, 0.25, None, ALU.add), ["tk", "tki"], ["tk"])
                V(lambda e: e.tensor_copy(tki, tk), ["tk"], ["tki"])
                V(lambda e: e.tensor_tensor(xb[:], tk, tki, ALU.subtract), ["tk", "tki"], ["xb"])
                Aop(lambda e: e.activation(cosT[:], xb[:], AF.Sin, scale=TWO_PI), ["xb"], ["cosT"])
                for q in range(4):
                    Xr = ps[2 + q][:, 0:256]; Xi = ps[2 + q][:, 256:512]
                    def f_(e, q=q, Xr=Xr, Xi=Xi):
                        e.tensor_tensor(xa[:, q, :], cosT[:, q, :], Xr, ALU.mult)
                        return e.tensor_tensor(xb[:, q, :], sinT[:, q, :], Xi, ALU.mult)
                    V(f_, [PSK(2 + q), "cosT", "sinT", "xa", "xb"], [("xa", q), ("xb", q)])
                    V(lambda e, q=q: e.tensor_tensor(Xp[:, 0, q, :], xa[:, q, :], xb[:, q, :], ALU.add), [("xa", q), ("xb", q)], [("Xp", 0, q)])
                    def f_(e, q=q, Xr=Xr, Xi=Xi):
                        e.tensor_tensor(xa[:, q, :], cosT[:, q, :], Xi, ALU.mult)
                        return e.tensor_tensor(xb[:, q, :], sinT[:, q, :], Xr, ALU.mult)
                    V(f_, [PSK(2 + q), "cosT", "sinT", ("Xp", 0, q)], [("xa", q), ("xb", q)])
                    V(lambda e, q=q: e.tensor_tensor(Xp[:, 1, q, :], xa[:, q, :], xb[:, q, :], ALU.subtract), [("xa", q), ("xb", q)], [("Xp", 1, q)])
                    for ri in range(2):
                        V(lambda e, q=q, ri=ri: e.tensor_tensor_scan(Ssc[:, ri, q, :], mag[:, 8, q:q + 1].to_broadcast([128, 256]),
                                                                     Xp[:, ri, q, :], 0.0, ALU.mult, ALU.add),
                          [("Xp", ri, q), "mag"], [("Ssc", ri, q), "tk", "tki"])
                allS = [("Ssc", ri, q) for ri in range(2) for q in range(4)]
                allx = [("xa", q) for q in range(4)] + [("xb", q) for q in range(4)]
                def f_(e):
                    e.tensor_tensor(xa[:], cosT[:], Ssc[:, 0], ALU.mult)
                    return e.tensor_tensor(xb[:], sinT[:], Ssc[:, 1], ALU.mult)
                V(f_, allS + ["cosT", "sinT"], allx + ["xa", "xb"])
                V(lambda e: e.tensor_tensor(Hb[:, 0, :, 1:257], xa[:], xb[:], ALU.subtract), ["xa", "xb"], ["Hb0"])
                def f_(e):
                    e.tensor_tensor(xa[:], cosT[:], Ssc[:, 1], ALU.mult)
                    return e.tensor_tensor(xb[:], sinT[:], Ssc[:, 0], ALU.mult)
                V(f_, allS + ["cosT", "sinT", "Hb0"], allx + ["xa", "xb"])
                V(lambda e: e.tensor_tensor(Hb[:, 1, :, 1:257], xa[:], xb[:], ALU.add), ["xa", "xb"], ["Hb1"])
            def partC(j):
                uv = uT[:, j, :].rearrange("p (k s) -> p s k", s=8)
                for tp in range(7, -1, -1):
                    bi = 6 + (tp % 2)
                    def mmy(e, tp=tp, bi=bi, uv=uv):
                        for s_ in range(tp + 1):
                            e.matmul(ps[bi][:, 0:256], lhsT=Tt[:, tp - s_, :], rhs=uv[:, s_, :], start=(s_ == 0), stop=False)
                        for ri in range(2):
                            for q in range(4):
                                r = e.matmul(ps[bi][32 * q:32 * q + 32, 0:256], lhsT=Qd[:, tp + 1, ri, q, :], rhs=Hb[:, ri, q, 0:256], start=False,
                                             stop=(ri == 1), tile_position=(0, 32 * q))
                        return r
                    P.pe(mmy, reads=[("uT", j, s_) for s_ in range(tp + 1)] + [("Tt", 0), ("Tt", 1), "Qpad", "Hb0", "Hb1"], writes=[PSK(bi)])
                    Aop(lambda e, tp=tp, bi=bi, uv=uv: e.activation(uv[:, tp, :], ps[bi][:, 0:256], AF.Gelu_apprx_tanh),
                        [PSK(bi)], [("uT", j, tp)])
            partA(0)
            for j in range(8):
                partB(j)
                if j < 7:
                    partA(j + 1)
                partC(j)
        P.barrier()
        with SBT(nc, "wg", [128, 2, 2, 8, 512], BF16) as wg, SBT(nc, "sig", [128, 2, 512], F32) as sig, \
                SBT(nc, "mixb", [128, 2, 512], F32) as mixb:
            wglu = T["ssm_w_glu"][0].rearrange("(k p) n -> p k n", p=128)
            it = 0
            for mg in range(2):
                wb = mg % 2
                for vg in range(2):
                    c0 = vg * D + mg * 512
                    P.dma(lambda e, wb=wb, vg=vg, c0=c0: e.dma_start(out=wg[:, wb, vg, :, :], in_=wglu[:, :, c0:c0 + 512]),
                          writes=[("wg", wb, vg)], q="pool")
                for ml in range(4):
                    m = mg * 4 + ml
                    for tq in range(4):
                        pb = it % 2
                        it += 1
                        for vg in range(2):
                            bi = 2 + 2 * vg + pb
                            def mm(e, wb=wb, vg=vg, bi=bi, tq=tq, ml=ml):
                                for k in range(8):
                                    r = e.matmul(ps[bi][:], lhsT=wg[:, wb, vg, k, ml * 128:(ml + 1) * 128], rhs=uT[:, k, tq * 512:(tq + 1) * 512],
                                                 start=(k == 0), stop=(k == 7))
                                return r
                            P.pe(mm, reads=[("wg", wb, vg)], writes=[PSK(bi)])
                        Aop(lambda e, pb=pb: e.activation(sig[:, pb, :], ps[4 + pb][:], AF.Sigmoid), [PSK(4 + pb)], [("sig", pb)])
                        V(lambda e, pb=pb: e.tensor_tensor(mixb[:, pb, :], ps[2 + pb][:], sig[:, pb, :], ALU.mult), [PSK(2 + pb), ("sig", pb)], [("mixb", pb)])
                        V(lambda e, pb=pb, m=m, tq=tq: e.tensor_tensor(xT[:, m, tq * 512:(tq + 1) * 512], xT[:, m, tq * 512:(tq + 1) * 512],
                                                                     mixb[:, pb, :], ALU.add), [("mixb", pb), ("xT", m)], [("xT", m)])


_CACHE = {}


def kernel(**inputs):
    if "prog" not in _CACHE:
        _CACHE["prog"] = build_program()
    nc, _ = _CACHE["prog"]
    consts = make_consts()
    in_maps = []
    for c in range(8):
        m = {}
        for name, shape in INPUT_SPECS:
            if name == "consts":
                m[name] = consts
            elif name in ("x", "mem"):
                m[name] = np.ascontiguousarray(np.asarray(inputs[name], dtype=np.float32)[c * NB:(c + 1) * NB])
            else:
                m[name] = np.ascontiguousarray(np.asarray(inputs[name], dtype=np.float32))
        in_maps.append(m)
    res = run_bass_kernel_spmd(nc, in_maps, core_ids=list(range(8)))
    return np.concatenate([r["out"] for r in res.results], axis=0)
```
